# Optimizing a Trainium2 kernel written in Bass

```python
import jax, jax.numpy as jnp
from jax import lax
import numpy as np

D_MODEL = 4096
BATCH = 1
SEQ = 8192
DEPTH = 1

HEAD_DIM = 128
N_HEADS = D_MODEL // HEAD_DIM
D_MIX = N_HEADS * HEAD_DIM
SB_HEADS = N_HEADS // 2
NSA_HEADS = N_HEADS - SB_HEADS
NSA_KV = 2
NSA_HPG = NSA_HEADS // NSA_KV
L_CMP = 32
STRIDE_CMP = 16
CMP_HIDDEN = 256
L_SLC = 64
N_TOPK = 16
WINDOW = 512
Q_BLOCK = 128
D_FF = -(-8 * D_MODEL // (3 * 256)) * 256
EPS = 1e-6
NEG_INF = -1e30
FORCE_BONUS = 1e6

SB_COLS = 3 * SB_HEADS * HEAD_DIM
NSA_Q_COLS = NSA_HEADS * HEAD_DIM
NSA_KV_COLS = 3 * 2 * NSA_KV * HEAD_DIM
NSA_GATE_COLS = 3 * NSA_HEADS
D_IN = SB_COLS + NSA_Q_COLS + NSA_KV_COLS + NSA_GATE_COLS

kernel_name = 'hymba_stickbreaking_nsa_block'


def rmsnorm(x, g):
    xf = x.astype(jnp.float32)
    xf = xf * lax.rsqrt(jnp.mean(xf * xf, axis=-1, keepdims=True) + EPS)
    return xf.astype(x.dtype) * g


def masked_softmax(s, mask):
    p = jax.nn.softmax(jnp.where(mask, s, NEG_INF), axis=-1)
    return p * mask


def to_heads(t, n):
    b, s, _ = t.shape
    return t.reshape(b, s, n, HEAD_DIM).transpose(0, 2, 1, 3)


def stick_breaking_attention(q, k, v):
    B, H, S, D = q.shape
    nqb = S // Q_BLOCK
    scale = D ** -0.5
    qb = q.reshape(B, H, nqb, Q_BLOCK, D).transpose(2, 0, 1, 3, 4)
    key_pos = jnp.arange(S)

    def block(args):
        i, q_blk = args
        t = i * Q_BLOCK + jnp.arange(Q_BLOCK)
        z = jnp.einsum('bhqd,bhkd->bhqk', q_blk, k).astype(jnp.float32) * scale
        mask = key_pos[None, :] < t[:, None]
        log_not = jnp.where(mask, jax.nn.log_sigmoid(-z), 0.0)
        tail = lax.cumsum(log_not, axis=3, reverse=True) - log_not
        a = jnp.where(mask, jnp.exp(jax.nn.log_sigmoid(z) + tail), 0.0)
        return jnp.einsum('bhqk,bhkd->bhqd', a.astype(v.dtype), v)

    out = lax.map(block, (jnp.arange(nqb), qb))
    return out.transpose(1, 2, 0, 3, 4).reshape(B, H, S, D)


def compress_blocks(kv, pos_emb, w1, w2):
    S = kv.shape[2]
    n_cmp = (S - L_CMP) // STRIDE_CMP + 1
    idx = jnp.arange(n_cmp)[:, None] * STRIDE_CMP + jnp.arange(L_CMP)[None, :]
    blocks = kv[:, :, idx] + pos_emb
    flat = blocks.reshape(blocks.shape[0], blocks.shape[1], n_cmp, L_CMP * HEAD_DIM)
    return jax.nn.gelu(flat @ w1) @ w2


def cmp_to_slc_matrix(n_cmp, n_slc):
    rs = L_SLC // STRIDE_CMP
    rc = L_CMP // STRIDE_CMP
    j = jnp.arange(n_slc)[:, None, None]
    m = jnp.arange(rs)[None, :, None]
    n = jnp.arange(rc)[None, None, :]
    src = (rs * j - m - n).reshape(n_slc, rs * rc)
    hit = src[None, :, :] == jnp.arange(n_cmp)[:, None, None]
    return jnp.sum(hit, axis=-1).astype(jnp.float32)


def gather_blocks(kb, idx):
    return jax.vmap(jax.vmap(lambda a, i: a[i]))(kb, idx)


def native_sparse_attention(q, k_cmp, v_cmp, k_slc, v_slc, k_win, v_win, gates, slopes):
    B, G, HPG, S, D = q.shape
    nqb = S // Q_BLOCK
    n_cmp = k_cmp.shape[2]
    n_slc = S // L_SLC
    n_top = min(N_TOPK, n_slc)
    scale = D ** -0.5
    sl = slopes[None, :, :, None, None]
    cmp_end = jnp.arange(n_cmp) * STRIDE_CMP + L_CMP - 1
    m_sel = cmp_to_slc_matrix(n_cmp, n_slc)
    k_slc_b = k_slc.reshape(B, G, n_slc, L_SLC, D)
    v_slc_b = v_slc.reshape(B, G, n_slc, L_SLC, D)
    pad = ((0, 0), (0, 0), (WINDOW, 0), (0, 0))
    k_win_p = jnp.pad(k_win, pad)
    v_win_p = jnp.pad(v_win, pad)
    blk = jnp.arange(n_slc)
    qb = q.reshape(B, G, HPG, nqb, Q_BLOCK, D).transpose(3, 0, 1, 2, 4, 5)
    gb = gates.reshape(B, G, HPG, nqb, Q_BLOCK, 3).transpose(3, 0, 1, 2, 4, 5)

    def block(args):
        i, q_blk, g_blk = args
        t = i * Q_BLOCK + jnp.arange(Q_BLOCK)
        dist_c = (t[:, None] - cmp_end[None, :]).astype(jnp.float32)
        s_c = jnp.einsum('bghqd,bgnd->bghqn', q_blk, k_cmp).astype(jnp.float32) * scale - sl * dist_c
        p_c = masked_softmax(s_c, cmp_end[None, :] <= t[:, None])
        o_c = jnp.einsum('bghqn,bgnd->bghqd', p_c.astype(v_cmp.dtype), v_cmp)
        imp = jnp.einsum('bghqn,nj->bgqj', p_c, m_sel)
        cur = t // L_SLC
        valid = blk[None, :] <= cur[:, None]
        forced = (blk[None, :] == 0) | (blk[None, :] == cur[:, None]) | (blk[None, :] == cur[:, None] - 1)
        score = jnp.where(valid, imp + jnp.where(forced, FORCE_BONUS, 0.0), NEG_INF)
        _, idx = lax.top_k(score, n_top)
        k_sel = gather_blocks(k_slc_b, idx).reshape(B, G, Q_BLOCK, n_top * L_SLC, D)
        v_sel = gather_blocks(v_slc_b, idx).reshape(B, G, Q_BLOCK, n_top * L_SLC, D)
        pos_sel = (idx[..., None] * L_SLC + jnp.arange(L_SLC)).reshape(B, G, Q_BLOCK, n_top * L_SLC)
        diff_s = (t[None, None, :, None] - pos_sel)[:, :, None]
        s_s = jnp.einsum('bghqd,bgqnd->bghqn', q_blk, k_sel).astype(jnp.float32) * scale - sl * diff_s.astype(jnp.float32)
        p_s = masked_softmax(s_s, diff_s >= 0)
        o_s = jnp.einsum('bghqn,bgqnd->bghqd', p_s.astype(v_sel.dtype), v_sel)
        k_w = lax.dynamic_slice_in_dim(k_win_p, i * Q_BLOCK, WINDOW + Q_BLOCK, axis=2)
        v_w = lax.dynamic_slice_in_dim(v_win_p, i * Q_BLOCK, WINDOW + Q_BLOCK, axis=2)
        pos_w = i * Q_BLOCK - WINDOW + jnp.arange(WINDOW + Q_BLOCK)
        diff_w = t[:, None] - pos_w[None, :]
        mask_w = (diff_w >= 0) & (diff_w < WINDOW) & (pos_w[None, :] >= 0)
        s_w = jnp.einsum('bghqd,bgkd->bghqk', q_blk, k_w).astype(jnp.float32) * scale - sl * diff_w.astype(jnp.float32)
        p_w = masked_softmax(s_w, mask_w)
        o_w = jnp.einsum('bghqk,bgkd->bghqd', p_w.astype(v_w.dtype), v_w)
        return g_blk[..., 0:1] * o_c + g_blk[..., 1:2] * o_s + g_blk[..., 2:3] * o_w

    out = lax.map(block, (jnp.arange(nqb), qb, gb))
    return out.transpose(1, 2, 3, 0, 4, 5).reshape(B, G, HPG, S, D)


def alibi_slopes():
    h = jnp.arange(1, NSA_HEADS + 1, dtype=jnp.float32)
    return (2.0 ** (-8.0 * h / NSA_HEADS)).reshape(NSA_KV, NSA_HPG)


def setup_inputs(seed: int = 0) -> dict:
    key = jax.random.key(seed)
    ks = jax.random.split(key, 20)
    f32 = jnp.float32
    nrm = lambda k, shape, s: jax.random.normal(k, shape, f32) * s
    gain = lambda k, n: 1.0 + 0.02 * jax.random.normal(k, (DEPTH, n), f32)
    return {
        'x': jax.random.normal(ks[0], (BATCH, SEQ, D_MODEL), f32),
        'attn_norm': gain(ks[1], D_MODEL),
        'w_in': nrm(ks[2], (DEPTH, D_MODEL, D_IN), D_MODEL ** -0.5),
        'pos_cmp_k': nrm(ks[3], (DEPTH, L_CMP, HEAD_DIM), 0.1),
        'pos_cmp_v': nrm(ks[4], (DEPTH, L_CMP, HEAD_DIM), 0.1),
        'w_cmp_k1': nrm(ks[5], (DEPTH, L_CMP * HEAD_DIM, CMP_HIDDEN), (L_CMP * HEAD_DIM) ** -0.5),
        'w_cmp_k2': nrm(ks[6], (DEPTH, CMP_HIDDEN, HEAD_DIM), CMP_HIDDEN ** -0.5),
        'w_cmp_v1': nrm(ks[7], (DEPTH, L_CMP * HEAD_DIM, CMP_HIDDEN), (L_CMP * HEAD_DIM) ** -0.5),
        'w_cmp_v2': nrm(ks[8], (DEPTH, CMP_HIDDEN, HEAD_DIM), CMP_HIDDEN ** -0.5),
        'norm_sb': gain(ks[9], SB_HEADS * HEAD_DIM),
        'norm_nsa': gain(ks[10], NSA_HEADS * HEAD_DIM),
        'w_out': nrm(ks[11], (DEPTH, D_MIX, D_MODEL), D_MIX ** -0.5),
        'ffn_norm': gain(ks[12], D_MODEL),
        'w_gate': nrm(ks[13], (DEPTH, D_MODEL, D_FF), D_MODEL ** -0.5),
        'w_up': nrm(ks[14], (DEPTH, D_MODEL, D_FF), D_MODEL ** -0.5),
        'w_down': nrm(ks[15], (DEPTH, D_FF, D_MODEL), D_FF ** -0.5),
        'final_norm': 1.0 + 0.02 * jax.random.normal(ks[16], (D_MODEL,), f32),
    }


def reference(x, attn_norm, w_in, pos_cmp_k, pos_cmp_v, w_cmp_k1, w_cmp_k2, w_cmp_v1, w_cmp_v2,
              norm_sb, norm_nsa, w_out, ffn_norm, w_gate, w_up, w_down, final_norm):
    B, S, _ = x.shape
    slopes = alibi_slopes()
    splits = np.cumsum([SB_COLS // 3] * 3 + [NSA_Q_COLS] + [NSA_KV * HEAD_DIM] * 6).tolist()
    for l in range(DEPTH):
        h = rmsnorm(x, attn_norm[l])
        proj = h @ w_in[l]
        (q_sb, k_sb, v_sb, q_n, kc, vc, ksl, vsl, kw, vw, g_n) = jnp.split(proj, splits, axis=-1)
        o_sb = stick_breaking_attention(to_heads(q_sb, SB_HEADS), to_heads(k_sb, SB_HEADS), to_heads(v_sb, SB_HEADS))
        o_sb = o_sb.transpose(0, 2, 1, 3).reshape(B, S, SB_HEADS * HEAD_DIM)
        q_nsa = to_heads(q_n, NSA_HEADS).reshape(B, NSA_KV, NSA_HPG, S, HEAD_DIM)
        k_cmp = compress_blocks(to_heads(kc, NSA_KV), pos_cmp_k[l], w_cmp_k1[l], w_cmp_k2[l])
        v_cmp = compress_blocks(to_heads(vc, NSA_KV), pos_cmp_v[l], w_cmp_v1[l], w_cmp_v2[l])
        gates = jax.nn.sigmoid(g_n).reshape(B, S, NSA_KV, NSA_HPG, 3).transpose(0, 2, 3, 1, 4)
        o_nsa = native_sparse_attention(q_nsa, k_cmp, v_cmp, to_heads(ksl, NSA_KV), to_heads(vsl, NSA_KV),
                                        to_heads(kw, NSA_KV), to_heads(vw, NSA_KV), gates, slopes)
        o_nsa = o_nsa.transpose(0, 3, 1, 2, 4).reshape(B, S, NSA_HEADS * HEAD_DIM)
        mixed = jnp.concatenate([rmsnorm(o_sb, norm_sb[l]), rmsnorm(o_nsa, norm_nsa[l])], axis=-1)
        x = x + mixed @ w_out[l]
        h = rmsnorm(x, ffn_norm[l])
        x = x + (jax.nn.silu(h @ w_gate[l]) * (h @ w_up[l])) @ w_down[l]
    return rmsnorm(x, final_norm)
```

```python
import contextlib
import numpy as np
import ml_dtypes
import concourse.bass as bass
import concourse.mybir as mybir
from concourse.bass_utils import run_bass_kernel_spmd

F32 = mybir.dt.float32
BF16 = mybir.dt.bfloat16
AF = mybir.ActivationFunctionType
ALU = mybir.AluOpType
NPBF = ml_dtypes.bfloat16

NCORES = 8
EPS = 1e-6
ALL_ENG = ("pe", "act", "dve", "pool", "sp")


class _Buf:
    __slots__ = ("writer", "readers")

    def __init__(self):
        self.writer = None
        self.readers = []


class _Op:
    __slots__ = ("eng", "fn", "is_dma", "dkey", "dval", "deps", "signal", "count")


class Prog:
    def __init__(self, nc, same_engine_sync=True):
        self.nc = nc
        self.ops = {e: [] for e in ALL_ENG}
        self.bufs = {}
        self.dma_counts = {}
        self.phase_keys = set()
        self.same_engine_sync = same_engine_sync
        self.ecount = {e: 0 for e in ALL_ENG}
        self.semstack = contextlib.ExitStack()
        self.esem = None
        self.dsem = {}
        self.barrier = []

    def _add(self, eng, fn, reads, writes, is_dma=False, dkey=None):
        op = _Op()
        op.eng, op.fn, op.is_dma, op.dkey = eng, fn, is_dma, dkey
        op.dval, op.signal, op.count = None, False, None
        deps = []
        for r in reads:
            b = self.bufs.get(r)
            if b is None:
                b = self.bufs[r] = _Buf()
            if b.writer is not None:
                deps.append(b.writer)
            b.readers.append(op)
        for w in writes:
            b = self.bufs.get(w)
            if b is None:
                b = self.bufs[w] = _Buf()
            if b.writer is not None:
                deps.append(b.writer)
            deps.extend(b.readers)
            b.writer = op
            b.readers = []
        out, seen = [], set()
        for d in deps:
            if d is op or id(d) in seen:
                continue
            seen.add(id(d))
            if not d.is_dma and d.eng == eng and (eng == "pe" or not self.same_engine_sync):
                continue
            out.append(d)
        op.deps = out
        if is_dma:
            c = self.dma_counts.get(dkey, 0) + 16
            self.dma_counts[dkey] = c
            self.phase_keys.add(dkey)
            op.dval = c
        self.ops[eng].append(op)
        return op

    def op(self, eng, fn, reads=(), writes=()):
        return self._add(eng, fn, reads, writes)

    def dma(self, eng, fn, reads=(), writes=(), key=None):
        return self._add(eng, fn, reads, writes, is_dma=True, dkey=key)

    def flush(self, final_wait_eng="sp"):
        nc = self.nc
        if self.esem is None:
            self.esem = {e: self.semstack.enter_context(nc.semaphore("s_" + e)) for e in ALL_ENG}
        for k in self.dma_counts:
            if k not in self.dsem:
                self.dsem[k] = self.semstack.enter_context(nc.semaphore("d_%d" % len(self.dsem)))
        esem, dsem = self.esem, self.dsem
        for e in ALL_ENG:
            ops = self.ops[e]
            for op in ops:
                for d in op.deps:
                    if not d.is_dma:
                        d.signal = True
            for op in reversed(ops):
                if not op.is_dma:
                    op.signal = True
                    break
        for e in ALL_ENG:
            c = self.ecount[e]
            for op in self.ops[e]:
                if op.signal and not op.is_dma:
                    c += 1
                    op.count = c
            self.ecount[e] = c
        barrier = self.barrier
        with nc.Block() as block:
            engmap = {"pe": block.tensor, "act": block.scalar, "dve": block.vector,
                      "pool": block.gpsimd, "sp": block.sync}

            def make(e):
                def body(eng):
                    known = {}
                    if self.ops[e]:
                        for key, sem, val in barrier:
                            if key == ("e", e):
                                known[key] = val
                                continue
                            eng.wait_ge(sem, val)
                            known[key] = val
                    for op in self.ops[e]:
                        for d in op.deps:
                            if d.is_dma:
                                key, val, sem = ("d", d.dkey), d.dval, dsem[d.dkey]
                            else:
                                key, val, sem = ("e", d.eng), d.count, esem[d.eng]
                            if known.get(key, 0) >= val:
                                continue
                            eng.wait_ge(sem, val)
                            known[key] = val
                        inst = op.fn(eng)
                        if op.is_dma:
                            inst.then_inc(dsem[op.dkey], 16)
                        elif op.signal:
                            inst.then_inc(esem[e], 1)
                    if e == final_wait_eng:
                        for k in self.phase_keys:
                            v = self.dma_counts[k]
                            if known.get(("d", k), 0) < v:
                                eng.wait_ge(dsem[k], v)
                return body

            for e in ALL_ENG:
                engmap[e](make(e))
        self.barrier = [(("e", e), esem[e], self.ecount[e]) for e in ALL_ENG if self.ecount[e] > 0]
        self.barrier += [(("d", k), dsem[k], self.dma_counts[k]) for k in self.dma_counts]
        self.ops = {e: [] for e in ALL_ENG}
        self.bufs = {}
        self.phase_keys = set()

    def close(self):
        self.semstack.close()


class Ctx:
    def __init__(self):
        self.nc = bass.Bass("TRN2", target_bir_lowering=False)
        self.P = Prog(self.nc)
        self.st = None
        self.rot = {}
        self.phase = -1
        self.begin()

    def begin(self):
        self.st = contextlib.ExitStack()
        self.rot = {}
        self.phase += 1

    def end(self):
        self.P.flush()
        self.st.close()
        self.st = None

    def sb(self, name, shape, dt):
        return self.st.enter_context(self.nc.sbuf_tensor("p%d_%s" % (self.phase, name), list(shape), dt))

    def ps(self, name, shape=(128, 512), dt=F32):
        return self.st.enter_context(self.nc.psum_tensor("p%d_%s" % (self.phase, name), list(shape), dt))

    def din(self, name, shape, dt):
        return self.nc.dram_tensor(name, list(shape), dt, kind="ExternalInput").ap()

    def dout(self, name, shape, dt):
        return self.nc.dram_tensor(name, list(shape), dt, kind="ExternalOutput").ap()

    def dint(self, name, shape, dt):
        return self.nc.dram_tensor(name, list(shape), dt, kind="Internal").ap()

    def nxt(self, key, n):
        i = self.rot.get(key, 0)
        self.rot[key] = i + 1
        return i % n

    def finish(self):
        if self.st is not None:
            self.end()
        self.P.close()
        return self.nc


class WStream:
    def __init__(self, cx, name, max_k, ncols, kpiece=4, npanel=2, nstage=3):
        self.cx, self.name = cx, name
        self.max_k, self.ncols, self.kpiece = max_k, ncols, kpiece
        self.panels = [cx.sb("%s_pan%d" % (name, i), (128, max_k, ncols), BF16) for i in range(npanel)]
        self.stages = [cx.sb("%s_stg%d" % (name, i), (128, kpiece, ncols), F32) for i in range(nstage)]
        self.npanel, self.nstage = npanel, nstage

    def load(self, W, kc0, nk, col0, ncols):
        cx, P = self.cx, self.cx.P
        pi = cx.nxt(self.name + "_p", self.npanel)
        pan = self.panels[pi]
        pname = "%s_pan%d" % (self.name, pi)
        k = 0
        while k < nk:
            kp = min(self.kpiece, nk - k)
            si = cx.nxt(self.name + "_s", self.nstage)
            stg = self.stages[si]
            sname = "%s_stg%d" % (self.name, si)
            src = W[(kc0 + k) * 128:(kc0 + k + kp) * 128, col0:col0 + ncols].rearrange("(k p) n -> p k n", p=128)
            dst = stg[:, 0:kp, 0:ncols]
            P.dma("sp", lambda e, dst=dst, src=src: e.dma_start(out=dst, in_=src), reads=[], writes=[sname],
                  key=sname)
            pdst = pan[:, k:k + kp, 0:ncols]
            P.op("pool", lambda e, pdst=pdst, dst=dst: e.tensor_copy(out=pdst, in_=dst), reads=[sname],
                 writes=[pname])
            k += kp
        return pan, pname

    def load_bf16(self, src, nk, ncols, dep):
        cx, P = self.cx, self.cx.P
        pi = cx.nxt(self.name + "_p", self.npanel)
        pan = self.panels[pi]
        pname = "%s_pan%d" % (self.name, pi)
        P.dma("sp", lambda e: e.dma_start(out=pan[:, 0:nk, 0:ncols], in_=src), reads=[dep], writes=[pname],
              key=pname + "_ld")
        return pan, pname


class Prefetch:
    def __init__(self, thunks, ahead=2):
        self.thunks, self.ahead, self.res = list(thunks), ahead, []

    def get(self, i):
        while len(self.res) < min(i + 1 + self.ahead, len(self.thunks)):
            self.res.append(self.thunks[len(self.res)]())
        return self.res[i]


def norm_transpose(cx, tag, src_blk, deps_blk, ntok_blocks, nfeat, gain_bc, gain_name, dstT, dst_name, dst_chunk0,
                   xt_bufs, ps_bufs, ident, small):
    P = cx.P
    nch = nfeat // 128
    sq, sqn = small["sq"]
    ss, ssn = small["ss"]
    for b in range(ntok_blocks):
        xi = cx.nxt(tag + "_xt", len(xt_bufs))
        xt, xname = xt_bufs[xi]
        srcb = src_blk(b)
        P.dma("sp", lambda e, xt=xt, srcb=srcb: e.dma_start(out=xt[:, 0:nfeat], in_=srcb), reads=deps_blk(b),
              writes=[xname], key=xname)
        P.op("act", lambda e, xt=xt: e.activation(out=sq[:, 0:nfeat], in_=xt[:, 0:nfeat], func=AF.Square,
                                                  accum_out=ss[:, 0:1]),
             reads=[xname], writes=[sqn, ssn])
        P.op("act", lambda e: e.activation(out=ss[:, 1:2], in_=ss[:, 0:1], func=AF.Sqrt, scale=1.0 / nfeat,
                                           bias=small["eps"][0][:, 0:1]),
             reads=[ssn, small["eps"][1]], writes=[ssn + "b"])
        P.op("dve", lambda e: e.reciprocal(out=ss[:, 2:3], in_=ss[:, 1:2]), reads=[ssn + "b"], writes=[ssn + "c"])
        P.op("dve", lambda e, xt=xt: e.scalar_tensor_tensor(out=xt[:, 0:nfeat], in0=xt[:, 0:nfeat], scalar=ss[:, 2:3],
                                                             in1=gain_bc[:, 0:nfeat], op0=ALU.mult, op1=ALU.mult),
             reads=[xname, ssn + "c", gain_name], writes=[xname])
        for c0 in range(0, nch, 4):
            nc4 = min(4, nch - c0)
            pi = cx.nxt("nt_ps", len(ps_bufs))
            ps, psn = ps_bufs[pi]
            for j in range(nc4):
                P.op("pe", lambda e, ps=ps, xt=xt, j=j, c0=c0: e.transpose(ps[:, j * 128:(j + 1) * 128],
                                                                          xt[:, (c0 + j) * 128:(c0 + j + 1) * 128],
                                                                          ident[:]),
                     reads=[xname, "ident"], writes=[psn])
            dst = dstT[:, dst_chunk0 + c0:dst_chunk0 + c0 + nc4, b * 128:(b + 1) * 128]
            src_ps = ps[:, 0:nc4 * 128].rearrange("p (c t) -> p c t", c=nc4)
            P.op("act", lambda e, dst=dst, src_ps=src_ps: e.activation(out=dst, in_=src_ps, func=AF.Copy),
                 reads=[psn], writes=[dst_name])


D_MODEL, SEQ, D_FF = 4096, 8192, 11008
D_IN, D_IN_PAD = 9776, 9856
NM_OWN = SEQ // 128 // NCORES
NT_OWN = NM_OWN * 128
QSCALE = 128 ** -0.5
KC = D_MODEL // 128
C_QSB, C_KSB, C_VSB, C_QN = 0, 2048, 4096, 6144
C_KC, C_VC, C_KSL, C_VSL, C_KW, C_VW, C_G = 8192, 8448, 8704, 8960, 9216, 9472, 9728
KROW_SB, KROW_KC, KROW_VC, KROW_KSL, KROW_KW, KROWS = 0, 2048, 2304, 2560, 2816, 3072
VCOL_SB, VCOL_VSL, VCOL_VW, VCOLS = 0, 2048, 2304, 2560
QROW_SB, QROW_N, QROWS = 0, 2048, 4096


def own_chunk(m):
    return 8 * m + 7


def small_tiles(cx, sq_ap, sq_name):
    return {"sq": (sq_ap, sq_name), "ss": (cx.sb("ss", (128, 4), F32), "ss"),
            "eps": (cx.sb("epsc", (128, 1), F32), "epsc")}


def phase_1a(cx, io):
    P = cx.P
    xu, w, qT, gnd = io["xu"], io["w_in"], io["qT"], io["gn"]
    ident = cx.sb("ident_sb", (128, 128), F32)
    gain = cx.sb("gain_sb", (128, D_MODEL), F32)
    hT = cx.sb("hT", (128, KC, NT_OWN), BF16)
    xts = [(cx.sb("xt%d" % i, (128, D_MODEL), F32), "xt%d" % i) for i in range(2)]
    sqt = cx.sb("sq", (128, D_MODEL), BF16)
    small = small_tiles(cx, sqt, "sq")
    pss = [(cx.ps("ps%d" % i), "ps%d" % i) for i in range(8)]
    ws = WStream(cx, "w", KC, 256)
    ostg = [(cx.sb("ostg%d" % i, (128, NT_OWN), BF16), "ostg%d" % i) for i in range(2)]
    gstg = [(cx.sb("gstg%d" % i, (128, 128), F32), "gstg%d" % i) for i in range(2)]

    P.dma("sp", lambda e: e.dma_start(out=ident[:], in_=io["ident"]), writes=["ident"], key="c_ident")
    P.dma("sp", lambda e: e.dma_start(out=gain[:], in_=io["g_attn"]), writes=["gain"], key="c_gain")
    P.op("dve", lambda e: e.memset(small["eps"][0][:], EPS), writes=["epsc"])
    norm_transpose(cx, "l1", lambda b: xu[own_chunk(b) * 128:(own_chunk(b) + 1) * 128, :], lambda b: [],
                   NM_OWN, D_MODEL, gain, "gain", hT, "hT", 0, xts, pss, ident, small)

    qpanels = [(c0 + pcol, r0 + pcol) for (c0, r0) in ((C_QSB, QROW_SB), (C_QN, QROW_N)) for pcol in range(0, 2048, 256)]
    pf = Prefetch([(lambda c=c: ws.load(w, 0, KC, c, 256)) for (c, _) in qpanels] +
                  [lambda: ws.load(w, 0, KC, C_G, 128)], ahead=1)
    for qi, (cabs, rabs) in enumerate(qpanels):
        pan, pname = pf.get(qi)
        for j0 in (0, 128):
            og, ogn = ostg[cx.nxt("ostg", 2)]
            for th in range(NT_OWN // 512):
                ps, psn = pss[cx.nxt("mm_ps", 8)]
                for k in range(KC):
                    P.op("pe", lambda e, ps=ps, pan=pan, k=k, j0=j0, th=th: e.matmul(
                        ps[:, 0:512], pan[:, k, j0:j0 + 128], hT[:, k, th * 512:(th + 1) * 512],
                        start=(k == 0), stop=(k == KC - 1)), reads=[pname, "hT"], writes=[psn])
                P.op("act", lambda e, og=og, ps=ps, th=th: e.activation(
                    out=og[:, th * 512:(th + 1) * 512], in_=ps[:, 0:512], func=AF.Copy, scale=QSCALE),
                    reads=[psn], writes=[ogn])
            row = rabs + j0
            P.dma("sp", lambda e, og=og, row=row: e.dma_start(out=qT[row:row + 128, :], in_=og[:]),
                  reads=[ogn], writes=[], key=ogn + "_st")
    pan, pname = pf.get(len(qpanels))
    for b in range(NM_OWN):
        ps, psn = pss[cx.nxt("mm_ps", 8)]
        for k in range(KC):
            P.op("pe", lambda e, ps=ps, k=k, b=b: e.matmul(ps[:, 0:128], hT[:, k, b * 128:(b + 1) * 128],
                                                           pan[:, k, 0:128], start=(k == 0), stop=(k == KC - 1)),
                 reads=[pname, "hT"], writes=[psn])
        gs, gsn = gstg[cx.nxt("gstg", 2)]
        P.op("act", lambda e, gs=gs, ps=ps: e.activation(out=gs[:], in_=ps[:, 0:128], func=AF.Copy), reads=[psn],
             writes=[gsn])
        P.dma("sp", lambda e, gs=gs, b=b: e.dma_start(out=gnd[b * 128:(b + 1) * 128, :], in_=gs[:]), reads=[gsn],
              writes=[], key=gsn + "_st")


K_PANELS = [(C_KSB + p, KROW_SB + p) for p in range(0, 2048, 256)] + \
           [(C_KC, KROW_KC), (C_VC, KROW_VC), (C_KSL, KROW_KSL), (C_KW, KROW_KW)]
V_PANELS = [(C_VSB + p, VCOL_SB + p) for p in range(0, 2048, 256)] + [(C_VSL, VCOL_VSL), (C_VW, VCOL_VW)]


def phase_1b(cx, io):
    P = cx.P
    xu, w, kT, vtok, wbf = io["xu"], io["w_in"], io["kT"], io["vtok"], io["wbf"]
    NTILE = SEQ // 1024
    ident = cx.sb("ident_sb", (128, 128), F32)
    gain = cx.sb("gain_sb", (128, D_MODEL), F32)
    hT = cx.sb("hT", (128, KC, 1024), BF16)
    xts = [(cx.sb("xt%d" % i, (128, D_MODEL), F32), "xt%d" % i) for i in range(2)]
    sqt = cx.sb("sq", (128, D_MODEL), BF16)
    small = small_tiles(cx, sqt, "sq")
    pss = [(cx.ps("ps%d" % i), "ps%d" % i) for i in range(8)]
    ws = WStream(cx, "w", KC, 256, npanel=3)
    ostg = [(cx.sb("ostg%d" % i, (128, 1024), BF16), "ostg%d" % i) for i in range(2)]
    vstg = [(cx.sb("vstg%d" % i, (128, 8, 256), BF16), "vstg%d" % i) for i in range(2)]

    P.dma("sp", lambda e: e.dma_start(out=ident[:], in_=io["ident"]), writes=["ident"], key="c_ident")
    P.dma("sp", lambda e: e.dma_start(out=gain[:], in_=io["g_attn"]), writes=["gain"], key="c_gain")
    P.op("dve", lambda e: e.memset(small["eps"][0][:], EPS), writes=["epsc"])
    panels = [("k",) + p for p in K_PANELS] + [("v",) + p for p in V_PANELS]

    def first_load(pi_, c0):
        pan, pname = ws.load(w, 0, KC, c0, 256)
        P.dma("sp", lambda e: e.dma_start(out=wbf[pi_], in_=pan[:, 0:KC, 0:256]), reads=[pname], writes=["wbf"],
              key="wbf_st")
        return pan, pname

    thunks = []
    for t in range(NTILE):
        for pi_, (kind, c0, r0) in enumerate(panels):
            if t == 0:
                thunks.append(lambda pi_=pi_, c0=c0: first_load(pi_, c0))
            else:
                thunks.append(lambda pi_=pi_: ws.load_bf16(wbf[pi_], KC, 256, "wbf"))
    pf = Prefetch(thunks, ahead=2)
    for t in range(NTILE):
        norm_transpose(cx, "l1", lambda b, t=t: xu[t * 1024 + b * 128:t * 1024 + (b + 1) * 128, :], lambda b: [],
                       8, D_MODEL, gain, "gain", hT, "hT", 0, xts, pss, ident, small)
        for pi_, (kind, c0, r0) in enumerate(panels):
            pan, pname = pf.get(t * len(panels) + pi_)
            if kind == "k":
                for j0 in (0, 128):
                    og, ogn = ostg[cx.nxt("ostg", 2)]
                    for th in range(2):
                        ps, psn = pss[cx.nxt("mm_ps", 8)]
                        for k in range(KC):
                            P.op("pe", lambda e, ps=ps, pan=pan, k=k, j0=j0, th=th: e.matmul(
                                ps[:, 0:512], pan[:, k, j0:j0 + 128], hT[:, k, th * 512:(th + 1) * 512],
                                start=(k == 0), stop=(k == KC - 1)), reads=[pname, "hT"], writes=[psn])
                        P.op("act", lambda e, og=og, ps=ps, th=th: e.activation(
                            out=og[:, th * 512:(th + 1) * 512], in_=ps[:, 0:512], func=AF.Copy),
                            reads=[psn], writes=[ogn])
                    row = r0 + j0
                    P.dma("sp", lambda e, og=og, row=row, t=t: e.dma_start(
                        out=kT[row:row + 128, t * 1024:(t + 1) * 1024], in_=og[:]), reads=[ogn], writes=[],
                        key=ogn + "_st")
            else:
                vs, vsn = vstg[cx.nxt("vstg", 2)]
                for b in range(8):
                    ps, psn = pss[cx.nxt("mm_ps", 8)]
                    for k in range(KC):
                        P.op("pe", lambda e, ps=ps, pan=pan, k=k, b=b: e.matmul(
                            ps[:, 0:256], hT[:, k, b * 128:(b + 1) * 128], pan[:, k, 0:256],
                            start=(k == 0), stop=(k == KC - 1)), reads=[pname, "hT"], writes=[psn])
                    P.op("act", lambda e, vs=vs, ps=ps, b=b: e.activation(out=vs[:, b, :], in_=ps[:, 0:256],
                                                                          func=AF.Copy), reads=[psn], writes=[vsn])
                dst = vtok[t * 1024:(t + 1) * 1024, r0:r0 + 256].rearrange("(b p) c -> p b c", p=128)
                P.dma("sp", lambda e, vs=vs, dst=dst: e.dma_start(out=dst, in_=vs[:]), reads=[vsn], writes=[],
                      key=vsn + "_st")


def sb_consts():
    j = np.arange(128)
    negU = np.where(j[:, None] >= j[None, :], -1.0, 0.0).astype(NPBF)
    tri = (j[:, None] < j[None, :]).astype(np.float32).astype(NPBF)
    negones = np.full((128, 1), -1.0, dtype=np.float32).astype(NPBF)
    return {"negU": negU, "tri_strict": tri, "negones": negones}


def nsa_slopes():
    h = np.arange(1, 17, dtype=np.float32)
    return (2.0 ** (-8.0 * h / 16)).astype(np.float32)


def nsa_tables(c, NM):
    NJ, NCU, NUC = 16 * NM, 64 * NM, 8 * NM
    NCC = max(1, NCU // 128)
    sl = nsa_slopes()
    i = np.arange(128)
    T = {}
    rel = np.arange(NUC)
    T["ak"] = (sl[None, :, None] * (i[:, None, None] - 64.0 - 128.0 * rel[None, None, :])).astype(np.float32)
    ac = np.zeros((128, 16, NM, NCC), np.float32)
    cm = np.zeros((128, NM, NCC, 128), np.float32)
    wm = np.zeros((128, NM, 5, 128), np.float32)
    bonus = np.zeros((128, NM, NJ), np.float32)
    tl = np.arange(128)
    for m in range(NM):
        u0 = 128 * (8 * m + 7)
        ut = u0 + tl
        for ncc in range(NCC):
            nu = 128 * ncc + i
            cend = 16 * nu + 31
            ac[:, :, m, ncc] = sl[None, :] * (cend[:, None] - (u0 + 64.0))
            cm[:, m, ncc, :] = ((cend[:, None] <= ut[None, :]) & (nu[:, None] >= 8 * (7 - c))).astype(np.float32)
        for r in range(5):
            uk = 128 * (8 * m + 3 + r) + i
            d = ut[None, :] - uk[:, None]
            wm[:, m, r, :] = ((d >= 0) & (d < 512) & (uk[:, None] >= 128 * (7 - c))).astype(np.float32)
        cur = ut // 64
        jj = np.arange(NJ)
        valid = (jj[None, :] <= cur[:, None]) & (jj[None, :] >= 2 * (7 - c))
        forced = (jj[None, :] == 2 * (7 - c)) | (jj[None, :] == cur[:, None]) | (jj[None, :] == cur[:, None] - 1)
        bonus[:, m, :] = np.where(valid, np.where(forced, 1e6, 0.0), -1e30)
    T["ac"] = np.minimum(ac, 45.0)
    T["cm"] = cm.astype(NPBF)
    T["wm"] = wm.astype(NPBF)
    T["bonus"] = bonus
    T["ex"] = np.broadcast_to((np.arange(NJ) >= 2 * (7 - c)).astype(np.float32), (128, NJ)).copy()
    T["tri_incl"] = (i[:, None] <= i[None, :]).astype(np.float32).astype(NPBF)
    T["exf"] = np.broadcast_to((np.arange(8) >= (7 - c)).astype(np.float32), (128, 8)).copy()
    E = np.zeros((NJ, NUC, 128), np.float32)
    for uc in range(NUC):
        for half in range(2):
            if 2 * uc + half < NJ:
                E[2 * uc + half, uc, half * 64:(half + 1) * 64] = 1.0
    T["E"] = E.astype(NPBF)
    caug = np.zeros((128, NCC, 1 + NJ), np.float32)
    caug[:, :, 0] = 1.0
    for ncc in range(NCC):
        for il in range(128):
            nu = 128 * ncc + il
            for j in range(NJ):
                for mm in range(4):
                    for nn in range(2):
                        if 4 * j - mm - nn == nu:
                            caug[il, ncc, 1 + j] += 1.0
    T["caug"] = caug.astype(NPBF)
    T["ident"] = np.eye(128, dtype=np.float32)
    return T


def _bcast_mid(ap2d, n):
    a = ap2d.ap
    return bass.AP(ap2d.tensor, ap2d.offset, [list(a[0]), [0, n], list(a[-1])])


def phase_2a(cx, io):
    P = cx.P
    S = SEQ
    NCH = S // 128
    kTd, vtokd, qTd, out = io["kT"], io["vtok"], io["qT"], io["osb"]
    negU = cx.sb("negU_sb", (128, 128), BF16)
    tri = cx.sb("tri_sb", (128, 128), BF16)
    negones = cx.sb("negones_sb", (128, 1), BF16)
    exf = cx.sb("exf_sb", (128, 8), F32)
    P.dma("sp", lambda e: e.dma_start(out=negU[:], in_=io["negU"]), writes=["negU"], key="ld_c0")
    P.dma("sp", lambda e: e.dma_start(out=tri[:], in_=io["tri_strict"]), writes=["tri"], key="ld_c1")
    P.dma("sp", lambda e: e.dma_start(out=negones[:], in_=io["negones"]), writes=["negones"], key="ld_c2")
    P.dma("sp", lambda e: e.dma_start(out=exf[:], in_=io["exf"]), writes=["exf"], key="ld_c3")
    HPC = 2
    sets = []
    for s_ in range(2):
        sets.append({
            "k": (cx.sb("kTb%d" % s_, (128, HPC, S), BF16), "kTb%d" % s_),
            "v": (cx.sb("vb%d" % s_, (128, HPC, NCH, 128), BF16), "vb%d" % s_),
            "q": (cx.sb("qb%d" % s_, (128, HPC, NT_OWN), BF16), "qb%d" % s_)})
    psZ = [[(cx.ps("psZ%d_%d" % (i, j)), "psZ%d_%d" % (i, j)) for j in range(2)] for i in range(HPC)]
    psC = [(cx.ps("psC%d" % i, (128, 8)), "psC%d" % i) for i in range(HPC)]
    psO = [(cx.ps("psO%d" % i), "psO%d" % i) for i in range(HPC)]
    esb = [[(cx.sb("esb%d_%d" % (i, j), (128, 512), F32), "esb%d_%d" % (i, j)) for j in range(2)] for i in range(HPC)]
    spb = [[(cx.sb("spb%d_%d" % (i, j), (128, 512), BF16), "spb%d_%d" % (i, j)) for j in range(2)] for i in range(HPC)]
    Ab = [[(cx.sb("Ab%d_%d" % (i, j), (128, 512), BF16), "Ab%d_%d" % (i, j)) for j in range(2)] for i in range(HPC)]
    acc = [(cx.sb("acc%d" % i, (128, 4, 128), F32), "acc%d" % i) for i in range(HPC * 2)]
    car = [(cx.sb("car%d" % i, (128, 4), F32), "car%d" % i) for i in range(HPC)]
    ecb = [(cx.sb("ec%d" % i, (128, 4), F32), "ec%d" % i) for i in range(HPC)]

    def stage_a(it):
        hh, uc, c0, q0, par = it["hh"], it["uc"], it["c0"], it["q0"], it["par"]
        (kT, kTn), (qT, qn) = it["k"], it["q"]
        pz, pzn = psZ[hh][par]
        es, esn = esb[hh][par]
        sp, spn = spb[hh][par]
        P.op("pe", lambda e: e.matmul(pz[:, c0:512], kT[:, hh, uc * 128:(uc + 1) * 128], qT[:, hh, q0 + c0:q0 + 512],
                                      start=True, stop=False), reads=[kTn, qn], writes=[pzn])
        P.op("act", lambda e: e.activation(out=es[:, c0:512], in_=pz[:, c0:512], func=AF.Exp), reads=[pzn],
             writes=[esn])
        P.op("act", lambda e: e.activation(out=sp[:, c0:512], in_=es[:, c0:512], func=AF.Ln, bias=1.0, scale=1.0),
             reads=[esn], writes=[spn])
        if it["diag"]:
            P.op("dve", lambda e: e.tensor_tensor(out=sp[:, c0:c0 + 128], in0=sp[:, c0:c0 + 128], in1=tri[:],
                                                  op=ALU.mult), reads=[spn, "tri"], writes=[spn])
        if uc < 7:
            P.op("dve", lambda e: e.tensor_scalar(out=sp[:, c0:512], in0=sp[:, c0:512], scalar1=exf[:, uc:uc + 1],
                                                  scalar2=None, op0=ALU.mult), reads=[spn, "exf"], writes=[spn])

    def stage_b(it):
        hh, c0, b0, par = it["hh"], it["c0"], it["b0"], it["par"]
        pz, pzn = psZ[hh][par]
        pc, pcn = psC[hh]
        sp, spn = spb[hh][par]
        A, An = Ab[hh][par]
        P.op("pe", lambda e: e.matmul(pz[:, c0:512], negU[:], sp[:, c0:512], start=False, stop=True),
             reads=[spn, "negU"], writes=[pzn])
        for b in range(b0, 4):
            P.op("pe", lambda e, b=b: e.matmul(pc[:, b:b + 1], sp[:, b * 128:(b + 1) * 128], negones[:], start=True,
                                               stop=True), reads=[spn, "negones"], writes=[pcn])
        P.op("act", lambda e: e.activation(out=A[:, c0:512], in_=pz[:, c0:512], func=AF.Exp), reads=[pzn],
             writes=[An])
        if it["diag"]:
            P.op("dve", lambda e: e.tensor_tensor(out=A[:, c0:c0 + 128], in0=A[:, c0:c0 + 128], in1=tri[:],
                                                  op=ALU.mult), reads=[An, "tri"], writes=[An])

    def stage_c(it):
        hh, uc, b0, par, uq = it["hh"], it["uc"], it["b0"], it["par"], it["uq"]
        (v, vn) = it["v"]
        pc, pcn = psC[hh]
        po, pon = psO[hh]
        A, An = Ab[hh][par]
        ac, acn = it["acc"]
        cr, crn = car[hh]
        ec, ecn = ecb[hh]
        for b in range(b0, 4):
            P.op("pe", lambda e, b=b: e.matmul(po[:, b * 128:(b + 1) * 128], A[:, b * 128:(b + 1) * 128],
                                               v[:, hh, uc, :], start=True, stop=True), reads=[An, vn], writes=[pon])
        for b in range(b0, 4):
            if uc == uq[b]:
                P.op("dve", lambda e, b=b: e.tensor_copy(out=ac[:, b, :], in_=po[:, b * 128:(b + 1) * 128]),
                     reads=[pon], writes=[acn])
            else:
                P.op("dve", lambda e, b=b: e.scalar_tensor_tensor(
                    out=ac[:, b, :], in0=po[:, b * 128:(b + 1) * 128], scalar=ec[:, b:b + 1], in1=ac[:, b, :],
                    op0=ALU.mult, op1=ALU.add), reads=[pon, ecn, acn], writes=[acn])
        if uc > 0:
            if it["diag"]:
                P.op("dve", lambda e: e.tensor_copy(out=cr[:, b0:b0 + 1], in_=pc[:, b0:b0 + 1]), reads=[pcn],
                     writes=[crn])
                if b0 + 1 < 4:
                    P.op("dve", lambda e: e.tensor_tensor(out=cr[:, b0 + 1:4], in0=cr[:, b0 + 1:4],
                                                          in1=pc[:, b0 + 1:4], op=ALU.add), reads=[pcn, crn],
                         writes=[crn])
            else:
                P.op("dve", lambda e: e.tensor_tensor(out=cr[:, b0:4], in0=cr[:, b0:4], in1=pc[:, b0:4], op=ALU.add),
                     reads=[pcn, crn], writes=[crn])
            P.op("act", lambda e: e.activation(out=ec[:, b0:4], in_=cr[:, b0:4], func=AF.Exp), reads=[crn],
                 writes=[ecn])
        if it["last"]:
            h, G = it["h"], it["G"]
            dst = out[512 * G:512 * G + 512, h * 128:(h + 1) * 128].rearrange("(b p) d -> p b d", p=128)
            P.dma("sp", lambda e: e.dma_start(out=dst, in_=ac[:]), reads=[acn], key=acn + "_st")

    for pair in range(16 // HPC):
        st = sets[pair % 2]
        (kT, kTn), (v, vn), (qT, qn) = st["k"], st["v"], st["q"]
        for hh in range(HPC):
            h = pair * HPC + hh
            P.dma("sp", lambda e, hh=hh, h=h, kT=kT: e.dma_start(
                out=kT[:, hh, :], in_=kTd[KROW_SB + 128 * h:KROW_SB + 128 * (h + 1), :]), writes=[kTn],
                key=kTn + "_%d" % hh)
            P.dma("sp", lambda e, hh=hh, h=h, v=v: e.dma_start(
                out=v[:, hh, :, :],
                in_=vtokd[:, VCOL_SB + 128 * h:VCOL_SB + 128 * (h + 1)].rearrange("(c i) d -> i c d", i=128)),
                writes=[vn], key=vn + "_%d" % hh)
            P.dma("sp", lambda e, hh=hh, h=h, qT=qT: e.dma_start(
                out=qT[:, hh, :], in_=qTd[QROW_SB + 128 * h:QROW_SB + 128 * (h + 1), :]), writes=[qn],
                key=qn + "_%d" % hh)
        items = []
        step = 0
        for G in range(NM_OWN // 4):
            uq = [own_chunk(4 * G + b) for b in range(4)]
            accs = {hh: acc[cx.nxt("acc%d" % hh, 2) * HPC + hh] for hh in range(HPC)}
            for uc in range(uq[3], -1, -1):
                b0 = min(b for b in range(4) if uq[b] >= uc)
                for hh in range(HPC):
                    items.append({"hh": hh, "h": pair * HPC + hh, "G": G, "uc": uc, "b0": b0, "c0": b0 * 128,
                                  "diag": uc == uq[b0], "par": step % 2, "uq": uq, "q0": 512 * G, "acc": accs[hh],
                                  "last": uc == 0, "k": (kT, kTn), "q": (qT, qn), "v": (v, vn)})
                step += 1
        n = len(items)
        for i in range(n + 2):
            if i < n:
                stage_a(items[i])
            if 0 <= i - 1 < n:
                stage_b(items[i - 1])
            if 0 <= i - 2 < n:
                stage_c(items[i - 2])


def phase_2b(cx, io):
    P = cx.P
    S, NM = SEQ, NM_OWN
    NJ, NCU, NUC = 16 * NM, 64 * NM, 8 * NM
    NCC = max(1, NCU // 128)
    NT = NM * 128
    NCV = 129 + NJ
    kTd, vtokd, qTd = io["kT"], io["vtok"], io["qT"]
    d_w1 = {"k": io["w_k1"], "v": io["w_v1"]}
    d_w2 = {"k": io["w_k2"], "v": io["w_v2"]}
    d_pos = {"k": io["posTk"], "v": io["posTv"]}
    d_ak, d_ac, d_cm, d_wm, d_tri = io["ak"], io["ac"], io["cm"], io["wm"], io["tri_incl"]
    d_bonus, d_ex, d_E, d_caug, d_ident = io["bonus"], io["ex"], io["E"], io["caug"], io["ident"]
    d_gn = io["gn"][:, 0:48].rearrange("(m p) c -> p m c", p=128)
    out = io["onsa"]

    def const(name, d, shape, dt):
        t = cx.sb(name + "_sb", shape, dt)
        P.dma("sp", lambda e: e.dma_start(out=t[:], in_=d), writes=[name], key="ld_" + name)
        return t

    ak = const("ak", d_ak, (128, 16, NUC), F32)
    ac = const("ac", d_ac, (128, 16, NM, NCC), F32)
    cm = const("cm", d_cm, (128, NM, NCC, 128), BF16)
    wm = const("wm", d_wm, (128, NM, 5, 128), BF16)
    tri = const("tri", d_tri, (128, 128), BF16)
    bonus = const("bonus", d_bonus, (128, NM, NJ), F32)
    ex = const("ex", d_ex, (128, NJ), F32)
    E = const("E", d_E, (NJ, NUC, 128), BF16)
    ident = const("ident", d_ident, (128, 128), F32)
    gn = const("gn", d_gn, (128, NM, 48), F32)

    qg = cx.sb("qg", (128, 8, NT), BF16)
    ksl = cx.sb("ksl", (128, S), BF16)
    vsl = cx.sb("vsl_sb", (128, NUC, 129), BF16)
    kw = cx.sb("kw", (128, NM, 5, 128), BF16)
    vw = cx.sb("vw_sb", (128, NM, 5, 129), BF16)
    cbuf = cx.sb("cbuf", (128, S + 16), BF16)
    kcmpT = cx.sb("kcmpT", (128, NCU), BF16)
    vcmp = cx.sb("vcmp", (128, NCC, NCV), BF16)
    gates = cx.sb("gates", (128, NM, 48), F32)
    gtmp = cx.sb("gtmp", (128, NM, 48), F32)
    posf = cx.sb("posf", (128, 32), F32)
    posb = cx.sb("posb", (128, 32), BF16)
    biasT = cx.sb("biasT", (128, 2), F32)
    HT = cx.sb("HT", (128, 2, NCU), BF16)
    ub = cx.sb("ub", (128, NCU), F32)
    u2 = cx.sb("u2", (128, NCU), F32)
    ws = WStream(cx, "w", 32, 256, kpiece=4, npanel=1, nstage=2)
    combs = [(cx.sb("comb%d" % i, (128, 8, 128), F32), "comb%d" % i) for i in range(2)]
    imp = cx.sb("imp", (128, NJ), F32)
    score = cx.sb("score", (128, NJ), F32)
    score2 = cx.sb("score2", (128, NJ), F32)
    m8 = cx.sb("m8", (128, 16), F32)
    sel = cx.sb("sel", (128, NJ), F32)
    selT = cx.sb("selT", (NJ, 128), BF16)
    Pts = [(cx.sb("Pt%d" % i, (128, 512), BF16), "Pt%d" % i) for i in range(2)]
    mts = [(cx.sb("mt%d" % i, (128, 128), BF16), "mt%d" % i) for i in range(2)]
    rts = [(cx.sb("rt%d" % i, (128, 4), F32), "rt%d" % i) for i in range(4)]
    psS = [(cx.ps("psS%d" % i), "psS%d" % i) for i in range(2)]
    psA = [(cx.ps("psA%d" % i), "psA%d" % i) for i in range(4)]
    psM = (cx.ps("psM"), "psM")
    psX = (cx.ps("psX"), "psX")

    P.op("pool", lambda e: e.memset(vsl[:, :, 128:129], 1.0), writes=["vsl"])
    P.op("pool", lambda e: e.memset(vw[:, :, :, 128:129], 1.0), writes=["vw"])
    P.op("pool", lambda e: e.memset(cbuf[:, S:S + 16], 0.0), writes=["cbuf"])
    P.op("act", lambda e: e.activation(out=gtmp[:], in_=gn[:], func=AF.Exp, scale=-1.0), reads=["gn"], writes=["gtmp"])
    P.op("dve", lambda e: e.tensor_scalar(out=gtmp[:], in0=gtmp[:], scalar1=1.0, scalar2=None, op0=ALU.add),
         reads=["gtmp"], writes=["gtmp"])
    P.op("dve", lambda e: e.reciprocal(out=gates[:], in_=gtmp[:]), reads=["gtmp"], writes=["gates"])

    def mlp(g, which):
        r0 = (KROW_KC if which == "k" else KROW_VC) + 128 * g
        P.dma("sp", lambda e: e.dma_start(out=cbuf[:, 0:S], in_=kTd[r0:r0 + 128, :]), writes=["cbuf"], key="ld_cbuf")
        P.dma("sp", lambda e: e.dma_start(out=posf[:], in_=d_pos[which]), writes=["posf"], key="ld_pos")
        P.op("pool", lambda e: e.tensor_copy(out=posb[:], in_=posf[:]), reads=["posf"], writes=["posb"])
        pan, pname = ws.load(d_w1[which], 0, 32, 0, 256)
        px, pxn = psX
        for hc in range(2):
            for l in range(32):
                P.op("pe", lambda e, hc=hc, l=l: e.matmul(px[:, hc:hc + 1], pan[:, l, hc * 128:(hc + 1) * 128],
                                                          posb[:, l:l + 1], start=(l == 0), stop=(l == 31)),
                     reads=[pname, "posb"], writes=[pxn])
            P.op("dve", lambda e, hc=hc: e.tensor_copy(out=biasT[:, hc:hc + 1], in_=px[:, hc:hc + 1]), reads=[pxn],
                 writes=["biasT"])
        for hc in range(2):
            ps, psn = psS[hc]
            for l in range(32):
                P.op("pe", lambda e, ps=ps, hc=hc, l=l: e.matmul(ps[:, 0:NCU], pan[:, l, hc * 128:(hc + 1) * 128],
                                                                cbuf[:, l:l + 16 * (NCU - 1) + 1:16], start=(l == 0),
                                                                stop=(l == 31)),
                     reads=[pname, "cbuf"], writes=[psn])
            P.op("act", lambda e, ps=ps, hc=hc: e.activation(out=ub[:], in_=ps[:, 0:NCU], func=AF.Identity,
                                                            bias=biasT[:, hc:hc + 1], scale=1.0),
                 reads=[psn, "biasT"], writes=["ub"])
            P.op("dve", lambda e: e.tensor_tensor(out=u2[:], in0=ub[:], in1=ub[:], op=ALU.mult), reads=["ub"],
                 writes=["u2"])
            P.op("dve", lambda e: e.tensor_scalar(out=u2[:], in0=u2[:], scalar1=0.044715, scalar2=1.0, op0=ALU.mult,
                                                  op1=ALU.add), reads=["u2"], writes=["u2"])
            P.op("dve", lambda e: e.tensor_tensor(out=u2[:], in0=u2[:], in1=ub[:], op=ALU.mult), reads=["u2", "ub"],
                 writes=["u2"])
            P.op("act", lambda e: e.activation(out=u2[:], in_=u2[:], func=AF.Exp, scale=-1.5957691216057308),
                 reads=["u2"], writes=["u2"])
            P.op("dve", lambda e: e.tensor_scalar(out=u2[:], in0=u2[:], scalar1=1.0, scalar2=None, op0=ALU.add),
                 reads=["u2"], writes=["u2"])
            P.op("dve", lambda e: e.reciprocal(out=u2[:], in_=u2[:]), reads=["u2"], writes=["u2"])
            P.op("dve", lambda e, hc=hc: e.tensor_tensor(out=HT[:, hc, :], in0=u2[:], in1=ub[:], op=ALU.mult),
                 reads=["u2", "ub"], writes=["HT"])
        pan2, p2name = ws.load(d_w2[which], 0, 2, 0, 128)
        if which == "k":
            ps, psn = psS[0]
            for hc in range(2):
                P.op("pe", lambda e, hc=hc: e.matmul(ps[:, 0:NCU], pan2[:, hc, 0:128], HT[:, hc, :], start=(hc == 0),
                                                     stop=(hc == 1)), reads=[p2name, "HT"], writes=[psn])
            P.op("act", lambda e: e.activation(out=kcmpT[:], in_=ps[:, 0:NCU], func=AF.Copy), reads=[psn],
                 writes=["kcmpT"])
        else:
            for ncc in range(NCC):
                ps, psn = psS[ncc % 2]
                for hc in range(2):
                    P.op("pe", lambda e, ps=ps, hc=hc, ncc=ncc: e.matmul(ps[:, 0:128],
                                                                        HT[:, hc, ncc * 128:(ncc + 1) * 128],
                                                                        pan2[:, hc, 0:128], start=(hc == 0),
                                                                        stop=(hc == 1)),
                         reads=[p2name, "HT"], writes=[psn])
                P.op("act", lambda e, ps=ps, ncc=ncc: e.activation(out=vcmp[:, ncc, 0:128], in_=ps[:, 0:128],
                                                                  func=AF.Copy), reads=[psn], writes=["vcmp"])

    def chunk(g, m, quad, keysT, bias_ap_fn, mask2d, mask_name, vrhs, vname, ncv, first, last, kname):
        si = cx.nxt("psS", 2)
        ps, psn = psS[si]
        pt, ptn = Pts[si]
        rhs = qg[:, 4 * quad:4 * quad + 4, m * 128:(m + 1) * 128]
        P.op("pe", lambda e: e.matmul(ps[:, 0:512].rearrange("p (h t) -> p h t", h=4), keysT, rhs, start=True,
                                      stop=True), reads=[kname, "qg"], writes=[psn])
        for h in range(4):
            b = bias_ap_fn(8 * g + 4 * quad + h)
            P.op("act", lambda e, h=h, b=b: e.activation(out=pt[:, h * 128:(h + 1) * 128],
                                                          in_=ps[:, h * 128:(h + 1) * 128], func=AF.Exp, bias=b,
                                                          scale=1.0), reads=[psn, "ak", "ac"], writes=[ptn])
        if mask2d is not None:
            pt3 = pt[:, 0:512].rearrange("p (h t) -> p h t", h=4)
            P.op("dve", lambda e: e.tensor_tensor(out=pt3, in0=pt3, in1=_bcast_mid(mask2d, 4), op=ALU.mult),
                 reads=[ptn, mask_name], writes=[ptn])
        for h in range(4):
            pa, pan_ = psA[h]
            P.op("pe", lambda e, h=h, pa=pa: e.matmul(pa[:, 0:ncv], pt[:, h * 128:(h + 1) * 128], vrhs, start=first,
                                                      stop=last), reads=[ptn, vname], writes=[pan_])

    def evac(g, m, quad, br, comb, cname, first_branch, do_imp):
        for h in range(4):
            pa, pan_ = psA[h]
            hl = 4 * quad + h
            head = 8 * g + hl
            rt, rtn = rts[cx.nxt("rt", 4)]
            P.op("dve", lambda e, pa=pa, rt=rt: e.tensor_scalar(out=rt[:, 0:1], in0=pa[:, 128:129], scalar1=1e-30,
                                                               scalar2=None, op0=ALU.max), reads=[pan_],
                 writes=[rtn])
            P.op("dve", lambda e, rt=rt: e.reciprocal(out=rt[:, 1:2], in_=rt[:, 0:1]), reads=[rtn], writes=[rtn])
            col = head * 3 + br
            P.op("dve", lambda e, rt=rt, col=col: e.tensor_tensor(out=rt[:, 2:3], in0=rt[:, 1:2],
                                                                 in1=gates[:, m, col:col + 1], op=ALU.mult),
                 reads=[rtn, "gates"], writes=[rtn])
            if first_branch:
                P.op("dve", lambda e, pa=pa, rt=rt, hl=hl: e.tensor_scalar(out=comb[:, hl, :], in0=pa[:, 0:128],
                                                                          scalar1=rt[:, 2:3], scalar2=None,
                                                                          op0=ALU.mult), reads=[pan_, rtn],
                     writes=[cname])
            else:
                P.op("dve", lambda e, pa=pa, rt=rt, hl=hl: e.scalar_tensor_tensor(
                    out=comb[:, hl, :], in0=pa[:, 0:128], scalar=rt[:, 2:3], in1=comb[:, hl, :], op0=ALU.mult,
                    op1=ALU.add), reads=[pan_, rtn, cname], writes=[cname])
            if do_imp:
                if hl == 0:
                    P.op("dve", lambda e, pa=pa, rt=rt: e.tensor_scalar(out=imp[:], in0=pa[:, 129:129 + NJ],
                                                                       scalar1=rt[:, 1:2], scalar2=None,
                                                                       op0=ALU.mult), reads=[pan_, rtn],
                         writes=["imp"])
                else:
                    P.op("dve", lambda e, pa=pa, rt=rt: e.scalar_tensor_tensor(
                        out=imp[:], in0=pa[:, 129:129 + NJ], scalar=rt[:, 1:2], in1=imp[:], op0=ALU.mult,
                        op1=ALU.add), reads=[pan_, rtn, "imp"], writes=["imp"])

    for g in range(2):
        P.dma("sp", lambda e, g=g: e.dma_start(
            out=qg[:], in_=qTd[QROW_N + 1024 * g:QROW_N + 1024 * (g + 1), :].rearrange("(h d) t -> d h t", d=128)),
            writes=["qg"], key="ld_qg")
        P.dma("sp", lambda e, g=g: e.dma_start(out=ksl[:], in_=kTd[KROW_KSL + 128 * g:KROW_KSL + 128 * (g + 1), :]),
              writes=["ksl"], key="ld_ksl")
        P.dma("sp", lambda e, g=g: e.dma_start(
            out=vsl[:, :, 0:128],
            in_=vtokd[:, VCOL_VSL + 128 * g:VCOL_VSL + 128 * (g + 1)].rearrange("(c i) d -> i c d", i=128)),
            writes=["vsl"], key="ld_vsl")
        for m in range(NM):
            u0 = (8 * m + 3) * 128
            P.dma("sp", lambda e, g=g, m=m, u0=u0: e.dma_start(
                out=kw[:, m, :, :],
                in_=kTd[KROW_KW + 128 * g:KROW_KW + 128 * (g + 1), u0:u0 + 640].rearrange("d (r i) -> d r i", i=128)),
                writes=["kw"], key="ld_kw")
            P.dma("sp", lambda e, g=g, m=m, u0=u0: e.dma_start(
                out=vw[:, m, :, 0:128],
                in_=vtokd[u0:u0 + 640, VCOL_VW + 128 * g:VCOL_VW + 128 * (g + 1)].rearrange("(r i) d -> i r d", i=128)),
                writes=["vw"], key="ld_vw")
        P.dma("sp", lambda e: e.dma_start(out=vcmp[:, :, 128:NCV], in_=d_caug), writes=["vcmp"], key="ld_caug")
        mlp(g, "k")
        mlp(g, "v")
        for m in range(NM):
            comb, cname = combs[cx.nxt("comb", 2)]
            nccs = list(range((64 * m + 62) // 128 + 1))
            for quad in range(2):
                for ii, ncc in enumerate(nccs):
                    chunk(g, m, quad, kcmpT[:, ncc * 128:(ncc + 1) * 128],
                          lambda head, ncc=ncc: ac[:, head, m, ncc:ncc + 1],
                          cm[:, m, ncc, :], "cm", vcmp[:, ncc, :], "vcmp", NCV, ii == 0, ii == len(nccs) - 1, "kcmpT")
                evac(g, m, quad, 0, comb, cname, True, True)
            P.op("dve", lambda e, m=m: e.tensor_tensor(out=score[:], in0=imp[:], in1=bonus[:, m, :], op=ALU.add),
                 reads=["imp", "bonus"], writes=["score"])
            P.op("dve", lambda e: e.max(out=m8[:, 0:8], in_=score[:]), reads=["score"], writes=["m8"])
            P.op("dve", lambda e: e.match_replace(out=score2[:], in_to_replace=m8[:, 0:8], in_values=score[:],
                                                  imm_value=-3.0e38), reads=["score", "m8"], writes=["score2"])
            P.op("dve", lambda e: e.max(out=m8[:, 8:16], in_=score2[:]), reads=["score2"], writes=["m8"])
            P.op("dve", lambda e: e.tensor_scalar(out=sel[:], in0=score[:], scalar1=m8[:, 15:16], scalar2=None,
                                                  op0=ALU.is_ge), reads=["score", "m8"], writes=["sel"])
            P.op("dve", lambda e: e.tensor_tensor(out=sel[:], in0=sel[:], in1=ex[:], op=ALU.mult),
                 reads=["sel", "ex"], writes=["sel"])
            px, pxn = psX
            P.op("pe", lambda e: e.transpose(px[0:NJ, 0:128], sel[:], ident[:]), reads=["sel", "ident"], writes=[pxn])
            P.op("act", lambda e: e.activation(out=selT[:], in_=px[0:NJ, 0:128], func=AF.Copy), reads=[pxn],
                 writes=["selT"])
            for quad in range(2):
                ucs = list(range(8 * m + 8))
                for ii, uc in enumerate(ucs):
                    pm, pmn = psM
                    mt, mtn = mts[cx.nxt("mt", 2)]
                    P.op("pe", lambda e, uc=uc: e.matmul(pm[:, 0:128], E[:, uc, :], selT[:], start=True, stop=True),
                         reads=["E", "selT"], writes=[pmn])
                    if uc == 8 * m + 7:
                        P.op("dve", lambda e, mt=mt: e.tensor_tensor(out=mt[:], in0=pm[:, 0:128], in1=tri[:],
                                                                    op=ALU.mult), reads=[pmn, "tri"], writes=[mtn])
                    else:
                        P.op("act", lambda e, mt=mt: e.activation(out=mt[:], in_=pm[:, 0:128], func=AF.Copy),
                             reads=[pmn], writes=[mtn])
                    rel = 8 * m + 7 - uc
                    chunk(g, m, quad, ksl[:, uc * 128:(uc + 1) * 128],
                          lambda head, rel=rel: ak[:, head, rel:rel + 1],
                          mt[:], mtn, vsl[:, uc, :], "vsl", 129, ii == 0, ii == len(ucs) - 1, "ksl")
                evac(g, m, quad, 1, comb, cname, False, False)
            for quad in range(2):
                for r in range(5):
                    chunk(g, m, quad, kw[:, m, r, :], lambda head, r=r: ak[:, head, 4 - r:5 - r],
                          wm[:, m, r, :], "wm", vw[:, m, r, :], "vw", 129, r == 0, r == 4, "kw")
                evac(g, m, quad, 2, comb, cname, False, False)
            dst = out[m * 128:(m + 1) * 128, g * 1024:(g + 1) * 1024]
            P.dma("sp", lambda e, comb=comb, dst=dst: e.dma_start(out=dst, in_=comb[:].rearrange("p h d -> p (h d)")),
                  reads=[cname], key=cname + "_st")


def phase_3(cx, io, NPASS=5):
    P = cx.P
    D, DFF, NTOK = D_MODEL, D_FF, NT_OWN
    NB = NTOK // 128
    HALF = D // 2
    xu, osb, onsa, out, x1d, yacc = io["xu"], io["osb"], io["onsa"], io["out"], io["x1d"], io["yacc"]
    w_out, w_gate, w_up, w_down = io["w_out"], io["w_gate"], io["w_up"], io["w_down"]
    xrow = lambda b: xu[own_chunk(b) * 128:(own_chunk(b) + 1) * 128, :]
    npan_ff = DFF // 256
    per = -(-npan_ff // NPASS)
    maxk_act = per * 2

    ident = cx.sb("ident_sb", (128, 128), F32)
    gain = cx.sb("gain_sb", (128, D), F32)
    hT = cx.sb("hT", (128, KC, NTOK), BF16)
    actT = cx.sb("actT", (128, maxk_act, NTOK), BF16)
    xts = [(cx.sb("xt0", (128, D), F32), "xt0")]
    ws = WStream(cx, "w", max(KC, maxk_act), 256, kpiece=4, npanel=3, nstage=2)
    sqv = actT[:].rearrange("p k n -> p (k n)")
    small = small_tiles(cx, sqv, "actT")
    thunks = [(lambda col=cp * 256: ws.load(w_out, 0, KC, col, 256)) for cp in range(D // 256)]
    for p_ in range(NPASS):
        a0, a1 = p_ * per, min(npan_ff, p_ * per + per)
        if a0 >= a1:
            continue
        for pp_ in range(a0, a1):
            thunks.append(lambda col=pp_ * 256: ws.load(w_gate, 0, KC, col, 256))
            thunks.append(lambda col=pp_ * 256: ws.load(w_up, 0, KC, col, 256))
        for cp in range(D // 256):
            thunks.append(lambda col=cp * 256, a0=a0, a1=a1: ws.load(w_down, a0 * 2, (a1 - a0) * 2, col, 256))
    pf = Prefetch(thunks, ahead=2)
    pfi = [0]

    def next_panel():
        r = pf.get(pfi[0])
        pfi[0] += 1
        return r

    pss = [(cx.ps("ps%d" % i), "ps%d" % i) for i in range(8)]
    ept = [(cx.sb("ept%d" % i, (128, 256), F32), "ept%d" % i) for i in range(2)]
    epo = [(cx.sb("epo%d" % i, (128, 256), F32), "epo%d" % i) for i in range(2)]
    sgt = [(cx.sb("sg%d" % i, (128, 512), F32), "sg%d" % i) for i in range(2)]

    P.dma("sp", lambda e: e.dma_start(out=ident[:], in_=io["ident"]), writes=["ident"], key="c_ident")
    P.op("dve", lambda e: e.memset(small["eps"][0][:], EPS), writes=["epsc"])

    P.dma("sp", lambda e: e.dma_start(out=gain[:, 0:HALF], in_=io["g_sb"]), writes=["gain"], key="c_gain")
    norm_transpose(cx, "sb", lambda b: osb[b * 128:(b + 1) * 128, :], lambda b: [], NB, HALF, gain, "gain", hT, "hT",
                   0, xts, pss, ident, small)
    P.dma("sp", lambda e: e.dma_start(out=gain[:, 0:HALF], in_=io["g_nsa"]), writes=["gain"], key="c_gain")
    norm_transpose(cx, "nsa", lambda b: onsa[b * 128:(b + 1) * 128, :], lambda b: [], NB, HALF, gain, "gain", hT, "hT",
                   KC // 2, xts, pss, ident, small)

    def tok_major_gemm(W, kc0, nk, actbuf, actname, prev_row, prev_name_fn, dst, dst_name_fn):
        for cp in range(D // 256):
            col = cp * 256
            pan, pname = next_panel()
            for b in range(NB):
                ps, psn = pss[cx.nxt("mm_ps", 8)]
                for k in range(nk):
                    P.op("pe", lambda e, ps=ps, pan=pan, k=k, b=b: e.matmul(
                        ps[:, 0:256], actbuf[:, k, b * 128:(b + 1) * 128], pan[:, k, 0:256],
                        start=(k == 0), stop=(k == nk - 1)), reads=[pname, actname], writes=[psn])
                ti = cx.nxt("ept", 2)
                pt, ptn = ept[ti]
                po, pon = epo[ti]
                srcp = prev_row(b)[:, col:col + 256]
                P.dma("sp", lambda e, pt=pt, srcp=srcp: e.dma_start(out=pt[:], in_=srcp),
                      reads=[prev_name_fn(b, cp)], writes=[ptn], key=ptn)
                P.op("dve", lambda e, po=po, ps=ps, pt=pt: e.tensor_tensor(out=po[:], in0=ps[:, 0:256], in1=pt[:],
                                                                          op=ALU.add),
                     reads=[psn, ptn], writes=[pon])
                dstp = dst[b * 128:(b + 1) * 128, col:col + 256]
                P.dma("sp", lambda e, po=po, dstp=dstp: e.dma_start(out=dstp, in_=po[:]),
                      reads=[pon], writes=[dst_name_fn(b, cp)], key=pon + "_st")

    tok_major_gemm(w_out, 0, KC, hT, "hT", xrow, lambda b, cp: "x_in", x1d, lambda b, cp: "x1d_%d_%d" % (b, cp))

    P.dma("sp", lambda e: e.dma_start(out=gain[:], in_=io["g_ffn"]), writes=["gain"], key="c_gain")
    norm_transpose(cx, "ffn", lambda b: x1d[b * 128:(b + 1) * 128, :],
                   lambda b: ["x1d_%d_%d" % (b, cp) for cp in range(D // 256)], NB, D, gain, "gain", hT, "hT", 0,
                   xts, pss, ident, small)

    TW = 512
    NTH = NTOK // TW
    lastp = 0
    for p in range(NPASS):
        pan0 = p * per
        pan1 = min(npan_ff, pan0 + per)
        if pan0 >= pan1:
            continue
        lastp = p
        for pp in range(pan0, pan1):
            col = pp * 256
            gpan, gname = next_panel()
            gps = {}
            for j in range(2):
                for th in range(NTH):
                    ps, psn = pss[cx.nxt("mm_ps", 8)]
                    gps[(j, th)] = (ps, psn)
                    for k in range(KC):
                        P.op("pe", lambda e, ps=ps, gpan=gpan, k=k, j=j, th=th: e.matmul(
                            ps[:, 0:TW], gpan[:, k, j * 128:(j + 1) * 128], hT[:, k, th * TW:(th + 1) * TW],
                            start=(k == 0), stop=(k == KC - 1)), reads=[gname, "hT"], writes=[psn])
            upan, uname = next_panel()
            for j in range(2):
                for th in range(NTH):
                    ps, psn = pss[cx.nxt("mm_ps", 8)]
                    for k in range(KC):
                        P.op("pe", lambda e, ps=ps, upan=upan, k=k, j=j, th=th: e.matmul(
                            ps[:, 0:TW], upan[:, k, j * 128:(j + 1) * 128], hT[:, k, th * TW:(th + 1) * TW],
                            start=(k == 0), stop=(k == KC - 1)), reads=[uname, "hT"], writes=[psn])
                    gp, gpn = gps[(j, th)]
                    sg, sgn = sgt[cx.nxt("sg", 2)]
                    P.op("act", lambda e, sg=sg, gp=gp: e.activation(out=sg[:, 0:TW], in_=gp[:, 0:TW], func=AF.Silu),
                         reads=[gpn], writes=[sgn])
                    kk = (pp - pan0) * 2 + j
                    P.op("dve", lambda e, sg=sg, ps=ps, kk=kk, th=th: e.tensor_tensor(
                        out=actT[:, kk, th * TW:(th + 1) * TW], in0=sg[:, 0:TW], in1=ps[:, 0:TW], op=ALU.mult),
                        reads=[sgn, psn], writes=["actT"])
        nk = (pan1 - pan0) * 2
        if p == 0:
            prow, pfn = (lambda b: x1d[b * 128:(b + 1) * 128, :]), (lambda b, cp: "x1d_%d_%d" % (b, cp))
        else:
            prow, pfn = (lambda b: yacc[b * 128:(b + 1) * 128, :]), (lambda b, cp, p=p: "yacc%d_%d_%d" % (p - 1, b, cp))
        tok_major_gemm(w_down, pan0 * 2, nk, actT, "actT", prow, pfn, yacc,
                       lambda b, cp, p=p: "yacc%d_%d_%d" % (p, b, cp))

    P.dma("sp", lambda e: e.dma_start(out=gain[:], in_=io["g_fin"]), writes=["gain"], key="c_gain")
    sq, sqn = small["sq"]
    ss = small["ss"][0]
    for b in range(NB):
        xt, xname = xts[0]
        deps = ["yacc%d_%d_%d" % (lastp, b, cp) for cp in range(D // 256)]
        P.dma("sp", lambda e, b=b: e.dma_start(out=xt[:], in_=yacc[b * 128:(b + 1) * 128, :]), reads=deps,
              writes=[xname], key=xname)
        P.op("act", lambda e: e.activation(out=sq[:, 0:D], in_=xt[:], func=AF.Square, accum_out=ss[:, 0:1]),
             reads=[xname], writes=[sqn, "ss"])
        P.op("act", lambda e: e.activation(out=ss[:, 1:2], in_=ss[:, 0:1], func=AF.Sqrt, scale=1.0 / D,
                                           bias=small["eps"][0][:, 0:1]), reads=["ss", "epsc"], writes=["ssb"])
        P.op("dve", lambda e: e.reciprocal(out=ss[:, 2:3], in_=ss[:, 1:2]), reads=["ssb"], writes=["ssc"])
        P.op("dve", lambda e: e.scalar_tensor_tensor(out=xt[:], in0=xt[:], scalar=ss[:, 2:3], in1=gain[:],
                                                     op0=ALU.mult, op1=ALU.mult),
             reads=[xname, "ssc", "gain"], writes=[xname])
        P.dma("sp", lambda e, b=b: e.dma_start(out=out[b * 128:(b + 1) * 128, :], in_=xt[:]), reads=[xname],
              writes=[], key="out_st")


def _tables_spec():
    NM = NM_OWN
    NJ, NCU, NUC = 16 * NM, 64 * NM, 8 * NM
    NCC = NCU // 128
    return {"ak": ((128, 16, NUC), F32), "ac": ((128, 16, NM, NCC), F32), "cm": ((128, NM, NCC, 128), BF16),
            "wm": ((128, NM, 5, 128), BF16), "tri_incl": ((128, 128), BF16), "bonus": ((128, NM, NJ), F32),
            "ex": ((128, NJ), F32), "E": ((NJ, NUC, 128), BF16), "caug": ((128, NCC, 1 + NJ), BF16),
            "ident": ((128, 128), F32), "exf": ((128, 8), F32), "negU": ((128, 128), BF16),
            "tri_strict": ((128, 128), BF16), "negones": ((128, 1), BF16)}


def build_program(phases=("1a", "1b", "2a", "2b", "3")):
    cx = Ctx()
    io = {}
    io["xu"] = cx.din("xu", (SEQ, D_MODEL), F32)
    io["w_in"] = cx.din("w_in", (D_MODEL, D_IN_PAD), F32)
    for n in ("g_attn", "g_ffn", "g_fin"):
        io[n] = cx.din(n, (128, D_MODEL), F32)
    for n in ("g_sb", "g_nsa"):
        io[n] = cx.din(n, (128, D_MODEL // 2), F32)
    for n, (shape, dt) in _tables_spec().items():
        io[n] = cx.din(n, shape, dt)
    for n in ("w_k1", "w_v1"):
        io[n] = cx.din(n, (4096, 256), F32)
    for n in ("w_k2", "w_v2"):
        io[n] = cx.din(n, (256, 128), F32)
    for n in ("posTk", "posTv"):
        io[n] = cx.din(n, (128, 32), F32)
    io["w_out"] = cx.din("w_out", (D_MODEL, D_MODEL), F32)
    io["w_gate"] = cx.din("w_gate", (D_MODEL, D_FF), F32)
    io["w_up"] = cx.din("w_up", (D_MODEL, D_FF), F32)
    io["w_down"] = cx.din("w_down", (D_FF, D_MODEL), F32)
    io["out"] = cx.dout("out", (NT_OWN, D_MODEL), F32)
    io["qT"] = cx.dint("qT_scr", (QROWS, NT_OWN), BF16)
    io["gn"] = cx.dint("gn_scr", (NT_OWN, 128), F32)
    io["kT"] = cx.dint("kT_scr", (KROWS, SEQ), BF16)
    io["vtok"] = cx.dint("vtok_scr", (SEQ, VCOLS), BF16)
    io["wbf"] = cx.dint("wbf_scr", (len(K_PANELS) + len(V_PANELS), 128, KC, 256), BF16)
    io["osb"] = cx.dint("osb_scr", (NT_OWN, 2048), F32)
    io["onsa"] = cx.dint("onsa_scr", (NT_OWN, 2048), F32)
    io["x1d"] = cx.dint("x1d_scr", (NT_OWN, D_MODEL), F32)
    io["yacc"] = cx.dint("yacc_scr", (NT_OWN, D_MODEL), F32)
    fns = {"1a": phase_1a, "1b": phase_1b, "2a": phase_2a, "2b": phase_2b, "3": phase_3}
    first = True
    for ph in phases:
        if not first:
            cx.begin()
        first = False
        fns[ph](cx, io)
        cx.end()
    return cx.finish()


def _bc(g, n=128):
    g = np.asarray(g, np.float32).reshape(-1)
    return np.ascontiguousarray(np.broadcast_to(g, (n, g.shape[0])))


def _own_rows(c):
    return np.concatenate([np.arange(128) + 128 * (c + 8 * m) for m in range(NM_OWN)])


def kernel(x, attn_norm, w_in, pos_cmp_k, pos_cmp_v, w_cmp_k1, w_cmp_k2, w_cmp_v1, w_cmp_v2,
           norm_sb, norm_nsa, w_out, ffn_norm, w_gate, w_up, w_down, final_norm):
    f32 = lambda a: np.ascontiguousarray(np.asarray(a, np.float32))
    x2 = f32(x)[0]
    cores = list(range(NCORES))
    w_pad = np.zeros((D_MODEL, D_IN_PAD), np.float32)
    w_pad[:, :D_IN] = f32(w_in)[0]
    common = {"w_in": w_pad, "g_attn": _bc(attn_norm), "g_ffn": _bc(ffn_norm), "g_fin": _bc(final_norm),
              "g_sb": _bc(norm_sb), "g_nsa": _bc(norm_nsa),
              "w_k1": f32(w_cmp_k1)[0], "w_k2": f32(w_cmp_k2)[0], "w_v1": f32(w_cmp_v1)[0], "w_v2": f32(w_cmp_v2)[0],
              "posTk": np.ascontiguousarray(f32(pos_cmp_k)[0].T), "posTv": np.ascontiguousarray(f32(pos_cmp_v)[0].T),
              "w_out": f32(w_out)[0], "w_gate": f32(w_gate)[0], "w_up": f32(w_up)[0], "w_down": f32(w_down)[0]}
    common.update(sb_consts())
    in_maps = []
    for c in cores:
        d = dict(common)
        shift = 128 * (7 - c)
        xu = np.zeros((SEQ, D_MODEL), np.float32)
        xu[shift:] = x2[:SEQ - shift]
        d["xu"] = xu
        d.update(nsa_tables(c, NM_OWN))
        in_maps.append(d)
    nc = build_program()
    res = run_bass_kernel_spmd(nc, in_maps, core_ids=cores).results
    out = np.zeros((1, SEQ, D_MODEL), np.float32)
    for c in cores:
        out[0, _own_rows(c)] = np.asarray(res[c]["out"])
    return out
```

```python
import contextlib
import numpy as np
import ml_dtypes
import concourse.bass as bass
import concourse.mybir as mybir
from concourse.bass_utils import run_bass_kernel_spmd

F32 = mybir.dt.float32
BF16 = mybir.dt.bfloat16
AF = mybir.ActivationFunctionType
ALU = mybir.AluOpType
NPBF = ml_dtypes.bfloat16

NCORES = 8
EPS = 1e-6
ALL_ENG = ("pe", "act", "dve", "pool", "sp")


class _Buf:
    __slots__ = ("writer", "readers")

    def __init__(self):
        self.writer = None
        self.readers = []


class _Op:
    __slots__ = ("eng", "fn", "is_dma", "dkey", "dval", "deps", "signal", "count")


class Prog:
    def __init__(self, nc, same_engine_sync=True):
        self.nc = nc
        self.ops = {e: [] for e in ALL_ENG}
        self.bufs = {}
        self.dma_counts = {}
        self.phase_keys = set()
        self.same_engine_sync = same_engine_sync
        self.ecount = {e: 0 for e in ALL_ENG}
        self.semstack = contextlib.ExitStack()
        self.esem = None
        self.dsem = {}
        self.barrier = []

    def _add(self, eng, fn, reads, writes, is_dma=False, dkey=None):
        op = _Op()
        op.eng, op.fn, op.is_dma, op.dkey = eng, fn, is_dma, dkey
        op.dval, op.signal, op.count = None, False, None
        deps = []
        for r in reads:
            b = self.bufs.get(r)
            if b is None:
                b = self.bufs[r] = _Buf()
            if b.writer is not None:
                deps.append(b.writer)
            b.readers.append(op)
        for w in writes:
            b = self.bufs.get(w)
            if b is None:
                b = self.bufs[w] = _Buf()
            if b.writer is not None:
                deps.append(b.writer)
            deps.extend(b.readers)
            b.writer = op
            b.readers = []
        out, seen = [], set()
        for d in deps:
            if d is op or id(d) in seen:
                continue
            seen.add(id(d))
            if not d.is_dma and d.eng == eng and (eng == "pe" or not self.same_engine_sync):
                continue
            out.append(d)
        op.deps = out
        if is_dma:
            c = self.dma_counts.get(dkey, 0) + 16
            self.dma_counts[dkey] = c
            self.phase_keys.add(dkey)
            op.dval = c
        self.ops[eng].append(op)
        return op

    def op(self, eng, fn, reads=(), writes=()):
        return self._add(eng, fn, reads, writes)

    def dma(self, eng, fn, reads=(), writes=(), key=None):
        return self._add(eng, fn, reads, writes, is_dma=True, dkey=key)

    def flush(self, final_wait_eng="sp"):
        nc = self.nc
        if self.esem is None:
            self.esem = {e: self.semstack.enter_context(nc.semaphore("s_" + e)) for e in ALL_ENG}
        for k in self.dma_counts:
            if k not in self.dsem:
                self.dsem[k] = self.semstack.enter_context(nc.semaphore("d_%d" % len(self.dsem)))
        esem, dsem = self.esem, self.dsem
        for e in ALL_ENG:
            ops = self.ops[e]
            for op in ops:
                for d in op.deps:
                    if not d.is_dma:
                        d.signal = True
            for op in reversed(ops):
                if not op.is_dma:
                    op.signal = True
                    break
        for e in ALL_ENG:
            c = self.ecount[e]
            for op in self.ops[e]:
                if op.signal and not op.is_dma:
                    c += 1
                    op.count = c
            self.ecount[e] = c
        barrier = self.barrier
        with nc.Block() as block:
            engmap = {"pe": block.tensor, "act": block.scalar, "dve": block.vector,
                      "pool": block.gpsimd, "sp": block.sync}

            def make(e):
                def body(eng):
                    known = {}
                    if self.ops[e]:
                        for key, sem, val in barrier:
                            if key == ("e", e):
                                known[key] = val
                                continue
                            eng.wait_ge(sem, val)
                            known[key] = val
                    for op in self.ops[e]:
                        for d in op.deps:
                            if d.is_dma:
                                key, val, sem = ("d", d.dkey), d.dval, dsem[d.dkey]
                            else:
                                key, val, sem = ("e", d.eng), d.count, esem[d.eng]
                            if known.get(key, 0) >= val:
                                continue
                            eng.wait_ge(sem, val)
                            known[key] = val
                        inst = op.fn(eng)
                        if op.is_dma:
                            inst.then_inc(dsem[op.dkey], 16)
                        elif op.signal:
                            inst.then_inc(esem[e], 1)
                    if e == final_wait_eng:
                        for k in self.phase_keys:
                            v = self.dma_counts[k]
                            if known.get(("d", k), 0) < v:
                                eng.wait_ge(dsem[k], v)
                return body

            for e in ALL_ENG:
                engmap[e](make(e))
        self.barrier = [(("e", e), esem[e], self.ecount[e]) for e in ALL_ENG if self.ecount[e] > 0]
        self.barrier += [(("d", k), dsem[k], self.dma_counts[k]) for k in self.dma_counts]
        self.ops = {e: [] for e in ALL_ENG}
        self.bufs = {}
        self.phase_keys = set()

    def close(self):
        self.semstack.close()


class Ctx:
    def __init__(self):
        self.nc = bass.Bass("TRN2", target_bir_lowering=False)
        self.P = Prog(self.nc)
        self.st = None
        self.rot = {}
        self.phase = -1
        self.begin()

    def begin(self):
        self.st = contextlib.ExitStack()
        self.rot = {}
        self.phase += 1

    def end(self):
        self.P.flush()
        self.st.close()
        self.st = None

    def sb(self, name, shape, dt):
        return self.st.enter_context(self.nc.sbuf_tensor("p%d_%s" % (self.phase, name), list(shape), dt))

    def ps(self, name, shape=(128, 512), dt=F32):
        return self.st.enter_context(self.nc.psum_tensor("p%d_%s" % (self.phase, name), list(shape), dt))

    def din(self, name, shape, dt):
        return self.nc.dram_tensor(name, list(shape), dt, kind="ExternalInput").ap()

    def dout(self, name, shape, dt):
        return self.nc.dram_tensor(name, list(shape), dt, kind="ExternalOutput").ap()

    def dint(self, name, shape, dt):
        return self.nc.dram_tensor(name, list(shape), dt, kind="Internal").ap()

    def nxt(self, key, n):
        i = self.rot.get(key, 0)
        self.rot[key] = i + 1
        return i % n

    def finish(self):
        if self.st is not None:
            self.end()
        self.P.close()
        return self.nc


class WStream:
    def __init__(self, cx, name, max_k, ncols, kpiece=4, npanel=2, nstage=3):
        self.cx, self.name = cx, name
        self.max_k, self.ncols, self.kpiece = max_k, ncols, kpiece
        self.panels = [cx.sb("%s_pan%d" % (name, i), (128, max_k, ncols), BF16) for i in range(npanel)]
        self.stages = [cx.sb("%s_stg%d" % (name, i), (128, kpiece, ncols), F32) for i in range(nstage)]
        self.npanel, self.nstage = npanel, nstage

    def load(self, W, kc0, nk, col0, ncols):
        cx, P = self.cx, self.cx.P
        pi = cx.nxt(self.name + "_p", self.npanel)
        pan = self.panels[pi]
        pname = "%s_pan%d" % (self.name, pi)
        k = 0
        while k < nk:
            kp = min(self.kpiece, nk - k)
            si = cx.nxt(self.name + "_s", self.nstage)
            stg = self.stages[si]
            sname = "%s_stg%d" % (self.name, si)
            src = W[(kc0 + k) * 128:(kc0 + k + kp) * 128, col0:col0 + ncols].rearrange("(k p) n -> p k n", p=128)
            dst = stg[:, 0:kp, 0:ncols]
            P.dma("sp", lambda e, dst=dst, src=src: e.dma_start(out=dst, in_=src), reads=[], writes=[sname],
                  key=sname)
            pdst = pan[:, k:k + kp, 0:ncols]
            P.op("pool", lambda e, pdst=pdst, dst=dst: e.tensor_copy(out=pdst, in_=dst), reads=[sname],
                 writes=[pname])
            k += kp
        return pan, pname

    def load_bf16(self, src, nk, ncols, dep):
        cx, P = self.cx, self.cx.P
        pi = cx.nxt(self.name + "_p", self.npanel)
        pan = self.panels[pi]
        pname = "%s_pan%d" % (self.name, pi)
        P.dma("sp", lambda e: e.dma_start(out=pan[:, 0:nk, 0:ncols], in_=src), reads=[dep], writes=[pname],
              key=pname + "_ld")
        return pan, pname


class Prefetch:
    def __init__(self, thunks, ahead=2):
        self.thunks, self.ahead, self.res = list(thunks), ahead, []

    def get(self, i):
        while len(self.res) < min(i + 1 + self.ahead, len(self.thunks)):
            self.res.append(self.thunks[len(self.res)]())
        return self.res[i]


def norm_transpose(cx, tag, src_blk, deps_blk, ntok_blocks, nfeat, gain_bc, gain_name, dstT, dst_name, dst_chunk0,
                   xt_bufs, ps_bufs, ident, small):
    P = cx.P
    nch = nfeat // 128
    sq, sqn = small["sq"]
    ss, ssn = small["ss"]
    for b in range(ntok_blocks):
        xi = cx.nxt(tag + "_xt", len(xt_bufs))
        xt, xname = xt_bufs[xi]
        srcb = src_blk(b)
        P.dma("sp", lambda e, xt=xt, srcb=srcb: e.dma_start(out=xt[:, 0:nfeat], in_=srcb), reads=deps_blk(b),
              writes=[xname], key=xname)
        P.op("act", lambda e, xt=xt: e.activation(out=sq[:, 0:nfeat], in_=xt[:, 0:nfeat], func=AF.Square,
                                                  accum_out=ss[:, 0:1]),
             reads=[xname], writes=[sqn, ssn])
        P.op("act", lambda e: e.activation(out=ss[:, 1:2], in_=ss[:, 0:1], func=AF.Sqrt, scale=1.0 / nfeat,
                                           bias=small["eps"][0][:, 0:1]),
             reads=[ssn, small["eps"][1]], writes=[ssn + "b"])
        P.op("dve", lambda e: e.reciprocal(out=ss[:, 2:3], in_=ss[:, 1:2]), reads=[ssn + "b"], writes=[ssn + "c"])
        P.op("dve", lambda e, xt=xt: e.scalar_tensor_tensor(out=xt[:, 0:nfeat], in0=xt[:, 0:nfeat], scalar=ss[:, 2:3],
                                                             in1=gain_bc[:, 0:nfeat], op0=ALU.mult, op1=ALU.mult),
             reads=[xname, ssn + "c", gain_name], writes=[xname])
        for c0 in range(0, nch, 4):
            nc4 = min(4, nch - c0)
            pi = cx.nxt("nt_ps", len(ps_bufs))
            ps, psn = ps_bufs[pi]
            for j in range(nc4):
                P.op("pe", lambda e, ps=ps, xt=xt, j=j, c0=c0: e.transpose(ps[:, j * 128:(j + 1) * 128],
                                                                          xt[:, (c0 + j) * 128:(c0 + j + 1) * 128],
                                                                          ident[:]),
                     reads=[xname, "ident"], writes=[psn])
            dst = dstT[:, dst_chunk0 + c0:dst_chunk0 + c0 + nc4, b * 128:(b + 1) * 128]
            src_ps = ps[:, 0:nc4 * 128].rearrange("p (c t) -> p c t", c=nc4)
            P.op("act", lambda e, dst=dst, src_ps=src_ps: e.activation(out=dst, in_=src_ps, func=AF.Copy),
                 reads=[psn], writes=[dst_name])


D_MODEL, SEQ, D_FF = 4096, 8192, 11008
D_IN, D_IN_PAD = 9776, 9856
NM_OWN = SEQ // 128 // NCORES
NT_OWN = NM_OWN * 128
QSCALE = 128 ** -0.5
KC = D_MODEL // 128
C_QSB, C_KSB, C_VSB, C_QN = 0, 2048, 4096, 6144
C_KC, C_VC, C_KSL, C_VSL, C_KW, C_VW, C_G = 8192, 8448, 8704, 8960, 9216, 9472, 9728
KROW_SB, KROW_KC, KROW_VC, KROW_KSL, KROW_KW, KROWS = 0, 2048, 2304, 2560, 2816, 3072
VCOL_SB, VCOL_VSL, VCOL_VW, VCOLS = 0, 2048, 2304, 2560
QROW_SB, QROW_N, QROWS = 0, 2048, 4096


def own_chunk(m):
    return 8 * m + 7


def small_tiles(cx, sq_ap, sq_name):
    return {"sq": (sq_ap, sq_name), "ss": (cx.sb("ss", (128, 4), F32), "ss"),
            "eps": (cx.sb("epsc", (128, 1), F32), "epsc")}


def phase_1a(cx, io):
    P = cx.P
    xu, w, qT, gnd = io["xu"], io["w_in"], io["qT"], io["gn"]
    ident = cx.sb("ident_sb", (128, 128), F32)
    gain = cx.sb("gain_sb", (128, D_MODEL), F32)
    hT = cx.sb("hT", (128, KC, NT_OWN), BF16)
    xts = [(cx.sb("xt%d" % i, (128, D_MODEL), F32), "xt%d" % i) for i in range(2)]
    sqt = cx.sb("sq", (128, D_MODEL), BF16)
    small = small_tiles(cx, sqt, "sq")
    pss = [(cx.ps("ps%d" % i), "ps%d" % i) for i in range(8)]
    ws = WStream(cx, "w", KC, 256)
    ostg = [(cx.sb("ostg%d" % i, (128, NT_OWN), BF16), "ostg%d" % i) for i in range(2)]
    gstg = [(cx.sb("gstg%d" % i, (128, 128), F32), "gstg%d" % i) for i in range(2)]

    P.dma("sp", lambda e: e.dma_start(out=ident[:], in_=io["ident"]), writes=["ident"], key="c_ident")
    P.dma("sp", lambda e: e.dma_start(out=gain[:], in_=io["g_attn"]), writes=["gain"], key="c_gain")
    P.op("dve", lambda e: e.memset(small["eps"][0][:], EPS), writes=["epsc"])
    norm_transpose(cx, "l1", lambda b: xu[own_chunk(b) * 128:(own_chunk(b) + 1) * 128, :], lambda b: [],
                   NM_OWN, D_MODEL, gain, "gain", hT, "hT", 0, xts, pss, ident, small)

    qpanels = [(c0 + pcol, r0 + pcol) for (c0, r0) in ((C_QSB, QROW_SB), (C_QN, QROW_N)) for pcol in range(0, 2048, 256)]
    pf = Prefetch([(lambda c=c: ws.load(w, 0, KC, c, 256)) for (c, _) in qpanels] +
                  [lambda: ws.load(w, 0, KC, C_G, 128)], ahead=1)
    for qi, (cabs, rabs) in enumerate(qpanels):
        pan, pname = pf.get(qi)
        for j0 in (0, 128):
            og, ogn = ostg[cx.nxt("ostg", 2)]
            for th in range(NT_OWN // 512):
                ps, psn = pss[cx.nxt("mm_ps", 8)]
                for k in range(KC):
                    P.op("pe", lambda e, ps=ps, pan=pan, k=k, j0=j0, th=th: e.matmul(
                        ps[:, 0:512], pan[:, k, j0:j0 + 128], hT[:, k, th * 512:(th + 1) * 512],
                        start=(k == 0), stop=(k == KC - 1)), reads=[pname, "hT"], writes=[psn])
                P.op("act", lambda e, og=og, ps=ps, th=th: e.activation(
                    out=og[:, th * 512:(th + 1) * 512], in_=ps[:, 0:512], func=AF.Copy, scale=QSCALE),
                    reads=[psn], writes=[ogn])
            row = rabs + j0
            P.dma("sp", lambda e, og=og, row=row: e.dma_start(out=qT[row:row + 128, :], in_=og[:]),
                  reads=[ogn], writes=[], key=ogn + "_st")
    pan, pname = pf.get(len(qpanels))
    for b in range(NM_OWN):
        ps, psn = pss[cx.nxt("mm_ps", 8)]
        for k in range(KC):
            P.op("pe", lambda e, ps=ps, k=k, b=b: e.matmul(ps[:, 0:128], hT[:, k, b * 128:(b + 1) * 128],
                                                           pan[:, k, 0:128], start=(k == 0), stop=(k == KC - 1)),
                 reads=[pname, "hT"], writes=[psn])
        gs, gsn = gstg[cx.nxt("gstg", 2)]
        P.op("act", lambda e, gs=gs, ps=ps: e.activation(out=gs[:], in_=ps[:, 0:128], func=AF.Copy), reads=[psn],
             writes=[gsn])
        P.dma("sp", lambda e, gs=gs, b=b: e.dma_start(out=gnd[b * 128:(b + 1) * 128, :], in_=gs[:]), reads=[gsn],
              writes=[], key=gsn + "_st")


K_PANELS = [(C_KSB + p, KROW_SB + p) for p in range(0, 2048, 256)] + \
           [(C_KC, KROW_KC), (C_VC, KROW_VC), (C_KSL, KROW_KSL), (C_KW, KROW_KW)]
V_PANELS = [(C_VSB + p, VCOL_SB + p) for p in range(0, 2048, 256)] + [(C_VSL, VCOL_VSL), (C_VW, VCOL_VW)]


def phase_1b(cx, io):
    P = cx.P
    xu, w, kT, vtok, wbf = io["xu"], io["w_in"], io["kT"], io["vtok"], io["wbf"]
    NTILE = SEQ // 1024
    ident = cx.sb("ident_sb", (128, 128), F32)
    gain = cx.sb("gain_sb", (128, D_MODEL), F32)
    hT = cx.sb("hT", (128, KC, 1024), BF16)
    xts = [(cx.sb("xt%d" % i, (128, D_MODEL), F32), "xt%d" % i) for i in range(2)]
    sqt = cx.sb("sq", (128, D_MODEL), BF16)
    small = small_tiles(cx, sqt, "sq")
    pss = [(cx.ps("ps%d" % i), "ps%d" % i) for i in range(8)]
    ws = WStream(cx, "w", KC, 256, npanel=3)
    ostg = [(cx.sb("ostg%d" % i, (128, 1024), BF16), "ostg%d" % i) for i in range(2)]
    vstg = [(cx.sb("vstg%d" % i, (128, 8, 256), BF16), "vstg%d" % i) for i in range(2)]

    P.dma("sp", lambda e: e.dma_start(out=ident[:], in_=io["ident"]), writes=["ident"], key="c_ident")
    P.dma("sp", lambda e: e.dma_start(out=gain[:], in_=io["g_attn"]), writes=["gain"], key="c_gain")
    P.op("dve", lambda e: e.memset(small["eps"][0][:], EPS), writes=["epsc"])
    panels = [("k",) + p for p in K_PANELS] + [("v",) + p for p in V_PANELS]

    def first_load(pi_, c0):
        pan, pname = ws.load(w, 0, KC, c0, 256)
        P.dma("sp", lambda e: e.dma_start(out=wbf[pi_], in_=pan[:, 0:KC, 0:256]), reads=[pname], writes=["wbf"],
              key="wbf_st")
        return pan, pname

    thunks = []
    for t in range(NTILE):
        for pi_, (kind, c0, r0) in enumerate(panels):
            if t == 0:
                thunks.append(lambda pi_=pi_, c0=c0: first_load(pi_, c0))
            else:
                thunks.append(lambda pi_=pi_: ws.load_bf16(wbf[pi_], KC, 256, "wbf"))
    pf = Prefetch(thunks, ahead=2)
    for t in range(NTILE):
        norm_transpose(cx, "l1", lambda b, t=t: xu[t * 1024 + b * 128:t * 1024 + (b + 1) * 128, :], lambda b: [],
                       8, D_MODEL, gain, "gain", hT, "hT", 0, xts, pss, ident, small)
        for pi_, (kind, c0, r0) in enumerate(panels):
            pan, pname = pf.get(t * len(panels) + pi_)
            if kind == "k":
                for j0 in (0, 128):
                    og, ogn = ostg[cx.nxt("ostg", 2)]
                    for th in range(2):
                        ps, psn = pss[cx.nxt("mm_ps", 8)]
                        for k in range(KC):
                            P.op("pe", lambda e, ps=ps, pan=pan, k=k, j0=j0, th=th: e.matmul(
                                ps[:, 0:512], pan[:, k, j0:j0 + 128], hT[:, k, th * 512:(th + 1) * 512],
                                start=(k == 0), stop=(k == KC - 1)), reads=[pname, "hT"], writes=[psn])
                        P.op("act", lambda e, og=og, ps=ps, th=th: e.activation(
                            out=og[:, th * 512:(th + 1) * 512], in_=ps[:, 0:512], func=AF.Copy),
                            reads=[psn], writes=[ogn])
                    row = r0 + j0
                    P.dma("sp", lambda e, og=og, row=row, t=t: e.dma_start(
                        out=kT[row:row + 128, t * 1024:(t + 1) * 1024], in_=og[:]), reads=[ogn], writes=[],
                        key=ogn + "_st")
            else:
                vs, vsn = vstg[cx.nxt("vstg", 2)]
                for b in range(8):
                    ps, psn = pss[cx.nxt("mm_ps", 8)]
                    for k in range(KC):
                        P.op("pe", lambda e, ps=ps, pan=pan, k=k, b=b: e.matmul(
                            ps[:, 0:256], hT[:, k, b * 128:(b + 1) * 128], pan[:, k, 0:256],
                            start=(k == 0), stop=(k == KC - 1)), reads=[pname, "hT"], writes=[psn])
                    P.op("act", lambda e, vs=vs, ps=ps, b=b: e.activation(out=vs[:, b, :], in_=ps[:, 0:256],
                                                                          func=AF.Copy), reads=[psn], writes=[vsn])
                dst = vtok[t * 1024:(t + 1) * 1024, r0:r0 + 256].rearrange("(b p) c -> p b c", p=128)
                P.dma("sp", lambda e, vs=vs, dst=dst: e.dma_start(out=dst, in_=vs[:]), reads=[vsn], writes=[],
                      key=vsn + "_st")


def sb_consts():
    j = np.arange(128)
    negU = np.where(j[:, None] >= j[None, :], -1.0, 0.0).astype(NPBF)
    tri = (j[:, None] < j[None, :]).astype(np.float32).astype(NPBF)
    negones = np.full((128, 1), -1.0, dtype=np.float32).astype(NPBF)
    return {"negU": negU, "tri_strict": tri, "negones": negones}


def nsa_slopes():
    h = np.arange(1, 17, dtype=np.float32)
    return (2.0 ** (-8.0 * h / 16)).astype(np.float32)


def nsa_tables(c, NM):
    NJ, NCU, NUC = 16 * NM, 64 * NM, 8 * NM
    NCC = max(1, NCU // 128)
    sl = nsa_slopes()
    i = np.arange(128)
    T = {}
    rel = np.arange(NUC)
    T["ak"] = (sl[None, :, None] * (i[:, None, None] - 64.0 - 128.0 * rel[None, None, :])).astype(np.float32)
    ac = np.zeros((128, 16, NM, NCC), np.float32)
    cm = np.zeros((128, NM, NCC, 128), np.float32)
    wm = np.zeros((128, NM, 5, 128), np.float32)
    bonus = np.zeros((128, NM, NJ), np.float32)
    tl = np.arange(128)
    for m in range(NM):
        u0 = 128 * (8 * m + 7)
        ut = u0 + tl
        for ncc in range(NCC):
            nu = 128 * ncc + i
            cend = 16 * nu + 31
            ac[:, :, m, ncc] = sl[None, :] * (cend[:, None] - (u0 + 64.0))
            cm[:, m, ncc, :] = ((cend[:, None] <= ut[None, :]) & (nu[:, None] >= 8 * (7 - c))).astype(np.float32)
        for r in range(5):
            uk = 128 * (8 * m + 3 + r) + i
            d = ut[None, :] - uk[:, None]
            wm[:, m, r, :] = ((d >= 0) & (d < 512) & (uk[:, None] >= 128 * (7 - c))).astype(np.float32)
        cur = ut // 64
        jj = np.arange(NJ)
        valid = (jj[None, :] <= cur[:, None]) & (jj[None, :] >= 2 * (7 - c))
        forced = (jj[None, :] == 2 * (7 - c)) | (jj[None, :] == cur[:, None]) | (jj[None, :] == cur[:, None] - 1)
        bonus[:, m, :] = np.where(valid, np.where(forced, 1e6, 0.0), -1e30)
    T["ac"] = np.minimum(ac, 45.0)
    T["cm"] = cm.astype(NPBF)
    T["wm"] = wm.astype(NPBF)
    T["bonus"] = bonus
    T["ex"] = np.broadcast_to((np.arange(NJ) >= 2 * (7 - c)).astype(np.float32), (128, NJ)).copy()
    T["tri_incl"] = (i[:, None] <= i[None, :]).astype(np.float32).astype(NPBF)
    T["exf"] = np.broadcast_to((np.arange(8) >= (7 - c)).astype(np.float32), (128, 8)).copy()
    E = np.zeros((NJ, NUC, 128), np.float32)
    for uc in range(NUC):
        for half in range(2):
            if 2 * uc + half < NJ:
                E[2 * uc + half, uc, half * 64:(half + 1) * 64] = 1.0
    T["E"] = E.astype(NPBF)
    caug = np.zeros((128, NCC, 1 + NJ), np.float32)
    caug[:, :, 0] = 1.0
    for ncc in range(NCC):
        for il in range(128):
            nu = 128 * ncc + il
            for j in range(NJ):
                for mm in range(4):
                    for nn in range(2):
                        if 4 * j - mm - nn == nu:
                            caug[il, ncc, 1 + j] += 1.0
    T["caug"] = caug.astype(NPBF)
    T["ident"] = np.eye(128, dtype=np.float32)
    return T


def _bcast_mid(ap2d, n):
    a = ap2d.ap
    return bass.AP(ap2d.tensor, ap2d.offset, [list(a[0]), [0, n], list(a[-1])])


def phase_2a(cx, io):
    P = cx.P
    S = SEQ
    NCH = S // 128
    kTd, vtokd, qTd, out = io["kT"], io["vtok"], io["qT"], io["osb"]
    negU = cx.sb("negU_sb", (128, 128), BF16)
    tri = cx.sb("tri_sb", (128, 128), BF16)
    negones = cx.sb("negones_sb", (128, 1), BF16)
    exf = cx.sb("exf_sb", (128, 8), F32)
    P.dma("sp", lambda e: e.dma_start(out=negU[:], in_=io["negU"]), writes=["negU"], key="ld_c0")
    P.dma("sp", lambda e: e.dma_start(out=tri[:], in_=io["tri_strict"]), writes=["tri"], key="ld_c1")
    P.dma("sp", lambda e: e.dma_start(out=negones[:], in_=io["negones"]), writes=["negones"], key="ld_c2")
    P.dma("sp", lambda e: e.dma_start(out=exf[:], in_=io["exf"]), writes=["exf"], key="ld_c3")
    HPC = 2
    sets = []
    for s_ in range(2):
        sets.append({
            "k": (cx.sb("kTb%d" % s_, (128, HPC, S), BF16), "kTb%d" % s_),
            "v": (cx.sb("vb%d" % s_, (128, HPC, NCH, 128), BF16), "vb%d" % s_),
            "q": (cx.sb("qb%d" % s_, (128, HPC, NT_OWN), BF16), "qb%d" % s_)})
    psZ = [[(cx.ps("psZ%d_%d" % (i, j)), "psZ%d_%d" % (i, j)) for j in range(2)] for i in range(HPC)]
    psC = [(cx.ps("psC%d" % i, (128, 8)), "psC%d" % i) for i in range(HPC)]
    psO = [(cx.ps("psO%d" % i), "psO%d" % i) for i in range(HPC)]
    esb = [[(cx.sb("esb%d_%d" % (i, j), (128, 512), F32), "esb%d_%d" % (i, j)) for j in range(2)] for i in range(HPC)]
    spb = [[(cx.sb("spb%d_%d" % (i, j), (128, 512), BF16), "spb%d_%d" % (i, j)) for j in range(2)] for i in range(HPC)]
    Ab = [[(cx.sb("Ab%d_%d" % (i, j), (128, 512), BF16), "Ab%d_%d" % (i, j)) for j in range(2)] for i in range(HPC)]
    acc = [(cx.sb("acc%d" % i, (128, 4, 128), F32), "acc%d" % i) for i in range(HPC * 2)]
    car = [(cx.sb("car%d" % i, (128, 4), F32), "car%d" % i) for i in range(HPC)]
    ecb = [(cx.sb("ec%d" % i, (128, 4), F32), "ec%d" % i) for i in range(HPC)]

    def stage_a(it):
        hh, uc, c0, q0, par = it["hh"], it["uc"], it["c0"], it["q0"], it["par"]
        (kT, kTn), (qT, qn) = it["k"], it["q"]
        pz, pzn = psZ[hh][par]
        es, esn = esb[hh][par]
        sp, spn = spb[hh][par]
        P.op("pe", lambda e: e.matmul(pz[:, c0:512], kT[:, hh, uc * 128:(uc + 1) * 128], qT[:, hh, q0 + c0:q0 + 512],
                                      start=True, stop=False), reads=[kTn, qn], writes=[pzn])
        P.op("act", lambda e: e.activation(out=es[:, c0:512], in_=pz[:, c0:512], func=AF.Exp), reads=[pzn],
             writes=[esn])
        P.op("act", lambda e: e.activation(out=sp[:, c0:512], in_=es[:, c0:512], func=AF.Ln, bias=1.0, scale=1.0),
             reads=[esn], writes=[spn])
        if it["diag"]:
            P.op("dve", lambda e: e.tensor_tensor(out=sp[:, c0:c0 + 128], in0=sp[:, c0:c0 + 128], in1=tri[:],
                                                  op=ALU.mult), reads=[spn, "tri"], writes=[spn])
        if uc < 7:
            P.op("dve", lambda e: e.tensor_scalar(out=sp[:, c0:512], in0=sp[:, c0:512], scalar1=exf[:, uc:uc + 1],
                                                  scalar2=None, op0=ALU.mult), reads=[spn, "exf"], writes=[spn])

    def stage_b(it):
        hh, c0, b0, par = it["hh"], it["c0"], it["b0"], it["par"]
        pz, pzn = psZ[hh][par]
        pc, pcn = psC[hh]
        sp, spn = spb[hh][par]
        A, An = Ab[hh][par]
        P.op("pe", lambda e: e.matmul(pz[:, c0:512], negU[:], sp[:, c0:512], start=False, stop=True),
             reads=[spn, "negU"], writes=[pzn])
        for b in range(b0, 4):
            P.op("pe", lambda e, b=b: e.matmul(pc[:, b:b + 1], sp[:, b * 128:(b + 1) * 128], negones[:], start=True,
                                               stop=True), reads=[spn, "negones"], writes=[pcn])
        P.op("act", lambda e: e.activation(out=A[:, c0:512], in_=pz[:, c0:512], func=AF.Exp), reads=[pzn],
             writes=[An])
        if it["diag"]:
            P.op("dve", lambda e: e.tensor_tensor(out=A[:, c0:c0 + 128], in0=A[:, c0:c0 + 128], in1=tri[:],
                                                  op=ALU.mult), reads=[An, "tri"], writes=[An])

    def stage_c(it):
        hh, uc, b0, par, uq = it["hh"], it["uc"], it["b0"], it["par"], it["uq"]
        (v, vn) = it["v"]
        pc, pcn = psC[hh]
        po, pon = psO[hh]
        A, An = Ab[hh][par]
        ac, acn = it["acc"]
        cr, crn = car[hh]
        ec, ecn = ecb[hh]
        for b in range(b0, 4):
            P.op("pe", lambda e, b=b: e.matmul(po[:, b * 128:(b + 1) * 128], A[:, b * 128:(b + 1) * 128],
                                               v[:, hh, uc, :], start=True, stop=True), reads=[An, vn], writes=[pon])
        for b in range(b0, 4):
            if uc == uq[b]:
                P.op("dve", lambda e, b=b: e.tensor_copy(out=ac[:, b, :], in_=po[:, b * 128:(b + 1) * 128]),
                     reads=[pon], writes=[acn])
            else:
                P.op("dve", lambda e, b=b: e.scalar_tensor_tensor(
                    out=ac[:, b, :], in0=po[:, b * 128:(b + 1) * 128], scalar=ec[:, b:b + 1], in1=ac[:, b, :],
                    op0=ALU.mult, op1=ALU.add), reads=[pon, ecn, acn], writes=[acn])
        if uc > 0:
            if it["diag"]:
                P.op("dve", lambda e: e.tensor_copy(out=cr[:, b0:b0 + 1], in_=pc[:, b0:b0 + 1]), reads=[pcn],
                     writes=[crn])
                if b0 + 1 < 4:
                    P.op("dve", lambda e: e.tensor_tensor(out=cr[:, b0 + 1:4], in0=cr[:, b0 + 1:4],
                                                          in1=pc[:, b0 + 1:4], op=ALU.add), reads=[pcn, crn],
                         writes=[crn])
            else:
                P.op("dve", lambda e: e.tensor_tensor(out=cr[:, b0:4], in0=cr[:, b0:4], in1=pc[:, b0:4], op=ALU.add),
                     reads=[pcn, crn], writes=[crn])
            P.op("act", lambda e: e.activation(out=ec[:, b0:4], in_=cr[:, b0:4], func=AF.Exp), reads=[crn],
                 writes=[ecn])
        if it["last"]:
            h, G = it["h"], it["G"]
            dst = out[512 * G:512 * G + 512, h * 128:(h + 1) * 128].rearrange("(b p) d -> p b d", p=128)
            P.dma("sp", lambda e: e.dma_start(out=dst, in_=ac[:]), reads=[acn], key=acn + "_st")

    for pair in range(16 // HPC):
        st = sets[pair % 2]
        (kT, kTn), (v, vn), (qT, qn) = st["k"], st["v"], st["q"]
        for hh in range(HPC):
            h = pair * HPC + hh
            P.dma("sp", lambda e, hh=hh, h=h, kT=kT: e.dma_start(
                out=kT[:, hh, :], in_=kTd[KROW_SB + 128 * h:KROW_SB + 128 * (h + 1), :]), writes=[kTn],
                key=kTn + "_%d" % hh)
            P.dma("sp", lambda e, hh=hh, h=h, v=v: e.dma_start(
                out=v[:, hh, :, :],
                in_=vtokd[:, VCOL_SB + 128 * h:VCOL_SB + 128 * (h + 1)].rearrange("(c i) d -> i c d", i=128)),
                writes=[vn], key=vn + "_%d" % hh)
            P.dma("sp", lambda e, hh=hh, h=h, qT=qT: e.dma_start(
                out=qT[:, hh, :], in_=qTd[QROW_SB + 128 * h:QROW_SB + 128 * (h + 1), :]), writes=[qn],
                key=qn + "_%d" % hh)
        items = []
        step = 0
        for G in range(NM_OWN // 4):
            uq = [own_chunk(4 * G + b) for b in range(4)]
            accs = {hh: acc[cx.nxt("acc%d" % hh, 2) * HPC + hh] for hh in range(HPC)}
            for uc in range(uq[3], -1, -1):
                b0 = min(b for b in range(4) if uq[b] >= uc)
                for hh in range(HPC):
                    items.append({"hh": hh, "h": pair * HPC + hh, "G": G, "uc": uc, "b0": b0, "c0": b0 * 128,
                                  "diag": uc == uq[b0], "par": step % 2, "uq": uq, "q0": 512 * G, "acc": accs[hh],
                                  "last": uc == 0, "k": (kT, kTn), "q": (qT, qn), "v": (v, vn)})
                step += 1
        n = len(items)
        for i in range(n + 2):
            if i < n:
                stage_a(items[i])
            if 0 <= i - 1 < n:
                stage_b(items[i - 1])
            if 0 <= i - 2 < n:
                stage_c(items[i - 2])


def phase_2b(cx, io):
    P = cx.P
    S, NM = SEQ, NM_OWN
    NJ, NCU, NUC = 16 * NM, 64 * NM, 8 * NM
    NCC = max(1, NCU // 128)
    NT = NM * 128
    NCV = 129 + NJ
    kTd, vtokd, qTd = io["kT"], io["vtok"], io["qT"]
    d_w1 = {"k": io["w_k1"], "v": io["w_v1"]}
    d_w2 = {"k": io["w_k2"], "v": io["w_v2"]}
    d_pos = {"k": io["posTk"], "v": io["posTv"]}
    d_ak, d_ac, d_cm, d_wm, d_tri = io["ak"], io["ac"], io["cm"], io["wm"], io["tri_incl"]
    d_bonus, d_ex, d_E, d_caug, d_ident = io["bonus"], io["ex"], io["E"], io["caug"], io["ident"]
    d_gn = io["gn"][:, 0:48].rearrange("(m p) c -> p m c", p=128)
    out = io["onsa"]

    def const(name, d, shape, dt):
        t = cx.sb(name + "_sb", shape, dt)
        P.dma("sp", lambda e: e.dma_start(out=t[:], in_=d), writes=[name], key="ld_" + name)
        return t

    ak = const("ak", d_ak, (128, 16, NUC), F32)
    ac = const("ac", d_ac, (128, 16, NM, NCC), F32)
    cm = const("cm", d_cm, (128, NM, NCC, 128), BF16)
    wm = const("wm", d_wm, (128, NM, 5, 128), BF16)
    tri = const("tri", d_tri, (128, 128), BF16)
    bonus = const("bonus", d_bonus, (128, NM, NJ), F32)
    ex = const("ex", d_ex, (128, NJ), F32)
    E = const("E", d_E, (NJ, NUC, 128), BF16)
    ident = const("ident", d_ident, (128, 128), F32)
    gn = const("gn", d_gn, (128, NM, 48), F32)

    qg = cx.sb("qg", (128, 8, NT), BF16)
    ksl = cx.sb("ksl", (128, S), BF16)
    vsl = cx.sb("vsl_sb", (128, NUC, 129), BF16)
    kw = cx.sb("kw", (128, NM, 5, 128), BF16)
    vw = cx.sb("vw_sb", (128, NM, 5, 129), BF16)
    cbuf = cx.sb("cbuf", (128, S + 16), BF16)
    kcmpT = cx.sb("kcmpT", (128, NCU), BF16)
    vcmp = cx.sb("vcmp", (128, NCC, NCV), BF16)
    gates = cx.sb("gates", (128, NM, 48), F32)
    gtmp = cx.sb("gtmp", (128, NM, 48), F32)
    posf = cx.sb("posf", (128, 32), F32)
    posb = cx.sb("posb", (128, 32), BF16)
    biasT = cx.sb("biasT", (128, 2), F32)
    HT = cx.sb("HT", (128, 2, NCU), BF16)
    ub = cx.sb("ub", (128, NCU), F32)
    u2 = cx.sb("u2", (128, NCU), F32)
    ws = WStream(cx, "w", 32, 256, kpiece=4, npanel=1, nstage=2)
    combs = [(cx.sb("comb%d" % i, (128, 8, 128), F32), "comb%d" % i) for i in range(2)]
    imp = cx.sb("imp", (128, NJ), F32)
    score = cx.sb("score", (128, NJ), F32)
    score2 = cx.sb("score2", (128, NJ), F32)
    m8 = cx.sb("m8", (128, 16), F32)
    sel = cx.sb("sel", (128, NJ), F32)
    selT = cx.sb("selT", (NJ, 128), BF16)
    Pts = [(cx.sb("Pt%d" % i, (128, 512), BF16), "Pt%d" % i) for i in range(2)]
    mts = [(cx.sb("mt%d" % i, (128, 128), BF16), "mt%d" % i) for i in range(2)]
    rts = [(cx.sb("rt%d" % i, (128, 4), F32), "rt%d" % i) for i in range(4)]
    psS = [(cx.ps("psS%d" % i), "psS%d" % i) for i in range(2)]
    psA = [(cx.ps("psA%d" % i), "psA%d" % i) for i in range(4)]
    psM = (cx.ps("psM"), "psM")
    psX = (cx.ps("psX"), "psX")

    P.op("pool", lambda e: e.memset(vsl[:, :, 128:129], 1.0), writes=["vsl"])
    P.op("pool", lambda e: e.memset(vw[:, :, :, 128:129], 1.0), writes=["vw"])
    P.op("pool", lambda e: e.memset(cbuf[:, S:S + 16], 0.0), writes=["cbuf"])
    P.op("act", lambda e: e.activation(out=gtmp[:], in_=gn[:], func=AF.Exp, scale=-1.0), reads=["gn"], writes=["gtmp"])
    P.op("dve", lambda e: e.tensor_scalar(out=gtmp[:], in0=gtmp[:], scalar1=1.0, scalar2=None, op0=ALU.add),
         reads=["gtmp"], writes=["gtmp"])
    P.op("dve", lambda e: e.reciprocal(out=gates[:], in_=gtmp[:]), reads=["gtmp"], writes=["gates"])

    def mlp(g, which):
        r0 = (KROW_KC if which == "k" else KROW_VC) + 128 * g
        P.dma("sp", lambda e: e.dma_start(out=cbuf[:, 0:S], in_=kTd[r0:r0 + 128, :]), writes=["cbuf"], key="ld_cbuf")
        P.dma("sp", lambda e: e.dma_start(out=posf[:], in_=d_pos[which]), writes=["posf"], key="ld_pos")
        P.op("pool", lambda e: e.tensor_copy(out=posb[:], in_=posf[:]), reads=["posf"], writes=["posb"])
        pan, pname = ws.load(d_w1[which], 0, 32, 0, 256)
        px, pxn = psX
        for hc in range(2):
            for l in range(32):
                P.op("pe", lambda e, hc=hc, l=l: e.matmul(px[:, hc:hc + 1], pan[:, l, hc * 128:(hc + 1) * 128],
                                                          posb[:, l:l + 1], start=(l == 0), stop=(l == 31)),
                     reads=[pname, "posb"], writes=[pxn])
            P.op("dve", lambda e, hc=hc: e.tensor_copy(out=biasT[:, hc:hc + 1], in_=px[:, hc:hc + 1]), reads=[pxn],
                 writes=["biasT"])
        for hc in range(2):
            ps, psn = psS[hc]
            for l in range(32):
                P.op("pe", lambda e, ps=ps, hc=hc, l=l: e.matmul(ps[:, 0:NCU], pan[:, l, hc * 128:(hc + 1) * 128],
                                                                cbuf[:, l:l + 16 * (NCU - 1) + 1:16], start=(l == 0),
                                                                stop=(l == 31)),
                     reads=[pname, "cbuf"], writes=[psn])
            P.op("act", lambda e, ps=ps, hc=hc: e.activation(out=ub[:], in_=ps[:, 0:NCU], func=AF.Identity,
                                                            bias=biasT[:, hc:hc + 1], scale=1.0),
                 reads=[psn, "biasT"], writes=["ub"])
            P.op("dve", lambda e: e.tensor_tensor(out=u2[:], in0=ub[:], in1=ub[:], op=ALU.mult), reads=["ub"],
                 writes=["u2"])
            P.op("dve", lambda e: e.tensor_scalar(out=u2[:], in0=u2[:], scalar1=0.044715, scalar2=1.0, op0=ALU.mult,
                                                  op1=ALU.add), reads=["u2"], writes=["u2"])
            P.op("dve", lambda e: e.tensor_tensor(out=u2[:], in0=u2[:], in1=ub[:], op=ALU.mult), reads=["u2", "ub"],
                 writes=["u2"])
            P.op("act", lambda e: e.activation(out=u2[:], in_=u2[:], func=AF.Exp, scale=-1.5957691216057308),
                 reads=["u2"], writes=["u2"])
            P.op("dve", lambda e: e.tensor_scalar(out=u2[:], in0=u2[:], scalar1=1.0, scalar2=None, op0=ALU.add),
                 reads=["u2"], writes=["u2"])
            P.op("dve", lambda e: e.reciprocal(out=u2[:], in_=u2[:]), reads=["u2"], writes=["u2"])
            P.op("dve", lambda e, hc=hc: e.tensor_tensor(out=HT[:, hc, :], in0=u2[:], in1=ub[:], op=ALU.mult),
                 reads=["u2", "ub"], writes=["HT"])
        pan2, p2name = ws.load(d_w2[which], 0, 2, 0, 128)
        if which == "k":
            ps, psn = psS[0]
            for hc in range(2):
                P.op("pe", lambda e, hc=hc: e.matmul(ps[:, 0:NCU], pan2[:, hc, 0:128], HT[:, hc, :], start=(hc == 0),
                                                     stop=(hc == 1)), reads=[p2name, "HT"], writes=[psn])
            P.op("act", lambda e: e.activation(out=kcmpT[:], in_=ps[:, 0:NCU], func=AF.Copy), reads=[psn],
                 writes=["kcmpT"])
        else:
            for ncc in range(NCC):
                ps, psn = psS[ncc % 2]
                for hc in range(2):
                    P.op("pe", lambda e, ps=ps, hc=hc, ncc=ncc: e.matmul(ps[:, 0:128],
                                                                        HT[:, hc, ncc * 128:(ncc + 1) * 128],
                                                                        pan2[:, hc, 0:128], start=(hc == 0),
                                                                        stop=(hc == 1)),
                         reads=[p2name, "HT"], writes=[psn])
                P.op("act", lambda e, ps=ps, ncc=ncc: e.activation(out=vcmp[:, ncc, 0:128], in_=ps[:, 0:128],
                                                                  func=AF.Copy), reads=[psn], writes=["vcmp"])

    def chunk(g, m, quad, keysT, bias_ap_fn, mask2d, mask_name, vrhs, vname, ncv, first, last, kname):
        si = cx.nxt("psS", 2)
        ps, psn = psS[si]
        pt, ptn = Pts[si]
        rhs = qg[:, 4 * quad:4 * quad + 4, m * 128:(m + 1) * 128]
        P.op("pe", lambda e: e.matmul(ps[:, 0:512].rearrange("p (h t) -> p h t", h=4), keysT, rhs, start=True,
                                      stop=True), reads=[kname, "qg"], writes=[psn])
        for h in range(4):
            b = bias_ap_fn(8 * g + 4 * quad + h)
            P.op("act", lambda e, h=h, b=b: e.activation(out=pt[:, h * 128:(h + 1) * 128],
                                                          in_=ps[:, h * 128:(h + 1) * 128], func=AF.Exp, bias=b,
                                                          scale=1.0), reads=[psn, "ak", "ac"], writes=[ptn])
        if mask2d is not None:
            pt3 = pt[:, 0:512].rearrange("p (h t) -> p h t", h=4)
            P.op("dve", lambda e: e.tensor_tensor(out=pt3, in0=pt3, in1=_bcast_mid(mask2d, 4), op=ALU.mult),
                 reads=[ptn, mask_name], writes=[ptn])
        def s2():
            for h in range(4):
                pa, pan_ = psA[h]
                P.op("pe", lambda e, h=h, pa=pa: e.matmul(pa[:, 0:ncv], pt[:, h * 128:(h + 1) * 128], vrhs,
                                                          start=first, stop=last), reads=[ptn, vname], writes=[pan_])
        return s2

    def evac(g, m, quad, br, comb, cname, first_branch, do_imp):
        for h in range(4):
            pa, pan_ = psA[h]
            hl = 4 * quad + h
            head = 8 * g + hl
            rt, rtn = rts[cx.nxt("rt", 4)]
            P.op("dve", lambda e, pa=pa, rt=rt: e.tensor_scalar(out=rt[:, 0:1], in0=pa[:, 128:129], scalar1=1e-30,
                                                               scalar2=None, op0=ALU.max), reads=[pan_],
                 writes=[rtn])
            P.op("dve", lambda e, rt=rt: e.reciprocal(out=rt[:, 1:2], in_=rt[:, 0:1]), reads=[rtn], writes=[rtn])
            col = head * 3 + br
            P.op("dve", lambda e, rt=rt, col=col: e.tensor_tensor(out=rt[:, 2:3], in0=rt[:, 1:2],
                                                                 in1=gates[:, m, col:col + 1], op=ALU.mult),
                 reads=[rtn, "gates"], writes=[rtn])
            if first_branch:
                P.op("dve", lambda e, pa=pa, rt=rt, hl=hl: e.tensor_scalar(out=comb[:, hl, :], in0=pa[:, 0:128],
                                                                          scalar1=rt[:, 2:3], scalar2=None,
                                                                          op0=ALU.mult), reads=[pan_, rtn],
                     writes=[cname])
            else:
                P.op("dve", lambda e, pa=pa, rt=rt, hl=hl: e.scalar_tensor_tensor(
                    out=comb[:, hl, :], in0=pa[:, 0:128], scalar=rt[:, 2:3], in1=comb[:, hl, :], op0=ALU.mult,
                    op1=ALU.add), reads=[pan_, rtn, cname], writes=[cname])
            if do_imp:
                if hl == 0:
                    P.op("dve", lambda e, pa=pa, rt=rt: e.tensor_scalar(out=imp[:], in0=pa[:, 129:129 + NJ],
                                                                       scalar1=rt[:, 1:2], scalar2=None,
                                                                       op0=ALU.mult), reads=[pan_, rtn],
                         writes=["imp"])
                else:
                    P.op("dve", lambda e, pa=pa, rt=rt: e.scalar_tensor_tensor(
                        out=imp[:], in0=pa[:, 129:129 + NJ], scalar=rt[:, 1:2], in1=imp[:], op0=ALU.mult,
                        op1=ALU.add), reads=[pan_, rtn, "imp"], writes=["imp"])

    class Skew:
        pending, after = None, []

        def chunk(self, s1):
            s2 = s1()
            self.flush()
            self.pending = s2

        def flush(self):
            if self.pending is not None:
                self.pending()
                self.pending = None
            for f in self.after:
                f()
            self.after = []

        def defer(self, f):
            if self.pending is None:
                f()
            else:
                self.after.append(f)

    sk = Skew()
    for g in range(2):
        sk.flush()
        P.dma("sp", lambda e, g=g: e.dma_start(
            out=qg[:], in_=qTd[QROW_N + 1024 * g:QROW_N + 1024 * (g + 1), :].rearrange("(h d) t -> d h t", d=128)),
            writes=["qg"], key="ld_qg")
        P.dma("sp", lambda e, g=g: e.dma_start(out=ksl[:], in_=kTd[KROW_KSL + 128 * g:KROW_KSL + 128 * (g + 1), :]),
              writes=["ksl"], key="ld_ksl")
        P.dma("sp", lambda e, g=g: e.dma_start(
            out=vsl[:, :, 0:128],
            in_=vtokd[:, VCOL_VSL + 128 * g:VCOL_VSL + 128 * (g + 1)].rearrange("(c i) d -> i c d", i=128)),
            writes=["vsl"], key="ld_vsl")
        for m in range(NM):
            u0 = (8 * m + 3) * 128
            P.dma("sp", lambda e, g=g, m=m, u0=u0: e.dma_start(
                out=kw[:, m, :, :],
                in_=kTd[KROW_KW + 128 * g:KROW_KW + 128 * (g + 1), u0:u0 + 640].rearrange("d (r i) -> d r i", i=128)),
                writes=["kw"], key="ld_kw")
            P.dma("sp", lambda e, g=g, m=m, u0=u0: e.dma_start(
                out=vw[:, m, :, 0:128],
                in_=vtokd[u0:u0 + 640, VCOL_VW + 128 * g:VCOL_VW + 128 * (g + 1)].rearrange("(r i) d -> i r d", i=128)),
                writes=["vw"], key="ld_vw")
        P.dma("sp", lambda e: e.dma_start(out=vcmp[:, :, 128:NCV], in_=d_caug), writes=["vcmp"], key="ld_caug")
        mlp(g, "k")
        mlp(g, "v")
        for m in range(NM):
            comb, cname = combs[cx.nxt("comb", 2)]
            nccs = list(range((64 * m + 62) // 128 + 1))
            for quad in range(2):
                for ii, ncc in enumerate(nccs):
                    sk.chunk(lambda quad=quad, ii=ii, ncc=ncc, m=m: chunk(
                        g, m, quad, kcmpT[:, ncc * 128:(ncc + 1) * 128],
                        lambda head, ncc=ncc: ac[:, head, m, ncc:ncc + 1],
                        cm[:, m, ncc, :], "cm", vcmp[:, ncc, :], "vcmp", NCV, ii == 0, ii == len(nccs) - 1, "kcmpT"))
                sk.defer(lambda g=g, quad=quad, m=m, comb=comb, cname=cname: evac(g, m, quad, 0, comb, cname, True, True))
            sk.flush()
            P.op("dve", lambda e, m=m: e.tensor_tensor(out=score[:], in0=imp[:], in1=bonus[:, m, :], op=ALU.add),
                 reads=["imp", "bonus"], writes=["score"])
            P.op("dve", lambda e: e.max(out=m8[:, 0:8], in_=score[:]), reads=["score"], writes=["m8"])
            P.op("dve", lambda e: e.match_replace(out=score2[:], in_to_replace=m8[:, 0:8], in_values=score[:],
                                                  imm_value=-3.0e38), reads=["score", "m8"], writes=["score2"])
            P.op("dve", lambda e: e.max(out=m8[:, 8:16], in_=score2[:]), reads=["score2"], writes=["m8"])
            P.op("dve", lambda e: e.tensor_scalar(out=sel[:], in0=score[:], scalar1=m8[:, 15:16], scalar2=None,
                                                  op0=ALU.is_ge), reads=["score", "m8"], writes=["sel"])
            P.op("dve", lambda e: e.tensor_tensor(out=sel[:], in0=sel[:], in1=ex[:], op=ALU.mult),
                 reads=["sel", "ex"], writes=["sel"])
            px, pxn = psX
            P.op("pe", lambda e: e.transpose(px[0:NJ, 0:128], sel[:], ident[:]), reads=["sel", "ident"], writes=[pxn])
            P.op("act", lambda e: e.activation(out=selT[:], in_=px[0:NJ, 0:128], func=AF.Copy), reads=[pxn],
                 writes=["selT"])
            for quad in range(2):
                ucs = list(range(8 * m + 8))
                for ii, uc in enumerate(ucs):
                    def s1(quad=quad, ii=ii, uc=uc, m=m, nuc=len(ucs)):
                        pm, pmn = psM if cx.nxt("psM", 2) == 0 else psX
                        mt, mtn = mts[cx.nxt("mt", 2)]
                        P.op("pe", lambda e: e.matmul(pm[:, 0:128], E[:, uc, :], selT[:], start=True, stop=True),
                             reads=["E", "selT"], writes=[pmn])
                        if uc == 8 * m + 7:
                            P.op("dve", lambda e: e.tensor_tensor(out=mt[:], in0=pm[:, 0:128], in1=tri[:],
                                                                  op=ALU.mult), reads=[pmn, "tri"], writes=[mtn])
                        else:
                            P.op("act", lambda e: e.activation(out=mt[:], in_=pm[:, 0:128], func=AF.Copy),
                                 reads=[pmn], writes=[mtn])
                        rel = 8 * m + 7 - uc
                        return chunk(g, m, quad, ksl[:, uc * 128:(uc + 1) * 128],
                                     lambda head, rel=rel: ak[:, head, rel:rel + 1],
                                     mt[:], mtn, vsl[:, uc, :], "vsl", 129, ii == 0, ii == nuc - 1, "ksl")
                    sk.chunk(s1)
                sk.defer(lambda g=g, quad=quad, m=m, comb=comb, cname=cname: evac(g, m, quad, 1, comb, cname, False, False))
            for quad in range(2):
                for r in range(5):
                    sk.chunk(lambda quad=quad, r=r, m=m: chunk(
                        g, m, quad, kw[:, m, r, :], lambda head, r=r: ak[:, head, 4 - r:5 - r],
                        wm[:, m, r, :], "wm", vw[:, m, r, :], "vw", 129, r == 0, r == 4, "kw"))
                sk.defer(lambda g=g, quad=quad, m=m, comb=comb, cname=cname: evac(g, m, quad, 2, comb, cname, False, False))
            dst = out[m * 128:(m + 1) * 128, g * 1024:(g + 1) * 1024]
            sk.defer(lambda comb=comb, dst=dst, cname=cname: P.dma(
                "sp", lambda e: e.dma_start(out=dst, in_=comb[:].rearrange("p h d -> p (h d)")),
                reads=[cname], key=cname + "_st"))
    sk.flush()


def phase_3(cx, io, NPASS=5):
    P = cx.P
    D, DFF, NTOK = D_MODEL, D_FF, NT_OWN
    NB = NTOK // 128
    HALF = D // 2
    xu, osb, onsa, out, x1d, yacc = io["xu"], io["osb"], io["onsa"], io["out"], io["x1d"], io["yacc"]
    w_out, w_gate, w_up, w_down = io["w_out"], io["w_gate"], io["w_up"], io["w_down"]
    xrow = lambda b: xu[own_chunk(b) * 128:(own_chunk(b) + 1) * 128, :]
    npan_ff = DFF // 256
    per = -(-npan_ff // NPASS)
    maxk_act = per * 2

    ident = cx.sb("ident_sb", (128, 128), F32)
    gain = cx.sb("gain_sb", (128, D), F32)
    hT = cx.sb("hT", (128, KC, NTOK), BF16)
    actT = cx.sb("actT", (128, maxk_act, NTOK), BF16)
    xts = [(cx.sb("xt0", (128, D), F32), "xt0")]
    ws = WStream(cx, "w", max(KC, maxk_act), 256, kpiece=4, npanel=3, nstage=2)
    sqv = actT[:].rearrange("p k n -> p (k n)")
    small = small_tiles(cx, sqv, "actT")
    thunks = [(lambda col=cp * 256: ws.load(w_out, 0, KC, col, 256)) for cp in range(D // 256)]
    for p_ in range(NPASS):
        a0, a1 = p_ * per, min(npan_ff, p_ * per + per)
        if a0 >= a1:
            continue
        for pp_ in range(a0, a1):
            thunks.append(lambda col=pp_ * 256: ws.load(w_gate, 0, KC, col, 256))
            thunks.append(lambda col=pp_ * 256: ws.load(w_up, 0, KC, col, 256))
        for cp in range(D // 256):
            thunks.append(lambda col=cp * 256, a0=a0, a1=a1: ws.load(w_down, a0 * 2, (a1 - a0) * 2, col, 256))
    pf = Prefetch(thunks, ahead=2)
    pfi = [0]

    def next_panel():
        r = pf.get(pfi[0])
        pfi[0] += 1
        return r

    pss = [(cx.ps("ps%d" % i), "ps%d" % i) for i in range(8)]
    ept = [(cx.sb("ept%d" % i, (128, 256), F32), "ept%d" % i) for i in range(2)]
    epo = [(cx.sb("epo%d" % i, (128, 256), F32), "epo%d" % i) for i in range(2)]
    sgt = [(cx.sb("sg%d" % i, (128, 512), F32), "sg%d" % i) for i in range(2)]

    P.dma("sp", lambda e: e.dma_start(out=ident[:], in_=io["ident"]), writes=["ident"], key="c_ident")
    P.op("dve", lambda e: e.memset(small["eps"][0][:], EPS), writes=["epsc"])

    P.dma("sp", lambda e: e.dma_start(out=gain[:, 0:HALF], in_=io["g_sb"]), writes=["gain"], key="c_gain")
    norm_transpose(cx, "sb", lambda b: osb[b * 128:(b + 1) * 128, :], lambda b: [], NB, HALF, gain, "gain", hT, "hT",
                   0, xts, pss, ident, small)
    P.dma("sp", lambda e: e.dma_start(out=gain[:, 0:HALF], in_=io["g_nsa"]), writes=["gain"], key="c_gain")
    norm_transpose(cx, "nsa", lambda b: onsa[b * 128:(b + 1) * 128, :], lambda b: [], NB, HALF, gain, "gain", hT, "hT",
                   KC // 2, xts, pss, ident, small)

    def tok_major_gemm(W, kc0, nk, actbuf, actname, prev_row, prev_name_fn, dst, dst_name_fn):
        for cp in range(D // 256):
            col = cp * 256
            pan, pname = next_panel()
            for b in range(NB):
                ps, psn = pss[cx.nxt("mm_ps", 8)]
                for k in range(nk):
                    P.op("pe", lambda e, ps=ps, pan=pan, k=k, b=b: e.matmul(
                        ps[:, 0:256], actbuf[:, k, b * 128:(b + 1) * 128], pan[:, k, 0:256],
                        start=(k == 0), stop=(k == nk - 1)), reads=[pname, actname], writes=[psn])
                ti = cx.nxt("ept", 2)
                pt, ptn = ept[ti]
                po, pon = epo[ti]
                srcp = prev_row(b)[:, col:col + 256]
                P.dma("sp", lambda e, pt=pt, srcp=srcp: e.dma_start(out=pt[:], in_=srcp),
                      reads=[prev_name_fn(b, cp)], writes=[ptn], key=ptn)
                P.op("dve", lambda e, po=po, ps=ps, pt=pt: e.tensor_tensor(out=po[:], in0=ps[:, 0:256], in1=pt[:],
                                                                          op=ALU.add),
                     reads=[psn, ptn], writes=[pon])
                dstp = dst[b * 128:(b + 1) * 128, col:col + 256]
                P.dma("sp", lambda e, po=po, dstp=dstp: e.dma_start(out=dstp, in_=po[:]),
                      reads=[pon], writes=[dst_name_fn(b, cp)], key=pon + "_st")

    tok_major_gemm(w_out, 0, KC, hT, "hT", xrow, lambda b, cp: "x_in", x1d, lambda b, cp: "x1d_%d_%d" % (b, cp))

    P.dma("sp", lambda e: e.dma_start(out=gain[:], in_=io["g_ffn"]), writes=["gain"], key="c_gain")
    norm_transpose(cx, "ffn", lambda b: x1d[b * 128:(b + 1) * 128, :],
                   lambda b: ["x1d_%d_%d" % (b, cp) for cp in range(D // 256)], NB, D, gain, "gain", hT, "hT", 0,
                   xts, pss, ident, small)

    TW = 512
    NTH = NTOK // TW
    lastp = 0
    for p in range(NPASS):
        pan0 = p * per
        pan1 = min(npan_ff, pan0 + per)
        if pan0 >= pan1:
            continue
        lastp = p
        for pp in range(pan0, pan1):
            col = pp * 256
            gpan, gname = next_panel()
            gps = {}
            for j in range(2):
                for th in range(NTH):
                    ps, psn = pss[cx.nxt("mm_ps", 8)]
                    gps[(j, th)] = (ps, psn)
                    for k in range(KC):
                        P.op("pe", lambda e, ps=ps, gpan=gpan, k=k, j=j, th=th: e.matmul(
                            ps[:, 0:TW], gpan[:, k, j * 128:(j + 1) * 128], hT[:, k, th * TW:(th + 1) * TW],
                            start=(k == 0), stop=(k == KC - 1)), reads=[gname, "hT"], writes=[psn])
            upan, uname = next_panel()
            for j in range(2):
                for th in range(NTH):
                    ps, psn = pss[cx.nxt("mm_ps", 8)]
                    for k in range(KC):
                        P.op("pe", lambda e, ps=ps, upan=upan, k=k, j=j, th=th: e.matmul(
                            ps[:, 0:TW], upan[:, k, j * 128:(j + 1) * 128], hT[:, k, th * TW:(th + 1) * TW],
                            start=(k == 0), stop=(k == KC - 1)), reads=[uname, "hT"], writes=[psn])
                    gp, gpn = gps[(j, th)]
                    sg, sgn = sgt[cx.nxt("sg", 2)]
                    P.op("act", lambda e, sg=sg, gp=gp: e.activation(out=sg[:, 0:TW], in_=gp[:, 0:TW], func=AF.Silu),
                         reads=[gpn], writes=[sgn])
                    kk = (pp - pan0) * 2 + j
                    P.op("dve", lambda e, sg=sg, ps=ps, kk=kk, th=th: e.tensor_tensor(
                        out=actT[:, kk, th * TW:(th + 1) * TW], in0=sg[:, 0:TW], in1=ps[:, 0:TW], op=ALU.mult),
                        reads=[sgn, psn], writes=["actT"])
        nk = (pan1 - pan0) * 2
        if p == 0:
            prow, pfn = (lambda b: x1d[b * 128:(b + 1) * 128, :]), (lambda b, cp: "x1d_%d_%d" % (b, cp))
        else:
            prow, pfn = (lambda b: yacc[b * 128:(b + 1) * 128, :]), (lambda b, cp, p=p: "yacc%d_%d_%d" % (p - 1, b, cp))
        tok_major_gemm(w_down, pan0 * 2, nk, actT, "actT", prow, pfn, yacc,
                       lambda b, cp, p=p: "yacc%d_%d_%d" % (p, b, cp))

    P.dma("sp", lambda e: e.dma_start(out=gain[:], in_=io["g_fin"]), writes=["gain"], key="c_gain")
    sq, sqn = small["sq"]
    ss = small["ss"][0]
    for b in range(NB):
        xt, xname = xts[0]
        deps = ["yacc%d_%d_%d" % (lastp, b, cp) for cp in range(D // 256)]
        P.dma("sp", lambda e, b=b: e.dma_start(out=xt[:], in_=yacc[b * 128:(b + 1) * 128, :]), reads=deps,
              writes=[xname], key=xname)
        P.op("act", lambda e: e.activation(out=sq[:, 0:D], in_=xt[:], func=AF.Square, accum_out=ss[:, 0:1]),
             reads=[xname], writes=[sqn, "ss"])
        P.op("act", lambda e: e.activation(out=ss[:, 1:2], in_=ss[:, 0:1], func=AF.Sqrt, scale=1.0 / D,
                                           bias=small["eps"][0][:, 0:1]), reads=["ss", "epsc"], writes=["ssb"])
        P.op("dve", lambda e: e.reciprocal(out=ss[:, 2:3], in_=ss[:, 1:2]), reads=["ssb"], writes=["ssc"])
        P.op("dve", lambda e: e.scalar_tensor_tensor(out=xt[:], in0=xt[:], scalar=ss[:, 2:3], in1=gain[:],
                                                     op0=ALU.mult, op1=ALU.mult),
             reads=[xname, "ssc", "gain"], writes=[xname])
        P.dma("sp", lambda e, b=b: e.dma_start(out=out[b * 128:(b + 1) * 128, :], in_=xt[:]), reads=[xname],
              writes=[], key="out_st")


def _tables_spec():
    NM = NM_OWN
    NJ, NCU, NUC = 16 * NM, 64 * NM, 8 * NM
    NCC = NCU // 128
    return {"ak": ((128, 16, NUC), F32), "ac": ((128, 16, NM, NCC), F32), "cm": ((128, NM, NCC, 128), BF16),
            "wm": ((128, NM, 5, 128), BF16), "tri_incl": ((128, 128), BF16), "bonus": ((128, NM, NJ), F32),
            "ex": ((128, NJ), F32), "E": ((NJ, NUC, 128), BF16), "caug": ((128, NCC, 1 + NJ), BF16),
            "ident": ((128, 128), F32), "exf": ((128, 8), F32), "negU": ((128, 128), BF16),
            "tri_strict": ((128, 128), BF16), "negones": ((128, 1), BF16)}


def build_program(phases=("1a", "1b", "2a", "2b", "3")):
    cx = Ctx()
    io = {}
    io["xu"] = cx.din("xu", (SEQ, D_MODEL), F32)
    io["w_in"] = cx.din("w_in", (D_MODEL, D_IN_PAD), F32)
    for n in ("g_attn", "g_ffn", "g_fin"):
        io[n] = cx.din(n, (128, D_MODEL), F32)
    for n in ("g_sb", "g_nsa"):
        io[n] = cx.din(n, (128, D_MODEL // 2), F32)
    for n, (shape, dt) in _tables_spec().items():
        io[n] = cx.din(n, shape, dt)
    for n in ("w_k1", "w_v1"):
        io[n] = cx.din(n, (4096, 256), F32)
    for n in ("w_k2", "w_v2"):
        io[n] = cx.din(n, (256, 128), F32)
    for n in ("posTk", "posTv"):
        io[n] = cx.din(n, (128, 32), F32)
    io["w_out"] = cx.din("w_out", (D_MODEL, D_MODEL), F32)
    io["w_gate"] = cx.din("w_gate", (D_MODEL, D_FF), F32)
    io["w_up"] = cx.din("w_up", (D_MODEL, D_FF), F32)
    io["w_down"] = cx.din("w_down", (D_FF, D_MODEL), F32)
    io["out"] = cx.dout("out", (NT_OWN, D_MODEL), F32)
    io["qT"] = cx.dint("qT_scr", (QROWS, NT_OWN), BF16)
    io["gn"] = cx.dint("gn_scr", (NT_OWN, 128), F32)
    io["kT"] = cx.dint("kT_scr", (KROWS, SEQ), BF16)
    io["vtok"] = cx.dint("vtok_scr", (SEQ, VCOLS), BF16)
    io["wbf"] = cx.dint("wbf_scr", (len(K_PANELS) + len(V_PANELS), 128, KC, 256), BF16)
    io["osb"] = cx.dint("osb_scr", (NT_OWN, 2048), F32)
    io["onsa"] = cx.dint("onsa_scr", (NT_OWN, 2048), F32)
    io["x1d"] = cx.dint("x1d_scr", (NT_OWN, D_MODEL), F32)
    io["yacc"] = cx.dint("yacc_scr", (NT_OWN, D_MODEL), F32)
    fns = {"1a": phase_1a, "1b": phase_1b, "2a": phase_2a, "2b": phase_2b, "3": phase_3}
    first = True
    for ph in phases:
        if not first:
            cx.begin()
        first = False
        fns[ph](cx, io)
        cx.end()
    return cx.finish()


def _bc(g, n=128):
    g = np.asarray(g, np.float32).reshape(-1)
    return np.ascontiguousarray(np.broadcast_to(g, (n, g.shape[0])))


def _own_rows(c):
    return np.concatenate([np.arange(128) + 128 * (c + 8 * m) for m in range(NM_OWN)])


def kernel(x, attn_norm, w_in, pos_cmp_k, pos_cmp_v, w_cmp_k1, w_cmp_k2, w_cmp_v1, w_cmp_v2,
           norm_sb, norm_nsa, w_out, ffn_norm, w_gate, w_up, w_down, final_norm):
    f32 = lambda a: np.ascontiguousarray(np.asarray(a, np.float32))
    x2 = f32(x)[0]
    cores = list(range(NCORES))
    w_pad = np.zeros((D_MODEL, D_IN_PAD), np.float32)
    w_pad[:, :D_IN] = f32(w_in)[0]
    common = {"w_in": w_pad, "g_attn": _bc(attn_norm), "g_ffn": _bc(ffn_norm), "g_fin": _bc(final_norm),
              "g_sb": _bc(norm_sb), "g_nsa": _bc(norm_nsa),
              "w_k1": f32(w_cmp_k1)[0], "w_k2": f32(w_cmp_k2)[0], "w_v1": f32(w_cmp_v1)[0], "w_v2": f32(w_cmp_v2)[0],
              "posTk": np.ascontiguousarray(f32(pos_cmp_k)[0].T), "posTv": np.ascontiguousarray(f32(pos_cmp_v)[0].T),
              "w_out": f32(w_out)[0], "w_gate": f32(w_gate)[0], "w_up": f32(w_up)[0], "w_down": f32(w_down)[0]}
    common.update(sb_consts())
    in_maps = []
    for c in cores:
        d = dict(common)
        shift = 128 * (7 - c)
        xu = np.zeros((SEQ, D_MODEL), np.float32)
        xu[shift:] = x2[:SEQ - shift]
        d["xu"] = xu
        d.update(nsa_tables(c, NM_OWN))
        in_maps.append(d)
    nc = build_program()
    res = run_bass_kernel_spmd(nc, in_maps, core_ids=cores).results
    out = np.zeros((1, SEQ, D_MODEL), np.float32)
    for c in cores:
        out[0, _own_rows(c)] = np.asarray(res[c]["out"])
    return out
```

```python
import contextlib
import numpy as np
import ml_dtypes
import concourse.bass as bass
import concourse.mybir as mybir
from concourse.bass_utils import run_bass_kernel_spmd

F32 = mybir.dt.float32
BF16 = mybir.dt.bfloat16
AF = mybir.ActivationFunctionType
ALU = mybir.AluOpType
NPBF = ml_dtypes.bfloat16

NCORES = 8
EPS = 1e-6
ALL_ENG = ("pe", "act", "dve", "pool", "sp")


class _Buf:
    __slots__ = ("writer", "readers")

    def __init__(self):
        self.writer = None
        self.readers = []


class _Op:
    __slots__ = ("eng", "fn", "is_dma", "dkey", "dval", "deps", "signal", "count")


class Prog:
    def __init__(self, nc, same_engine_sync=True):
        self.nc = nc
        self.ops = {e: [] for e in ALL_ENG}
        self.bufs = {}
        self.dma_counts = {}
        self.phase_keys = set()
        self.same_engine_sync = same_engine_sync
        self.ecount = {e: 0 for e in ALL_ENG}
        self.semstack = contextlib.ExitStack()
        self.esem = None
        self.dsem = {}
        self.barrier = []

    def _add(self, eng, fn, reads, writes, is_dma=False, dkey=None):
        op = _Op()
        op.eng, op.fn, op.is_dma, op.dkey = eng, fn, is_dma, dkey
        op.dval, op.signal, op.count = None, False, None
        deps = []
        reads, writes = _expand(reads), _expand(writes)
        for r in reads:
            b = self.bufs.get(r)
            if b is None:
                b = self.bufs[r] = _Buf()
            if b.writer is not None:
                deps.append(b.writer)
            b.readers.append(op)
        for w in writes:
            b = self.bufs.get(w)
            if b is None:
                b = self.bufs[w] = _Buf()
            if b.writer is not None:
                deps.append(b.writer)
            deps.extend(b.readers)
            b.writer = op
            b.readers = []
        out, seen = [], set()
        for d in deps:
            if d is op or id(d) in seen:
                continue
            seen.add(id(d))
            if not d.is_dma and d.eng == eng and (eng == "pe" or not self.same_engine_sync):
                continue
            out.append(d)
        op.deps = out
        if is_dma:
            c = self.dma_counts.get(dkey, 0) + 16
            self.dma_counts[dkey] = c
            self.phase_keys.add(dkey)
            op.dval = c
        self.ops[eng].append(op)
        return op

    def op(self, eng, fn, reads=(), writes=()):
        return self._add(eng, fn, reads, writes)

    def dma(self, eng, fn, reads=(), writes=(), key=None):
        return self._add(eng, fn, reads, writes, is_dma=True, dkey=key)

    def flush(self, final_wait_eng="sp"):
        nc = self.nc
        if self.esem is None:
            self.esem = {e: self.semstack.enter_context(nc.semaphore("s_" + e)) for e in ALL_ENG}
        for k in self.dma_counts:
            if k not in self.dsem:
                self.dsem[k] = self.semstack.enter_context(nc.semaphore("d_%d" % len(self.dsem)))
        esem, dsem = self.esem, self.dsem
        for e in ALL_ENG:
            ops = self.ops[e]
            for op in ops:
                for d in op.deps:
                    if not d.is_dma:
                        d.signal = True
            for op in reversed(ops):
                if not op.is_dma:
                    op.signal = True
                    break
        for e in ALL_ENG:
            c = self.ecount[e]
            for op in self.ops[e]:
                if op.signal and not op.is_dma:
                    c += 1
                    op.count = c
            self.ecount[e] = c
        barrier = self.barrier
        with nc.Block() as block:
            engmap = {"pe": block.tensor, "act": block.scalar, "dve": block.vector,
                      "pool": block.gpsimd, "sp": block.sync}

            def make(e):
                def body(eng):
                    known = {}
                    if self.ops[e]:
                        for key, sem, val in barrier:
                            if key == ("e", e):
                                known[key] = val
                                continue
                            eng.wait_ge(sem, val)
                            known[key] = val
                    for op in self.ops[e]:
                        for d in op.deps:
                            if d.is_dma:
                                key, val, sem = ("d", d.dkey), d.dval, dsem[d.dkey]
                            else:
                                key, val, sem = ("e", d.eng), d.count, esem[d.eng]
                            if known.get(key, 0) >= val:
                                continue
                            eng.wait_ge(sem, val)
                            known[key] = val
                        inst = op.fn(eng)
                        if op.is_dma:
                            inst.then_inc(dsem[op.dkey], 16)
                        elif op.signal:
                            inst.then_inc(esem[e], 1)
                    if e == final_wait_eng:
                        for k in self.phase_keys:
                            v = self.dma_counts[k]
                            if known.get(("d", k), 0) < v:
                                eng.wait_ge(dsem[k], v)
                return body

            for e in ALL_ENG:
                engmap[e](make(e))
        self.barrier = [(("e", e), esem[e], self.ecount[e]) for e in ALL_ENG if self.ecount[e] > 0]
        self.barrier += [(("d", k), dsem[k], self.dma_counts[k]) for k in self.dma_counts]
        self.ops = {e: [] for e in ALL_ENG}
        self.bufs = {}
        self.phase_keys = set()

    def close(self):
        self.semstack.close()


class Ctx:
    def __init__(self):
        self.nc = bass.Bass("TRN2", target_bir_lowering=False)
        self.P = Prog(self.nc)
        self.st = None
        self.rot = {}
        self.phase = -1
        self.begin()

    def begin(self):
        self.st = contextlib.ExitStack()
        self.rot = {}
        self.phase += 1

    def end(self):
        self.P.flush()
        self.st.close()
        self.st = None

    def sb(self, name, shape, dt):
        return self.st.enter_context(self.nc.sbuf_tensor("p%d_%s" % (self.phase, name), list(shape), dt))

    def ps(self, name, shape=(128, 512), dt=F32):
        return self.st.enter_context(self.nc.psum_tensor("p%d_%s" % (self.phase, name), list(shape), dt))

    def din(self, name, shape, dt):
        return self.nc.dram_tensor(name, list(shape), dt, kind="ExternalInput").ap()

    def dout(self, name, shape, dt):
        return self.nc.dram_tensor(name, list(shape), dt, kind="ExternalOutput").ap()

    def dint(self, name, shape, dt):
        return self.nc.dram_tensor(name, list(shape), dt, kind="Internal").ap()

    def nxt(self, key, n):
        i = self.rot.get(key, 0)
        self.rot[key] = i + 1
        return i % n

    def finish(self):
        if self.st is not None:
            self.end()
        self.P.close()
        return self.nc


class PanelName(str):
    pass


def _expand(names):
    out = []
    for n in names:
        if isinstance(n, PanelName):
            out += [str(n), str(n) + "_a", str(n) + "_b"]
        else:
            out.append(n)
    return out


class WStream:
    def __init__(self, cx, name, max_k, ncols, kpiece=4, npanel=2, nstage=3):
        self.cx, self.name = cx, name
        self.max_k, self.ncols, self.kpiece = max_k, ncols, kpiece
        self.panels = [cx.sb("%s_pan%d" % (name, i), (128, max_k, ncols), BF16) for i in range(npanel)]
        self.stages = [cx.sb("%s_stg%d" % (name, i), (128, kpiece, ncols), F32) for i in range(nstage)]
        self.npanel, self.nstage = npanel, nstage

    def load(self, W, kc0, nk, col0, ncols):
        cx, P = self.cx, self.cx.P
        pi = cx.nxt(self.name + "_p", self.npanel)
        pan = self.panels[pi]
        pname = "%s_pan%d" % (self.name, pi)
        k = 0
        while k < nk:
            kp = min(self.kpiece, nk - k)
            si = cx.nxt(self.name + "_s", self.nstage)
            stg = self.stages[si]
            sname = "%s_stg%d" % (self.name, si)
            src = W[(kc0 + k) * 128:(kc0 + k + kp) * 128, col0:col0 + ncols].rearrange("(k p) n -> p k n", p=128)
            dst = stg[:, 0:kp, 0:ncols]
            P.dma("sp", lambda e, dst=dst, src=src: e.dma_start(out=dst, in_=src), reads=[], writes=[sname],
                  key=sname)
            pdst = pan[:, k:k + kp, 0:ncols]
            ceng, cname = ("pool", pname + "_a") if cx.nxt(self.name + "_ce", 2) == 0 else ("dve", pname + "_b")
            P.op(ceng, lambda e, pdst=pdst, dst=dst: e.tensor_copy(out=pdst, in_=dst), reads=[sname],
                 writes=[cname])
            k += kp
        return pan, PanelName(pname)

    def load_bf16(self, src, nk, ncols, dep):
        cx, P = self.cx, self.cx.P
        pi = cx.nxt(self.name + "_p", self.npanel)
        pan = self.panels[pi]
        pname = "%s_pan%d" % (self.name, pi)
        P.dma("sp", lambda e: e.dma_start(out=pan[:, 0:nk, 0:ncols], in_=src), reads=[dep], writes=[PanelName(pname)],
              key=pname + "_ld")
        return pan, PanelName(pname)


class Prefetch:
    def __init__(self, thunks, ahead=2):
        self.thunks, self.ahead, self.res = list(thunks), ahead, []

    def get(self, i):
        while len(self.res) < min(i + 1 + self.ahead, len(self.thunks)):
            self.res.append(self.thunks[len(self.res)]())
        return self.res[i]


def norm_transpose(cx, tag, src_blk, deps_blk, ntok_blocks, nfeat, gain_bc, gain_name, dstT, dst_name, dst_chunk0,
                   xt_bufs, ps_bufs, ident, small):
    P = cx.P
    nch = nfeat // 128
    sq, sqn = small["sq"]
    ss, ssn = small["ss"]
    for b in range(ntok_blocks):
        xi = cx.nxt(tag + "_xt", len(xt_bufs))
        xt, xname = xt_bufs[xi]
        srcb = src_blk(b)
        P.dma("sp", lambda e, xt=xt, srcb=srcb: e.dma_start(out=xt[:, 0:nfeat], in_=srcb), reads=deps_blk(b),
              writes=[xname], key=xname)
        P.op("act", lambda e, xt=xt: e.activation(out=sq[:, 0:nfeat], in_=xt[:, 0:nfeat], func=AF.Square,
                                                  accum_out=ss[:, 0:1]),
             reads=[xname], writes=[sqn, ssn])
        P.op("act", lambda e: e.activation(out=ss[:, 1:2], in_=ss[:, 0:1], func=AF.Sqrt, scale=1.0 / nfeat,
                                           bias=small["eps"][0][:, 0:1]),
             reads=[ssn, small["eps"][1]], writes=[ssn + "b"])
        P.op("dve", lambda e: e.reciprocal(out=ss[:, 2:3], in_=ss[:, 1:2]), reads=[ssn + "b"], writes=[ssn + "c"])
        P.op("dve", lambda e, xt=xt: e.scalar_tensor_tensor(out=xt[:, 0:nfeat], in0=xt[:, 0:nfeat], scalar=ss[:, 2:3],
                                                             in1=gain_bc[:, 0:nfeat], op0=ALU.mult, op1=ALU.mult),
             reads=[xname, ssn + "c", gain_name], writes=[xname])
        for c0 in range(0, nch, 4):
            nc4 = min(4, nch - c0)
            pi = cx.nxt("nt_ps", len(ps_bufs))
            ps, psn = ps_bufs[pi]
            for j in range(nc4):
                P.op("pe", lambda e, ps=ps, xt=xt, j=j, c0=c0: e.transpose(ps[:, j * 128:(j + 1) * 128],
                                                                          xt[:, (c0 + j) * 128:(c0 + j + 1) * 128],
                                                                          ident[:]),
                     reads=[xname, "ident"], writes=[psn])
            dst = dstT[:, dst_chunk0 + c0:dst_chunk0 + c0 + nc4, b * 128:(b + 1) * 128]
            src_ps = ps[:, 0:nc4 * 128].rearrange("p (c t) -> p c t", c=nc4)
            P.op("act", lambda e, dst=dst, src_ps=src_ps: e.activation(out=dst, in_=src_ps, func=AF.Copy),
                 reads=[psn], writes=[dst_name])


D_MODEL, SEQ, D_FF = 4096, 8192, 11008
D_IN, D_IN_PAD = 9776, 9856
NM_OWN = SEQ // 128 // NCORES
NT_OWN = NM_OWN * 128
QSCALE = 128 ** -0.5
KC = D_MODEL // 128
C_QSB, C_KSB, C_VSB, C_QN = 0, 2048, 4096, 6144
C_KC, C_VC, C_KSL, C_VSL, C_KW, C_VW, C_G = 8192, 8448, 8704, 8960, 9216, 9472, 9728
KROW_SB, KROW_KC, KROW_VC, KROW_KSL, KROW_KW, KROWS = 0, 2048, 2304, 2560, 2816, 3072
VCOL_SB, VCOL_VSL, VCOL_VW, VCOLS = 0, 2048, 2304, 2560
QROW_SB, QROW_N, QROWS = 0, 2048, 4096


def own_chunk(m):
    return 8 * m + 7


def small_tiles(cx, sq_ap, sq_name):
    return {"sq": (sq_ap, sq_name), "ss": (cx.sb("ss", (128, 4), F32), "ss"),
            "eps": (cx.sb("epsc", (128, 1), F32), "epsc")}


def phase_1a(cx, io):
    P = cx.P
    xu, w, qT, gnd = io["xu"], io["w_in"], io["qT"], io["gn"]
    ident = cx.sb("ident_sb", (128, 128), F32)
    gain = cx.sb("gain_sb", (128, D_MODEL), F32)
    hT = cx.sb("hT", (128, KC, NT_OWN), BF16)
    xts = [(cx.sb("xt%d" % i, (128, D_MODEL), F32), "xt%d" % i) for i in range(2)]
    sqt = cx.sb("sq", (128, D_MODEL), BF16)
    small = small_tiles(cx, sqt, "sq")
    pss = [(cx.ps("ps%d" % i), "ps%d" % i) for i in range(8)]
    ws = WStream(cx, "w", KC, 256)
    ostg = [(cx.sb("ostg%d" % i, (128, NT_OWN), BF16), "ostg%d" % i) for i in range(2)]
    gstg = [(cx.sb("gstg%d" % i, (128, 128), F32), "gstg%d" % i) for i in range(2)]

    P.dma("sp", lambda e: e.dma_start(out=ident[:], in_=io["ident"]), writes=["ident"], key="c_ident")
    P.dma("sp", lambda e: e.dma_start(out=gain[:], in_=io["g_attn"]), writes=["gain"], key="c_gain")
    P.op("dve", lambda e: e.memset(small["eps"][0][:], EPS), writes=["epsc"])
    norm_transpose(cx, "l1", lambda b: xu[own_chunk(b) * 128:(own_chunk(b) + 1) * 128, :], lambda b: [],
                   NM_OWN, D_MODEL, gain, "gain", hT, "hT", 0, xts, pss, ident, small)

    qpanels = [(c0 + pcol, r0 + pcol) for (c0, r0) in ((C_QSB, QROW_SB), (C_QN, QROW_N)) for pcol in range(0, 2048, 256)]
    pf = Prefetch([(lambda c=c: ws.load(w, 0, KC, c, 256)) for (c, _) in qpanels] +
                  [lambda: ws.load(w, 0, KC, C_G, 128)], ahead=1)
    for qi, (cabs, rabs) in enumerate(qpanels):
        pan, pname = pf.get(qi)
        for j0 in (0, 128):
            og, ogn = ostg[cx.nxt("ostg", 2)]
            for th in range(NT_OWN // 512):
                ps, psn = pss[cx.nxt("mm_ps", 8)]
                for k in range(KC):
                    P.op("pe", lambda e, ps=ps, pan=pan, k=k, j0=j0, th=th: e.matmul(
                        ps[:, 0:512], pan[:, k, j0:j0 + 128], hT[:, k, th * 512:(th + 1) * 512],
                        start=(k == 0), stop=(k == KC - 1)), reads=[pname, "hT"], writes=[psn])
                P.op("act", lambda e, og=og, ps=ps, th=th: e.activation(
                    out=og[:, th * 512:(th + 1) * 512], in_=ps[:, 0:512], func=AF.Copy, scale=QSCALE),
                    reads=[psn], writes=[ogn])
            row = rabs + j0
            P.dma("sp", lambda e, og=og, row=row: e.dma_start(out=qT[row:row + 128, :], in_=og[:]),
                  reads=[ogn], writes=[], key=ogn + "_st")
    pan, pname = pf.get(len(qpanels))
    for b in range(NM_OWN):
        ps, psn = pss[cx.nxt("mm_ps", 8)]
        for k in range(KC):
            P.op("pe", lambda e, ps=ps, k=k, b=b: e.matmul(ps[:, 0:128], hT[:, k, b * 128:(b + 1) * 128],
                                                           pan[:, k, 0:128], start=(k == 0), stop=(k == KC - 1)),
                 reads=[pname, "hT"], writes=[psn])
        gs, gsn = gstg[cx.nxt("gstg", 2)]
        P.op("act", lambda e, gs=gs, ps=ps: e.activation(out=gs[:], in_=ps[:, 0:128], func=AF.Copy), reads=[psn],
             writes=[gsn])
        P.dma("sp", lambda e, gs=gs, b=b: e.dma_start(out=gnd[b * 128:(b + 1) * 128, :], in_=gs[:]), reads=[gsn],
              writes=[], key=gsn + "_st")


K_PANELS = [(C_KSB + p, KROW_SB + p) for p in range(0, 2048, 256)] + \
           [(C_KC, KROW_KC), (C_VC, KROW_VC), (C_KSL, KROW_KSL), (C_KW, KROW_KW)]
V_PANELS = [(C_VSB + p, VCOL_SB + p) for p in range(0, 2048, 256)] + [(C_VSL, VCOL_VSL), (C_VW, VCOL_VW)]


def phase_1b(cx, io):
    P = cx.P
    xu, w, kT, vtok, wbf = io["xu"], io["w_in"], io["kT"], io["vtok"], io["wbf"]
    NTILE = SEQ // 1024
    ident = cx.sb("ident_sb", (128, 128), F32)
    gain = cx.sb("gain_sb", (128, D_MODEL), F32)
    hT = cx.sb("hT", (128, KC, 1024), BF16)
    xts = [(cx.sb("xt%d" % i, (128, D_MODEL), F32), "xt%d" % i) for i in range(2)]
    sqt = cx.sb("sq", (128, D_MODEL), BF16)
    small = small_tiles(cx, sqt, "sq")
    pss = [(cx.ps("ps%d" % i), "ps%d" % i) for i in range(8)]
    ws = WStream(cx, "w", KC, 256, npanel=3)
    ostg = [(cx.sb("ostg%d" % i, (128, 1024), BF16), "ostg%d" % i) for i in range(2)]
    vstg = [(cx.sb("vstg%d" % i, (128, 8, 256), BF16), "vstg%d" % i) for i in range(2)]

    P.dma("sp", lambda e: e.dma_start(out=ident[:], in_=io["ident"]), writes=["ident"], key="c_ident")
    P.dma("sp", lambda e: e.dma_start(out=gain[:], in_=io["g_attn"]), writes=["gain"], key="c_gain")
    P.op("dve", lambda e: e.memset(small["eps"][0][:], EPS), writes=["epsc"])
    panels = [("k",) + p for p in K_PANELS] + [("v",) + p for p in V_PANELS]

    def first_load(pi_, c0):
        pan, pname = ws.load(w, 0, KC, c0, 256)
        P.dma("sp", lambda e: e.dma_start(out=wbf[pi_], in_=pan[:, 0:KC, 0:256]), reads=[pname], writes=["wbf"],
              key="wbf_st")
        return pan, pname

    thunks = []
    for t in range(NTILE):
        for pi_, (kind, c0, r0) in enumerate(panels):
            if t == 0:
                thunks.append(lambda pi_=pi_, c0=c0: first_load(pi_, c0))
            else:
                thunks.append(lambda pi_=pi_: ws.load_bf16(wbf[pi_], KC, 256, "wbf"))
    pf = Prefetch(thunks, ahead=2)
    for t in range(NTILE):
        norm_transpose(cx, "l1", lambda b, t=t: xu[t * 1024 + b * 128:t * 1024 + (b + 1) * 128, :], lambda b: [],
                       8, D_MODEL, gain, "gain", hT, "hT", 0, xts, pss, ident, small)
        for pi_, (kind, c0, r0) in enumerate(panels):
            pan, pname = pf.get(t * len(panels) + pi_)
            if kind == "k":
                for j0 in (0, 128):
                    og, ogn = ostg[cx.nxt("ostg", 2)]
                    for th in range(2):
                        ps, psn = pss[cx.nxt("mm_ps", 8)]
                        for k in range(KC):
                            P.op("pe", lambda e, ps=ps, pan=pan, k=k, j0=j0, th=th: e.matmul(
                                ps[:, 0:512], pan[:, k, j0:j0 + 128], hT[:, k, th * 512:(th + 1) * 512],
                                start=(k == 0), stop=(k == KC - 1)), reads=[pname, "hT"], writes=[psn])
                        P.op("act", lambda e, og=og, ps=ps, th=th: e.activation(
                            out=og[:, th * 512:(th + 1) * 512], in_=ps[:, 0:512], func=AF.Copy),
                            reads=[psn], writes=[ogn])
                    row = r0 + j0
                    P.dma("sp", lambda e, og=og, row=row, t=t: e.dma_start(
                        out=kT[row:row + 128, t * 1024:(t + 1) * 1024], in_=og[:]), reads=[ogn], writes=[],
                        key=ogn + "_st")
            else:
                vs, vsn = vstg[cx.nxt("vstg", 2)]
                for b in range(8):
                    ps, psn = pss[cx.nxt("mm_ps", 8)]
                    for k in range(KC):
                        P.op("pe", lambda e, ps=ps, pan=pan, k=k, b=b: e.matmul(
                            ps[:, 0:256], hT[:, k, b * 128:(b + 1) * 128], pan[:, k, 0:256],
                            start=(k == 0), stop=(k == KC - 1)), reads=[pname, "hT"], writes=[psn])
                    P.op("act", lambda e, vs=vs, ps=ps, b=b: e.activation(out=vs[:, b, :], in_=ps[:, 0:256],
                                                                          func=AF.Copy), reads=[psn], writes=[vsn])
                dst = vtok[t * 1024:(t + 1) * 1024, r0:r0 + 256].rearrange("(b p) c -> p b c", p=128)
                P.dma("sp", lambda e, vs=vs, dst=dst: e.dma_start(out=dst, in_=vs[:]), reads=[vsn], writes=[],
                      key=vsn + "_st")


def sb_consts():
    j = np.arange(128)
    negU = np.where(j[:, None] >= j[None, :], -1.0, 0.0).astype(NPBF)
    tri = (j[:, None] < j[None, :]).astype(np.float32).astype(NPBF)
    negones = np.full((128, 1), -1.0, dtype=np.float32).astype(NPBF)
    return {"negU": negU, "tri_strict": tri, "negones": negones}


def nsa_slopes():
    h = np.arange(1, 17, dtype=np.float32)
    return (2.0 ** (-8.0 * h / 16)).astype(np.float32)


def nsa_tables(c, NM):
    NJ, NCU, NUC = 16 * NM, 64 * NM, 8 * NM
    NCC = max(1, NCU // 128)
    sl = nsa_slopes()
    i = np.arange(128)
    T = {}
    rel = np.arange(NUC)
    T["ak"] = (sl[None, :, None] * (i[:, None, None] - 64.0 - 128.0 * rel[None, None, :])).astype(np.float32)
    ac = np.zeros((128, 16, NM, NCC), np.float32)
    cm = np.zeros((128, NM, NCC, 128), np.float32)
    wm = np.zeros((128, NM, 5, 128), np.float32)
    bonus = np.zeros((128, NM, NJ), np.float32)
    tl = np.arange(128)
    for m in range(NM):
        u0 = 128 * (8 * m + 7)
        ut = u0 + tl
        for ncc in range(NCC):
            nu = 128 * ncc + i
            cend = 16 * nu + 31
            ac[:, :, m, ncc] = sl[None, :] * (cend[:, None] - (u0 + 64.0))
            cm[:, m, ncc, :] = ((cend[:, None] <= ut[None, :]) & (nu[:, None] >= 8 * (7 - c))).astype(np.float32)
        for r in range(5):
            uk = 128 * (8 * m + 3 + r) + i
            d = ut[None, :] - uk[:, None]
            wm[:, m, r, :] = ((d >= 0) & (d < 512) & (uk[:, None] >= 128 * (7 - c))).astype(np.float32)
        cur = ut // 64
        jj = np.arange(NJ)
        valid = (jj[None, :] <= cur[:, None]) & (jj[None, :] >= 2 * (7 - c))
        forced = (jj[None, :] == 2 * (7 - c)) | (jj[None, :] == cur[:, None]) | (jj[None, :] == cur[:, None] - 1)
        bonus[:, m, :] = np.where(valid, np.where(forced, 1e6, 0.0), -1e30)
    T["ac"] = np.minimum(ac, 45.0)
    T["cm"] = cm.astype(NPBF)
    T["wm"] = wm.astype(NPBF)
    T["bonus"] = bonus
    T["ex"] = np.broadcast_to((np.arange(NJ) >= 2 * (7 - c)).astype(np.float32), (128, NJ)).copy()
    T["tri_incl"] = (i[:, None] <= i[None, :]).astype(np.float32).astype(NPBF)
    T["exf"] = np.broadcast_to((np.arange(8) >= (7 - c)).astype(np.float32), (128, 8)).copy()
    E = np.zeros((NJ, NUC, 128), np.float32)
    for uc in range(NUC):
        for half in range(2):
            if 2 * uc + half < NJ:
                E[2 * uc + half, uc, half * 64:(half + 1) * 64] = 1.0
    T["E"] = E.astype(NPBF)
    caug = np.zeros((128, NCC, 1 + NJ), np.float32)
    caug[:, :, 0] = 1.0
    for ncc in range(NCC):
        for il in range(128):
            nu = 128 * ncc + il
            for j in range(NJ):
                for mm in range(4):
                    for nn in range(2):
                        if 4 * j - mm - nn == nu:
                            caug[il, ncc, 1 + j] += 1.0
    T["caug"] = caug.astype(NPBF)
    T["ident"] = np.eye(128, dtype=np.float32)
    return T


def _bcast_mid(ap2d, n):
    a = ap2d.ap
    return bass.AP(ap2d.tensor, ap2d.offset, [list(a[0]), [0, n], list(a[-1])])


def phase_2a(cx, io):
    P = cx.P
    S = SEQ
    NCH = S // 128
    kTd, vtokd, qTd, out = io["kT"], io["vtok"], io["qT"], io["osb"]
    negU = cx.sb("negU_sb", (128, 128), BF16)
    tri = cx.sb("tri_sb", (128, 128), BF16)
    negones = cx.sb("negones_sb", (128, 1), BF16)
    exf = cx.sb("exf_sb", (128, 8), F32)
    P.dma("sp", lambda e: e.dma_start(out=negU[:], in_=io["negU"]), writes=["negU"], key="ld_c0")
    P.dma("sp", lambda e: e.dma_start(out=tri[:], in_=io["tri_strict"]), writes=["tri"], key="ld_c1")
    P.dma("sp", lambda e: e.dma_start(out=negones[:], in_=io["negones"]), writes=["negones"], key="ld_c2")
    P.dma("sp", lambda e: e.dma_start(out=exf[:], in_=io["exf"]), writes=["exf"], key="ld_c3")
    HPC = 2
    sets = []
    for s_ in range(2):
        sets.append({
            "k": (cx.sb("kTb%d" % s_, (128, HPC, S), BF16), "kTb%d" % s_),
            "v": (cx.sb("vb%d" % s_, (128, HPC, NCH, 128), BF16), "vb%d" % s_),
            "q": (cx.sb("qb%d" % s_, (128, HPC, NT_OWN), BF16), "qb%d" % s_)})
    psZ = [[(cx.ps("psZ%d_%d" % (i, j)), "psZ%d_%d" % (i, j)) for j in range(2)] for i in range(HPC)]
    psC = [(cx.ps("psC%d" % i, (128, 8)), "psC%d" % i) for i in range(HPC)]
    psO = [(cx.ps("psO%d" % i), "psO%d" % i) for i in range(HPC)]
    esb = [[(cx.sb("esb%d_%d" % (i, j), (128, 512), F32), "esb%d_%d" % (i, j)) for j in range(2)] for i in range(HPC)]
    spb = [[(cx.sb("spb%d_%d" % (i, j), (128, 512), BF16), "spb%d_%d" % (i, j)) for j in range(2)] for i in range(HPC)]
    Ab = [[(cx.sb("Ab%d_%d" % (i, j), (128, 512), BF16), "Ab%d_%d" % (i, j)) for j in range(2)] for i in range(HPC)]
    acc = [(cx.sb("acc%d" % i, (128, 4, 128), F32), "acc%d" % i) for i in range(HPC * 2)]
    car = [(cx.sb("car%d" % i, (128, 4), F32), "car%d" % i) for i in range(HPC)]
    ecb = [(cx.sb("ec%d" % i, (128, 4), F32), "ec%d" % i) for i in range(HPC)]

    def stage_a(it):
        hh, uc, c0, q0, par = it["hh"], it["uc"], it["c0"], it["q0"], it["par"]
        (kT, kTn), (qT, qn) = it["k"], it["q"]
        pz, pzn = psZ[hh][par]
        es, esn = esb[hh][par]
        sp, spn = spb[hh][par]
        P.op("pe", lambda e: e.matmul(pz[:, c0:512], kT[:, hh, uc * 128:(uc + 1) * 128], qT[:, hh, q0 + c0:q0 + 512],
                                      start=True, stop=False), reads=[kTn, qn], writes=[pzn])
        P.op("act", lambda e: e.activation(out=es[:, c0:512], in_=pz[:, c0:512], func=AF.Exp), reads=[pzn],
             writes=[esn])
        P.op("act", lambda e: e.activation(out=sp[:, c0:512], in_=es[:, c0:512], func=AF.Ln, bias=1.0, scale=1.0),
             reads=[esn], writes=[spn])
        if it["diag"]:
            P.op("dve", lambda e: e.tensor_tensor(out=sp[:, c0:c0 + 128], in0=sp[:, c0:c0 + 128], in1=tri[:],
                                                  op=ALU.mult), reads=[spn, "tri"], writes=[spn])
        if uc < 7:
            P.op("dve", lambda e: e.tensor_scalar(out=sp[:, c0:512], in0=sp[:, c0:512], scalar1=exf[:, uc:uc + 1],
                                                  scalar2=None, op0=ALU.mult), reads=[spn, "exf"], writes=[spn])

    def stage_b(it):
        hh, c0, b0, par = it["hh"], it["c0"], it["b0"], it["par"]
        pz, pzn = psZ[hh][par]
        pc, pcn = psC[hh]
        sp, spn = spb[hh][par]
        A, An = Ab[hh][par]
        P.op("pe", lambda e: e.matmul(pz[:, c0:512], negU[:], sp[:, c0:512], start=False, stop=True),
             reads=[spn, "negU"], writes=[pzn])
        for b in range(b0, 4):
            P.op("pe", lambda e, b=b: e.matmul(pc[:, b:b + 1], sp[:, b * 128:(b + 1) * 128], negones[:], start=True,
                                               stop=True), reads=[spn, "negones"], writes=[pcn])
        P.op("act", lambda e: e.activation(out=A[:, c0:512], in_=pz[:, c0:512], func=AF.Exp), reads=[pzn],
             writes=[An])
        if it["diag"]:
            P.op("dve", lambda e: e.tensor_tensor(out=A[:, c0:c0 + 128], in0=A[:, c0:c0 + 128], in1=tri[:],
                                                  op=ALU.mult), reads=[An, "tri"], writes=[An])

    def stage_c(it):
        hh, uc, b0, par, uq = it["hh"], it["uc"], it["b0"], it["par"], it["uq"]
        (v, vn) = it["v"]
        pc, pcn = psC[hh]
        po, pon = psO[hh]
        A, An = Ab[hh][par]
        ac, acn = it["acc"]
        cr, crn = car[hh]
        ec, ecn = ecb[hh]
        for b in range(b0, 4):
            P.op("pe", lambda e, b=b: e.matmul(po[:, b * 128:(b + 1) * 128], A[:, b * 128:(b + 1) * 128],
                                               v[:, hh, uc, :], start=True, stop=True), reads=[An, vn], writes=[pon])
        for b in range(b0, 4):
            if uc == uq[b]:
                P.op("dve", lambda e, b=b: e.tensor_copy(out=ac[:, b, :], in_=po[:, b * 128:(b + 1) * 128]),
                     reads=[pon], writes=[acn])
            else:
                P.op("dve", lambda e, b=b: e.scalar_tensor_tensor(
                    out=ac[:, b, :], in0=po[:, b * 128:(b + 1) * 128], scalar=ec[:, b:b + 1], in1=ac[:, b, :],
                    op0=ALU.mult, op1=ALU.add), reads=[pon, ecn, acn], writes=[acn])
        if uc > 0:
            if it["diag"]:
                P.op("dve", lambda e: e.tensor_copy(out=cr[:, b0:b0 + 1], in_=pc[:, b0:b0 + 1]), reads=[pcn],
                     writes=[crn])
                if b0 + 1 < 4:
                    P.op("dve", lambda e: e.tensor_tensor(out=cr[:, b0 + 1:4], in0=cr[:, b0 + 1:4],
                                                          in1=pc[:, b0 + 1:4], op=ALU.add), reads=[pcn, crn],
                         writes=[crn])
            else:
                P.op("dve", lambda e: e.tensor_tensor(out=cr[:, b0:4], in0=cr[:, b0:4], in1=pc[:, b0:4], op=ALU.add),
                     reads=[pcn, crn], writes=[crn])
            P.op("act", lambda e: e.activation(out=ec[:, b0:4], in_=cr[:, b0:4], func=AF.Exp), reads=[crn],
                 writes=[ecn])
        if it["last"]:
            h, G = it["h"], it["G"]
            dst = out[512 * G:512 * G + 512, h * 128:(h + 1) * 128].rearrange("(b p) d -> p b d", p=128)
            P.dma("sp", lambda e: e.dma_start(out=dst, in_=ac[:]), reads=[acn], key=acn + "_st")

    for pair in range(16 // HPC):
        st = sets[pair % 2]
        (kT, kTn), (v, vn), (qT, qn) = st["k"], st["v"], st["q"]
        for hh in range(HPC):
            h = pair * HPC + hh
            P.dma("sp", lambda e, hh=hh, h=h, kT=kT: e.dma_start(
                out=kT[:, hh, :], in_=kTd[KROW_SB + 128 * h:KROW_SB + 128 * (h + 1), :]), writes=[kTn],
                key=kTn + "_%d" % hh)
            P.dma("sp", lambda e, hh=hh, h=h, v=v: e.dma_start(
                out=v[:, hh, :, :],
                in_=vtokd[:, VCOL_SB + 128 * h:VCOL_SB + 128 * (h + 1)].rearrange("(c i) d -> i c d", i=128)),
                writes=[vn], key=vn + "_%d" % hh)
            P.dma("sp", lambda e, hh=hh, h=h, qT=qT: e.dma_start(
                out=qT[:, hh, :], in_=qTd[QROW_SB + 128 * h:QROW_SB + 128 * (h + 1), :]), writes=[qn],
                key=qn + "_%d" % hh)
        items = []
        step = 0
        for G in range(NM_OWN // 4):
            uq = [own_chunk(4 * G + b) for b in range(4)]
            accs = {hh: acc[cx.nxt("acc%d" % hh, 2) * HPC + hh] for hh in range(HPC)}
            for uc in range(uq[3], -1, -1):
                b0 = min(b for b in range(4) if uq[b] >= uc)
                for hh in range(HPC):
                    items.append({"hh": hh, "h": pair * HPC + hh, "G": G, "uc": uc, "b0": b0, "c0": b0 * 128,
                                  "diag": uc == uq[b0], "par": step % 2, "uq": uq, "q0": 512 * G, "acc": accs[hh],
                                  "last": uc == 0, "k": (kT, kTn), "q": (qT, qn), "v": (v, vn)})
                step += 1
        n = len(items)
        for i in range(n + 2):
            if i < n:
                stage_a(items[i])
            if 0 <= i - 1 < n:
                stage_b(items[i - 1])
            if 0 <= i - 2 < n:
                stage_c(items[i - 2])


def phase_2b(cx, io):
    P = cx.P
    S, NM = SEQ, NM_OWN
    NJ, NCU, NUC = 16 * NM, 64 * NM, 8 * NM
    NCC = max(1, NCU // 128)
    NT = NM * 128
    NCV = 129 + NJ
    kTd, vtokd, qTd = io["kT"], io["vtok"], io["qT"]
    d_w1 = {"k": io["w_k1"], "v": io["w_v1"]}
    d_w2 = {"k": io["w_k2"], "v": io["w_v2"]}
    d_pos = {"k": io["posTk"], "v": io["posTv"]}
    d_ak, d_ac, d_cm, d_wm, d_tri = io["ak"], io["ac"], io["cm"], io["wm"], io["tri_incl"]
    d_bonus, d_ex, d_E, d_caug, d_ident = io["bonus"], io["ex"], io["E"], io["caug"], io["ident"]
    d_gn = io["gn"][:, 0:48].rearrange("(m p) c -> p m c", p=128)
    out = io["onsa"]

    def const(name, d, shape, dt):
        t = cx.sb(name + "_sb", shape, dt)
        P.dma("sp", lambda e: e.dma_start(out=t[:], in_=d), writes=[name], key="ld_" + name)
        return t

    ak = const("ak", d_ak, (128, 16, NUC), F32)
    ac = const("ac", d_ac, (128, 16, NM, NCC), F32)
    cm = const("cm", d_cm, (128, NM, NCC, 128), BF16)
    wm = const("wm", d_wm, (128, NM, 5, 128), BF16)
    tri = const("tri", d_tri, (128, 128), BF16)
    bonus = const("bonus", d_bonus, (128, NM, NJ), F32)
    ex = const("ex", d_ex, (128, NJ), F32)
    E = const("E", d_E, (NJ, NUC, 128), BF16)
    ident = const("ident", d_ident, (128, 128), F32)
    gn = const("gn", d_gn, (128, NM, 48), F32)

    qg = cx.sb("qg", (128, 8, NT), BF16)
    ksl = cx.sb("ksl", (128, S), BF16)
    vsl = cx.sb("vsl_sb", (128, NUC, 129), BF16)
    kw = cx.sb("kw", (128, NM, 5, 128), BF16)
    vw = cx.sb("vw_sb", (128, NM, 5, 129), BF16)
    cbuf = cx.sb("cbuf", (128, S + 16), BF16)
    kcmpT = cx.sb("kcmpT", (128, NCU), BF16)
    vcmp = cx.sb("vcmp", (128, NCC, NCV), BF16)
    gates = cx.sb("gates", (128, NM, 48), F32)
    gtmp = cx.sb("gtmp", (128, NM, 48), F32)
    posf = cx.sb("posf", (128, 32), F32)
    posb = cx.sb("posb", (128, 32), BF16)
    biasT = cx.sb("biasT", (128, 2), F32)
    HT = cx.sb("HT", (128, 2, NCU), BF16)
    ub = cx.sb("ub", (128, NCU), F32)
    u2 = cx.sb("u2", (128, NCU), F32)
    ws = WStream(cx, "w", 32, 256, kpiece=4, npanel=1, nstage=2)
    combs = [(cx.sb("comb%d" % i, (128, 8, 128), F32), "comb%d" % i) for i in range(2)]
    imp = cx.sb("imp", (128, NJ), F32)
    score = cx.sb("score", (128, NJ), F32)
    score2 = cx.sb("score2", (128, NJ), F32)
    m8 = cx.sb("m8", (128, 16), F32)
    sel = cx.sb("sel", (128, NJ), F32)
    selT = cx.sb("selT", (NJ, 128), BF16)
    Pts = [(cx.sb("Pt%d" % i, (128, 512), BF16), "Pt%d" % i) for i in range(2)]
    mts = [(cx.sb("mt%d" % i, (128, 128), BF16), "mt%d" % i) for i in range(2)]
    rts = [(cx.sb("rt%d" % i, (128, 4), F32), "rt%d" % i) for i in range(4)]
    psS = [(cx.ps("psS%d" % i), "psS%d" % i) for i in range(2)]
    psA = [(cx.ps("psA%d" % i), "psA%d" % i) for i in range(4)]
    psM = (cx.ps("psM"), "psM")
    psX = (cx.ps("psX"), "psX")

    P.op("pool", lambda e: e.memset(vsl[:, :, 128:129], 1.0), writes=["vsl"])
    P.op("pool", lambda e: e.memset(vw[:, :, :, 128:129], 1.0), writes=["vw"])
    P.op("pool", lambda e: e.memset(cbuf[:, S:S + 16], 0.0), writes=["cbuf"])
    P.op("act", lambda e: e.activation(out=gtmp[:], in_=gn[:], func=AF.Exp, scale=-1.0), reads=["gn"], writes=["gtmp"])
    P.op("dve", lambda e: e.tensor_scalar(out=gtmp[:], in0=gtmp[:], scalar1=1.0, scalar2=None, op0=ALU.add),
         reads=["gtmp"], writes=["gtmp"])
    P.op("dve", lambda e: e.reciprocal(out=gates[:], in_=gtmp[:]), reads=["gtmp"], writes=["gates"])

    def mlp(g, which):
        r0 = (KROW_KC if which == "k" else KROW_VC) + 128 * g
        P.dma("sp", lambda e: e.dma_start(out=cbuf[:, 0:S], in_=kTd[r0:r0 + 128, :]), writes=["cbuf"], key="ld_cbuf")
        P.dma("sp", lambda e: e.dma_start(out=posf[:], in_=d_pos[which]), writes=["posf"], key="ld_pos")
        P.op("pool", lambda e: e.tensor_copy(out=posb[:], in_=posf[:]), reads=["posf"], writes=["posb"])
        pan, pname = ws.load(d_w1[which], 0, 32, 0, 256)
        px, pxn = psX
        for hc in range(2):
            for l in range(32):
                P.op("pe", lambda e, hc=hc, l=l: e.matmul(px[:, hc:hc + 1], pan[:, l, hc * 128:(hc + 1) * 128],
                                                          posb[:, l:l + 1], start=(l == 0), stop=(l == 31)),
                     reads=[pname, "posb"], writes=[pxn])
            P.op("dve", lambda e, hc=hc: e.tensor_copy(out=biasT[:, hc:hc + 1], in_=px[:, hc:hc + 1]), reads=[pxn],
                 writes=["biasT"])
        for hc in range(2):
            ps, psn = psS[hc]
            for l in range(32):
                P.op("pe", lambda e, ps=ps, hc=hc, l=l: e.matmul(ps[:, 0:NCU], pan[:, l, hc * 128:(hc + 1) * 128],
                                                                cbuf[:, l:l + 16 * (NCU - 1) + 1:16], start=(l == 0),
                                                                stop=(l == 31)),
                     reads=[pname, "cbuf"], writes=[psn])
            P.op("act", lambda e, ps=ps, hc=hc: e.activation(out=ub[:], in_=ps[:, 0:NCU], func=AF.Identity,
                                                            bias=biasT[:, hc:hc + 1], scale=1.0),
                 reads=[psn, "biasT"], writes=["ub"])
            P.op("dve", lambda e: e.tensor_tensor(out=u2[:], in0=ub[:], in1=ub[:], op=ALU.mult), reads=["ub"],
                 writes=["u2"])
            P.op("dve", lambda e: e.tensor_scalar(out=u2[:], in0=u2[:], scalar1=0.044715, scalar2=1.0, op0=ALU.mult,
                                                  op1=ALU.add), reads=["u2"], writes=["u2"])
            P.op("dve", lambda e: e.tensor_tensor(out=u2[:], in0=u2[:], in1=ub[:], op=ALU.mult), reads=["u2", "ub"],
                 writes=["u2"])
            P.op("act", lambda e: e.activation(out=u2[:], in_=u2[:], func=AF.Exp, scale=-1.5957691216057308),
                 reads=["u2"], writes=["u2"])
            P.op("dve", lambda e: e.tensor_scalar(out=u2[:], in0=u2[:], scalar1=1.0, scalar2=None, op0=ALU.add),
                 reads=["u2"], writes=["u2"])
            P.op("dve", lambda e: e.reciprocal(out=u2[:], in_=u2[:]), reads=["u2"], writes=["u2"])
            P.op("dve", lambda e, hc=hc: e.tensor_tensor(out=HT[:, hc, :], in0=u2[:], in1=ub[:], op=ALU.mult),
                 reads=["u2", "ub"], writes=["HT"])
        pan2, p2name = ws.load(d_w2[which], 0, 2, 0, 128)
        if which == "k":
            ps, psn = psS[0]
            for hc in range(2):
                P.op("pe", lambda e, hc=hc: e.matmul(ps[:, 0:NCU], pan2[:, hc, 0:128], HT[:, hc, :], start=(hc == 0),
                                                     stop=(hc == 1)), reads=[p2name, "HT"], writes=[psn])
            P.op("act", lambda e: e.activation(out=kcmpT[:], in_=ps[:, 0:NCU], func=AF.Copy), reads=[psn],
                 writes=["kcmpT"])
        else:
            for ncc in range(NCC):
                ps, psn = psS[ncc % 2]
                for hc in range(2):
                    P.op("pe", lambda e, ps=ps, hc=hc, ncc=ncc: e.matmul(ps[:, 0:128],
                                                                        HT[:, hc, ncc * 128:(ncc + 1) * 128],
                                                                        pan2[:, hc, 0:128], start=(hc == 0),
                                                                        stop=(hc == 1)),
                         reads=[p2name, "HT"], writes=[psn])
                P.op("act", lambda e, ps=ps, ncc=ncc: e.activation(out=vcmp[:, ncc, 0:128], in_=ps[:, 0:128],
                                                                  func=AF.Copy), reads=[psn], writes=["vcmp"])

    def chunk(g, m, quad, keysT, bias_ap_fn, mask2d, mask_name, vrhs, vname, ncv, first, last, kname):
        si = cx.nxt("psS", 2)
        ps, psn = psS[si]
        pt, ptn = Pts[si]
        rhs = qg[:, 4 * quad:4 * quad + 4, m * 128:(m + 1) * 128]
        P.op("pe", lambda e: e.matmul(ps[:, 0:512].rearrange("p (h t) -> p h t", h=4), keysT, rhs, start=True,
                                      stop=True), reads=[kname, "qg"], writes=[psn])
        for h in range(4):
            b = bias_ap_fn(8 * g + 4 * quad + h)
            P.op("act", lambda e, h=h, b=b: e.activation(out=pt[:, h * 128:(h + 1) * 128],
                                                          in_=ps[:, h * 128:(h + 1) * 128], func=AF.Exp, bias=b,
                                                          scale=1.0), reads=[psn, "ak", "ac"], writes=[ptn])
        if mask2d is not None:
            pt3 = pt[:, 0:512].rearrange("p (h t) -> p h t", h=4)
            P.op("dve", lambda e: e.tensor_tensor(out=pt3, in0=pt3, in1=_bcast_mid(mask2d, 4), op=ALU.mult),
                 reads=[ptn, mask_name], writes=[ptn])
        def s2():
            for h in range(4):
                pa, pan_ = psA[h]
                P.op("pe", lambda e, h=h, pa=pa: e.matmul(pa[:, 0:ncv], pt[:, h * 128:(h + 1) * 128], vrhs,
                                                          start=first, stop=last), reads=[ptn, vname], writes=[pan_])
        return s2

    def evac(g, m, quad, br, comb, cname, first_branch, do_imp):
        for h in range(4):
            pa, pan_ = psA[h]
            hl = 4 * quad + h
            head = 8 * g + hl
            rt, rtn = rts[cx.nxt("rt", 4)]
            P.op("dve", lambda e, pa=pa, rt=rt: e.tensor_scalar(out=rt[:, 0:1], in0=pa[:, 128:129], scalar1=1e-30,
                                                               scalar2=None, op0=ALU.max), reads=[pan_],
                 writes=[rtn])
            P.op("dve", lambda e, rt=rt: e.reciprocal(out=rt[:, 1:2], in_=rt[:, 0:1]), reads=[rtn], writes=[rtn])
            col = head * 3 + br
            P.op("dve", lambda e, rt=rt, col=col: e.tensor_tensor(out=rt[:, 2:3], in0=rt[:, 1:2],
                                                                 in1=gates[:, m, col:col + 1], op=ALU.mult),
                 reads=[rtn, "gates"], writes=[rtn])
            if first_branch:
                P.op("dve", lambda e, pa=pa, rt=rt, hl=hl: e.tensor_scalar(out=comb[:, hl, :], in0=pa[:, 0:128],
                                                                          scalar1=rt[:, 2:3], scalar2=None,
                                                                          op0=ALU.mult), reads=[pan_, rtn],
                     writes=[cname])
            else:
                P.op("dve", lambda e, pa=pa, rt=rt, hl=hl: e.scalar_tensor_tensor(
                    out=comb[:, hl, :], in0=pa[:, 0:128], scalar=rt[:, 2:3], in1=comb[:, hl, :], op0=ALU.mult,
                    op1=ALU.add), reads=[pan_, rtn, cname], writes=[cname])
            if do_imp:
                if hl == 0:
                    P.op("dve", lambda e, pa=pa, rt=rt: e.tensor_scalar(out=imp[:], in0=pa[:, 129:129 + NJ],
                                                                       scalar1=rt[:, 1:2], scalar2=None,
                                                                       op0=ALU.mult), reads=[pan_, rtn],
                         writes=["imp"])
                else:
                    P.op("dve", lambda e, pa=pa, rt=rt: e.scalar_tensor_tensor(
                        out=imp[:], in0=pa[:, 129:129 + NJ], scalar=rt[:, 1:2], in1=imp[:], op0=ALU.mult,
                        op1=ALU.add), reads=[pan_, rtn, "imp"], writes=["imp"])

    class Skew:
        pending, after = None, []

        def chunk(self, s1):
            s2 = s1()
            self.flush()
            self.pending = s2

        def flush(self):
            if self.pending is not None:
                self.pending()
                self.pending = None
            for f in self.after:
                f()
            self.after = []

        def defer(self, f):
            if self.pending is None:
                f()
            else:
                self.after.append(f)

    sk = Skew()
    for g in range(2):
        sk.flush()
        P.dma("sp", lambda e, g=g: e.dma_start(
            out=qg[:], in_=qTd[QROW_N + 1024 * g:QROW_N + 1024 * (g + 1), :].rearrange("(h d) t -> d h t", d=128)),
            writes=["qg"], key="ld_qg")
        P.dma("sp", lambda e, g=g: e.dma_start(out=ksl[:], in_=kTd[KROW_KSL + 128 * g:KROW_KSL + 128 * (g + 1), :]),
              writes=["ksl"], key="ld_ksl")
        P.dma("sp", lambda e, g=g: e.dma_start(
            out=vsl[:, :, 0:128],
            in_=vtokd[:, VCOL_VSL + 128 * g:VCOL_VSL + 128 * (g + 1)].rearrange("(c i) d -> i c d", i=128)),
            writes=["vsl"], key="ld_vsl")
        for m in range(NM):
            u0 = (8 * m + 3) * 128
            P.dma("sp", lambda e, g=g, m=m, u0=u0: e.dma_start(
                out=kw[:, m, :, :],
                in_=kTd[KROW_KW + 128 * g:KROW_KW + 128 * (g + 1), u0:u0 + 640].rearrange("d (r i) -> d r i", i=128)),
                writes=["kw"], key="ld_kw")
            P.dma("sp", lambda e, g=g, m=m, u0=u0: e.dma_start(
                out=vw[:, m, :, 0:128],
                in_=vtokd[u0:u0 + 640, VCOL_VW + 128 * g:VCOL_VW + 128 * (g + 1)].rearrange("(r i) d -> i r d", i=128)),
                writes=["vw"], key="ld_vw")
        P.dma("sp", lambda e: e.dma_start(out=vcmp[:, :, 128:NCV], in_=d_caug), writes=["vcmp"], key="ld_caug")
        mlp(g, "k")
        mlp(g, "v")
        for m in range(NM):
            comb, cname = combs[cx.nxt("comb", 2)]
            nccs = list(range((64 * m + 62) // 128 + 1))
            for quad in range(2):
                for ii, ncc in enumerate(nccs):
                    sk.chunk(lambda quad=quad, ii=ii, ncc=ncc, m=m: chunk(
                        g, m, quad, kcmpT[:, ncc * 128:(ncc + 1) * 128],
                        lambda head, ncc=ncc: ac[:, head, m, ncc:ncc + 1],
                        cm[:, m, ncc, :], "cm", vcmp[:, ncc, :], "vcmp", NCV, ii == 0, ii == len(nccs) - 1, "kcmpT"))
                sk.defer(lambda g=g, quad=quad, m=m, comb=comb, cname=cname: evac(g, m, quad, 0, comb, cname, True, True))
            sk.flush()
            P.op("dve", lambda e, m=m: e.tensor_tensor(out=score[:], in0=imp[:], in1=bonus[:, m, :], op=ALU.add),
                 reads=["imp", "bonus"], writes=["score"])
            P.op("dve", lambda e: e.max(out=m8[:, 0:8], in_=score[:]), reads=["score"], writes=["m8"])
            P.op("dve", lambda e: e.match_replace(out=score2[:], in_to_replace=m8[:, 0:8], in_values=score[:],
                                                  imm_value=-3.0e38), reads=["score", "m8"], writes=["score2"])
            P.op("dve", lambda e: e.max(out=m8[:, 8:16], in_=score2[:]), reads=["score2"], writes=["m8"])
            P.op("dve", lambda e: e.tensor_scalar(out=sel[:], in0=score[:], scalar1=m8[:, 15:16], scalar2=None,
                                                  op0=ALU.is_ge), reads=["score", "m8"], writes=["sel"])
            P.op("dve", lambda e: e.tensor_tensor(out=sel[:], in0=sel[:], in1=ex[:], op=ALU.mult),
                 reads=["sel", "ex"], writes=["sel"])
            px, pxn = psX
            P.op("pe", lambda e: e.transpose(px[0:NJ, 0:128], sel[:], ident[:]), reads=["sel", "ident"], writes=[pxn])
            P.op("act", lambda e: e.activation(out=selT[:], in_=px[0:NJ, 0:128], func=AF.Copy), reads=[pxn],
                 writes=["selT"])
            for quad in range(2):
                ucs = list(range(8 * m + 8))
                for ii, uc in enumerate(ucs):
                    def s1(quad=quad, ii=ii, uc=uc, m=m, nuc=len(ucs)):
                        pm, pmn = psM if cx.nxt("psM", 2) == 0 else psX
                        mt, mtn = mts[cx.nxt("mt", 2)]
                        P.op("pe", lambda e: e.matmul(pm[:, 0:128], E[:, uc, :], selT[:], start=True, stop=True),
                             reads=["E", "selT"], writes=[pmn])
                        if uc == 8 * m + 7:
                            P.op("dve", lambda e: e.tensor_tensor(out=mt[:], in0=pm[:, 0:128], in1=tri[:],
                                                                  op=ALU.mult), reads=[pmn, "tri"], writes=[mtn])
                        else:
                            P.op("act", lambda e: e.activation(out=mt[:], in_=pm[:, 0:128], func=AF.Copy),
                                 reads=[pmn], writes=[mtn])
                        rel = 8 * m + 7 - uc
                        return chunk(g, m, quad, ksl[:, uc * 128:(uc + 1) * 128],
                                     lambda head, rel=rel: ak[:, head, rel:rel + 1],
                                     mt[:], mtn, vsl[:, uc, :], "vsl", 129, ii == 0, ii == nuc - 1, "ksl")
                    sk.chunk(s1)
                sk.defer(lambda g=g, quad=quad, m=m, comb=comb, cname=cname: evac(g, m, quad, 1, comb, cname, False, False))
            for quad in range(2):
                for r in range(5):
                    sk.chunk(lambda quad=quad, r=r, m=m: chunk(
                        g, m, quad, kw[:, m, r, :], lambda head, r=r: ak[:, head, 4 - r:5 - r],
                        wm[:, m, r, :], "wm", vw[:, m, r, :], "vw", 129, r == 0, r == 4, "kw"))
                sk.defer(lambda g=g, quad=quad, m=m, comb=comb, cname=cname: evac(g, m, quad, 2, comb, cname, False, False))
            dst = out[m * 128:(m + 1) * 128, g * 1024:(g + 1) * 1024]
            sk.defer(lambda comb=comb, dst=dst, cname=cname: P.dma(
                "sp", lambda e: e.dma_start(out=dst, in_=comb[:].rearrange("p h d -> p (h d)")),
                reads=[cname], key=cname + "_st"))
    sk.flush()


def phase_3(cx, io, NPASS=5):
    P = cx.P
    D, DFF, NTOK = D_MODEL, D_FF, NT_OWN
    NB = NTOK // 128
    HALF = D // 2
    xu, osb, onsa, out, x1d, yacc = io["xu"], io["osb"], io["onsa"], io["out"], io["x1d"], io["yacc"]
    w_out, w_gate, w_up, w_down = io["w_out"], io["w_gate"], io["w_up"], io["w_down"]
    xrow = lambda b: xu[own_chunk(b) * 128:(own_chunk(b) + 1) * 128, :]
    npan_ff = DFF // 256
    per = -(-npan_ff // NPASS)
    maxk_act = per * 2

    ident = cx.sb("ident_sb", (128, 128), F32)
    gain = cx.sb("gain_sb", (128, D), F32)
    hT = cx.sb("hT", (128, KC, NTOK), BF16)
    actT = cx.sb("actT", (128, maxk_act, NTOK), BF16)
    xts = [(cx.sb("xt0", (128, D), F32), "xt0")]
    ws = WStream(cx, "w", max(KC, maxk_act), 256, kpiece=4, npanel=3, nstage=2)
    sqv = actT[:].rearrange("p k n -> p (k n)")
    small = small_tiles(cx, sqv, "actT")
    thunks = [(lambda col=cp * 256: ws.load(w_out, 0, KC, col, 256)) for cp in range(D // 256)]
    for p_ in range(NPASS):
        a0, a1 = p_ * per, min(npan_ff, p_ * per + per)
        if a0 >= a1:
            continue
        for pp_ in range(a0, a1):
            thunks.append(lambda col=pp_ * 256: ws.load(w_gate, 0, KC, col, 256))
            thunks.append(lambda col=pp_ * 256: ws.load(w_up, 0, KC, col, 256))
        for cp in range(D // 256):
            thunks.append(lambda col=cp * 256, a0=a0, a1=a1: ws.load(w_down, a0 * 2, (a1 - a0) * 2, col, 256))
    pf = Prefetch(thunks, ahead=2)
    pfi = [0]

    def next_panel():
        r = pf.get(pfi[0])
        pfi[0] += 1
        return r

    pss = [(cx.ps("ps%d" % i), "ps%d" % i) for i in range(8)]
    ept = [(cx.sb("ept%d" % i, (128, 256), F32), "ept%d" % i) for i in range(2)]
    epo = [(cx.sb("epo%d" % i, (128, 256), F32), "epo%d" % i) for i in range(2)]
    sgt = [(cx.sb("sg%d" % i, (128, 512), F32), "sg%d" % i) for i in range(2)]

    P.dma("sp", lambda e: e.dma_start(out=ident[:], in_=io["ident"]), writes=["ident"], key="c_ident")
    P.op("dve", lambda e: e.memset(small["eps"][0][:], EPS), writes=["epsc"])

    P.dma("sp", lambda e: e.dma_start(out=gain[:, 0:HALF], in_=io["g_sb"]), writes=["gain"], key="c_gain")
    norm_transpose(cx, "sb", lambda b: osb[b * 128:(b + 1) * 128, :], lambda b: [], NB, HALF, gain, "gain", hT, "hT",
                   0, xts, pss, ident, small)
    P.dma("sp", lambda e: e.dma_start(out=gain[:, 0:HALF], in_=io["g_nsa"]), writes=["gain"], key="c_gain")
    norm_transpose(cx, "nsa", lambda b: onsa[b * 128:(b + 1) * 128, :], lambda b: [], NB, HALF, gain, "gain", hT, "hT",
                   KC // 2, xts, pss, ident, small)

    def tok_major_gemm(W, kc0, nk, actbuf, actname, prev_row, prev_name_fn, dst, dst_name_fn):
        for cp in range(D // 256):
            col = cp * 256
            pan, pname = next_panel()
            for b in range(NB):
                ps, psn = pss[cx.nxt("mm_ps", 8)]
                for k in range(nk):
                    P.op("pe", lambda e, ps=ps, pan=pan, k=k, b=b: e.matmul(
                        ps[:, 0:256], actbuf[:, k, b * 128:(b + 1) * 128], pan[:, k, 0:256],
                        start=(k == 0), stop=(k == nk - 1)), reads=[pname, actname], writes=[psn])
                ti = cx.nxt("ept", 2)
                pt, ptn = ept[ti]
                po, pon = epo[ti]
                srcp = prev_row(b)[:, col:col + 256]
                P.dma("sp", lambda e, pt=pt, srcp=srcp: e.dma_start(out=pt[:], in_=srcp),
                      reads=[prev_name_fn(b, cp)], writes=[ptn], key=ptn)
                P.op("dve", lambda e, po=po, ps=ps, pt=pt: e.tensor_tensor(out=po[:], in0=ps[:, 0:256], in1=pt[:],
                                                                          op=ALU.add),
                     reads=[psn, ptn], writes=[pon])
                dstp = dst[b * 128:(b + 1) * 128, col:col + 256]
                P.dma("sp", lambda e, po=po, dstp=dstp: e.dma_start(out=dstp, in_=po[:]),
                      reads=[pon], writes=[dst_name_fn(b, cp)], key=pon + "_st")

    tok_major_gemm(w_out, 0, KC, hT, "hT", xrow, lambda b, cp: "x_in", x1d, lambda b, cp: "x1d_%d_%d" % (b, cp))

    P.dma("sp", lambda e: e.dma_start(out=gain[:], in_=io["g_ffn"]), writes=["gain"], key="c_gain")
    norm_transpose(cx, "ffn", lambda b: x1d[b * 128:(b + 1) * 128, :],
                   lambda b: ["x1d_%d_%d" % (b, cp) for cp in range(D // 256)], NB, D, gain, "gain", hT, "hT", 0,
                   xts, pss, ident, small)

    TW = 512
    NTH = NTOK // TW
    lastp = 0
    for p in range(NPASS):
        pan0 = p * per
        pan1 = min(npan_ff, pan0 + per)
        if pan0 >= pan1:
            continue
        lastp = p
        for pp in range(pan0, pan1):
            col = pp * 256
            gpan, gname = next_panel()
            gps = {}
            for j in range(2):
                for th in range(NTH):
                    ps, psn = pss[cx.nxt("mm_ps", 8)]
                    gps[(j, th)] = (ps, psn)
                    for k in range(KC):
                        P.op("pe", lambda e, ps=ps, gpan=gpan, k=k, j=j, th=th: e.matmul(
                            ps[:, 0:TW], gpan[:, k, j * 128:(j + 1) * 128], hT[:, k, th * TW:(th + 1) * TW],
                            start=(k == 0), stop=(k == KC - 1)), reads=[gname, "hT"], writes=[psn])
            upan, uname = next_panel()
            for j in range(2):
                for th in range(NTH):
                    ps, psn = pss[cx.nxt("mm_ps", 8)]
                    for k in range(KC):
                        P.op("pe", lambda e, ps=ps, upan=upan, k=k, j=j, th=th: e.matmul(
                            ps[:, 0:TW], upan[:, k, j * 128:(j + 1) * 128], hT[:, k, th * TW:(th + 1) * TW],
                            start=(k == 0), stop=(k == KC - 1)), reads=[uname, "hT"], writes=[psn])
                    gp, gpn = gps[(j, th)]
                    sg, sgn = sgt[cx.nxt("sg", 2)]
                    P.op("act", lambda e, sg=sg, gp=gp: e.activation(out=sg[:, 0:TW], in_=gp[:, 0:TW], func=AF.Silu),
                         reads=[gpn], writes=[sgn])
                    kk = (pp - pan0) * 2 + j
                    P.op("dve", lambda e, sg=sg, ps=ps, kk=kk, th=th: e.tensor_tensor(
                        out=actT[:, kk, th * TW:(th + 1) * TW], in0=sg[:, 0:TW], in1=ps[:, 0:TW], op=ALU.mult),
                        reads=[sgn, psn], writes=["actT"])
        nk = (pan1 - pan0) * 2
        if p == 0:
            prow, pfn = (lambda b: x1d[b * 128:(b + 1) * 128, :]), (lambda b, cp: "x1d_%d_%d" % (b, cp))
        else:
            prow, pfn = (lambda b: yacc[b * 128:(b + 1) * 128, :]), (lambda b, cp, p=p: "yacc%d_%d_%d" % (p - 1, b, cp))
        tok_major_gemm(w_down, pan0 * 2, nk, actT, "actT", prow, pfn, yacc,
                       lambda b, cp, p=p: "yacc%d_%d_%d" % (p, b, cp))

    P.dma("sp", lambda e: e.dma_start(out=gain[:], in_=io["g_fin"]), writes=["gain"], key="c_gain")
    sq, sqn = small["sq"]
    ss = small["ss"][0]
    for b in range(NB):
        xt, xname = xts[0]
        deps = ["yacc%d_%d_%d" % (lastp, b, cp) for cp in range(D // 256)]
        P.dma("sp", lambda e, b=b: e.dma_start(out=xt[:], in_=yacc[b * 128:(b + 1) * 128, :]), reads=deps,
              writes=[xname], key=xname)
        P.op("act", lambda e: e.activation(out=sq[:, 0:D], in_=xt[:], func=AF.Square, accum_out=ss[:, 0:1]),
             reads=[xname], writes=[sqn, "ss"])
        P.op("act", lambda e: e.activation(out=ss[:, 1:2], in_=ss[:, 0:1], func=AF.Sqrt, scale=1.0 / D,
                                           bias=small["eps"][0][:, 0:1]), reads=["ss", "epsc"], writes=["ssb"])
        P.op("dve", lambda e: e.reciprocal(out=ss[:, 2:3], in_=ss[:, 1:2]), reads=["ssb"], writes=["ssc"])
        P.op("dve", lambda e: e.scalar_tensor_tensor(out=xt[:], in0=xt[:], scalar=ss[:, 2:3], in1=gain[:],
                                                     op0=ALU.mult, op1=ALU.mult),
             reads=[xname, "ssc", "gain"], writes=[xname])
        P.dma("sp", lambda e, b=b: e.dma_start(out=out[b * 128:(b + 1) * 128, :], in_=xt[:]), reads=[xname],
              writes=[], key="out_st")


def _tables_spec():
    NM = NM_OWN
    NJ, NCU, NUC = 16 * NM, 64 * NM, 8 * NM
    NCC = NCU // 128
    return {"ak": ((128, 16, NUC), F32), "ac": ((128, 16, NM, NCC), F32), "cm": ((128, NM, NCC, 128), BF16),
            "wm": ((128, NM, 5, 128), BF16), "tri_incl": ((128, 128), BF16), "bonus": ((128, NM, NJ), F32),
            "ex": ((128, NJ), F32), "E": ((NJ, NUC, 128), BF16), "caug": ((128, NCC, 1 + NJ), BF16),
            "ident": ((128, 128), F32), "exf": ((128, 8), F32), "negU": ((128, 128), BF16),
            "tri_strict": ((128, 128), BF16), "negones": ((128, 1), BF16)}


def build_program(phases=("1a", "1b", "2a", "2b", "3")):
    cx = Ctx()
    io = {}
    io["xu"] = cx.din("xu", (SEQ, D_MODEL), F32)
    io["w_in"] = cx.din("w_in", (D_MODEL, D_IN_PAD), F32)
    for n in ("g_attn", "g_ffn", "g_fin"):
        io[n] = cx.din(n, (128, D_MODEL), F32)
    for n in ("g_sb", "g_nsa"):
        io[n] = cx.din(n, (128, D_MODEL // 2), F32)
    for n, (shape, dt) in _tables_spec().items():
        io[n] = cx.din(n, shape, dt)
    for n in ("w_k1", "w_v1"):
        io[n] = cx.din(n, (4096, 256), F32)
    for n in ("w_k2", "w_v2"):
        io[n] = cx.din(n, (256, 128), F32)
    for n in ("posTk", "posTv"):
        io[n] = cx.din(n, (128, 32), F32)
    io["w_out"] = cx.din("w_out", (D_MODEL, D_MODEL), F32)
    io["w_gate"] = cx.din("w_gate", (D_MODEL, D_FF), F32)
    io["w_up"] = cx.din("w_up", (D_MODEL, D_FF), F32)
    io["w_down"] = cx.din("w_down", (D_FF, D_MODEL), F32)
    io["out"] = cx.dout("out", (NT_OWN, D_MODEL), F32)
    io["qT"] = cx.dint("qT_scr", (QROWS, NT_OWN), BF16)
    io["gn"] = cx.dint("gn_scr", (NT_OWN, 128), F32)
    io["kT"] = cx.dint("kT_scr", (KROWS, SEQ), BF16)
    io["vtok"] = cx.dint("vtok_scr", (SEQ, VCOLS), BF16)
    io["wbf"] = cx.dint("wbf_scr", (len(K_PANELS) + len(V_PANELS), 128, KC, 256), BF16)
    io["osb"] = cx.dint("osb_scr", (NT_OWN, 2048), F32)
    io["onsa"] = cx.dint("onsa_scr", (NT_OWN, 2048), F32)
    io["x1d"] = cx.dint("x1d_scr", (NT_OWN, D_MODEL), F32)
    io["yacc"] = cx.dint("yacc_scr", (NT_OWN, D_MODEL), F32)
    fns = {"1a": phase_1a, "1b": phase_1b, "2a": phase_2a, "2b": phase_2b, "3": phase_3}
    first = True
    for ph in phases:
        if not first:
            cx.begin()
        first = False
        fns[ph](cx, io)
        cx.end()
    return cx.finish()


def _bc(g, n=128):
    g = np.asarray(g, np.float32).reshape(-1)
    return np.ascontiguousarray(np.broadcast_to(g, (n, g.shape[0])))


def _own_rows(c):
    return np.concatenate([np.arange(128) + 128 * (c + 8 * m) for m in range(NM_OWN)])


def kernel(x, attn_norm, w_in, pos_cmp_k, pos_cmp_v, w_cmp_k1, w_cmp_k2, w_cmp_v1, w_cmp_v2,
           norm_sb, norm_nsa, w_out, ffn_norm, w_gate, w_up, w_down, final_norm):
    f32 = lambda a: np.ascontiguousarray(np.asarray(a, np.float32))
    x2 = f32(x)[0]
    cores = list(range(NCORES))
    w_pad = np.zeros((D_MODEL, D_IN_PAD), np.float32)
    w_pad[:, :D_IN] = f32(w_in)[0]
    common = {"w_in": w_pad, "g_attn": _bc(attn_norm), "g_ffn": _bc(ffn_norm), "g_fin": _bc(final_norm),
              "g_sb": _bc(norm_sb), "g_nsa": _bc(norm_nsa),
              "w_k1": f32(w_cmp_k1)[0], "w_k2": f32(w_cmp_k2)[0], "w_v1": f32(w_cmp_v1)[0], "w_v2": f32(w_cmp_v2)[0],
              "posTk": np.ascontiguousarray(f32(pos_cmp_k)[0].T), "posTv": np.ascontiguousarray(f32(pos_cmp_v)[0].T),
              "w_out": f32(w_out)[0], "w_gate": f32(w_gate)[0], "w_up": f32(w_up)[0], "w_down": f32(w_down)[0]}
    common.update(sb_consts())
    in_maps = []
    for c in cores:
        d = dict(common)
        shift = 128 * (7 - c)
        xu = np.zeros((SEQ, D_MODEL), np.float32)
        xu[shift:] = x2[:SEQ - shift]
        d["xu"] = xu
        d.update(nsa_tables(c, NM_OWN))
        in_maps.append(d)
    nc = build_program()
    res = run_bass_kernel_spmd(nc, in_maps, core_ids=cores).results
    out = np.zeros((1, SEQ, D_MODEL), np.float32)
    for c in cores:
        out[0, _own_rows(c)] = np.asarray(res[c]["out"])
    return out
```

```python
import contextlib
import numpy as np
import ml_dtypes
import concourse.bass as bass
import concourse.mybir as mybir
from concourse.bass_utils import run_bass_kernel_spmd

F32 = mybir.dt.float32
BF16 = mybir.dt.bfloat16
AF = mybir.ActivationFunctionType
ALU = mybir.AluOpType
NPBF = ml_dtypes.bfloat16

NCORES = 8
EPS = 1e-6
ALL_ENG = ("pe", "act", "dve", "pool", "sp")


class _Buf:
    __slots__ = ("writer", "readers")

    def __init__(self):
        self.writer = None
        self.readers = []


class _Op:
    __slots__ = ("eng", "fn", "is_dma", "dkey", "dval", "deps", "signal", "count")


class Prog:
    def __init__(self, nc, same_engine_sync=True):
        self.nc = nc
        self.ops = {e: [] for e in ALL_ENG}
        self.bufs = {}
        self.dma_counts = {}
        self.phase_keys = set()
        self.same_engine_sync = same_engine_sync
        self.ecount = {e: 0 for e in ALL_ENG}
        self.semstack = contextlib.ExitStack()
        self.esem = None
        self.dsem = {}
        self.barrier = []

    def _add(self, eng, fn, reads, writes, is_dma=False, dkey=None):
        op = _Op()
        op.eng, op.fn, op.is_dma, op.dkey = eng, fn, is_dma, dkey
        op.dval, op.signal, op.count = None, False, None
        deps = []
        reads, writes = _expand(reads), _expand(writes)
        for r in reads:
            b = self.bufs.get(r)
            if b is None:
                b = self.bufs[r] = _Buf()
            if b.writer is not None:
                deps.append(b.writer)
            b.readers.append(op)
        for w in writes:
            b = self.bufs.get(w)
            if b is None:
                b = self.bufs[w] = _Buf()
            if b.writer is not None:
                deps.append(b.writer)
            deps.extend(b.readers)
            b.writer = op
            b.readers = []
        out, seen = [], set()
        for d in deps:
            if d is op or id(d) in seen:
                continue
            seen.add(id(d))
            if not d.is_dma and d.eng == eng and (eng == "pe" or not self.same_engine_sync):
                continue
            out.append(d)
        op.deps = out
        if is_dma:
            c = self.dma_counts.get(dkey, 0) + 16
            self.dma_counts[dkey] = c
            self.phase_keys.add(dkey)
            op.dval = c
        self.ops[eng].append(op)
        return op

    def op(self, eng, fn, reads=(), writes=()):
        return self._add(eng, fn, reads, writes)

    def dma(self, eng, fn, reads=(), writes=(), key=None):
        return self._add(eng, fn, reads, writes, is_dma=True, dkey=key)

    def flush(self, final_wait_eng="sp"):
        nc = self.nc
        if self.esem is None:
            self.esem = {e: self.semstack.enter_context(nc.semaphore("s_" + e)) for e in ALL_ENG}
        for k in self.dma_counts:
            if k not in self.dsem:
                self.dsem[k] = self.semstack.enter_context(nc.semaphore("d_%d" % len(self.dsem)))
        esem, dsem = self.esem, self.dsem
        for e in ALL_ENG:
            ops = self.ops[e]
            for op in ops:
                for d in op.deps:
                    if not d.is_dma:
                        d.signal = True
            for op in reversed(ops):
                if not op.is_dma:
                    op.signal = True
                    break
        for e in ALL_ENG:
            c = self.ecount[e]
            for op in self.ops[e]:
                if op.signal and not op.is_dma:
                    c += 1
                    op.count = c
            self.ecount[e] = c
        barrier = self.barrier
        with nc.Block() as block:
            engmap = {"pe": block.tensor, "act": block.scalar, "dve": block.vector,
                      "pool": block.gpsimd, "sp": block.sync}

            def make(e):
                def body(eng):
                    known = {}
                    if self.ops[e]:
                        for key, sem, val in barrier:
                            if key == ("e", e):
                                known[key] = val
                                continue
                            eng.wait_ge(sem, val)
                            known[key] = val
                    for op in self.ops[e]:
                        for d in op.deps:
                            if d.is_dma:
                                key, val, sem = ("d", d.dkey), d.dval, dsem[d.dkey]
                            else:
                                key, val, sem = ("e", d.eng), d.count, esem[d.eng]
                            if known.get(key, 0) >= val:
                                continue
                            eng.wait_ge(sem, val)
                            known[key] = val
                        inst = op.fn(eng)
                        if op.is_dma:
                            inst.then_inc(dsem[op.dkey], 16)
                        elif op.signal:
                            inst.then_inc(esem[e], 1)
                    if e == final_wait_eng:
                        for k in self.phase_keys:
                            v = self.dma_counts[k]
                            if known.get(("d", k), 0) < v:
                                eng.wait_ge(dsem[k], v)
                return body

            for e in ALL_ENG:
                engmap[e](make(e))
        self.barrier = [(("e", e), esem[e], self.ecount[e]) for e in ALL_ENG if self.ecount[e] > 0]
        self.barrier += [(("d", k), dsem[k], self.dma_counts[k]) for k in self.dma_counts]
        self.ops = {e: [] for e in ALL_ENG}
        self.bufs = {}
        self.phase_keys = set()

    def close(self):
        self.semstack.close()


class Ctx:
    def __init__(self):
        self.nc = bass.Bass("TRN2", target_bir_lowering=False)
        self.P = Prog(self.nc)
        self.st = None
        self.rot = {}
        self.phase = -1
        self.begin()

    def begin(self):
        self.st = contextlib.ExitStack()
        self.rot = {}
        self.phase += 1

    def end(self):
        self.P.flush()
        self.st.close()
        self.st = None

    def sb(self, name, shape, dt):
        return self.st.enter_context(self.nc.sbuf_tensor("p%d_%s" % (self.phase, name), list(shape), dt))

    def ps(self, name, shape=(128, 512), dt=F32):
        return self.st.enter_context(self.nc.psum_tensor("p%d_%s" % (self.phase, name), list(shape), dt))

    def din(self, name, shape, dt):
        return self.nc.dram_tensor(name, list(shape), dt, kind="ExternalInput").ap()

    def dout(self, name, shape, dt):
        return self.nc.dram_tensor(name, list(shape), dt, kind="ExternalOutput").ap()

    def dint(self, name, shape, dt):
        return self.nc.dram_tensor(name, list(shape), dt, kind="Internal").ap()

    def nxt(self, key, n):
        i = self.rot.get(key, 0)
        self.rot[key] = i + 1
        return i % n

    def finish(self):
        if self.st is not None:
            self.end()
        self.P.close()
        return self.nc


class PanelName(str):
    pass


def _expand(names):
    out = []
    for n in names:
        if isinstance(n, PanelName):
            out += [str(n), str(n) + "_a", str(n) + "_b"]
        else:
            out.append(n)
    return out


class PanelLoad:
    def __init__(self, ws, W, kc0, nk, col0, ncols):
        cx = ws.cx
        self.ws, self.W, self.kc0, self.col0, self.ncols = ws, W, kc0, col0, ncols
        pi = cx.nxt(ws.name + "_p", ws.npanel)
        self.pan = ws.panels[pi]
        self.pname = "%s_pan%d" % (ws.name, pi)
        self.pieces = []
        k = 0
        while k < nk:
            kp = min(ws.kpiece, nk - k)
            self.pieces.append((k, kp))
            k += kp
        self.i = 0
        self.result = (self.pan, PanelName(self.pname))

    @property
    def done(self):
        return self.i >= len(self.pieces)

    def step(self):
        ws, cx, P = self.ws, self.ws.cx, self.ws.cx.P
        k, kp = self.pieces[self.i]
        self.i += 1
        si = cx.nxt(ws.name + "_s", ws.nstage)
        stg = ws.stages[si]
        sname = "%s_stg%d" % (ws.name, si)
        W, kc0, col0, ncols, pan, pname = self.W, self.kc0, self.col0, self.ncols, self.pan, self.pname
        src = W[(kc0 + k) * 128:(kc0 + k + kp) * 128, col0:col0 + ncols].rearrange("(k p) n -> p k n", p=128)
        dst = stg[:, 0:kp, 0:ncols]
        P.dma("sp", lambda e: e.dma_start(out=dst, in_=src), reads=[], writes=[sname], key=sname)
        pdst = pan[:, k:k + kp, 0:ncols]
        ceng, cname = ("pool", pname + "_a") if cx.nxt(ws.name + "_ce", 2) == 0 else ("dve", pname + "_b")
        P.op(ceng, lambda e: e.tensor_copy(out=pdst, in_=dst), reads=[sname], writes=[cname])


class WStream:
    def __init__(self, cx, name, max_k, ncols, kpiece=4, npanel=2, nstage=3):
        self.cx, self.name = cx, name
        self.max_k, self.ncols, self.kpiece = max_k, ncols, kpiece
        self.panels = [cx.sb("%s_pan%d" % (name, i), (128, max_k, ncols), BF16) for i in range(npanel)]
        self.stages = [cx.sb("%s_stg%d" % (name, i), (128, kpiece, ncols), F32) for i in range(nstage)]
        self.npanel, self.nstage = npanel, nstage

    def begin(self, W, kc0, nk, col0, ncols):
        return PanelLoad(self, W, kc0, nk, col0, ncols)

    def load(self, W, kc0, nk, col0, ncols):
        pl = self.begin(W, kc0, nk, col0, ncols)
        while not pl.done:
            pl.step()
        return pl.result

    def load_bf16(self, src, nk, ncols, dep):
        cx, P = self.cx, self.cx.P
        pi = cx.nxt(self.name + "_p", self.npanel)
        pan = self.panels[pi]
        pname = "%s_pan%d" % (self.name, pi)
        P.dma("sp", lambda e: e.dma_start(out=pan[:, 0:nk, 0:ncols], in_=src), reads=[dep], writes=[PanelName(pname)],
              key=pname + "_ld")
        return pan, PanelName(pname)


class Prefetch:
    def __init__(self, thunks, ahead=2):
        self.thunks, self.ahead, self.res = list(thunks), ahead, []

    def get(self, i):
        while len(self.res) < min(i + 1 + self.ahead, len(self.thunks)):
            self.res.append(self.thunks[len(self.res)]())
        o = self.res[i]
        if isinstance(o, PanelLoad):
            while not o.done:
                o.step()
            return o.result
        return o

    def tick(self):
        for o in self.res:
            if isinstance(o, PanelLoad) and not o.done:
                o.step()
                return


def norm_transpose(cx, tag, src_blk, deps_blk, ntok_blocks, nfeat, gain_bc, gain_name, dstT, dst_name, dst_chunk0,
                   xt_bufs, ps_bufs, ident, small):
    P = cx.P
    nch = nfeat // 128
    sq, sqn = small["sq"]
    ss, ssn = small["ss"]
    for b in range(ntok_blocks):
        xi = cx.nxt(tag + "_xt", len(xt_bufs))
        xt, xname = xt_bufs[xi]
        srcb = src_blk(b)
        P.dma("sp", lambda e, xt=xt, srcb=srcb: e.dma_start(out=xt[:, 0:nfeat], in_=srcb), reads=deps_blk(b),
              writes=[xname], key=xname)
        P.op("act", lambda e, xt=xt: e.activation(out=sq[:, 0:nfeat], in_=xt[:, 0:nfeat], func=AF.Square,
                                                  accum_out=ss[:, 0:1]),
             reads=[xname], writes=[sqn, ssn])
        P.op("act", lambda e: e.activation(out=ss[:, 1:2], in_=ss[:, 0:1], func=AF.Sqrt, scale=1.0 / nfeat,
                                           bias=small["eps"][0][:, 0:1]),
             reads=[ssn, small["eps"][1]], writes=[ssn + "b"])
        P.op("dve", lambda e: e.reciprocal(out=ss[:, 2:3], in_=ss[:, 1:2]), reads=[ssn + "b"], writes=[ssn + "c"])
        P.op("dve", lambda e, xt=xt: e.scalar_tensor_tensor(out=xt[:, 0:nfeat], in0=xt[:, 0:nfeat], scalar=ss[:, 2:3],
                                                             in1=gain_bc[:, 0:nfeat], op0=ALU.mult, op1=ALU.mult),
             reads=[xname, ssn + "c", gain_name], writes=[xname])
        for c0 in range(0, nch, 4):
            nc4 = min(4, nch - c0)
            pi = cx.nxt("nt_ps", len(ps_bufs))
            ps, psn = ps_bufs[pi]
            for j in range(nc4):
                P.op("pe", lambda e, ps=ps, xt=xt, j=j, c0=c0: e.transpose(ps[:, j * 128:(j + 1) * 128],
                                                                          xt[:, (c0 + j) * 128:(c0 + j + 1) * 128],
                                                                          ident[:]),
                     reads=[xname, "ident"], writes=[psn])
            dst = dstT[:, dst_chunk0 + c0:dst_chunk0 + c0 + nc4, b * 128:(b + 1) * 128]
            src_ps = ps[:, 0:nc4 * 128].rearrange("p (c t) -> p c t", c=nc4)
            P.op("act", lambda e, dst=dst, src_ps=src_ps: e.activation(out=dst, in_=src_ps, func=AF.Copy),
                 reads=[psn], writes=[dst_name])


D_MODEL, SEQ, D_FF = 4096, 8192, 11008
D_IN, D_IN_PAD = 9776, 9856
NM_OWN = SEQ // 128 // NCORES
NT_OWN = NM_OWN * 128
QSCALE = 128 ** -0.5
KC = D_MODEL // 128
C_QSB, C_KSB, C_VSB, C_QN = 0, 2048, 4096, 6144
C_KC, C_VC, C_KSL, C_VSL, C_KW, C_VW, C_G = 8192, 8448, 8704, 8960, 9216, 9472, 9728
KROW_SB, KROW_KC, KROW_VC, KROW_KSL, KROW_KW, KROWS = 0, 2048, 2304, 2560, 2816, 3072
VCOL_SB, VCOL_VSL, VCOL_VW, VCOLS = 0, 2048, 2304, 2560
QROW_SB, QROW_N, QROWS = 0, 2048, 4096


def own_chunk(m):
    return 8 * m + 7


def small_tiles(cx, sq_ap, sq_name):
    return {"sq": (sq_ap, sq_name), "ss": (cx.sb("ss", (128, 4), F32), "ss"),
            "eps": (cx.sb("epsc", (128, 1), F32), "epsc")}


def phase_1a(cx, io):
    P = cx.P
    xu, w, qT, gnd = io["xu"], io["w_in"], io["qT"], io["gn"]
    ident = cx.sb("ident_sb", (128, 128), F32)
    gain = cx.sb("gain_sb", (128, D_MODEL), F32)
    hT = cx.sb("hT", (128, KC, NT_OWN), BF16)
    xts = [(cx.sb("xt%d" % i, (128, D_MODEL), F32), "xt%d" % i) for i in range(2)]
    sqt = cx.sb("sq", (128, D_MODEL), BF16)
    small = small_tiles(cx, sqt, "sq")
    pss = [(cx.ps("ps%d" % i), "ps%d" % i) for i in range(8)]
    ws = WStream(cx, "w", KC, 256)
    ostg = [(cx.sb("ostg%d" % i, (128, NT_OWN), BF16), "ostg%d" % i) for i in range(2)]
    gstg = [(cx.sb("gstg%d" % i, (128, 128), F32), "gstg%d" % i) for i in range(2)]

    P.dma("sp", lambda e: e.dma_start(out=ident[:], in_=io["ident"]), writes=["ident"], key="c_ident")
    P.dma("sp", lambda e: e.dma_start(out=gain[:], in_=io["g_attn"]), writes=["gain"], key="c_gain")
    P.op("dve", lambda e: e.memset(small["eps"][0][:], EPS), writes=["epsc"])
    norm_transpose(cx, "l1", lambda b: xu[own_chunk(b) * 128:(own_chunk(b) + 1) * 128, :], lambda b: [],
                   NM_OWN, D_MODEL, gain, "gain", hT, "hT", 0, xts, pss, ident, small)

    qpanels = [(c0 + pcol, r0 + pcol) for (c0, r0) in ((C_QSB, QROW_SB), (C_QN, QROW_N)) for pcol in range(0, 2048, 256)]
    pf = Prefetch([(lambda c=c: ws.load(w, 0, KC, c, 256)) for (c, _) in qpanels] +
                  [lambda: ws.load(w, 0, KC, C_G, 128)], ahead=1)
    for qi, (cabs, rabs) in enumerate(qpanels):
        pan, pname = pf.get(qi)
        for j0 in (0, 128):
            og, ogn = ostg[cx.nxt("ostg", 2)]
            for th in range(NT_OWN // 512):
                ps, psn = pss[cx.nxt("mm_ps", 8)]
                for k in range(KC):
                    P.op("pe", lambda e, ps=ps, pan=pan, k=k, j0=j0, th=th: e.matmul(
                        ps[:, 0:512], pan[:, k, j0:j0 + 128], hT[:, k, th * 512:(th + 1) * 512],
                        start=(k == 0), stop=(k == KC - 1)), reads=[pname, "hT"], writes=[psn])
                P.op("act", lambda e, og=og, ps=ps, th=th: e.activation(
                    out=og[:, th * 512:(th + 1) * 512], in_=ps[:, 0:512], func=AF.Copy, scale=QSCALE),
                    reads=[psn], writes=[ogn])
            row = rabs + j0
            P.dma("sp", lambda e, og=og, row=row: e.dma_start(out=qT[row:row + 128, :], in_=og[:]),
                  reads=[ogn], writes=[], key=ogn + "_st")
    pan, pname = pf.get(len(qpanels))
    for b in range(NM_OWN):
        ps, psn = pss[cx.nxt("mm_ps", 8)]
        for k in range(KC):
            P.op("pe", lambda e, ps=ps, k=k, b=b: e.matmul(ps[:, 0:128], hT[:, k, b * 128:(b + 1) * 128],
                                                           pan[:, k, 0:128], start=(k == 0), stop=(k == KC - 1)),
                 reads=[pname, "hT"], writes=[psn])
        gs, gsn = gstg[cx.nxt("gstg", 2)]
        P.op("act", lambda e, gs=gs, ps=ps: e.activation(out=gs[:], in_=ps[:, 0:128], func=AF.Copy), reads=[psn],
             writes=[gsn])
        P.dma("sp", lambda e, gs=gs, b=b: e.dma_start(out=gnd[b * 128:(b + 1) * 128, :], in_=gs[:]), reads=[gsn],
              writes=[], key=gsn + "_st")


K_PANELS = [(C_KSB + p, KROW_SB + p) for p in range(0, 2048, 256)] + \
           [(C_KC, KROW_KC), (C_VC, KROW_VC), (C_KSL, KROW_KSL), (C_KW, KROW_KW)]
V_PANELS = [(C_VSB + p, VCOL_SB + p) for p in range(0, 2048, 256)] + [(C_VSL, VCOL_VSL), (C_VW, VCOL_VW)]


def phase_1b(cx, io):
    P = cx.P
    xu, w, kT, vtok, wbf = io["xu"], io["w_in"], io["kT"], io["vtok"], io["wbf"]
    NTILE = SEQ // 1024
    ident = cx.sb("ident_sb", (128, 128), F32)
    gain = cx.sb("gain_sb", (128, D_MODEL), F32)
    hT = cx.sb("hT", (128, KC, 1024), BF16)
    xts = [(cx.sb("xt%d" % i, (128, D_MODEL), F32), "xt%d" % i) for i in range(2)]
    sqt = cx.sb("sq", (128, D_MODEL), BF16)
    small = small_tiles(cx, sqt, "sq")
    pss = [(cx.ps("ps%d" % i), "ps%d" % i) for i in range(8)]
    ws = WStream(cx, "w", KC, 256, npanel=3)
    ostg = [(cx.sb("ostg%d" % i, (128, 1024), BF16), "ostg%d" % i) for i in range(2)]
    vstg = [(cx.sb("vstg%d" % i, (128, 8, 256), BF16), "vstg%d" % i) for i in range(2)]

    P.dma("sp", lambda e: e.dma_start(out=ident[:], in_=io["ident"]), writes=["ident"], key="c_ident")
    P.dma("sp", lambda e: e.dma_start(out=gain[:], in_=io["g_attn"]), writes=["gain"], key="c_gain")
    P.op("dve", lambda e: e.memset(small["eps"][0][:], EPS), writes=["epsc"])
    panels = [("k",) + p for p in K_PANELS] + [("v",) + p for p in V_PANELS]

    def first_load(pi_, c0):
        pan, pname = ws.load(w, 0, KC, c0, 256)
        P.dma("sp", lambda e: e.dma_start(out=wbf[pi_], in_=pan[:, 0:KC, 0:256]), reads=[pname], writes=["wbf"],
              key="wbf_st")
        return pan, pname

    thunks = []
    for t in range(NTILE):
        for pi_, (kind, c0, r0) in enumerate(panels):
            if t == 0:
                thunks.append(lambda pi_=pi_, c0=c0: first_load(pi_, c0))
            else:
                thunks.append(lambda pi_=pi_: ws.load_bf16(wbf[pi_], KC, 256, "wbf"))
    pf = Prefetch(thunks, ahead=2)
    for t in range(NTILE):
        norm_transpose(cx, "l1", lambda b, t=t: xu[t * 1024 + b * 128:t * 1024 + (b + 1) * 128, :], lambda b: [],
                       8, D_MODEL, gain, "gain", hT, "hT", 0, xts, pss, ident, small)
        for pi_, (kind, c0, r0) in enumerate(panels):
            pan, pname = pf.get(t * len(panels) + pi_)
            if kind == "k":
                for j0 in (0, 128):
                    og, ogn = ostg[cx.nxt("ostg", 2)]
                    for th in range(2):
                        ps, psn = pss[cx.nxt("mm_ps", 8)]
                        for k in range(KC):
                            P.op("pe", lambda e, ps=ps, pan=pan, k=k, j0=j0, th=th: e.matmul(
                                ps[:, 0:512], pan[:, k, j0:j0 + 128], hT[:, k, th * 512:(th + 1) * 512],
                                start=(k == 0), stop=(k == KC - 1)), reads=[pname, "hT"], writes=[psn])
                        P.op("act", lambda e, og=og, ps=ps, th=th: e.activation(
                            out=og[:, th * 512:(th + 1) * 512], in_=ps[:, 0:512], func=AF.Copy),
                            reads=[psn], writes=[ogn])
                    row = r0 + j0
                    P.dma("sp", lambda e, og=og, row=row, t=t: e.dma_start(
                        out=kT[row:row + 128, t * 1024:(t + 1) * 1024], in_=og[:]), reads=[ogn], writes=[],
                        key=ogn + "_st")
            else:
                vs, vsn = vstg[cx.nxt("vstg", 2)]
                for b in range(8):
                    ps, psn = pss[cx.nxt("mm_ps", 8)]
                    for k in range(KC):
                        P.op("pe", lambda e, ps=ps, pan=pan, k=k, b=b: e.matmul(
                            ps[:, 0:256], hT[:, k, b * 128:(b + 1) * 128], pan[:, k, 0:256],
                            start=(k == 0), stop=(k == KC - 1)), reads=[pname, "hT"], writes=[psn])
                    P.op("act", lambda e, vs=vs, ps=ps, b=b: e.activation(out=vs[:, b, :], in_=ps[:, 0:256],
                                                                          func=AF.Copy), reads=[psn], writes=[vsn])
                dst = vtok[t * 1024:(t + 1) * 1024, r0:r0 + 256].rearrange("(b p) c -> p b c", p=128)
                P.dma("sp", lambda e, vs=vs, dst=dst: e.dma_start(out=dst, in_=vs[:]), reads=[vsn], writes=[],
                      key=vsn + "_st")


def sb_consts():
    j = np.arange(128)
    negU = np.where(j[:, None] >= j[None, :], -1.0, 0.0).astype(NPBF)
    tri = (j[:, None] < j[None, :]).astype(np.float32).astype(NPBF)
    negones = np.full((128, 1), -1.0, dtype=np.float32).astype(NPBF)
    return {"negU": negU, "tri_strict": tri, "negones": negones}


def nsa_slopes():
    h = np.arange(1, 17, dtype=np.float32)
    return (2.0 ** (-8.0 * h / 16)).astype(np.float32)


def nsa_tables(c, NM):
    NJ, NCU, NUC = 16 * NM, 64 * NM, 8 * NM
    NCC = max(1, NCU // 128)
    sl = nsa_slopes()
    i = np.arange(128)
    T = {}
    rel = np.arange(NUC)
    T["ak"] = (sl[None, :, None] * (i[:, None, None] - 64.0 - 128.0 * rel[None, None, :])).astype(np.float32)
    ac = np.zeros((128, 16, NM, NCC), np.float32)
    cm = np.zeros((128, NM, NCC, 128), np.float32)
    wm = np.zeros((128, NM, 5, 128), np.float32)
    bonus = np.zeros((128, NM, NJ), np.float32)
    tl = np.arange(128)
    for m in range(NM):
        u0 = 128 * (8 * m + 7)
        ut = u0 + tl
        for ncc in range(NCC):
            nu = 128 * ncc + i
            cend = 16 * nu + 31
            ac[:, :, m, ncc] = sl[None, :] * (cend[:, None] - (u0 + 64.0))
            cm[:, m, ncc, :] = ((cend[:, None] <= ut[None, :]) & (nu[:, None] >= 8 * (7 - c))).astype(np.float32)
        for r in range(5):
            uk = 128 * (8 * m + 3 + r) + i
            d = ut[None, :] - uk[:, None]
            wm[:, m, r, :] = ((d >= 0) & (d < 512) & (uk[:, None] >= 128 * (7 - c))).astype(np.float32)
        cur = ut // 64
        jj = np.arange(NJ)
        valid = (jj[None, :] <= cur[:, None]) & (jj[None, :] >= 2 * (7 - c))
        forced = (jj[None, :] == 2 * (7 - c)) | (jj[None, :] == cur[:, None]) | (jj[None, :] == cur[:, None] - 1)
        bonus[:, m, :] = np.where(valid, np.where(forced, 1e6, 0.0), -1e30)
    T["ac"] = np.minimum(ac, 45.0)
    T["cm"] = cm.astype(NPBF)
    T["wm"] = wm.astype(NPBF)
    T["bonus"] = bonus
    T["ex"] = np.broadcast_to((np.arange(NJ) >= 2 * (7 - c)).astype(np.float32), (128, NJ)).copy()
    T["tri_incl"] = (i[:, None] <= i[None, :]).astype(np.float32).astype(NPBF)
    T["exf"] = np.broadcast_to((np.arange(8) >= (7 - c)).astype(np.float32), (128, 8)).copy()
    E = np.zeros((NJ, NUC, 128), np.float32)
    for uc in range(NUC):
        for half in range(2):
            if 2 * uc + half < NJ:
                E[2 * uc + half, uc, half * 64:(half + 1) * 64] = 1.0
    T["E"] = E.astype(NPBF)
    caug = np.zeros((128, NCC, 1 + NJ), np.float32)
    caug[:, :, 0] = 1.0
    for ncc in range(NCC):
        for il in range(128):
            nu = 128 * ncc + il
            for j in range(NJ):
                for mm in range(4):
                    for nn in range(2):
                        if 4 * j - mm - nn == nu:
                            caug[il, ncc, 1 + j] += 1.0
    T["caug"] = caug.astype(NPBF)
    T["ident"] = np.eye(128, dtype=np.float32)
    return T


def _bcast_mid(ap2d, n):
    a = ap2d.ap
    return bass.AP(ap2d.tensor, ap2d.offset, [list(a[0]), [0, n], list(a[-1])])


def phase_2a(cx, io):
    P = cx.P
    S = SEQ
    NCH = S // 128
    kTd, vtokd, qTd, out = io["kT"], io["vtok"], io["qT"], io["osb"]
    negU = cx.sb("negU_sb", (128, 128), BF16)
    tri = cx.sb("tri_sb", (128, 128), BF16)
    negones = cx.sb("negones_sb", (128, 1), BF16)
    exf = cx.sb("exf_sb", (128, 8), F32)
    P.dma("sp", lambda e: e.dma_start(out=negU[:], in_=io["negU"]), writes=["negU"], key="ld_c0")
    P.dma("sp", lambda e: e.dma_start(out=tri[:], in_=io["tri_strict"]), writes=["tri"], key="ld_c1")
    P.dma("sp", lambda e: e.dma_start(out=negones[:], in_=io["negones"]), writes=["negones"], key="ld_c2")
    P.dma("sp", lambda e: e.dma_start(out=exf[:], in_=io["exf"]), writes=["exf"], key="ld_c3")
    HPC = 2
    sets = []
    for s_ in range(2):
        sets.append({
            "k": (cx.sb("kTb%d" % s_, (128, HPC, S), BF16), "kTb%d" % s_),
            "v": (cx.sb("vb%d" % s_, (128, HPC, NCH, 128), BF16), "vb%d" % s_),
            "q": (cx.sb("qb%d" % s_, (128, HPC, NT_OWN), BF16), "qb%d" % s_)})
    psZ = [[(cx.ps("psZ%d_%d" % (i, j)), "psZ%d_%d" % (i, j)) for j in range(2)] for i in range(HPC)]
    psC = [(cx.ps("psC%d" % i, (128, 8)), "psC%d" % i) for i in range(HPC)]
    psO = [(cx.ps("psO%d" % i), "psO%d" % i) for i in range(HPC)]
    esb = [[(cx.sb("esb%d_%d" % (i, j), (128, 512), F32), "esb%d_%d" % (i, j)) for j in range(2)] for i in range(HPC)]
    spb = [[(cx.sb("spb%d_%d" % (i, j), (128, 512), BF16), "spb%d_%d" % (i, j)) for j in range(2)] for i in range(HPC)]
    Ab = [[(cx.sb("Ab%d_%d" % (i, j), (128, 512), BF16), "Ab%d_%d" % (i, j)) for j in range(2)] for i in range(HPC)]
    acc = [(cx.sb("acc%d" % i, (128, 4, 128), F32), "acc%d" % i) for i in range(HPC * 2)]
    car = [(cx.sb("car%d" % i, (128, 4), F32), "car%d" % i) for i in range(HPC)]
    ecb = [(cx.sb("ec%d" % i, (128, 4), F32), "ec%d" % i) for i in range(HPC)]

    def stage_a(it):
        hh, uc, c0, q0, par = it["hh"], it["uc"], it["c0"], it["q0"], it["par"]
        (kT, kTn), (qT, qn) = it["k"], it["q"]
        pz, pzn = psZ[hh][par]
        es, esn = esb[hh][par]
        sp, spn = spb[hh][par]
        P.op("pe", lambda e: e.matmul(pz[:, c0:512], kT[:, hh, uc * 128:(uc + 1) * 128], qT[:, hh, q0 + c0:q0 + 512],
                                      start=True, stop=False), reads=[kTn, qn], writes=[pzn])
        P.op("act", lambda e: e.activation(out=es[:, c0:512], in_=pz[:, c0:512], func=AF.Exp), reads=[pzn],
             writes=[esn])
        P.op("act", lambda e: e.activation(out=sp[:, c0:512], in_=es[:, c0:512], func=AF.Ln, bias=1.0, scale=1.0),
             reads=[esn], writes=[spn])
        if it["diag"]:
            P.op("dve", lambda e: e.tensor_tensor(out=sp[:, c0:c0 + 128], in0=sp[:, c0:c0 + 128], in1=tri[:],
                                                  op=ALU.mult), reads=[spn, "tri"], writes=[spn])
        if uc < 7:
            P.op("dve", lambda e: e.tensor_scalar(out=sp[:, c0:512], in0=sp[:, c0:512], scalar1=exf[:, uc:uc + 1],
                                                  scalar2=None, op0=ALU.mult), reads=[spn, "exf"], writes=[spn])

    def stage_b(it):
        hh, c0, b0, par = it["hh"], it["c0"], it["b0"], it["par"]
        pz, pzn = psZ[hh][par]
        pc, pcn = psC[hh]
        sp, spn = spb[hh][par]
        A, An = Ab[hh][par]
        P.op("pe", lambda e: e.matmul(pz[:, c0:512], negU[:], sp[:, c0:512], start=False, stop=True),
             reads=[spn, "negU"], writes=[pzn])
        for b in range(b0, 4):
            P.op("pe", lambda e, b=b: e.matmul(pc[:, b:b + 1], sp[:, b * 128:(b + 1) * 128], negones[:], start=True,
                                               stop=True), reads=[spn, "negones"], writes=[pcn])
        P.op("act", lambda e: e.activation(out=A[:, c0:512], in_=pz[:, c0:512], func=AF.Exp), reads=[pzn],
             writes=[An])
        if it["diag"]:
            P.op("dve", lambda e: e.tensor_tensor(out=A[:, c0:c0 + 128], in0=A[:, c0:c0 + 128], in1=tri[:],
                                                  op=ALU.mult), reads=[An, "tri"], writes=[An])

    def stage_c(it):
        hh, uc, b0, par, uq = it["hh"], it["uc"], it["b0"], it["par"], it["uq"]
        (v, vn) = it["v"]
        pc, pcn = psC[hh]
        po, pon = psO[hh]
        A, An = Ab[hh][par]
        ac, acn = it["acc"]
        cr, crn = car[hh]
        ec, ecn = ecb[hh]
        for b in range(b0, 4):
            P.op("pe", lambda e, b=b: e.matmul(po[:, b * 128:(b + 1) * 128], A[:, b * 128:(b + 1) * 128],
                                               v[:, hh, uc, :], start=True, stop=True), reads=[An, vn], writes=[pon])
        for b in range(b0, 4):
            if uc == uq[b]:
                P.op("dve", lambda e, b=b: e.tensor_copy(out=ac[:, b, :], in_=po[:, b * 128:(b + 1) * 128]),
                     reads=[pon], writes=[acn])
            else:
                P.op("dve", lambda e, b=b: e.scalar_tensor_tensor(
                    out=ac[:, b, :], in0=po[:, b * 128:(b + 1) * 128], scalar=ec[:, b:b + 1], in1=ac[:, b, :],
                    op0=ALU.mult, op1=ALU.add), reads=[pon, ecn, acn], writes=[acn])
        if uc > 0:
            if it["diag"]:
                P.op("dve", lambda e: e.tensor_copy(out=cr[:, b0:b0 + 1], in_=pc[:, b0:b0 + 1]), reads=[pcn],
                     writes=[crn])
                if b0 + 1 < 4:
                    P.op("dve", lambda e: e.tensor_tensor(out=cr[:, b0 + 1:4], in0=cr[:, b0 + 1:4],
                                                          in1=pc[:, b0 + 1:4], op=ALU.add), reads=[pcn, crn],
                         writes=[crn])
            else:
                P.op("dve", lambda e: e.tensor_tensor(out=cr[:, b0:4], in0=cr[:, b0:4], in1=pc[:, b0:4], op=ALU.add),
                     reads=[pcn, crn], writes=[crn])
            P.op("act", lambda e: e.activation(out=ec[:, b0:4], in_=cr[:, b0:4], func=AF.Exp), reads=[crn],
                 writes=[ecn])
        if it["last"]:
            h, G = it["h"], it["G"]
            dst = out[512 * G:512 * G + 512, h * 128:(h + 1) * 128].rearrange("(b p) d -> p b d", p=128)
            P.dma("sp", lambda e: e.dma_start(out=dst, in_=ac[:]), reads=[acn], key=acn + "_st")

    for pair in range(16 // HPC):
        st = sets[pair % 2]
        (kT, kTn), (v, vn), (qT, qn) = st["k"], st["v"], st["q"]
        for hh in range(HPC):
            h = pair * HPC + hh
            P.dma("sp", lambda e, hh=hh, h=h, kT=kT: e.dma_start(
                out=kT[:, hh, :], in_=kTd[KROW_SB + 128 * h:KROW_SB + 128 * (h + 1), :]), writes=[kTn],
                key=kTn + "_%d" % hh)
            P.dma("sp", lambda e, hh=hh, h=h, v=v: e.dma_start(
                out=v[:, hh, :, :],
                in_=vtokd[:, VCOL_SB + 128 * h:VCOL_SB + 128 * (h + 1)].rearrange("(c i) d -> i c d", i=128)),
                writes=[vn], key=vn + "_%d" % hh)
            P.dma("sp", lambda e, hh=hh, h=h, qT=qT: e.dma_start(
                out=qT[:, hh, :], in_=qTd[QROW_SB + 128 * h:QROW_SB + 128 * (h + 1), :]), writes=[qn],
                key=qn + "_%d" % hh)
        items = []
        step = 0
        for G in range(NM_OWN // 4):
            uq = [own_chunk(4 * G + b) for b in range(4)]
            accs = {hh: acc[cx.nxt("acc%d" % hh, 2) * HPC + hh] for hh in range(HPC)}
            for uc in range(uq[3], -1, -1):
                b0 = min(b for b in range(4) if uq[b] >= uc)
                for hh in range(HPC):
                    items.append({"hh": hh, "h": pair * HPC + hh, "G": G, "uc": uc, "b0": b0, "c0": b0 * 128,
                                  "diag": uc == uq[b0], "par": step % 2, "uq": uq, "q0": 512 * G, "acc": accs[hh],
                                  "last": uc == 0, "k": (kT, kTn), "q": (qT, qn), "v": (v, vn)})
                step += 1
        n = len(items)
        for i in range(n + 2):
            if i < n:
                stage_a(items[i])
            if 0 <= i - 1 < n:
                stage_b(items[i - 1])
            if 0 <= i - 2 < n:
                stage_c(items[i - 2])


def phase_2b(cx, io):
    P = cx.P
    S, NM = SEQ, NM_OWN
    NJ, NCU, NUC = 16 * NM, 64 * NM, 8 * NM
    NCC = max(1, NCU // 128)
    NT = NM * 128
    NCV = 129 + NJ
    kTd, vtokd, qTd = io["kT"], io["vtok"], io["qT"]
    d_w1 = {"k": io["w_k1"], "v": io["w_v1"]}
    d_w2 = {"k": io["w_k2"], "v": io["w_v2"]}
    d_pos = {"k": io["posTk"], "v": io["posTv"]}
    d_ak, d_ac, d_cm, d_wm, d_tri = io["ak"], io["ac"], io["cm"], io["wm"], io["tri_incl"]
    d_bonus, d_ex, d_E, d_caug, d_ident = io["bonus"], io["ex"], io["E"], io["caug"], io["ident"]
    d_gn = io["gn"][:, 0:48].rearrange("(m p) c -> p m c", p=128)
    out = io["onsa"]

    def const(name, d, shape, dt):
        t = cx.sb(name + "_sb", shape, dt)
        P.dma("sp", lambda e: e.dma_start(out=t[:], in_=d), writes=[name], key="ld_" + name)
        return t

    ak = const("ak", d_ak, (128, 16, NUC), F32)
    ac = const("ac", d_ac, (128, 16, NM, NCC), F32)
    cm = const("cm", d_cm, (128, NM, NCC, 128), BF16)
    wm = const("wm", d_wm, (128, NM, 5, 128), BF16)
    tri = const("tri", d_tri, (128, 128), BF16)
    bonus = const("bonus", d_bonus, (128, NM, NJ), F32)
    ex = const("ex", d_ex, (128, NJ), F32)
    E = const("E", d_E, (NJ, NUC, 128), BF16)
    ident = const("ident", d_ident, (128, 128), F32)
    gn = const("gn", d_gn, (128, NM, 48), F32)

    qg = cx.sb("qg", (128, 8, NT), BF16)
    ksl = cx.sb("ksl", (128, S), BF16)
    vsl = cx.sb("vsl_sb", (128, NUC, 129), BF16)
    kw = cx.sb("kw", (128, NM, 5, 128), BF16)
    vw = cx.sb("vw_sb", (128, NM, 5, 129), BF16)
    cbuf = cx.sb("cbuf", (128, S + 16), BF16)
    kcmpT = cx.sb("kcmpT", (128, NCU), BF16)
    vcmp = cx.sb("vcmp", (128, NCC, NCV), BF16)
    gates = cx.sb("gates", (128, NM, 48), F32)
    gtmp = cx.sb("gtmp", (128, NM, 48), F32)
    posf = cx.sb("posf", (128, 32), F32)
    posb = cx.sb("posb", (128, 32), BF16)
    biasT = cx.sb("biasT", (128, 2), F32)
    HT = cx.sb("HT", (128, 2, NCU), BF16)
    ub = cx.sb("ub", (128, NCU), F32)
    u2 = cx.sb("u2", (128, NCU), F32)
    ws = WStream(cx, "w", 32, 256, kpiece=4, npanel=1, nstage=2)
    combs = [(cx.sb("comb%d" % i, (128, 8, 128), F32), "comb%d" % i) for i in range(2)]
    imp = cx.sb("imp", (128, NJ), F32)
    score = cx.sb("score", (128, NJ), F32)
    score2 = cx.sb("score2", (128, NJ), F32)
    m8 = cx.sb("m8", (128, 16), F32)
    sel = cx.sb("sel", (128, NJ), F32)
    selT = cx.sb("selT", (NJ, 128), BF16)
    Pts = [(cx.sb("Pt%d" % i, (128, 512), BF16), "Pt%d" % i) for i in range(2)]
    mts = [(cx.sb("mt%d" % i, (128, 128), BF16), "mt%d" % i) for i in range(2)]
    rts = [(cx.sb("rt%d" % i, (128, 4), F32), "rt%d" % i) for i in range(4)]
    psS = [(cx.ps("psS%d" % i), "psS%d" % i) for i in range(2)]
    psA = [(cx.ps("psA%d" % i), "psA%d" % i) for i in range(4)]
    psM = (cx.ps("psM"), "psM")
    psX = (cx.ps("psX"), "psX")

    P.op("pool", lambda e: e.memset(vsl[:, :, 128:129], 1.0), writes=["vsl"])
    P.op("pool", lambda e: e.memset(vw[:, :, :, 128:129], 1.0), writes=["vw"])
    P.op("pool", lambda e: e.memset(cbuf[:, S:S + 16], 0.0), writes=["cbuf"])
    P.op("act", lambda e: e.activation(out=gtmp[:], in_=gn[:], func=AF.Exp, scale=-1.0), reads=["gn"], writes=["gtmp"])
    P.op("dve", lambda e: e.tensor_scalar(out=gtmp[:], in0=gtmp[:], scalar1=1.0, scalar2=None, op0=ALU.add),
         reads=["gtmp"], writes=["gtmp"])
    P.op("dve", lambda e: e.reciprocal(out=gates[:], in_=gtmp[:]), reads=["gtmp"], writes=["gates"])

    def mlp(g, which):
        r0 = (KROW_KC if which == "k" else KROW_VC) + 128 * g
        P.dma("sp", lambda e: e.dma_start(out=cbuf[:, 0:S], in_=kTd[r0:r0 + 128, :]), writes=["cbuf"], key="ld_cbuf")
        P.dma("sp", lambda e: e.dma_start(out=posf[:], in_=d_pos[which]), writes=["posf"], key="ld_pos")
        P.op("pool", lambda e: e.tensor_copy(out=posb[:], in_=posf[:]), reads=["posf"], writes=["posb"])
        pan, pname = ws.load(d_w1[which], 0, 32, 0, 256)
        px, pxn = psX
        for hc in range(2):
            for l in range(32):
                P.op("pe", lambda e, hc=hc, l=l: e.matmul(px[:, hc:hc + 1], pan[:, l, hc * 128:(hc + 1) * 128],
                                                          posb[:, l:l + 1], start=(l == 0), stop=(l == 31)),
                     reads=[pname, "posb"], writes=[pxn])
            P.op("dve", lambda e, hc=hc: e.tensor_copy(out=biasT[:, hc:hc + 1], in_=px[:, hc:hc + 1]), reads=[pxn],
                 writes=["biasT"])
        for hc in range(2):
            ps, psn = psS[hc]
            for l in range(32):
                P.op("pe", lambda e, ps=ps, hc=hc, l=l: e.matmul(ps[:, 0:NCU], pan[:, l, hc * 128:(hc + 1) * 128],
                                                                cbuf[:, l:l + 16 * (NCU - 1) + 1:16], start=(l == 0),
                                                                stop=(l == 31)),
                     reads=[pname, "cbuf"], writes=[psn])
            P.op("act", lambda e, ps=ps, hc=hc: e.activation(out=ub[:], in_=ps[:, 0:NCU], func=AF.Identity,
                                                            bias=biasT[:, hc:hc + 1], scale=1.0),
                 reads=[psn, "biasT"], writes=["ub"])
            P.op("dve", lambda e: e.tensor_tensor(out=u2[:], in0=ub[:], in1=ub[:], op=ALU.mult), reads=["ub"],
                 writes=["u2"])
            P.op("dve", lambda e: e.tensor_scalar(out=u2[:], in0=u2[:], scalar1=0.044715, scalar2=1.0, op0=ALU.mult,
                                                  op1=ALU.add), reads=["u2"], writes=["u2"])
            P.op("dve", lambda e: e.tensor_tensor(out=u2[:], in0=u2[:], in1=ub[:], op=ALU.mult), reads=["u2", "ub"],
                 writes=["u2"])
            P.op("act", lambda e: e.activation(out=u2[:], in_=u2[:], func=AF.Exp, scale=-1.5957691216057308),
                 reads=["u2"], writes=["u2"])
            P.op("dve", lambda e: e.tensor_scalar(out=u2[:], in0=u2[:], scalar1=1.0, scalar2=None, op0=ALU.add),
                 reads=["u2"], writes=["u2"])
            P.op("dve", lambda e: e.reciprocal(out=u2[:], in_=u2[:]), reads=["u2"], writes=["u2"])
            P.op("dve", lambda e, hc=hc: e.tensor_tensor(out=HT[:, hc, :], in0=u2[:], in1=ub[:], op=ALU.mult),
                 reads=["u2", "ub"], writes=["HT"])
        pan2, p2name = ws.load(d_w2[which], 0, 2, 0, 128)
        if which == "k":
            ps, psn = psS[0]
            for hc in range(2):
                P.op("pe", lambda e, hc=hc: e.matmul(ps[:, 0:NCU], pan2[:, hc, 0:128], HT[:, hc, :], start=(hc == 0),
                                                     stop=(hc == 1)), reads=[p2name, "HT"], writes=[psn])
            P.op("act", lambda e: e.activation(out=kcmpT[:], in_=ps[:, 0:NCU], func=AF.Copy), reads=[psn],
                 writes=["kcmpT"])
        else:
            for ncc in range(NCC):
                ps, psn = psS[ncc % 2]
                for hc in range(2):
                    P.op("pe", lambda e, ps=ps, hc=hc, ncc=ncc: e.matmul(ps[:, 0:128],
                                                                        HT[:, hc, ncc * 128:(ncc + 1) * 128],
                                                                        pan2[:, hc, 0:128], start=(hc == 0),
                                                                        stop=(hc == 1)),
                         reads=[p2name, "HT"], writes=[psn])
                P.op("act", lambda e, ps=ps, ncc=ncc: e.activation(out=vcmp[:, ncc, 0:128], in_=ps[:, 0:128],
                                                                  func=AF.Copy), reads=[psn], writes=["vcmp"])

    def chunk(g, m, quad, keysT, bias_ap_fn, mask2d, mask_name, vrhs, vname, ncv, first, last, kname):
        si = cx.nxt("psS", 2)
        ps, psn = psS[si]
        pt, ptn = Pts[si]
        rhs = qg[:, 4 * quad:4 * quad + 4, m * 128:(m + 1) * 128]
        P.op("pe", lambda e: e.matmul(ps[:, 0:512].rearrange("p (h t) -> p h t", h=4), keysT, rhs, start=True,
                                      stop=True), reads=[kname, "qg"], writes=[psn])
        for h in range(4):
            b = bias_ap_fn(8 * g + 4 * quad + h)
            P.op("act", lambda e, h=h, b=b: e.activation(out=pt[:, h * 128:(h + 1) * 128],
                                                          in_=ps[:, h * 128:(h + 1) * 128], func=AF.Exp, bias=b,
                                                          scale=1.0), reads=[psn, "ak", "ac"], writes=[ptn])
        if mask2d is not None:
            pt3 = pt[:, 0:512].rearrange("p (h t) -> p h t", h=4)
            P.op("dve", lambda e: e.tensor_tensor(out=pt3, in0=pt3, in1=_bcast_mid(mask2d, 4), op=ALU.mult),
                 reads=[ptn, mask_name], writes=[ptn])
        def s2():
            for h in range(4):
                pa, pan_ = psA[h]
                P.op("pe", lambda e, h=h, pa=pa: e.matmul(pa[:, 0:ncv], pt[:, h * 128:(h + 1) * 128], vrhs,
                                                          start=first, stop=last), reads=[ptn, vname], writes=[pan_])
        return s2

    def evac(g, m, quad, br, comb, cname, first_branch, do_imp):
        for h in range(4):
            pa, pan_ = psA[h]
            hl = 4 * quad + h
            head = 8 * g + hl
            rt, rtn = rts[cx.nxt("rt", 4)]
            P.op("dve", lambda e, pa=pa, rt=rt: e.tensor_scalar(out=rt[:, 0:1], in0=pa[:, 128:129], scalar1=1e-30,
                                                               scalar2=None, op0=ALU.max), reads=[pan_],
                 writes=[rtn])
            P.op("dve", lambda e, rt=rt: e.reciprocal(out=rt[:, 1:2], in_=rt[:, 0:1]), reads=[rtn], writes=[rtn])
            col = head * 3 + br
            P.op("dve", lambda e, rt=rt, col=col: e.tensor_tensor(out=rt[:, 2:3], in0=rt[:, 1:2],
                                                                 in1=gates[:, m, col:col + 1], op=ALU.mult),
                 reads=[rtn, "gates"], writes=[rtn])
            if first_branch:
                P.op("dve", lambda e, pa=pa, rt=rt, hl=hl: e.tensor_scalar(out=comb[:, hl, :], in0=pa[:, 0:128],
                                                                          scalar1=rt[:, 2:3], scalar2=None,
                                                                          op0=ALU.mult), reads=[pan_, rtn],
                     writes=[cname])
            else:
                P.op("dve", lambda e, pa=pa, rt=rt, hl=hl: e.scalar_tensor_tensor(
                    out=comb[:, hl, :], in0=pa[:, 0:128], scalar=rt[:, 2:3], in1=comb[:, hl, :], op0=ALU.mult,
                    op1=ALU.add), reads=[pan_, rtn, cname], writes=[cname])
            if do_imp:
                if hl == 0:
                    P.op("dve", lambda e, pa=pa, rt=rt: e.tensor_scalar(out=imp[:], in0=pa[:, 129:129 + NJ],
                                                                       scalar1=rt[:, 1:2], scalar2=None,
                                                                       op0=ALU.mult), reads=[pan_, rtn],
                         writes=["imp"])
                else:
                    P.op("dve", lambda e, pa=pa, rt=rt: e.scalar_tensor_tensor(
                        out=imp[:], in0=pa[:, 129:129 + NJ], scalar=rt[:, 1:2], in1=imp[:], op0=ALU.mult,
                        op1=ALU.add), reads=[pan_, rtn, "imp"], writes=["imp"])

    class Skew:
        pending, after = None, []

        def chunk(self, s1):
            s2 = s1()
            self.flush()
            self.pending = s2

        def flush(self):
            if self.pending is not None:
                self.pending()
                self.pending = None
            for f in self.after:
                f()
            self.after = []

        def defer(self, f):
            if self.pending is None:
                f()
            else:
                self.after.append(f)

    sk = Skew()
    for g in range(2):
        sk.flush()
        P.dma("sp", lambda e, g=g: e.dma_start(
            out=qg[:], in_=qTd[QROW_N + 1024 * g:QROW_N + 1024 * (g + 1), :].rearrange("(h d) t -> d h t", d=128)),
            writes=["qg"], key="ld_qg")
        P.dma("sp", lambda e, g=g: e.dma_start(out=ksl[:], in_=kTd[KROW_KSL + 128 * g:KROW_KSL + 128 * (g + 1), :]),
              writes=["ksl"], key="ld_ksl")
        P.dma("sp", lambda e, g=g: e.dma_start(
            out=vsl[:, :, 0:128],
            in_=vtokd[:, VCOL_VSL + 128 * g:VCOL_VSL + 128 * (g + 1)].rearrange("(c i) d -> i c d", i=128)),
            writes=["vsl"], key="ld_vsl")
        for m in range(NM):
            u0 = (8 * m + 3) * 128
            P.dma("sp", lambda e, g=g, m=m, u0=u0: e.dma_start(
                out=kw[:, m, :, :],
                in_=kTd[KROW_KW + 128 * g:KROW_KW + 128 * (g + 1), u0:u0 + 640].rearrange("d (r i) -> d r i", i=128)),
                writes=["kw"], key="ld_kw")
            P.dma("sp", lambda e, g=g, m=m, u0=u0: e.dma_start(
                out=vw[:, m, :, 0:128],
                in_=vtokd[u0:u0 + 640, VCOL_VW + 128 * g:VCOL_VW + 128 * (g + 1)].rearrange("(r i) d -> i r d", i=128)),
                writes=["vw"], key="ld_vw")
        P.dma("sp", lambda e: e.dma_start(out=vcmp[:, :, 128:NCV], in_=d_caug), writes=["vcmp"], key="ld_caug")
        mlp(g, "k")
        mlp(g, "v")
        for m in range(NM):
            comb, cname = combs[cx.nxt("comb", 2)]
            nccs = list(range((64 * m + 62) // 128 + 1))
            for quad in range(2):
                for ii, ncc in enumerate(nccs):
                    sk.chunk(lambda quad=quad, ii=ii, ncc=ncc, m=m: chunk(
                        g, m, quad, kcmpT[:, ncc * 128:(ncc + 1) * 128],
                        lambda head, ncc=ncc: ac[:, head, m, ncc:ncc + 1],
                        cm[:, m, ncc, :], "cm", vcmp[:, ncc, :], "vcmp", NCV, ii == 0, ii == len(nccs) - 1, "kcmpT"))
                sk.defer(lambda g=g, quad=quad, m=m, comb=comb, cname=cname: evac(g, m, quad, 0, comb, cname, True, True))
            sk.flush()
            P.op("dve", lambda e, m=m: e.tensor_tensor(out=score[:], in0=imp[:], in1=bonus[:, m, :], op=ALU.add),
                 reads=["imp", "bonus"], writes=["score"])
            P.op("dve", lambda e: e.max(out=m8[:, 0:8], in_=score[:]), reads=["score"], writes=["m8"])
            P.op("dve", lambda e: e.match_replace(out=score2[:], in_to_replace=m8[:, 0:8], in_values=score[:],
                                                  imm_value=-3.0e38), reads=["score", "m8"], writes=["score2"])
            P.op("dve", lambda e: e.max(out=m8[:, 8:16], in_=score2[:]), reads=["score2"], writes=["m8"])
            P.op("dve", lambda e: e.tensor_scalar(out=sel[:], in0=score[:], scalar1=m8[:, 15:16], scalar2=None,
                                                  op0=ALU.is_ge), reads=["score", "m8"], writes=["sel"])
            P.op("dve", lambda e: e.tensor_tensor(out=sel[:], in0=sel[:], in1=ex[:], op=ALU.mult),
                 reads=["sel", "ex"], writes=["sel"])
            px, pxn = psX
            P.op("pe", lambda e: e.transpose(px[0:NJ, 0:128], sel[:], ident[:]), reads=["sel", "ident"], writes=[pxn])
            P.op("act", lambda e: e.activation(out=selT[:], in_=px[0:NJ, 0:128], func=AF.Copy), reads=[pxn],
                 writes=["selT"])
            for quad in range(2):
                ucs = list(range(8 * m + 8))
                for ii, uc in enumerate(ucs):
                    def s1(quad=quad, ii=ii, uc=uc, m=m, nuc=len(ucs)):
                        pm, pmn = psM if cx.nxt("psM", 2) == 0 else psX
                        mt, mtn = mts[cx.nxt("mt", 2)]
                        P.op("pe", lambda e: e.matmul(pm[:, 0:128], E[:, uc, :], selT[:], start=True, stop=True),
                             reads=["E", "selT"], writes=[pmn])
                        if uc == 8 * m + 7:
                            P.op("dve", lambda e: e.tensor_tensor(out=mt[:], in0=pm[:, 0:128], in1=tri[:],
                                                                  op=ALU.mult), reads=[pmn, "tri"], writes=[mtn])
                        else:
                            P.op("act", lambda e: e.activation(out=mt[:], in_=pm[:, 0:128], func=AF.Copy),
                                 reads=[pmn], writes=[mtn])
                        rel = 8 * m + 7 - uc
                        return chunk(g, m, quad, ksl[:, uc * 128:(uc + 1) * 128],
                                     lambda head, rel=rel: ak[:, head, rel:rel + 1],
                                     mt[:], mtn, vsl[:, uc, :], "vsl", 129, ii == 0, ii == nuc - 1, "ksl")
                    sk.chunk(s1)
                sk.defer(lambda g=g, quad=quad, m=m, comb=comb, cname=cname: evac(g, m, quad, 1, comb, cname, False, False))
            for quad in range(2):
                for r in range(5):
                    sk.chunk(lambda quad=quad, r=r, m=m: chunk(
                        g, m, quad, kw[:, m, r, :], lambda head, r=r: ak[:, head, 4 - r:5 - r],
                        wm[:, m, r, :], "wm", vw[:, m, r, :], "vw", 129, r == 0, r == 4, "kw"))
                sk.defer(lambda g=g, quad=quad, m=m, comb=comb, cname=cname: evac(g, m, quad, 2, comb, cname, False, False))
            dst = out[m * 128:(m + 1) * 128, g * 1024:(g + 1) * 1024]
            sk.defer(lambda comb=comb, dst=dst, cname=cname: P.dma(
                "sp", lambda e: e.dma_start(out=dst, in_=comb[:].rearrange("p h d -> p (h d)")),
                reads=[cname], key=cname + "_st"))
    sk.flush()


def phase_3(cx, io, NPASS=5):
    P = cx.P
    D, DFF, NTOK = D_MODEL, D_FF, NT_OWN
    NB = NTOK // 128
    HALF = D // 2
    xu, osb, onsa, out, x1d, yacc = io["xu"], io["osb"], io["onsa"], io["out"], io["x1d"], io["yacc"]
    w_out, w_gate, w_up, w_down = io["w_out"], io["w_gate"], io["w_up"], io["w_down"]
    xrow = lambda b: xu[own_chunk(b) * 128:(own_chunk(b) + 1) * 128, :]
    npan_ff = DFF // 256
    per = -(-npan_ff // NPASS)
    maxk_act = per * 2

    ident = cx.sb("ident_sb", (128, 128), F32)
    gain = cx.sb("gain_sb", (128, D), F32)
    hT = cx.sb("hT", (128, KC, NTOK), BF16)
    actT = cx.sb("actT", (128, maxk_act, NTOK), BF16)
    xts = [(cx.sb("xt0", (128, D), F32), "xt0")]
    ws = WStream(cx, "w", max(KC, maxk_act), 256, kpiece=4, npanel=3, nstage=3)
    sqv = actT[:].rearrange("p k n -> p (k n)")
    small = small_tiles(cx, sqv, "actT")
    thunks = [(lambda col=cp * 256: ws.begin(w_out, 0, KC, col, 256)) for cp in range(D // 256)]
    for p_ in range(NPASS):
        a0, a1 = p_ * per, min(npan_ff, p_ * per + per)
        if a0 >= a1:
            continue
        for pp_ in range(a0, a1):
            thunks.append(lambda col=pp_ * 256: ws.begin(w_gate, 0, KC, col, 256))
            thunks.append(lambda col=pp_ * 256: ws.begin(w_up, 0, KC, col, 256))
        for cp in range(D // 256):
            thunks.append(lambda col=cp * 256, a0=a0, a1=a1: ws.begin(w_down, a0 * 2, (a1 - a0) * 2, col, 256))
    pf = Prefetch(thunks, ahead=2)
    pfi = [0]

    def next_panel():
        r = pf.get(pfi[0])
        pfi[0] += 1
        return r

    pss = [(cx.ps("ps%d" % i), "ps%d" % i) for i in range(8)]
    ept = [(cx.sb("ept%d" % i, (128, 256), F32), "ept%d" % i) for i in range(2)]
    epo = [(cx.sb("epo%d" % i, (128, 256), F32), "epo%d" % i) for i in range(2)]
    sgt = [(cx.sb("sg%d" % i, (128, 512), F32), "sg%d" % i) for i in range(2)]

    P.dma("sp", lambda e: e.dma_start(out=ident[:], in_=io["ident"]), writes=["ident"], key="c_ident")
    P.op("dve", lambda e: e.memset(small["eps"][0][:], EPS), writes=["epsc"])

    P.dma("sp", lambda e: e.dma_start(out=gain[:, 0:HALF], in_=io["g_sb"]), writes=["gain"], key="c_gain")
    norm_transpose(cx, "sb", lambda b: osb[b * 128:(b + 1) * 128, :], lambda b: [], NB, HALF, gain, "gain", hT, "hT",
                   0, xts, pss, ident, small)
    P.dma("sp", lambda e: e.dma_start(out=gain[:, 0:HALF], in_=io["g_nsa"]), writes=["gain"], key="c_gain")
    norm_transpose(cx, "nsa", lambda b: onsa[b * 128:(b + 1) * 128, :], lambda b: [], NB, HALF, gain, "gain", hT, "hT",
                   KC // 2, xts, pss, ident, small)

    def tok_major_gemm(W, kc0, nk, actbuf, actname, prev_row, prev_name_fn, dst, dst_name_fn):
        for cp in range(D // 256):
            col = cp * 256
            pan, pname = next_panel()
            for b in range(NB):
                ps, psn = pss[cx.nxt("mm_ps", 8)]
                for k in range(nk):
                    P.op("pe", lambda e, ps=ps, pan=pan, k=k, b=b: e.matmul(
                        ps[:, 0:256], actbuf[:, k, b * 128:(b + 1) * 128], pan[:, k, 0:256],
                        start=(k == 0), stop=(k == nk - 1)), reads=[pname, actname], writes=[psn])
                ti = cx.nxt("ept", 2)
                pt, ptn = ept[ti]
                po, pon = epo[ti]
                srcp = prev_row(b)[:, col:col + 256]
                P.dma("sp", lambda e, pt=pt, srcp=srcp: e.dma_start(out=pt[:], in_=srcp),
                      reads=[prev_name_fn(b, cp)], writes=[ptn], key=ptn)
                P.op("dve", lambda e, po=po, ps=ps, pt=pt: e.tensor_tensor(out=po[:], in0=ps[:, 0:256], in1=pt[:],
                                                                          op=ALU.add),
                     reads=[psn, ptn], writes=[pon])
                dstp = dst[b * 128:(b + 1) * 128, col:col + 256]
                P.dma("sp", lambda e, po=po, dstp=dstp: e.dma_start(out=dstp, in_=po[:]),
                      reads=[pon], writes=[dst_name_fn(b, cp)], key=pon + "_st")
                pf.tick()

    tok_major_gemm(w_out, 0, KC, hT, "hT", xrow, lambda b, cp: "x_in", x1d, lambda b, cp: "x1d_%d_%d" % (b, cp))

    P.dma("sp", lambda e: e.dma_start(out=gain[:], in_=io["g_ffn"]), writes=["gain"], key="c_gain")
    norm_transpose(cx, "ffn", lambda b: x1d[b * 128:(b + 1) * 128, :],
                   lambda b: ["x1d_%d_%d" % (b, cp) for cp in range(D // 256)], NB, D, gain, "gain", hT, "hT", 0,
                   xts, pss, ident, small)

    TW = 512
    NTH = NTOK // TW
    lastp = 0
    for p in range(NPASS):
        pan0 = p * per
        pan1 = min(npan_ff, pan0 + per)
        if pan0 >= pan1:
            continue
        lastp = p
        for pp in range(pan0, pan1):
            col = pp * 256
            gpan, gname = next_panel()
            gps = {}
            for j in range(2):
                for th in range(NTH):
                    ps, psn = pss[cx.nxt("mm_ps", 8)]
                    gps[(j, th)] = (ps, psn)
                    for k in range(KC):
                        P.op("pe", lambda e, ps=ps, gpan=gpan, k=k, j=j, th=th: e.matmul(
                            ps[:, 0:TW], gpan[:, k, j * 128:(j + 1) * 128], hT[:, k, th * TW:(th + 1) * TW],
                            start=(k == 0), stop=(k == KC - 1)), reads=[gname, "hT"], writes=[psn])
                    pf.tick()
            upan, uname = next_panel()
            for j in range(2):
                for th in range(NTH):
                    ps, psn = pss[cx.nxt("mm_ps", 8)]
                    for k in range(KC):
                        P.op("pe", lambda e, ps=ps, upan=upan, k=k, j=j, th=th: e.matmul(
                            ps[:, 0:TW], upan[:, k, j * 128:(j + 1) * 128], hT[:, k, th * TW:(th + 1) * TW],
                            start=(k == 0), stop=(k == KC - 1)), reads=[uname, "hT"], writes=[psn])
                    pf.tick()
                    gp, gpn = gps[(j, th)]
                    sg, sgn = sgt[cx.nxt("sg", 2)]
                    P.op("act", lambda e, sg=sg, gp=gp: e.activation(out=sg[:, 0:TW], in_=gp[:, 0:TW], func=AF.Silu),
                         reads=[gpn], writes=[sgn])
                    kk = (pp - pan0) * 2 + j
                    P.op("dve", lambda e, sg=sg, ps=ps, kk=kk, th=th: e.tensor_tensor(
                        out=actT[:, kk, th * TW:(th + 1) * TW], in0=sg[:, 0:TW], in1=ps[:, 0:TW], op=ALU.mult),
                        reads=[sgn, psn], writes=["actT"])
        nk = (pan1 - pan0) * 2
        if p == 0:
            prow, pfn = (lambda b: x1d[b * 128:(b + 1) * 128, :]), (lambda b, cp: "x1d_%d_%d" % (b, cp))
        else:
            prow, pfn = (lambda b: yacc[b * 128:(b + 1) * 128, :]), (lambda b, cp, p=p: "yacc%d_%d_%d" % (p - 1, b, cp))
        tok_major_gemm(w_down, pan0 * 2, nk, actT, "actT", prow, pfn, yacc,
                       lambda b, cp, p=p: "yacc%d_%d_%d" % (p, b, cp))

    P.dma("sp", lambda e: e.dma_start(out=gain[:], in_=io["g_fin"]), writes=["gain"], key="c_gain")
    sq, sqn = small["sq"]
    ss = small["ss"][0]
    for b in range(NB):
        xt, xname = xts[0]
        deps = ["yacc%d_%d_%d" % (lastp, b, cp) for cp in range(D // 256)]
        P.dma("sp", lambda e, b=b: e.dma_start(out=xt[:], in_=yacc[b * 128:(b + 1) * 128, :]), reads=deps,
              writes=[xname], key=xname)
        P.op("act", lambda e: e.activation(out=sq[:, 0:D], in_=xt[:], func=AF.Square, accum_out=ss[:, 0:1]),
             reads=[xname], writes=[sqn, "ss"])
        P.op("act", lambda e: e.activation(out=ss[:, 1:2], in_=ss[:, 0:1], func=AF.Sqrt, scale=1.0 / D,
                                           bias=small["eps"][0][:, 0:1]), reads=["ss", "epsc"], writes=["ssb"])
        P.op("dve", lambda e: e.reciprocal(out=ss[:, 2:3], in_=ss[:, 1:2]), reads=["ssb"], writes=["ssc"])
        P.op("dve", lambda e: e.scalar_tensor_tensor(out=xt[:], in0=xt[:], scalar=ss[:, 2:3], in1=gain[:],
                                                     op0=ALU.mult, op1=ALU.mult),
             reads=[xname, "ssc", "gain"], writes=[xname])
        P.dma("sp", lambda e, b=b: e.dma_start(out=out[b * 128:(b + 1) * 128, :], in_=xt[:]), reads=[xname],
              writes=[], key="out_st")


def _tables_spec():
    NM = NM_OWN
    NJ, NCU, NUC = 16 * NM, 64 * NM, 8 * NM
    NCC = NCU // 128
    return {"ak": ((128, 16, NUC), F32), "ac": ((128, 16, NM, NCC), F32), "cm": ((128, NM, NCC, 128), BF16),
            "wm": ((128, NM, 5, 128), BF16), "tri_incl": ((128, 128), BF16), "bonus": ((128, NM, NJ), F32),
            "ex": ((128, NJ), F32), "E": ((NJ, NUC, 128), BF16), "caug": ((128, NCC, 1 + NJ), BF16),
            "ident": ((128, 128), F32), "exf": ((128, 8), F32), "negU": ((128, 128), BF16),
            "tri_strict": ((128, 128), BF16), "negones": ((128, 1), BF16)}


def build_program(phases=("1a", "1b", "2a", "2b", "3")):
    cx = Ctx()
    io = {}
    io["xu"] = cx.din("xu", (SEQ, D_MODEL), F32)
    io["w_in"] = cx.din("w_in", (D_MODEL, D_IN_PAD), F32)
    for n in ("g_attn", "g_ffn", "g_fin"):
        io[n] = cx.din(n, (128, D_MODEL), F32)
    for n in ("g_sb", "g_nsa"):
        io[n] = cx.din(n, (128, D_MODEL // 2), F32)
    for n, (shape, dt) in _tables_spec().items():
        io[n] = cx.din(n, shape, dt)
    for n in ("w_k1", "w_v1"):
        io[n] = cx.din(n, (4096, 256), F32)
    for n in ("w_k2", "w_v2"):
        io[n] = cx.din(n, (256, 128), F32)
    for n in ("posTk", "posTv"):
        io[n] = cx.din(n, (128, 32), F32)
    io["w_out"] = cx.din("w_out", (D_MODEL, D_MODEL), F32)
    io["w_gate"] = cx.din("w_gate", (D_MODEL, D_FF), F32)
    io["w_up"] = cx.din("w_up", (D_MODEL, D_FF), F32)
    io["w_down"] = cx.din("w_down", (D_FF, D_MODEL), F32)
    io["out"] = cx.dout("out", (NT_OWN, D_MODEL), F32)
    io["qT"] = cx.dint("qT_scr", (QROWS, NT_OWN), BF16)
    io["gn"] = cx.dint("gn_scr", (NT_OWN, 128), F32)
    io["kT"] = cx.dint("kT_scr", (KROWS, SEQ), BF16)
    io["vtok"] = cx.dint("vtok_scr", (SEQ, VCOLS), BF16)
    io["wbf"] = cx.dint("wbf_scr", (len(K_PANELS) + len(V_PANELS), 128, KC, 256), BF16)
    io["osb"] = cx.dint("osb_scr", (NT_OWN, 2048), F32)
    io["onsa"] = cx.dint("onsa_scr", (NT_OWN, 2048), F32)
    io["x1d"] = cx.dint("x1d_scr", (NT_OWN, D_MODEL), F32)
    io["yacc"] = cx.dint("yacc_scr", (NT_OWN, D_MODEL), F32)
    fns = {"1a": phase_1a, "1b": phase_1b, "2a": phase_2a, "2b": phase_2b, "3": phase_3}
    first = True
    for ph in phases:
        if not first:
            cx.begin()
        first = False
        fns[ph](cx, io)
        cx.end()
    return cx.finish()


def _bc(g, n=128):
    g = np.asarray(g, np.float32).reshape(-1)
    return np.ascontiguousarray(np.broadcast_to(g, (n, g.shape[0])))


def _own_rows(c):
    return np.concatenate([np.arange(128) + 128 * (c + 8 * m) for m in range(NM_OWN)])


def kernel(x, attn_norm, w_in, pos_cmp_k, pos_cmp_v, w_cmp_k1, w_cmp_k2, w_cmp_v1, w_cmp_v2,
           norm_sb, norm_nsa, w_out, ffn_norm, w_gate, w_up, w_down, final_norm):
    f32 = lambda a: np.ascontiguousarray(np.asarray(a, np.float32))
    x2 = f32(x)[0]
    cores = list(range(NCORES))
    w_pad = np.zeros((D_MODEL, D_IN_PAD), np.float32)
    w_pad[:, :D_IN] = f32(w_in)[0]
    common = {"w_in": w_pad, "g_attn": _bc(attn_norm), "g_ffn": _bc(ffn_norm), "g_fin": _bc(final_norm),
              "g_sb": _bc(norm_sb), "g_nsa": _bc(norm_nsa),
              "w_k1": f32(w_cmp_k1)[0], "w_k2": f32(w_cmp_k2)[0], "w_v1": f32(w_cmp_v1)[0], "w_v2": f32(w_cmp_v2)[0],
              "posTk": np.ascontiguousarray(f32(pos_cmp_k)[0].T), "posTv": np.ascontiguousarray(f32(pos_cmp_v)[0].T),
              "w_out": f32(w_out)[0], "w_gate": f32(w_gate)[0], "w_up": f32(w_up)[0], "w_down": f32(w_down)[0]}
    common.update(sb_consts())
    in_maps = []
    for c in cores:
        d = dict(common)
        shift = 128 * (7 - c)
        xu = np.zeros((SEQ, D_MODEL), np.float32)
        xu[shift:] = x2[:SEQ - shift]
        d["xu"] = xu
        d.update(nsa_tables(c, NM_OWN))
        in_maps.append(d)
    nc = build_program()
    res = run_bass_kernel_spmd(nc, in_maps, core_ids=cores).results
    out = np.zeros((1, SEQ, D_MODEL), np.float32)
    for c in cores:
        out[0, _own_rows(c)] = np.asarray(res[c]["out"])
    return out
```

```python
import contextlib
import numpy as np
import ml_dtypes
import concourse.bass as bass
import concourse.mybir as mybir
from concourse.bass_utils import run_bass_kernel_spmd

F32 = mybir.dt.float32
BF16 = mybir.dt.bfloat16
AF = mybir.ActivationFunctionType
ALU = mybir.AluOpType
NPBF = ml_dtypes.bfloat16

NCORES = 8
EPS = 1e-6
ALL_ENG = ("pe", "act", "dve", "pool", "sp")


class _Buf:
    __slots__ = ("writer", "readers")

    def __init__(self):
        self.writer = None
        self.readers = []


class _Op:
    __slots__ = ("eng", "fn", "is_dma", "dkey", "dval", "deps", "signal", "count")


class Prog:
    def __init__(self, nc, same_engine_sync=True):
        self.nc = nc
        self.ops = {e: [] for e in ALL_ENG}
        self.bufs = {}
        self.dma_counts = {}
        self.phase_keys = set()
        self.same_engine_sync = same_engine_sync
        self.ecount = {e: 0 for e in ALL_ENG}
        self.semstack = contextlib.ExitStack()
        self.esem = None
        self.dsem = {}
        self.barrier = []

    def _add(self, eng, fn, reads, writes, is_dma=False, dkey=None):
        op = _Op()
        op.eng, op.fn, op.is_dma, op.dkey = eng, fn, is_dma, dkey
        op.dval, op.signal, op.count = None, False, None
        deps = []
        reads, writes = _expand(reads), _expand(writes)
        for r in reads:
            b = self.bufs.get(r)
            if b is None:
                b = self.bufs[r] = _Buf()
            if b.writer is not None:
                deps.append(b.writer)
            b.readers.append(op)
        for w in writes:
            b = self.bufs.get(w)
            if b is None:
                b = self.bufs[w] = _Buf()
            if b.writer is not None:
                deps.append(b.writer)
            deps.extend(b.readers)
            b.writer = op
            b.readers = []
        out, seen = [], set()
        for d in deps:
            if d is op or id(d) in seen:
                continue
            seen.add(id(d))
            if not d.is_dma and d.eng == eng and (eng == "pe" or not self.same_engine_sync):
                continue
            out.append(d)
        op.deps = out
        if is_dma:
            c = self.dma_counts.get(dkey, 0) + 16
            self.dma_counts[dkey] = c
            self.phase_keys.add(dkey)
            op.dval = c
        self.ops[eng].append(op)
        return op

    def op(self, eng, fn, reads=(), writes=()):
        return self._add(eng, fn, reads, writes)

    def dma(self, eng, fn, reads=(), writes=(), key=None):
        return self._add(eng, fn, reads, writes, is_dma=True, dkey=key)

    def flush(self, final_wait_eng="sp"):
        nc = self.nc
        if self.esem is None:
            self.esem = {e: self.semstack.enter_context(nc.semaphore("s_" + e)) for e in ALL_ENG}
        for k in self.dma_counts:
            if k not in self.dsem:
                self.dsem[k] = self.semstack.enter_context(nc.semaphore("d_%d" % len(self.dsem)))
        esem, dsem = self.esem, self.dsem
        for e in ALL_ENG:
            ops = self.ops[e]
            for op in ops:
                for d in op.deps:
                    if not d.is_dma:
                        d.signal = True
            for op in reversed(ops):
                if not op.is_dma:
                    op.signal = True
                    break
        for e in ALL_ENG:
            c = self.ecount[e]
            for op in self.ops[e]:
                if op.signal and not op.is_dma:
                    c += 1
                    op.count = c
            self.ecount[e] = c
        barrier = self.barrier
        with nc.Block() as block:
            engmap = {"pe": block.tensor, "act": block.scalar, "dve": block.vector,
                      "pool": block.gpsimd, "sp": block.sync}

            def make(e):
                def body(eng):
                    known = {}
                    if self.ops[e]:
                        for key, sem, val in barrier:
                            if key == ("e", e):
                                known[key] = val
                                continue
                            eng.wait_ge(sem, val)
                            known[key] = val
                    for op in self.ops[e]:
                        for d in op.deps:
                            if d.is_dma:
                                key, val, sem = ("d", d.dkey), d.dval, dsem[d.dkey]
                            else:
                                key, val, sem = ("e", d.eng), d.count, esem[d.eng]
                            if known.get(key, 0) >= val:
                                continue
                            eng.wait_ge(sem, val)
                            known[key] = val
                        inst = op.fn(eng)
                        if op.is_dma:
                            inst.then_inc(dsem[op.dkey], 16)
                        elif op.signal:
                            inst.then_inc(esem[e], 1)
                    if e == final_wait_eng:
                        for k in self.phase_keys:
                            v = self.dma_counts[k]
                            if known.get(("d", k), 0) < v:
                                eng.wait_ge(dsem[k], v)
                return body

            for e in ALL_ENG:
                engmap[e](make(e))
        self.barrier = [(("e", e), esem[e], self.ecount[e]) for e in ALL_ENG if self.ecount[e] > 0]
        self.barrier += [(("d", k), dsem[k], self.dma_counts[k]) for k in self.dma_counts]
        self.ops = {e: [] for e in ALL_ENG}
        self.bufs = {}
        self.phase_keys = set()

    def close(self):
        self.semstack.close()


class Ctx:
    def __init__(self):
        self.nc = bass.Bass("TRN2", target_bir_lowering=False)
        self.P = Prog(self.nc)
        self.st = None
        self.rot = {}
        self.phase = -1
        self.begin()

    def begin(self):
        self.st = contextlib.ExitStack()
        self.rot = {}
        self.phase += 1

    def end(self):
        self.P.flush()
        self.st.close()
        self.st = None

    def sb(self, name, shape, dt):
        return self.st.enter_context(self.nc.sbuf_tensor("p%d_%s" % (self.phase, name), list(shape), dt))

    def ps(self, name, shape=(128, 512), dt=F32):
        return self.st.enter_context(self.nc.psum_tensor("p%d_%s" % (self.phase, name), list(shape), dt))

    def din(self, name, shape, dt):
        return self.nc.dram_tensor(name, list(shape), dt, kind="ExternalInput").ap()

    def dout(self, name, shape, dt):
        return self.nc.dram_tensor(name, list(shape), dt, kind="ExternalOutput").ap()

    def dint(self, name, shape, dt):
        return self.nc.dram_tensor(name, list(shape), dt, kind="Internal").ap()

    def nxt(self, key, n):
        i = self.rot.get(key, 0)
        self.rot[key] = i + 1
        return i % n

    def finish(self):
        if self.st is not None:
            self.end()
        self.P.close()
        return self.nc


class PanelName(str):
    pass


def _expand(names):
    out = []
    for n in names:
        if isinstance(n, PanelName):
            out += [str(n), str(n) + "_a", str(n) + "_b"]
        else:
            out.append(n)
    return out


class PanelLoad:
    def __init__(self, ws, W, kc0, nk, col0, ncols):
        cx = ws.cx
        self.ws, self.W, self.kc0, self.col0, self.ncols = ws, W, kc0, col0, ncols
        pi = cx.nxt(ws.name + "_p", ws.npanel)
        self.pan = ws.panels[pi]
        self.pname = "%s_pan%d" % (ws.name, pi)
        self.pieces = []
        k = 0
        while k < nk:
            kp = min(ws.kpiece, nk - k)
            self.pieces.append((k, kp))
            k += kp
        self.i = 0
        self.result = (self.pan, PanelName(self.pname))

    @property
    def done(self):
        return self.i >= len(self.pieces)

    def step(self):
        ws, cx, P = self.ws, self.ws.cx, self.ws.cx.P
        k, kp = self.pieces[self.i]
        self.i += 1
        si = cx.nxt(ws.name + "_s", ws.nstage)
        stg = ws.stages[si]
        sname = "%s_stg%d" % (ws.name, si)
        W, kc0, col0, ncols, pan, pname = self.W, self.kc0, self.col0, self.ncols, self.pan, self.pname
        src = W[(kc0 + k) * 128:(kc0 + k + kp) * 128, col0:col0 + ncols].rearrange("(k p) n -> p k n", p=128)
        dst = stg[:, 0:kp, 0:ncols]
        P.dma("sp", lambda e: e.dma_start(out=dst, in_=src), reads=[], writes=[sname], key=sname)
        pdst = pan[:, k:k + kp, 0:ncols]
        ceng, cname = ("pool", pname + "_a") if cx.nxt(ws.name + "_ce", 2) == 0 else ("dve", pname + "_b")
        P.op(ceng, lambda e: e.tensor_copy(out=pdst, in_=dst), reads=[sname], writes=[cname])


class WStream:
    def __init__(self, cx, name, max_k, ncols, kpiece=4, npanel=2, nstage=3):
        self.cx, self.name = cx, name
        self.max_k, self.ncols, self.kpiece = max_k, ncols, kpiece
        self.panels = [cx.sb("%s_pan%d" % (name, i), (128, max_k, ncols), BF16) for i in range(npanel)]
        self.stages = [cx.sb("%s_stg%d" % (name, i), (128, kpiece, ncols), F32) for i in range(nstage)]
        self.npanel, self.nstage = npanel, nstage

    def begin(self, W, kc0, nk, col0, ncols):
        return PanelLoad(self, W, kc0, nk, col0, ncols)

    def load(self, W, kc0, nk, col0, ncols):
        pl = self.begin(W, kc0, nk, col0, ncols)
        while not pl.done:
            pl.step()
        return pl.result

    def load_bf16(self, src, nk, ncols, dep):
        cx, P = self.cx, self.cx.P
        pi = cx.nxt(self.name + "_p", self.npanel)
        pan = self.panels[pi]
        pname = "%s_pan%d" % (self.name, pi)
        P.dma("sp", lambda e: e.dma_start(out=pan[:, 0:nk, 0:ncols], in_=src), reads=[dep], writes=[PanelName(pname)],
              key=pname + "_ld")
        return pan, PanelName(pname)


class Prefetch:
    def __init__(self, thunks, ahead=2):
        self.thunks, self.ahead, self.res = list(thunks), ahead, []

    def get(self, i):
        while len(self.res) < min(i + 1 + self.ahead, len(self.thunks)):
            self.res.append(self.thunks[len(self.res)]())
        o = self.res[i]
        if isinstance(o, PanelLoad):
            while not o.done:
                o.step()
            return o.result
        return o

    def tick(self):
        for o in self.res:
            if isinstance(o, PanelLoad) and not o.done:
                o.step()
                return


def norm_transpose(cx, tag, src_blk, deps_blk, ntok_blocks, nfeat, gain_bc, gain_name, dstT, dst_name, dst_chunk0,
                   xt_bufs, ps_bufs, ident, small):
    P = cx.P
    nch = nfeat // 128
    sq, sqn = small["sq"]
    ss, ssn = small["ss"]
    for b in range(ntok_blocks):
        xi = cx.nxt(tag + "_xt", len(xt_bufs))
        xt, xname = xt_bufs[xi]
        srcb = src_blk(b)
        P.dma("sp", lambda e, xt=xt, srcb=srcb: e.dma_start(out=xt[:, 0:nfeat], in_=srcb), reads=deps_blk(b),
              writes=[xname], key=xname)
        P.op("act", lambda e, xt=xt: e.activation(out=sq[:, 0:nfeat], in_=xt[:, 0:nfeat], func=AF.Square,
                                                  accum_out=ss[:, 0:1]),
             reads=[xname], writes=[sqn, ssn])
        P.op("act", lambda e: e.activation(out=ss[:, 1:2], in_=ss[:, 0:1], func=AF.Sqrt, scale=1.0 / nfeat,
                                           bias=small["eps"][0][:, 0:1]),
             reads=[ssn, small["eps"][1]], writes=[ssn + "b"])
        P.op("dve", lambda e: e.reciprocal(out=ss[:, 2:3], in_=ss[:, 1:2]), reads=[ssn + "b"], writes=[ssn + "c"])
        P.op("dve", lambda e, xt=xt: e.scalar_tensor_tensor(out=xt[:, 0:nfeat], in0=xt[:, 0:nfeat], scalar=ss[:, 2:3],
                                                             in1=gain_bc[:, 0:nfeat], op0=ALU.mult, op1=ALU.mult),
             reads=[xname, ssn + "c", gain_name], writes=[xname])
        for c0 in range(0, nch, 4):
            nc4 = min(4, nch - c0)
            pi = cx.nxt("nt_ps", len(ps_bufs))
            ps, psn = ps_bufs[pi]
            for j in range(nc4):
                P.op("pe", lambda e, ps=ps, xt=xt, j=j, c0=c0: e.transpose(ps[:, j * 128:(j + 1) * 128],
                                                                          xt[:, (c0 + j) * 128:(c0 + j + 1) * 128],
                                                                          ident[:]),
                     reads=[xname, "ident"], writes=[psn])
            dst = dstT[:, dst_chunk0 + c0:dst_chunk0 + c0 + nc4, b * 128:(b + 1) * 128]
            src_ps = ps[:, 0:nc4 * 128].rearrange("p (c t) -> p c t", c=nc4)
            P.op("act", lambda e, dst=dst, src_ps=src_ps: e.activation(out=dst, in_=src_ps, func=AF.Copy),
                 reads=[psn], writes=[dst_name])


D_MODEL, SEQ, D_FF = 4096, 8192, 11008
D_IN, D_IN_PAD = 9776, 9856
NM_OWN = SEQ // 128 // NCORES
NT_OWN = NM_OWN * 128
QSCALE = 128 ** -0.5
KC = D_MODEL // 128
C_QSB, C_KSB, C_VSB, C_QN = 0, 2048, 4096, 6144
C_KC, C_VC, C_KSL, C_VSL, C_KW, C_VW, C_G = 8192, 8448, 8704, 8960, 9216, 9472, 9728
KROW_SB, KROW_KC, KROW_VC, KROW_KSL, KROW_KW, KROWS = 0, 2048, 2304, 2560, 2816, 3072
VCOL_SB, VCOL_VSL, VCOL_VW, VCOLS = 0, 2048, 2304, 2560
QROW_SB, QROW_N, QROWS = 0, 2048, 4096


def own_chunk(m):
    return 8 * m + 7


def small_tiles(cx, sq_ap, sq_name):
    return {"sq": (sq_ap, sq_name), "ss": (cx.sb("ss", (128, 4), F32), "ss"),
            "eps": (cx.sb("epsc", (128, 1), F32), "epsc")}


def phase_1a(cx, io):
    P = cx.P
    xu, w, qT, gnd = io["xu"], io["w_in"], io["qT"], io["gn"]
    ident = cx.sb("ident_sb", (128, 128), F32)
    gain = cx.sb("gain_sb", (128, D_MODEL), F32)
    hT = cx.sb("hT", (128, KC, NT_OWN), BF16)
    xts = [(cx.sb("xt%d" % i, (128, D_MODEL), F32), "xt%d" % i) for i in range(2)]
    sqt = cx.sb("sq", (128, D_MODEL), BF16)
    small = small_tiles(cx, sqt, "sq")
    pss = [(cx.ps("ps%d" % i), "ps%d" % i) for i in range(8)]
    ws = WStream(cx, "w", KC, 256)
    ostg = [(cx.sb("ostg%d" % i, (128, NT_OWN), BF16), "ostg%d" % i) for i in range(2)]
    gstg = [(cx.sb("gstg%d" % i, (128, 128), F32), "gstg%d" % i) for i in range(2)]

    P.dma("sp", lambda e: e.dma_start(out=ident[:], in_=io["ident"]), writes=["ident"], key="c_ident")
    P.dma("sp", lambda e: e.dma_start(out=gain[:], in_=io["g_attn"]), writes=["gain"], key="c_gain")
    P.op("dve", lambda e: e.memset(small["eps"][0][:], EPS), writes=["epsc"])
    norm_transpose(cx, "l1", lambda b: xu[own_chunk(b) * 128:(own_chunk(b) + 1) * 128, :], lambda b: [],
                   NM_OWN, D_MODEL, gain, "gain", hT, "hT", 0, xts, pss, ident, small)

    qpanels = [(c0 + pcol, r0 + pcol) for (c0, r0) in ((C_QSB, QROW_SB), (C_QN, QROW_N)) for pcol in range(0, 2048, 256)]
    pf = Prefetch([(lambda c=c: ws.begin(w, 0, KC, c, 256)) for (c, _) in qpanels] +
                  [lambda: ws.begin(w, 0, KC, C_G, 128)], ahead=1)
    for qi, (cabs, rabs) in enumerate(qpanels):
        pan, pname = pf.get(qi)
        for j0 in (0, 128):
            og, ogn = ostg[cx.nxt("ostg", 2)]
            for th in range(NT_OWN // 512):
                ps, psn = pss[cx.nxt("mm_ps", 8)]
                for k in range(KC):
                    P.op("pe", lambda e, ps=ps, pan=pan, k=k, j0=j0, th=th: e.matmul(
                        ps[:, 0:512], pan[:, k, j0:j0 + 128], hT[:, k, th * 512:(th + 1) * 512],
                        start=(k == 0), stop=(k == KC - 1)), reads=[pname, "hT"], writes=[psn])
                P.op("act", lambda e, og=og, ps=ps, th=th: e.activation(
                    out=og[:, th * 512:(th + 1) * 512], in_=ps[:, 0:512], func=AF.Copy, scale=QSCALE),
                    reads=[psn], writes=[ogn])
                pf.tick()
                pf.tick()
            row = rabs + j0
            P.dma("sp", lambda e, og=og, row=row: e.dma_start(out=qT[row:row + 128, :], in_=og[:]),
                  reads=[ogn], writes=[], key=ogn + "_st")
    pan, pname = pf.get(len(qpanels))
    for b in range(NM_OWN):
        ps, psn = pss[cx.nxt("mm_ps", 8)]
        for k in range(KC):
            P.op("pe", lambda e, ps=ps, k=k, b=b: e.matmul(ps[:, 0:128], hT[:, k, b * 128:(b + 1) * 128],
                                                           pan[:, k, 0:128], start=(k == 0), stop=(k == KC - 1)),
                 reads=[pname, "hT"], writes=[psn])
        gs, gsn = gstg[cx.nxt("gstg", 2)]
        P.op("act", lambda e, gs=gs, ps=ps: e.activation(out=gs[:], in_=ps[:, 0:128], func=AF.Copy), reads=[psn],
             writes=[gsn])
        P.dma("sp", lambda e, gs=gs, b=b: e.dma_start(out=gnd[b * 128:(b + 1) * 128, :], in_=gs[:]), reads=[gsn],
              writes=[], key=gsn + "_st")


K_PANELS = [(C_KSB + p, KROW_SB + p) for p in range(0, 2048, 256)] + \
           [(C_KC, KROW_KC), (C_VC, KROW_VC), (C_KSL, KROW_KSL), (C_KW, KROW_KW)]
V_PANELS = [(C_VSB + p, VCOL_SB + p) for p in range(0, 2048, 256)] + [(C_VSL, VCOL_VSL), (C_VW, VCOL_VW)]


def phase_1b(cx, io):
    P = cx.P
    xu, w, kT, vtok, wbf = io["xu"], io["w_in"], io["kT"], io["vtok"], io["wbf"]
    NTILE = SEQ // 1024
    ident = cx.sb("ident_sb", (128, 128), F32)
    gain = cx.sb("gain_sb", (128, D_MODEL), F32)
    hT = cx.sb("hT", (128, KC, 1024), BF16)
    xts = [(cx.sb("xt%d" % i, (128, D_MODEL), F32), "xt%d" % i) for i in range(2)]
    sqt = cx.sb("sq", (128, D_MODEL), BF16)
    small = small_tiles(cx, sqt, "sq")
    pss = [(cx.ps("ps%d" % i), "ps%d" % i) for i in range(8)]
    ws = WStream(cx, "w", KC, 256, npanel=3)
    ostg = [(cx.sb("ostg%d" % i, (128, 1024), BF16), "ostg%d" % i) for i in range(2)]
    vstg = [(cx.sb("vstg%d" % i, (128, 8, 256), BF16), "vstg%d" % i) for i in range(2)]

    P.dma("sp", lambda e: e.dma_start(out=ident[:], in_=io["ident"]), writes=["ident"], key="c_ident")
    P.dma("sp", lambda e: e.dma_start(out=gain[:], in_=io["g_attn"]), writes=["gain"], key="c_gain")
    P.op("dve", lambda e: e.memset(small["eps"][0][:], EPS), writes=["epsc"])
    panels = [("k",) + p for p in K_PANELS] + [("v",) + p for p in V_PANELS]

    def first_load(pi_, c0):
        pan, pname = ws.load(w, 0, KC, c0, 256)
        P.dma("sp", lambda e: e.dma_start(out=wbf[pi_], in_=pan[:, 0:KC, 0:256]), reads=[pname], writes=["wbf"],
              key="wbf_st")
        return pan, pname

    thunks = []
    for t in range(NTILE):
        for pi_, (kind, c0, r0) in enumerate(panels):
            if t == 0:
                thunks.append(lambda pi_=pi_, c0=c0: first_load(pi_, c0))
            else:
                thunks.append(lambda pi_=pi_: ws.load_bf16(wbf[pi_], KC, 256, "wbf"))
    pf = Prefetch(thunks, ahead=2)
    for t in range(NTILE):
        norm_transpose(cx, "l1", lambda b, t=t: xu[t * 1024 + b * 128:t * 1024 + (b + 1) * 128, :], lambda b: [],
                       8, D_MODEL, gain, "gain", hT, "hT", 0, xts, pss, ident, small)
        for pi_, (kind, c0, r0) in enumerate(panels):
            pan, pname = pf.get(t * len(panels) + pi_)
            if kind == "k":
                for j0 in (0, 128):
                    og, ogn = ostg[cx.nxt("ostg", 2)]
                    for th in range(2):
                        ps, psn = pss[cx.nxt("mm_ps", 8)]
                        for k in range(KC):
                            P.op("pe", lambda e, ps=ps, pan=pan, k=k, j0=j0, th=th: e.matmul(
                                ps[:, 0:512], pan[:, k, j0:j0 + 128], hT[:, k, th * 512:(th + 1) * 512],
                                start=(k == 0), stop=(k == KC - 1)), reads=[pname, "hT"], writes=[psn])
                        P.op("act", lambda e, og=og, ps=ps, th=th: e.activation(
                            out=og[:, th * 512:(th + 1) * 512], in_=ps[:, 0:512], func=AF.Copy),
                            reads=[psn], writes=[ogn])
                    row = r0 + j0
                    P.dma("sp", lambda e, og=og, row=row, t=t: e.dma_start(
                        out=kT[row:row + 128, t * 1024:(t + 1) * 1024], in_=og[:]), reads=[ogn], writes=[],
                        key=ogn + "_st")
            else:
                vs, vsn = vstg[cx.nxt("vstg", 2)]
                for b in range(8):
                    ps, psn = pss[cx.nxt("mm_ps", 8)]
                    for k in range(KC):
                        P.op("pe", lambda e, ps=ps, pan=pan, k=k, b=b: e.matmul(
                            ps[:, 0:256], hT[:, k, b * 128:(b + 1) * 128], pan[:, k, 0:256],
                            start=(k == 0), stop=(k == KC - 1)), reads=[pname, "hT"], writes=[psn])
                    P.op("act", lambda e, vs=vs, ps=ps, b=b: e.activation(out=vs[:, b, :], in_=ps[:, 0:256],
                                                                          func=AF.Copy), reads=[psn], writes=[vsn])
                dst = vtok[t * 1024:(t + 1) * 1024, r0:r0 + 256].rearrange("(b p) c -> p b c", p=128)
                P.dma("sp", lambda e, vs=vs, dst=dst: e.dma_start(out=dst, in_=vs[:]), reads=[vsn], writes=[],
                      key=vsn + "_st")


def sb_consts():
    j = np.arange(128)
    negU = np.where(j[:, None] >= j[None, :], -1.0, 0.0).astype(NPBF)
    tri = (j[:, None] < j[None, :]).astype(np.float32).astype(NPBF)
    negones = np.full((128, 1), -1.0, dtype=np.float32).astype(NPBF)
    return {"negU": negU, "tri_strict": tri, "negones": negones}


def nsa_slopes():
    h = np.arange(1, 17, dtype=np.float32)
    return (2.0 ** (-8.0 * h / 16)).astype(np.float32)


def nsa_tables(c, NM):
    NJ, NCU, NUC = 16 * NM, 64 * NM, 8 * NM
    NCC = max(1, NCU // 128)
    sl = nsa_slopes()
    i = np.arange(128)
    T = {}
    rel = np.arange(NUC)
    T["ak"] = (sl[None, :, None] * (i[:, None, None] - 64.0 - 128.0 * rel[None, None, :])).astype(np.float32)
    ac = np.zeros((128, 16, NM, NCC), np.float32)
    cm = np.zeros((128, NM, NCC, 128), np.float32)
    wm = np.zeros((128, NM, 5, 128), np.float32)
    bonus = np.zeros((128, NM, NJ), np.float32)
    tl = np.arange(128)
    for m in range(NM):
        u0 = 128 * (8 * m + 7)
        ut = u0 + tl
        for ncc in range(NCC):
            nu = 128 * ncc + i
            cend = 16 * nu + 31
            ac[:, :, m, ncc] = sl[None, :] * (cend[:, None] - (u0 + 64.0))
            cm[:, m, ncc, :] = ((cend[:, None] <= ut[None, :]) & (nu[:, None] >= 8 * (7 - c))).astype(np.float32)
        for r in range(5):
            uk = 128 * (8 * m + 3 + r) + i
            d = ut[None, :] - uk[:, None]
            wm[:, m, r, :] = ((d >= 0) & (d < 512) & (uk[:, None] >= 128 * (7 - c))).astype(np.float32)
        cur = ut // 64
        jj = np.arange(NJ)
        valid = (jj[None, :] <= cur[:, None]) & (jj[None, :] >= 2 * (7 - c))
        forced = (jj[None, :] == 2 * (7 - c)) | (jj[None, :] == cur[:, None]) | (jj[None, :] == cur[:, None] - 1)
        bonus[:, m, :] = np.where(valid, np.where(forced, 1e6, 0.0), -1e30)
    T["ac"] = np.minimum(ac, 45.0)
    T["cm"] = cm.astype(NPBF)
    T["wm"] = wm.astype(NPBF)
    T["bonus"] = bonus
    T["ex"] = np.broadcast_to((np.arange(NJ) >= 2 * (7 - c)).astype(np.float32), (128, NJ)).copy()
    T["tri_incl"] = (i[:, None] <= i[None, :]).astype(np.float32).astype(NPBF)
    T["exf"] = np.broadcast_to((np.arange(8) >= (7 - c)).astype(np.float32), (128, 8)).copy()
    E = np.zeros((NJ, NUC, 128), np.float32)
    for uc in range(NUC):
        for half in range(2):
            if 2 * uc + half < NJ:
                E[2 * uc + half, uc, half * 64:(half + 1) * 64] = 1.0
    T["E"] = E.astype(NPBF)
    caug = np.zeros((128, NCC, 1 + NJ), np.float32)
    caug[:, :, 0] = 1.0
    for ncc in range(NCC):
        for il in range(128):
            nu = 128 * ncc + il
            for j in range(NJ):
                for mm in range(4):
                    for nn in range(2):
                        if 4 * j - mm - nn == nu:
                            caug[il, ncc, 1 + j] += 1.0
    T["caug"] = caug.astype(NPBF)
    T["ident"] = np.eye(128, dtype=np.float32)
    return T


def _bcast_mid(ap2d, n):
    a = ap2d.ap
    return bass.AP(ap2d.tensor, ap2d.offset, [list(a[0]), [0, n], list(a[-1])])


def phase_2a(cx, io):
    P = cx.P
    S = SEQ
    NCH = S // 128
    kTd, vtokd, qTd, out = io["kT"], io["vtok"], io["qT"], io["osb"]
    negU = cx.sb("negU_sb", (128, 128), BF16)
    tri = cx.sb("tri_sb", (128, 128), BF16)
    negones = cx.sb("negones_sb", (128, 1), BF16)
    exf = cx.sb("exf_sb", (128, 8), F32)
    P.dma("sp", lambda e: e.dma_start(out=negU[:], in_=io["negU"]), writes=["negU"], key="ld_c0")
    P.dma("sp", lambda e: e.dma_start(out=tri[:], in_=io["tri_strict"]), writes=["tri"], key="ld_c1")
    P.dma("sp", lambda e: e.dma_start(out=negones[:], in_=io["negones"]), writes=["negones"], key="ld_c2")
    P.dma("sp", lambda e: e.dma_start(out=exf[:], in_=io["exf"]), writes=["exf"], key="ld_c3")
    HPC = 2
    sets = []
    for s_ in range(2):
        sets.append({
            "k": (cx.sb("kTb%d" % s_, (128, HPC, S), BF16), "kTb%d" % s_),
            "v": (cx.sb("vb%d" % s_, (128, HPC, NCH, 128), BF16), "vb%d" % s_),
            "q": (cx.sb("qb%d" % s_, (128, HPC, NT_OWN), BF16), "qb%d" % s_)})
    psZ = [[(cx.ps("psZ%d_%d" % (i, j)), "psZ%d_%d" % (i, j)) for j in range(2)] for i in range(HPC)]
    psC = [(cx.ps("psC%d" % i, (128, 8)), "psC%d" % i) for i in range(HPC)]
    psO = [(cx.ps("psO%d" % i), "psO%d" % i) for i in range(HPC)]
    esb = [[(cx.sb("esb%d_%d" % (i, j), (128, 512), F32), "esb%d_%d" % (i, j)) for j in range(2)] for i in range(HPC)]
    spb = [[(cx.sb("spb%d_%d" % (i, j), (128, 512), BF16), "spb%d_%d" % (i, j)) for j in range(2)] for i in range(HPC)]
    Ab = [[(cx.sb("Ab%d_%d" % (i, j), (128, 512), BF16), "Ab%d_%d" % (i, j)) for j in range(2)] for i in range(HPC)]
    acc = [(cx.sb("acc%d" % i, (128, 4, 128), F32), "acc%d" % i) for i in range(HPC * 2)]
    car = [(cx.sb("car%d" % i, (128, 4), F32), "car%d" % i) for i in range(HPC)]
    ecb = [(cx.sb("ec%d" % i, (128, 4), F32), "ec%d" % i) for i in range(HPC)]

    def stage_a(it):
        hh, uc, c0, q0, par = it["hh"], it["uc"], it["c0"], it["q0"], it["par"]
        (kT, kTn), (qT, qn) = it["k"], it["q"]
        pz, pzn = psZ[hh][par]
        es, esn = esb[hh][par]
        sp, spn = spb[hh][par]
        P.op("pe", lambda e: e.matmul(pz[:, c0:512], kT[:, hh, uc * 128:(uc + 1) * 128], qT[:, hh, q0 + c0:q0 + 512],
                                      start=True, stop=False), reads=[kTn, qn], writes=[pzn])
        P.op("act", lambda e: e.activation(out=es[:, c0:512], in_=pz[:, c0:512], func=AF.Exp), reads=[pzn],
             writes=[esn])
        P.op("act", lambda e: e.activation(out=sp[:, c0:512], in_=es[:, c0:512], func=AF.Ln, bias=1.0, scale=1.0),
             reads=[esn], writes=[spn])
        if it["diag"]:
            P.op("dve", lambda e: e.tensor_tensor(out=sp[:, c0:c0 + 128], in0=sp[:, c0:c0 + 128], in1=tri[:],
                                                  op=ALU.mult), reads=[spn, "tri"], writes=[spn])
        if uc < 7:
            P.op("dve", lambda e: e.tensor_scalar(out=sp[:, c0:512], in0=sp[:, c0:512], scalar1=exf[:, uc:uc + 1],
                                                  scalar2=None, op0=ALU.mult), reads=[spn, "exf"], writes=[spn])

    def stage_b(it):
        hh, c0, b0, par = it["hh"], it["c0"], it["b0"], it["par"]
        pz, pzn = psZ[hh][par]
        pc, pcn = psC[hh]
        sp, spn = spb[hh][par]
        A, An = Ab[hh][par]
        P.op("pe", lambda e: e.matmul(pz[:, c0:512], negU[:], sp[:, c0:512], start=False, stop=True),
             reads=[spn, "negU"], writes=[pzn])
        for b in range(b0, 4):
            P.op("pe", lambda e, b=b: e.matmul(pc[:, b:b + 1], sp[:, b * 128:(b + 1) * 128], negones[:], start=True,
                                               stop=True), reads=[spn, "negones"], writes=[pcn])
        P.op("act", lambda e: e.activation(out=A[:, c0:512], in_=pz[:, c0:512], func=AF.Exp), reads=[pzn],
             writes=[An])
        if it["diag"]:
            P.op("dve", lambda e: e.tensor_tensor(out=A[:, c0:c0 + 128], in0=A[:, c0:c0 + 128], in1=tri[:],
                                                  op=ALU.mult), reads=[An, "tri"], writes=[An])

    def stage_c(it):
        hh, uc, b0, par, uq = it["hh"], it["uc"], it["b0"], it["par"], it["uq"]
        (v, vn) = it["v"]
        pc, pcn = psC[hh]
        po, pon = psO[hh]
        A, An = Ab[hh][par]
        ac, acn = it["acc"]
        cr, crn = car[hh]
        ec, ecn = ecb[hh]
        for b in range(b0, 4):
            P.op("pe", lambda e, b=b: e.matmul(po[:, b * 128:(b + 1) * 128], A[:, b * 128:(b + 1) * 128],
                                               v[:, hh, uc, :], start=True, stop=True), reads=[An, vn], writes=[pon])
        for b in range(b0, 4):
            if uc == uq[b]:
                P.op("dve", lambda e, b=b: e.tensor_copy(out=ac[:, b, :], in_=po[:, b * 128:(b + 1) * 128]),
                     reads=[pon], writes=[acn])
            else:
                P.op("dve", lambda e, b=b: e.scalar_tensor_tensor(
                    out=ac[:, b, :], in0=po[:, b * 128:(b + 1) * 128], scalar=ec[:, b:b + 1], in1=ac[:, b, :],
                    op0=ALU.mult, op1=ALU.add), reads=[pon, ecn, acn], writes=[acn])
        if uc > 0:
            if it["diag"]:
                P.op("dve", lambda e: e.tensor_copy(out=cr[:, b0:b0 + 1], in_=pc[:, b0:b0 + 1]), reads=[pcn],
                     writes=[crn])
                if b0 + 1 < 4:
                    P.op("dve", lambda e: e.tensor_tensor(out=cr[:, b0 + 1:4], in0=cr[:, b0 + 1:4],
                                                          in1=pc[:, b0 + 1:4], op=ALU.add), reads=[pcn, crn],
                         writes=[crn])
            else:
                P.op("dve", lambda e: e.tensor_tensor(out=cr[:, b0:4], in0=cr[:, b0:4], in1=pc[:, b0:4], op=ALU.add),
                     reads=[pcn, crn], writes=[crn])
            P.op("act", lambda e: e.activation(out=ec[:, b0:4], in_=cr[:, b0:4], func=AF.Exp), reads=[crn],
                 writes=[ecn])
        if it["last"]:
            h, G = it["h"], it["G"]
            dst = out[512 * G:512 * G + 512, h * 128:(h + 1) * 128].rearrange("(b p) d -> p b d", p=128)
            P.dma("sp", lambda e: e.dma_start(out=dst, in_=ac[:]), reads=[acn], key=acn + "_st")

    for pair in range(16 // HPC):
        st = sets[pair % 2]
        (kT, kTn), (v, vn), (qT, qn) = st["k"], st["v"], st["q"]
        for hh in range(HPC):
            h = pair * HPC + hh
            P.dma("sp", lambda e, hh=hh, h=h, kT=kT: e.dma_start(
                out=kT[:, hh, :], in_=kTd[KROW_SB + 128 * h:KROW_SB + 128 * (h + 1), :]), writes=[kTn],
                key=kTn + "_%d" % hh)
            P.dma("sp", lambda e, hh=hh, h=h, v=v: e.dma_start(
                out=v[:, hh, :, :],
                in_=vtokd[:, VCOL_SB + 128 * h:VCOL_SB + 128 * (h + 1)].rearrange("(c i) d -> i c d", i=128)),
                writes=[vn], key=vn + "_%d" % hh)
            P.dma("sp", lambda e, hh=hh, h=h, qT=qT: e.dma_start(
                out=qT[:, hh, :], in_=qTd[QROW_SB + 128 * h:QROW_SB + 128 * (h + 1), :]), writes=[qn],
                key=qn + "_%d" % hh)
        items = []
        step = 0
        for G in range(NM_OWN // 4):
            uq = [own_chunk(4 * G + b) for b in range(4)]
            accs = {hh: acc[cx.nxt("acc%d" % hh, 2) * HPC + hh] for hh in range(HPC)}
            for uc in range(uq[3], -1, -1):
                b0 = min(b for b in range(4) if uq[b] >= uc)
                for hh in range(HPC):
                    items.append({"hh": hh, "h": pair * HPC + hh, "G": G, "uc": uc, "b0": b0, "c0": b0 * 128,
                                  "diag": uc == uq[b0], "par": step % 2, "uq": uq, "q0": 512 * G, "acc": accs[hh],
                                  "last": uc == 0, "k": (kT, kTn), "q": (qT, qn), "v": (v, vn)})
                step += 1
        n = len(items)
        for i in range(n + 2):
            if i < n:
                stage_a(items[i])
            if 0 <= i - 1 < n:
                stage_b(items[i - 1])
            if 0 <= i - 2 < n:
                stage_c(items[i - 2])


def phase_2b(cx, io):
    P = cx.P
    S, NM = SEQ, NM_OWN
    NJ, NCU, NUC = 16 * NM, 64 * NM, 8 * NM
    NCC = max(1, NCU // 128)
    NT = NM * 128
    NCV = 129 + NJ
    kTd, vtokd, qTd = io["kT"], io["vtok"], io["qT"]
    d_w1 = {"k": io["w_k1"], "v": io["w_v1"]}
    d_w2 = {"k": io["w_k2"], "v": io["w_v2"]}
    d_pos = {"k": io["posTk"], "v": io["posTv"]}
    d_ak, d_ac, d_cm, d_wm, d_tri = io["ak"], io["ac"], io["cm"], io["wm"], io["tri_incl"]
    d_bonus, d_ex, d_E, d_caug, d_ident = io["bonus"], io["ex"], io["E"], io["caug"], io["ident"]
    d_gn = io["gn"][:, 0:48].rearrange("(m p) c -> p m c", p=128)
    out = io["onsa"]

    def const(name, d, shape, dt):
        t = cx.sb(name + "_sb", shape, dt)
        P.dma("sp", lambda e: e.dma_start(out=t[:], in_=d), writes=[name], key="ld_" + name)
        return t

    ak = const("ak", d_ak, (128, 16, NUC), F32)
    ac = const("ac", d_ac, (128, 16, NM, NCC), F32)
    cm = const("cm", d_cm, (128, NM, NCC, 128), BF16)
    wm = const("wm", d_wm, (128, NM, 5, 128), BF16)
    tri = const("tri", d_tri, (128, 128), BF16)
    bonus = const("bonus", d_bonus, (128, NM, NJ), F32)
    ex = const("ex", d_ex, (128, NJ), F32)
    E = const("E", d_E, (NJ, NUC, 128), BF16)
    ident = const("ident", d_ident, (128, 128), F32)
    gn = const("gn", d_gn, (128, NM, 48), F32)

    qg = cx.sb("qg", (128, 8, NT), BF16)
    ksl = cx.sb("ksl", (128, S), BF16)
    vsl = cx.sb("vsl_sb", (128, NUC, 129), BF16)
    kw = cx.sb("kw", (128, NM, 5, 128), BF16)
    vw = cx.sb("vw_sb", (128, NM, 5, 129), BF16)
    cbuf = cx.sb("cbuf", (128, S + 16), BF16)
    kcmpT = cx.sb("kcmpT", (128, NCU), BF16)
    vcmp = cx.sb("vcmp", (128, NCC, NCV), BF16)
    gates = cx.sb("gates", (128, NM, 48), F32)
    gtmp = cx.sb("gtmp", (128, NM, 48), F32)
    posf = cx.sb("posf", (128, 32), F32)
    posb = cx.sb("posb", (128, 32), BF16)
    biasT = cx.sb("biasT", (128, 2), F32)
    HT = cx.sb("HT", (128, 2, NCU), BF16)
    ub = cx.sb("ub", (128, NCU), F32)
    u2 = cx.sb("u2", (128, NCU), F32)
    ws = WStream(cx, "w", 32, 256, kpiece=4, npanel=1, nstage=2)
    combs = [(cx.sb("comb%d" % i, (128, 8, 128), F32), "comb%d" % i) for i in range(2)]
    imp = cx.sb("imp", (128, NJ), F32)
    score = cx.sb("score", (128, NJ), F32)
    score2 = cx.sb("score2", (128, NJ), F32)
    m8 = cx.sb("m8", (128, 16), F32)
    sel = cx.sb("sel", (128, NJ), F32)
    selT = cx.sb("selT", (NJ, 128), BF16)
    Pts = [(cx.sb("Pt%d" % i, (128, 512), BF16), "Pt%d" % i) for i in range(2)]
    mts = [(cx.sb("mt%d" % i, (128, 128), BF16), "mt%d" % i) for i in range(2)]
    rts = [(cx.sb("rt%d" % i, (128, 4), F32), "rt%d" % i) for i in range(4)]
    psS = [(cx.ps("psS%d" % i), "psS%d" % i) for i in range(2)]
    psA = [(cx.ps("psA%d" % i), "psA%d" % i) for i in range(4)]
    psM = (cx.ps("psM"), "psM")
    psX = (cx.ps("psX"), "psX")

    P.op("pool", lambda e: e.memset(vsl[:, :, 128:129], 1.0), writes=["vsl"])
    P.op("pool", lambda e: e.memset(vw[:, :, :, 128:129], 1.0), writes=["vw"])
    P.op("pool", lambda e: e.memset(cbuf[:, S:S + 16], 0.0), writes=["cbuf"])
    P.op("act", lambda e: e.activation(out=gtmp[:], in_=gn[:], func=AF.Exp, scale=-1.0), reads=["gn"], writes=["gtmp"])
    P.op("dve", lambda e: e.tensor_scalar(out=gtmp[:], in0=gtmp[:], scalar1=1.0, scalar2=None, op0=ALU.add),
         reads=["gtmp"], writes=["gtmp"])
    P.op("dve", lambda e: e.reciprocal(out=gates[:], in_=gtmp[:]), reads=["gtmp"], writes=["gates"])

    def mlp(g, which):
        r0 = (KROW_KC if which == "k" else KROW_VC) + 128 * g
        P.dma("sp", lambda e: e.dma_start(out=cbuf[:, 0:S], in_=kTd[r0:r0 + 128, :]), writes=["cbuf"], key="ld_cbuf")
        P.dma("sp", lambda e: e.dma_start(out=posf[:], in_=d_pos[which]), writes=["posf"], key="ld_pos")
        P.op("pool", lambda e: e.tensor_copy(out=posb[:], in_=posf[:]), reads=["posf"], writes=["posb"])
        pan, pname = ws.load(d_w1[which], 0, 32, 0, 256)
        px, pxn = psX
        for hc in range(2):
            for l in range(32):
                P.op("pe", lambda e, hc=hc, l=l: e.matmul(px[:, hc:hc + 1], pan[:, l, hc * 128:(hc + 1) * 128],
                                                          posb[:, l:l + 1], start=(l == 0), stop=(l == 31)),
                     reads=[pname, "posb"], writes=[pxn])
            P.op("dve", lambda e, hc=hc: e.tensor_copy(out=biasT[:, hc:hc + 1], in_=px[:, hc:hc + 1]), reads=[pxn],
                 writes=["biasT"])
        for hc in range(2):
            ps, psn = psS[hc]
            for l in range(32):
                P.op("pe", lambda e, ps=ps, hc=hc, l=l: e.matmul(ps[:, 0:NCU], pan[:, l, hc * 128:(hc + 1) * 128],
                                                                cbuf[:, l:l + 16 * (NCU - 1) + 1:16], start=(l == 0),
                                                                stop=(l == 31)),
                     reads=[pname, "cbuf"], writes=[psn])
            P.op("act", lambda e, ps=ps, hc=hc: e.activation(out=ub[:], in_=ps[:, 0:NCU], func=AF.Identity,
                                                            bias=biasT[:, hc:hc + 1], scale=1.0),
                 reads=[psn, "biasT"], writes=["ub"])
            P.op("dve", lambda e: e.tensor_tensor(out=u2[:], in0=ub[:], in1=ub[:], op=ALU.mult), reads=["ub"],
                 writes=["u2"])
            P.op("dve", lambda e: e.tensor_scalar(out=u2[:], in0=u2[:], scalar1=0.044715, scalar2=1.0, op0=ALU.mult,
                                                  op1=ALU.add), reads=["u2"], writes=["u2"])
            P.op("dve", lambda e: e.tensor_tensor(out=u2[:], in0=u2[:], in1=ub[:], op=ALU.mult), reads=["u2", "ub"],
                 writes=["u2"])
            P.op("act", lambda e: e.activation(out=u2[:], in_=u2[:], func=AF.Exp, scale=-1.5957691216057308),
                 reads=["u2"], writes=["u2"])
            P.op("dve", lambda e: e.tensor_scalar(out=u2[:], in0=u2[:], scalar1=1.0, scalar2=None, op0=ALU.add),
                 reads=["u2"], writes=["u2"])
            P.op("dve", lambda e: e.reciprocal(out=u2[:], in_=u2[:]), reads=["u2"], writes=["u2"])
            P.op("dve", lambda e, hc=hc: e.tensor_tensor(out=HT[:, hc, :], in0=u2[:], in1=ub[:], op=ALU.mult),
                 reads=["u2", "ub"], writes=["HT"])
        pan2, p2name = ws.load(d_w2[which], 0, 2, 0, 128)
        if which == "k":
            ps, psn = psS[0]
            for hc in range(2):
                P.op("pe", lambda e, hc=hc: e.matmul(ps[:, 0:NCU], pan2[:, hc, 0:128], HT[:, hc, :], start=(hc == 0),
                                                     stop=(hc == 1)), reads=[p2name, "HT"], writes=[psn])
            P.op("act", lambda e: e.activation(out=kcmpT[:], in_=ps[:, 0:NCU], func=AF.Copy), reads=[psn],
                 writes=["kcmpT"])
        else:
            for ncc in range(NCC):
                ps, psn = psS[ncc % 2]
                for hc in range(2):
                    P.op("pe", lambda e, ps=ps, hc=hc, ncc=ncc: e.matmul(ps[:, 0:128],
                                                                        HT[:, hc, ncc * 128:(ncc + 1) * 128],
                                                                        pan2[:, hc, 0:128], start=(hc == 0),
                                                                        stop=(hc == 1)),
                         reads=[p2name, "HT"], writes=[psn])
                P.op("act", lambda e, ps=ps, ncc=ncc: e.activation(out=vcmp[:, ncc, 0:128], in_=ps[:, 0:128],
                                                                  func=AF.Copy), reads=[psn], writes=["vcmp"])

    def chunk(g, m, quad, keysT, bias_ap_fn, mask2d, mask_name, vrhs, vname, ncv, first, last, kname):
        si = cx.nxt("psS", 2)
        ps, psn = psS[si]
        pt, ptn = Pts[si]
        rhs = qg[:, 4 * quad:4 * quad + 4, m * 128:(m + 1) * 128]
        P.op("pe", lambda e: e.matmul(ps[:, 0:512].rearrange("p (h t) -> p h t", h=4), keysT, rhs, start=True,
                                      stop=True), reads=[kname, "qg"], writes=[psn])
        for h in range(4):
            b = bias_ap_fn(8 * g + 4 * quad + h)
            P.op("act", lambda e, h=h, b=b: e.activation(out=pt[:, h * 128:(h + 1) * 128],
                                                          in_=ps[:, h * 128:(h + 1) * 128], func=AF.Exp, bias=b,
                                                          scale=1.0), reads=[psn, "ak", "ac"], writes=[ptn])
        if mask2d is not None:
            pt3 = pt[:, 0:512].rearrange("p (h t) -> p h t", h=4)
            P.op("dve", lambda e: e.tensor_tensor(out=pt3, in0=pt3, in1=_bcast_mid(mask2d, 4), op=ALU.mult),
                 reads=[ptn, mask_name], writes=[ptn])
        def s2():
            for h in range(4):
                pa, pan_ = psA[h]
                P.op("pe", lambda e, h=h, pa=pa: e.matmul(pa[:, 0:ncv], pt[:, h * 128:(h + 1) * 128], vrhs,
                                                          start=first, stop=last), reads=[ptn, vname], writes=[pan_])
        return s2

    def evac(g, m, quad, br, comb, cname, first_branch, do_imp):
        for h in range(4):
            pa, pan_ = psA[h]
            hl = 4 * quad + h
            head = 8 * g + hl
            rt, rtn = rts[cx.nxt("rt", 4)]
            P.op("dve", lambda e, pa=pa, rt=rt: e.tensor_scalar(out=rt[:, 0:1], in0=pa[:, 128:129], scalar1=1e-30,
                                                               scalar2=None, op0=ALU.max), reads=[pan_],
                 writes=[rtn])
            P.op("dve", lambda e, rt=rt: e.reciprocal(out=rt[:, 1:2], in_=rt[:, 0:1]), reads=[rtn], writes=[rtn])
            col = head * 3 + br
            P.op("dve", lambda e, rt=rt, col=col: e.tensor_tensor(out=rt[:, 2:3], in0=rt[:, 1:2],
                                                                 in1=gates[:, m, col:col + 1], op=ALU.mult),
                 reads=[rtn, "gates"], writes=[rtn])
            if first_branch:
                P.op("dve", lambda e, pa=pa, rt=rt, hl=hl: e.tensor_scalar(out=comb[:, hl, :], in0=pa[:, 0:128],
                                                                          scalar1=rt[:, 2:3], scalar2=None,
                                                                          op0=ALU.mult), reads=[pan_, rtn],
                     writes=[cname])
            else:
                P.op("dve", lambda e, pa=pa, rt=rt, hl=hl: e.scalar_tensor_tensor(
                    out=comb[:, hl, :], in0=pa[:, 0:128], scalar=rt[:, 2:3], in1=comb[:, hl, :], op0=ALU.mult,
                    op1=ALU.add), reads=[pan_, rtn, cname], writes=[cname])
            if do_imp:
                if hl == 0:
                    P.op("dve", lambda e, pa=pa, rt=rt: e.tensor_scalar(out=imp[:], in0=pa[:, 129:129 + NJ],
                                                                       scalar1=rt[:, 1:2], scalar2=None,
                                                                       op0=ALU.mult), reads=[pan_, rtn],
                         writes=["imp"])
                else:
                    P.op("dve", lambda e, pa=pa, rt=rt: e.scalar_tensor_tensor(
                        out=imp[:], in0=pa[:, 129:129 + NJ], scalar=rt[:, 1:2], in1=imp[:], op0=ALU.mult,
                        op1=ALU.add), reads=[pan_, rtn, "imp"], writes=["imp"])

    class Skew:
        pending, after = None, []

        def chunk(self, s1):
            s2 = s1()
            self.flush()
            self.pending = s2

        def flush(self):
            if self.pending is not None:
                self.pending()
                self.pending = None
            for f in self.after:
                f()
            self.after = []

        def defer(self, f):
            if self.pending is None:
                f()
            else:
                self.after.append(f)

    sk = Skew()
    for g in range(2):
        sk.flush()
        P.dma("sp", lambda e, g=g: e.dma_start(
            out=qg[:], in_=qTd[QROW_N + 1024 * g:QROW_N + 1024 * (g + 1), :].rearrange("(h d) t -> d h t", d=128)),
            writes=["qg"], key="ld_qg")
        P.dma("sp", lambda e, g=g: e.dma_start(out=ksl[:], in_=kTd[KROW_KSL + 128 * g:KROW_KSL + 128 * (g + 1), :]),
              writes=["ksl"], key="ld_ksl")
        P.dma("sp", lambda e, g=g: e.dma_start(
            out=vsl[:, :, 0:128],
            in_=vtokd[:, VCOL_VSL + 128 * g:VCOL_VSL + 128 * (g + 1)].rearrange("(c i) d -> i c d", i=128)),
            writes=["vsl"], key="ld_vsl")
        for m in range(NM):
            u0 = (8 * m + 3) * 128
            P.dma("sp", lambda e, g=g, m=m, u0=u0: e.dma_start(
                out=kw[:, m, :, :],
                in_=kTd[KROW_KW + 128 * g:KROW_KW + 128 * (g + 1), u0:u0 + 640].rearrange("d (r i) -> d r i", i=128)),
                writes=["kw"], key="ld_kw")
            P.dma("sp", lambda e, g=g, m=m, u0=u0: e.dma_start(
                out=vw[:, m, :, 0:128],
                in_=vtokd[u0:u0 + 640, VCOL_VW + 128 * g:VCOL_VW + 128 * (g + 1)].rearrange("(r i) d -> i r d", i=128)),
                writes=["vw"], key="ld_vw")
        P.dma("sp", lambda e: e.dma_start(out=vcmp[:, :, 128:NCV], in_=d_caug), writes=["vcmp"], key="ld_caug")
        mlp(g, "k")
        mlp(g, "v")
        for m in range(NM):
            comb, cname = combs[cx.nxt("comb", 2)]
            nccs = list(range((64 * m + 62) // 128 + 1))
            for quad in range(2):
                for ii, ncc in enumerate(nccs):
                    sk.chunk(lambda quad=quad, ii=ii, ncc=ncc, m=m: chunk(
                        g, m, quad, kcmpT[:, ncc * 128:(ncc + 1) * 128],
                        lambda head, ncc=ncc: ac[:, head, m, ncc:ncc + 1],
                        cm[:, m, ncc, :], "cm", vcmp[:, ncc, :], "vcmp", NCV, ii == 0, ii == len(nccs) - 1, "kcmpT"))
                sk.defer(lambda g=g, quad=quad, m=m, comb=comb, cname=cname: evac(g, m, quad, 0, comb, cname, True, True))
            sk.flush()
            P.op("dve", lambda e, m=m: e.tensor_tensor(out=score[:], in0=imp[:], in1=bonus[:, m, :], op=ALU.add),
                 reads=["imp", "bonus"], writes=["score"])
            P.op("dve", lambda e: e.max(out=m8[:, 0:8], in_=score[:]), reads=["score"], writes=["m8"])
            P.op("dve", lambda e: e.match_replace(out=score2[:], in_to_replace=m8[:, 0:8], in_values=score[:],
                                                  imm_value=-3.0e38), reads=["score", "m8"], writes=["score2"])
            P.op("dve", lambda e: e.max(out=m8[:, 8:16], in_=score2[:]), reads=["score2"], writes=["m8"])
            P.op("dve", lambda e: e.tensor_scalar(out=sel[:], in0=score[:], scalar1=m8[:, 15:16], scalar2=None,
                                                  op0=ALU.is_ge), reads=["score", "m8"], writes=["sel"])
            P.op("dve", lambda e: e.tensor_tensor(out=sel[:], in0=sel[:], in1=ex[:], op=ALU.mult),
                 reads=["sel", "ex"], writes=["sel"])
            px, pxn = psX
            P.op("pe", lambda e: e.transpose(px[0:NJ, 0:128], sel[:], ident[:]), reads=["sel", "ident"], writes=[pxn])
            P.op("act", lambda e: e.activation(out=selT[:], in_=px[0:NJ, 0:128], func=AF.Copy), reads=[pxn],
                 writes=["selT"])
            for quad in range(2):
                ucs = list(range(8 * m + 8))
                for ii, uc in enumerate(ucs):
                    def s1(quad=quad, ii=ii, uc=uc, m=m, nuc=len(ucs)):
                        pm, pmn = psM if cx.nxt("psM", 2) == 0 else psX
                        mt, mtn = mts[cx.nxt("mt", 2)]
                        P.op("pe", lambda e: e.matmul(pm[:, 0:128], E[:, uc, :], selT[:], start=True, stop=True),
                             reads=["E", "selT"], writes=[pmn])
                        if uc == 8 * m + 7:
                            P.op("dve", lambda e: e.tensor_tensor(out=mt[:], in0=pm[:, 0:128], in1=tri[:],
                                                                  op=ALU.mult), reads=[pmn, "tri"], writes=[mtn])
                        else:
                            P.op("act", lambda e: e.activation(out=mt[:], in_=pm[:, 0:128], func=AF.Copy),
                                 reads=[pmn], writes=[mtn])
                        rel = 8 * m + 7 - uc
                        return chunk(g, m, quad, ksl[:, uc * 128:(uc + 1) * 128],
                                     lambda head, rel=rel: ak[:, head, rel:rel + 1],
                                     mt[:], mtn, vsl[:, uc, :], "vsl", 129, ii == 0, ii == nuc - 1, "ksl")
                    sk.chunk(s1)
                sk.defer(lambda g=g, quad=quad, m=m, comb=comb, cname=cname: evac(g, m, quad, 1, comb, cname, False, False))
            for quad in range(2):
                for r in range(5):
                    sk.chunk(lambda quad=quad, r=r, m=m: chunk(
                        g, m, quad, kw[:, m, r, :], lambda head, r=r: ak[:, head, 4 - r:5 - r],
                        wm[:, m, r, :], "wm", vw[:, m, r, :], "vw", 129, r == 0, r == 4, "kw"))
                sk.defer(lambda g=g, quad=quad, m=m, comb=comb, cname=cname: evac(g, m, quad, 2, comb, cname, False, False))
            dst = out[m * 128:(m + 1) * 128, g * 1024:(g + 1) * 1024]
            sk.defer(lambda comb=comb, dst=dst, cname=cname: P.dma(
                "sp", lambda e: e.dma_start(out=dst, in_=comb[:].rearrange("p h d -> p (h d)")),
                reads=[cname], key=cname + "_st"))
    sk.flush()


def phase_3(cx, io, NPASS=5):
    P = cx.P
    D, DFF, NTOK = D_MODEL, D_FF, NT_OWN
    NB = NTOK // 128
    HALF = D // 2
    xu, osb, onsa, out, x1d, yacc = io["xu"], io["osb"], io["onsa"], io["out"], io["x1d"], io["yacc"]
    w_out, w_gate, w_up, w_down = io["w_out"], io["w_gate"], io["w_up"], io["w_down"]
    xrow = lambda b: xu[own_chunk(b) * 128:(own_chunk(b) + 1) * 128, :]
    npan_ff = DFF // 256
    per = -(-npan_ff // NPASS)
    maxk_act = per * 2

    ident = cx.sb("ident_sb", (128, 128), F32)
    gain = cx.sb("gain_sb", (128, D), F32)
    hT = cx.sb("hT", (128, KC, NTOK), BF16)
    actT = cx.sb("actT", (128, maxk_act, NTOK), BF16)
    xts = [(cx.sb("xt0", (128, D), F32), "xt0")]
    ws = WStream(cx, "w", max(KC, maxk_act), 256, kpiece=4, npanel=3, nstage=4)
    sqv = actT[:].rearrange("p k n -> p (k n)")
    small = small_tiles(cx, sqv, "actT")
    thunks = [(lambda col=cp * 256: ws.begin(w_out, 0, KC, col, 256)) for cp in range(D // 256)]
    for p_ in range(NPASS):
        a0, a1 = p_ * per, min(npan_ff, p_ * per + per)
        if a0 >= a1:
            continue
        for pp_ in range(a0, a1):
            thunks.append(lambda col=pp_ * 256: ws.begin(w_gate, 0, KC, col, 256))
            thunks.append(lambda col=pp_ * 256: ws.begin(w_up, 0, KC, col, 256))
        for cp in range(D // 256):
            thunks.append(lambda col=cp * 256, a0=a0, a1=a1: ws.begin(w_down, a0 * 2, (a1 - a0) * 2, col, 256))
    pf = Prefetch(thunks, ahead=2)
    pfi = [0]

    def next_panel():
        r = pf.get(pfi[0])
        pfi[0] += 1
        return r

    pss = [(cx.ps("ps%d" % i), "ps%d" % i) for i in range(8)]
    ept = [(cx.sb("ept%d" % i, (128, 256), F32), "ept%d" % i) for i in range(2)]
    epo = [(cx.sb("epo%d" % i, (128, 256), F32), "epo%d" % i) for i in range(2)]
    sgt = [(cx.sb("sg%d" % i, (128, 512), F32), "sg%d" % i) for i in range(2)]

    P.dma("sp", lambda e: e.dma_start(out=ident[:], in_=io["ident"]), writes=["ident"], key="c_ident")
    P.op("dve", lambda e: e.memset(small["eps"][0][:], EPS), writes=["epsc"])

    P.dma("sp", lambda e: e.dma_start(out=gain[:, 0:HALF], in_=io["g_sb"]), writes=["gain"], key="c_gain")
    norm_transpose(cx, "sb", lambda b: osb[b * 128:(b + 1) * 128, :], lambda b: [], NB, HALF, gain, "gain", hT, "hT",
                   0, xts, pss, ident, small)
    P.dma("sp", lambda e: e.dma_start(out=gain[:, 0:HALF], in_=io["g_nsa"]), writes=["gain"], key="c_gain")
    norm_transpose(cx, "nsa", lambda b: onsa[b * 128:(b + 1) * 128, :], lambda b: [], NB, HALF, gain, "gain", hT, "hT",
                   KC // 2, xts, pss, ident, small)

    def tok_major_gemm(W, kc0, nk, actbuf, actname, prev_row, prev_name_fn, dst, dst_name_fn):
        for cp in range(D // 256):
            col = cp * 256
            pan, pname = next_panel()
            for b in range(NB):
                ps, psn = pss[cx.nxt("mm_ps", 8)]
                for k in range(nk):
                    P.op("pe", lambda e, ps=ps, pan=pan, k=k, b=b: e.matmul(
                        ps[:, 0:256], actbuf[:, k, b * 128:(b + 1) * 128], pan[:, k, 0:256],
                        start=(k == 0), stop=(k == nk - 1)), reads=[pname, actname], writes=[psn])
                ti = cx.nxt("ept", 2)
                pt, ptn = ept[ti]
                po, pon = epo[ti]
                srcp = prev_row(b)[:, col:col + 256]
                P.dma("sp", lambda e, pt=pt, srcp=srcp: e.dma_start(out=pt[:], in_=srcp),
                      reads=[prev_name_fn(b, cp)], writes=[ptn], key=ptn)
                P.op("dve", lambda e, po=po, ps=ps, pt=pt: e.tensor_tensor(out=po[:], in0=ps[:, 0:256], in1=pt[:],
                                                                          op=ALU.add),
                     reads=[psn, ptn], writes=[pon])
                dstp = dst[b * 128:(b + 1) * 128, col:col + 256]
                P.dma("sp", lambda e, po=po, dstp=dstp: e.dma_start(out=dstp, in_=po[:]),
                      reads=[pon], writes=[dst_name_fn(b, cp)], key=pon + "_st")
                pf.tick()

    tok_major_gemm(w_out, 0, KC, hT, "hT", xrow, lambda b, cp: "x_in", x1d, lambda b, cp: "x1d_%d_%d" % (b, cp))

    P.dma("sp", lambda e: e.dma_start(out=gain[:], in_=io["g_ffn"]), writes=["gain"], key="c_gain")
    norm_transpose(cx, "ffn", lambda b: x1d[b * 128:(b + 1) * 128, :],
                   lambda b: ["x1d_%d_%d" % (b, cp) for cp in range(D // 256)], NB, D, gain, "gain", hT, "hT", 0,
                   xts, pss, ident, small)

    TW = 512
    NTH = NTOK // TW
    lastp = 0
    for p in range(NPASS):
        pan0 = p * per
        pan1 = min(npan_ff, pan0 + per)
        if pan0 >= pan1:
            continue
        lastp = p
        for pp in range(pan0, pan1):
            col = pp * 256
            gpan, gname = next_panel()
            gps = {}
            for j in range(2):
                for th in range(NTH):
                    ps, psn = pss[cx.nxt("mm_ps", 8)]
                    gps[(j, th)] = (ps, psn)
                    for k in range(KC):
                        P.op("pe", lambda e, ps=ps, gpan=gpan, k=k, j=j, th=th: e.matmul(
                            ps[:, 0:TW], gpan[:, k, j * 128:(j + 1) * 128], hT[:, k, th * TW:(th + 1) * TW],
                            start=(k == 0), stop=(k == KC - 1)), reads=[gname, "hT"], writes=[psn])
                    pf.tick()
            upan, uname = next_panel()
            for j in range(2):
                for th in range(NTH):
                    ps, psn = pss[cx.nxt("mm_ps", 8)]
                    for k in range(KC):
                        P.op("pe", lambda e, ps=ps, upan=upan, k=k, j=j, th=th: e.matmul(
                            ps[:, 0:TW], upan[:, k, j * 128:(j + 1) * 128], hT[:, k, th * TW:(th + 1) * TW],
                            start=(k == 0), stop=(k == KC - 1)), reads=[uname, "hT"], writes=[psn])
                    pf.tick()
                    gp, gpn = gps[(j, th)]
                    sg, sgn = sgt[cx.nxt("sg", 2)]
                    P.op("act", lambda e, sg=sg, gp=gp: e.activation(out=sg[:, 0:TW], in_=gp[:, 0:TW], func=AF.Silu),
                         reads=[gpn], writes=[sgn])
                    kk = (pp - pan0) * 2 + j
                    P.op("dve", lambda e, sg=sg, ps=ps, kk=kk, th=th: e.tensor_tensor(
                        out=actT[:, kk, th * TW:(th + 1) * TW], in0=sg[:, 0:TW], in1=ps[:, 0:TW], op=ALU.mult),
                        reads=[sgn, psn], writes=["actT"])
        nk = (pan1 - pan0) * 2
        if p == 0:
            prow, pfn = (lambda b: x1d[b * 128:(b + 1) * 128, :]), (lambda b, cp: "x1d_%d_%d" % (b, cp))
        else:
            prow, pfn = (lambda b: yacc[b * 128:(b + 1) * 128, :]), (lambda b, cp, p=p: "yacc%d_%d_%d" % (p - 1, b, cp))
        tok_major_gemm(w_down, pan0 * 2, nk, actT, "actT", prow, pfn, yacc,
                       lambda b, cp, p=p: "yacc%d_%d_%d" % (p, b, cp))

    P.dma("sp", lambda e: e.dma_start(out=gain[:], in_=io["g_fin"]), writes=["gain"], key="c_gain")
    sq, sqn = small["sq"]
    ss = small["ss"][0]
    for b in range(NB):
        xt, xname = xts[0]
        deps = ["yacc%d_%d_%d" % (lastp, b, cp) for cp in range(D // 256)]
        P.dma("sp", lambda e, b=b: e.dma_start(out=xt[:], in_=yacc[b * 128:(b + 1) * 128, :]), reads=deps,
              writes=[xname], key=xname)
        P.op("act", lambda e: e.activation(out=sq[:, 0:D], in_=xt[:], func=AF.Square, accum_out=ss[:, 0:1]),
             reads=[xname], writes=[sqn, "ss"])
        P.op("act", lambda e: e.activation(out=ss[:, 1:2], in_=ss[:, 0:1], func=AF.Sqrt, scale=1.0 / D,
                                           bias=small["eps"][0][:, 0:1]), reads=["ss", "epsc"], writes=["ssb"])
        P.op("dve", lambda e: e.reciprocal(out=ss[:, 2:3], in_=ss[:, 1:2]), reads=["ssb"], writes=["ssc"])
        P.op("dve", lambda e: e.scalar_tensor_tensor(out=xt[:], in0=xt[:], scalar=ss[:, 2:3], in1=gain[:],
                                                     op0=ALU.mult, op1=ALU.mult),
             reads=[xname, "ssc", "gain"], writes=[xname])
        P.dma("sp", lambda e, b=b: e.dma_start(out=out[b * 128:(b + 1) * 128, :], in_=xt[:]), reads=[xname],
              writes=[], key="out_st")


def _tables_spec():
    NM = NM_OWN
    NJ, NCU, NUC = 16 * NM, 64 * NM, 8 * NM
    NCC = NCU // 128
    return {"ak": ((128, 16, NUC), F32), "ac": ((128, 16, NM, NCC), F32), "cm": ((128, NM, NCC, 128), BF16),
            "wm": ((128, NM, 5, 128), BF16), "tri_incl": ((128, 128), BF16), "bonus": ((128, NM, NJ), F32),
            "ex": ((128, NJ), F32), "E": ((NJ, NUC, 128), BF16), "caug": ((128, NCC, 1 + NJ), BF16),
            "ident": ((128, 128), F32), "exf": ((128, 8), F32), "negU": ((128, 128), BF16),
            "tri_strict": ((128, 128), BF16), "negones": ((128, 1), BF16)}


def build_program(phases=("1a", "1b", "2a", "2b", "3")):
    cx = Ctx()
    io = {}
    io["xu"] = cx.din("xu", (SEQ, D_MODEL), F32)
    io["w_in"] = cx.din("w_in", (D_MODEL, D_IN_PAD), F32)
    for n in ("g_attn", "g_ffn", "g_fin"):
        io[n] = cx.din(n, (128, D_MODEL), F32)
    for n in ("g_sb", "g_nsa"):
        io[n] = cx.din(n, (128, D_MODEL // 2), F32)
    for n, (shape, dt) in _tables_spec().items():
        io[n] = cx.din(n, shape, dt)
    for n in ("w_k1", "w_v1"):
        io[n] = cx.din(n, (4096, 256), F32)
    for n in ("w_k2", "w_v2"):
        io[n] = cx.din(n, (256, 128), F32)
    for n in ("posTk", "posTv"):
        io[n] = cx.din(n, (128, 32), F32)
    io["w_out"] = cx.din("w_out", (D_MODEL, D_MODEL), F32)
    io["w_gate"] = cx.din("w_gate", (D_MODEL, D_FF), F32)
    io["w_up"] = cx.din("w_up", (D_MODEL, D_FF), F32)
    io["w_down"] = cx.din("w_down", (D_FF, D_MODEL), F32)
    io["out"] = cx.dout("out", (NT_OWN, D_MODEL), F32)
    io["qT"] = cx.dint("qT_scr", (QROWS, NT_OWN), BF16)
    io["gn"] = cx.dint("gn_scr", (NT_OWN, 128), F32)
    io["kT"] = cx.dint("kT_scr", (KROWS, SEQ), BF16)
    io["vtok"] = cx.dint("vtok_scr", (SEQ, VCOLS), BF16)
    io["wbf"] = cx.dint("wbf_scr", (len(K_PANELS) + len(V_PANELS), 128, KC, 256), BF16)
    io["osb"] = cx.dint("osb_scr", (NT_OWN, 2048), F32)
    io["onsa"] = cx.dint("onsa_scr", (NT_OWN, 2048), F32)
    io["x1d"] = cx.dint("x1d_scr", (NT_OWN, D_MODEL), F32)
    io["yacc"] = cx.dint("yacc_scr", (NT_OWN, D_MODEL), F32)
    fns = {"1a": phase_1a, "1b": phase_1b, "2a": phase_2a, "2b": phase_2b, "3": phase_3}
    first = True
    for ph in phases:
        if not first:
            cx.begin()
        first = False
        fns[ph](cx, io)
        cx.end()
    return cx.finish()


def _bc(g, n=128):
    g = np.asarray(g, np.float32).reshape(-1)
    return np.ascontiguousarray(np.broadcast_to(g, (n, g.shape[0])))


def _own_rows(c):
    return np.concatenate([np.arange(128) + 128 * (c + 8 * m) for m in range(NM_OWN)])


def kernel(x, attn_norm, w_in, pos_cmp_k, pos_cmp_v, w_cmp_k1, w_cmp_k2, w_cmp_v1, w_cmp_v2,
           norm_sb, norm_nsa, w_out, ffn_norm, w_gate, w_up, w_down, final_norm):
    f32 = lambda a: np.ascontiguousarray(np.asarray(a, np.float32))
    x2 = f32(x)[0]
    cores = list(range(NCORES))
    w_pad = np.zeros((D_MODEL, D_IN_PAD), np.float32)
    w_pad[:, :D_IN] = f32(w_in)[0]
    common = {"w_in": w_pad, "g_attn": _bc(attn_norm), "g_ffn": _bc(ffn_norm), "g_fin": _bc(final_norm),
              "g_sb": _bc(norm_sb), "g_nsa": _bc(norm_nsa),
              "w_k1": f32(w_cmp_k1)[0], "w_k2": f32(w_cmp_k2)[0], "w_v1": f32(w_cmp_v1)[0], "w_v2": f32(w_cmp_v2)[0],
              "posTk": np.ascontiguousarray(f32(pos_cmp_k)[0].T), "posTv": np.ascontiguousarray(f32(pos_cmp_v)[0].T),
              "w_out": f32(w_out)[0], "w_gate": f32(w_gate)[0], "w_up": f32(w_up)[0], "w_down": f32(w_down)[0]}
    common.update(sb_consts())
    in_maps = []
    for c in cores:
        d = dict(common)
        shift = 128 * (7 - c)
        xu = np.zeros((SEQ, D_MODEL), np.float32)
        xu[shift:] = x2[:SEQ - shift]
        d["xu"] = xu
        d.update(nsa_tables(c, NM_OWN))
        in_maps.append(d)
    nc = build_program()
    res = run_bass_kernel_spmd(nc, in_maps, core_ids=cores).results
    out = np.zeros((1, SEQ, D_MODEL), np.float32)
    for c in cores:
        out[0, _own_rows(c)] = np.asarray(res[c]["out"])
    return out
```

```python
import contextlib
import numpy as np
import ml_dtypes
import concourse.bass as bass
import concourse.mybir as mybir
from concourse.bass_utils import run_bass_kernel_spmd

F32 = mybir.dt.float32
BF16 = mybir.dt.bfloat16
AF = mybir.ActivationFunctionType
ALU = mybir.AluOpType
NPBF = ml_dtypes.bfloat16

NCORES = 8
EPS = 1e-6
ALL_ENG = ("pe", "act", "dve", "pool", "sp")


class _Buf:
    __slots__ = ("writer", "readers")

    def __init__(self):
        self.writer = None
        self.readers = []


class _Op:
    __slots__ = ("eng", "fn", "is_dma", "dkey", "dval", "deps", "signal", "count")


class Prog:
    def __init__(self, nc, same_engine_sync=True):
        self.nc = nc
        self.ops = {e: [] for e in ALL_ENG}
        self.bufs = {}
        self.dma_counts = {}
        self.phase_keys = set()
        self.same_engine_sync = same_engine_sync
        self.ecount = {e: 0 for e in ALL_ENG}
        self.semstack = contextlib.ExitStack()
        self.esem = None
        self.dsem = {}
        self.barrier = []

    def _add(self, eng, fn, reads, writes, is_dma=False, dkey=None):
        op = _Op()
        op.eng, op.fn, op.is_dma, op.dkey = eng, fn, is_dma, dkey
        op.dval, op.signal, op.count = None, False, None
        deps = []
        reads, writes = _expand(reads), _expand(writes)
        for r in reads:
            b = self.bufs.get(r)
            if b is None:
                b = self.bufs[r] = _Buf()
            if b.writer is not None:
                deps.append(b.writer)
            b.readers.append(op)
        for w in writes:
            b = self.bufs.get(w)
            if b is None:
                b = self.bufs[w] = _Buf()
            if b.writer is not None:
                deps.append(b.writer)
            deps.extend(b.readers)
            b.writer = op
            b.readers = []
        out, seen = [], set()
        for d in deps:
            if d is op or id(d) in seen:
                continue
            seen.add(id(d))
            if not d.is_dma and d.eng == eng and (eng == "pe" or not self.same_engine_sync):
                continue
            out.append(d)
        op.deps = out
        if is_dma:
            c = self.dma_counts.get(dkey, 0) + 16
            self.dma_counts[dkey] = c
            self.phase_keys.add(dkey)
            op.dval = c
        self.ops[eng].append(op)
        return op

    def op(self, eng, fn, reads=(), writes=()):
        return self._add(eng, fn, reads, writes)

    def dma(self, eng, fn, reads=(), writes=(), key=None):
        return self._add(eng, fn, reads, writes, is_dma=True, dkey=key)

    def flush(self, final_wait_eng="sp"):
        nc = self.nc
        if self.esem is None:
            self.esem = {e: self.semstack.enter_context(nc.semaphore("s_" + e)) for e in ALL_ENG}
        for k in self.dma_counts:
            if k not in self.dsem:
                self.dsem[k] = self.semstack.enter_context(nc.semaphore("d_%d" % len(self.dsem)))
        esem, dsem = self.esem, self.dsem
        for e in ALL_ENG:
            ops = self.ops[e]
            for op in ops:
                for d in op.deps:
                    if not d.is_dma:
                        d.signal = True
            for op in reversed(ops):
                if not op.is_dma:
                    op.signal = True
                    break
        for e in ALL_ENG:
            c = self.ecount[e]
            for op in self.ops[e]:
                if op.signal and not op.is_dma:
                    c += 1
                    op.count = c
            self.ecount[e] = c
        barrier = self.barrier
        with nc.Block() as block:
            engmap = {"pe": block.tensor, "act": block.scalar, "dve": block.vector,
                      "pool": block.gpsimd, "sp": block.sync}

            def make(e):
                def body(eng):
                    known = {}
                    if self.ops[e]:
                        for key, sem, val in barrier:
                            if key == ("e", e):
                                known[key] = val
                                continue
                            eng.wait_ge(sem, val)
                            known[key] = val
                    for op in self.ops[e]:
                        for d in op.deps:
                            if d.is_dma:
                                key, val, sem = ("d", d.dkey), d.dval, dsem[d.dkey]
                            else:
                                key, val, sem = ("e", d.eng), d.count, esem[d.eng]
                            if known.get(key, 0) >= val:
                                continue
                            eng.wait_ge(sem, val)
                            known[key] = val
                        inst = op.fn(eng)
                        if op.is_dma:
                            inst.then_inc(dsem[op.dkey], 16)
                        elif op.signal:
                            inst.then_inc(esem[e], 1)
                    if e == final_wait_eng:
                        for k in self.phase_keys:
                            v = self.dma_counts[k]
                            if known.get(("d", k), 0) < v:
                                eng.wait_ge(dsem[k], v)
                return body

            for e in ALL_ENG:
                engmap[e](make(e))
        self.barrier = [(("e", e), esem[e], self.ecount[e]) for e in ALL_ENG if self.ecount[e] > 0]
        self.barrier += [(("d", k), dsem[k], self.dma_counts[k]) for k in self.dma_counts]
        self.ops = {e: [] for e in ALL_ENG}
        self.bufs = {}
        self.phase_keys = set()

    def close(self):
        self.semstack.close()


class Ctx:
    def __init__(self):
        self.nc = bass.Bass("TRN2", target_bir_lowering=False)
        self.P = Prog(self.nc)
        self.st = None
        self.rot = {}
        self.phase = -1
        self.begin()

    def begin(self):
        self.st = contextlib.ExitStack()
        self.rot = {}
        self.phase += 1

    def end(self):
        self.P.flush()
        self.st.close()
        self.st = None

    def sb(self, name, shape, dt):
        return self.st.enter_context(self.nc.sbuf_tensor("p%d_%s" % (self.phase, name), list(shape), dt))

    def ps(self, name, shape=(128, 512), dt=F32):
        return self.st.enter_context(self.nc.psum_tensor("p%d_%s" % (self.phase, name), list(shape), dt))

    def din(self, name, shape, dt):
        return self.nc.dram_tensor(name, list(shape), dt, kind="ExternalInput").ap()

    def dout(self, name, shape, dt):
        return self.nc.dram_tensor(name, list(shape), dt, kind="ExternalOutput").ap()

    def dint(self, name, shape, dt):
        return self.nc.dram_tensor(name, list(shape), dt, kind="Internal").ap()

    def nxt(self, key, n):
        i = self.rot.get(key, 0)
        self.rot[key] = i + 1
        return i % n

    def finish(self):
        if self.st is not None:
            self.end()
        self.P.close()
        return self.nc


class PanelName(str):
    pass


def _expand(names):
    out = []
    for n in names:
        if isinstance(n, PanelName):
            out += [str(n), str(n) + "_a", str(n) + "_b"]
        else:
            out.append(n)
    return out


class PanelLoad:
    def __init__(self, ws, W, kc0, nk, col0, ncols):
        cx = ws.cx
        self.ws, self.W, self.kc0, self.col0, self.ncols = ws, W, kc0, col0, ncols
        pi = cx.nxt(ws.name + "_p", ws.npanel)
        self.pan = ws.panels[pi]
        self.pname = "%s_pan%d" % (ws.name, pi)
        self.pieces = []
        k = 0
        while k < nk:
            kp = min(ws.kpiece, nk - k)
            self.pieces.append((k, kp))
            k += kp
        self.i = 0
        self.result = (self.pan, PanelName(self.pname))

    @property
    def done(self):
        return self.i >= len(self.pieces)

    def step(self):
        ws, cx, P = self.ws, self.ws.cx, self.ws.cx.P
        k, kp = self.pieces[self.i]
        self.i += 1
        si = cx.nxt(ws.name + "_s", ws.nstage)
        stg = ws.stages[si]
        sname = "%s_stg%d" % (ws.name, si)
        W, kc0, col0, ncols, pan, pname = self.W, self.kc0, self.col0, self.ncols, self.pan, self.pname
        src = W[(kc0 + k) * 128:(kc0 + k + kp) * 128, col0:col0 + ncols].rearrange("(k p) n -> p k n", p=128)
        dst = stg[:, 0:kp, 0:ncols]
        P.dma("sp", lambda e: e.dma_start(out=dst, in_=src), reads=[], writes=[sname], key=sname)
        pdst = pan[:, k:k + kp, 0:ncols]
        ceng, cname = ("pool", pname + "_a") if cx.nxt(ws.name + "_ce", 2) == 0 else ("dve", pname + "_b")
        P.op(ceng, lambda e: e.tensor_copy(out=pdst, in_=dst), reads=[sname], writes=[cname])


class WStream:
    def __init__(self, cx, name, max_k, ncols, kpiece=4, npanel=2, nstage=3):
        self.cx, self.name = cx, name
        self.max_k, self.ncols, self.kpiece = max_k, ncols, kpiece
        self.panels = [cx.sb("%s_pan%d" % (name, i), (128, max_k, ncols), BF16) for i in range(npanel)]
        self.stages = [cx.sb("%s_stg%d" % (name, i), (128, kpiece, ncols), F32) for i in range(nstage)]
        self.npanel, self.nstage = npanel, nstage

    def begin(self, W, kc0, nk, col0, ncols):
        return PanelLoad(self, W, kc0, nk, col0, ncols)

    def load(self, W, kc0, nk, col0, ncols):
        pl = self.begin(W, kc0, nk, col0, ncols)
        while not pl.done:
            pl.step()
        return pl.result

    def load_bf16(self, src, nk, ncols, dep):
        cx, P = self.cx, self.cx.P
        pi = cx.nxt(self.name + "_p", self.npanel)
        pan = self.panels[pi]
        pname = "%s_pan%d" % (self.name, pi)
        P.dma("sp", lambda e: e.dma_start(out=pan[:, 0:nk, 0:ncols], in_=src), reads=[dep], writes=[PanelName(pname)],
              key=pname + "_ld")
        return pan, PanelName(pname)


class Prefetch:
    def __init__(self, thunks, ahead=2):
        self.thunks, self.ahead, self.res = list(thunks), ahead, []

    def get(self, i):
        while len(self.res) < min(i + 1 + self.ahead, len(self.thunks)):
            self.res.append(self.thunks[len(self.res)]())
        o = self.res[i]
        if isinstance(o, PanelLoad):
            while not o.done:
                o.step()
            return o.result
        return o

    def tick(self):
        for o in self.res:
            if isinstance(o, PanelLoad) and not o.done:
                o.step()
                return


def norm_transpose(cx, tag, src_blk, deps_blk, ntok_blocks, nfeat, gain_bc, gain_name, dstT, dst_name, dst_chunk0,
                   xt_bufs, ps_bufs, ident, small):
    P = cx.P
    nch = nfeat // 128
    sq, sqn = small["sq"]
    ss, ssn = small["ss"]
    for b in range(ntok_blocks):
        xi = cx.nxt(tag + "_xt", len(xt_bufs))
        xt, xname = xt_bufs[xi]
        srcb = src_blk(b)
        P.dma("sp", lambda e, xt=xt, srcb=srcb: e.dma_start(out=xt[:, 0:nfeat], in_=srcb), reads=deps_blk(b),
              writes=[xname], key=xname)
        P.op("act", lambda e, xt=xt: e.activation(out=sq[:, 0:nfeat], in_=xt[:, 0:nfeat], func=AF.Square,
                                                  accum_out=ss[:, 0:1]),
             reads=[xname], writes=[sqn, ssn])
        P.op("act", lambda e: e.activation(out=ss[:, 1:2], in_=ss[:, 0:1], func=AF.Sqrt, scale=1.0 / nfeat,
                                           bias=small["eps"][0][:, 0:1]),
             reads=[ssn, small["eps"][1]], writes=[ssn + "b"])
        P.op("dve", lambda e: e.reciprocal(out=ss[:, 2:3], in_=ss[:, 1:2]), reads=[ssn + "b"], writes=[ssn + "c"])
        P.op("dve", lambda e, xt=xt: e.scalar_tensor_tensor(out=xt[:, 0:nfeat], in0=xt[:, 0:nfeat], scalar=ss[:, 2:3],
                                                             in1=gain_bc[:, 0:nfeat], op0=ALU.mult, op1=ALU.mult),
             reads=[xname, ssn + "c", gain_name], writes=[xname])
        for c0 in range(0, nch, 4):
            nc4 = min(4, nch - c0)
            pi = cx.nxt("nt_ps", len(ps_bufs))
            ps, psn = ps_bufs[pi]
            for j in range(nc4):
                P.op("pe", lambda e, ps=ps, xt=xt, j=j, c0=c0: e.transpose(ps[:, j * 128:(j + 1) * 128],
                                                                          xt[:, (c0 + j) * 128:(c0 + j + 1) * 128],
                                                                          ident[:]),
                     reads=[xname, "ident"], writes=[psn])
            dst = dstT[:, dst_chunk0 + c0:dst_chunk0 + c0 + nc4, b * 128:(b + 1) * 128]
            src_ps = ps[:, 0:nc4 * 128].rearrange("p (c t) -> p c t", c=nc4)
            P.op("act", lambda e, dst=dst, src_ps=src_ps: e.activation(out=dst, in_=src_ps, func=AF.Copy),
                 reads=[psn], writes=[dst_name])


D_MODEL, SEQ, D_FF = 4096, 8192, 11008
D_IN, D_IN_PAD = 9776, 9856
NM_OWN = SEQ // 128 // NCORES
NT_OWN = NM_OWN * 128
QSCALE = 128 ** -0.5
KC = D_MODEL // 128
C_QSB, C_KSB, C_VSB, C_QN = 0, 2048, 4096, 6144
C_KC, C_VC, C_KSL, C_VSL, C_KW, C_VW, C_G = 8192, 8448, 8704, 8960, 9216, 9472, 9728
KROW_SB, KROW_KC, KROW_VC, KROW_KSL, KROW_KW, KROWS = 0, 2048, 2304, 2560, 2816, 3072
VCOL_SB, VCOL_VSL, VCOL_VW, VCOLS = 0, 2048, 2304, 2560
QROW_SB, QROW_N, QROWS = 0, 2048, 4096


def own_chunk(m):
    return 8 * m + 7


def small_tiles(cx, sq_ap, sq_name):
    return {"sq": (sq_ap, sq_name), "ss": (cx.sb("ss", (128, 4), F32), "ss"),
            "eps": (cx.sb("epsc", (128, 1), F32), "epsc")}


def phase_1a(cx, io):
    P = cx.P
    xu, w, qT, gnd = io["xu"], io["w_in"], io["qT"], io["gn"]
    ident = cx.sb("ident_sb", (128, 128), F32)
    gain = cx.sb("gain_sb", (128, D_MODEL), F32)
    hT = cx.sb("hT", (128, KC, NT_OWN), BF16)
    xts = [(cx.sb("xt%d" % i, (128, D_MODEL), F32), "xt%d" % i) for i in range(2)]
    sqt = cx.sb("sq", (128, D_MODEL), BF16)
    small = small_tiles(cx, sqt, "sq")
    pss = [(cx.ps("ps%d" % i), "ps%d" % i) for i in range(8)]
    ws = WStream(cx, "w", KC, 256)
    ostg = [(cx.sb("ostg%d" % i, (128, NT_OWN), BF16), "ostg%d" % i) for i in range(2)]
    gstg = [(cx.sb("gstg%d" % i, (128, 128), F32), "gstg%d" % i) for i in range(2)]

    P.dma("sp", lambda e: e.dma_start(out=ident[:], in_=io["ident"]), writes=["ident"], key="c_ident")
    P.dma("sp", lambda e: e.dma_start(out=gain[:], in_=io["g_attn"]), writes=["gain"], key="c_gain")
    P.op("dve", lambda e: e.memset(small["eps"][0][:], EPS), writes=["epsc"])
    norm_transpose(cx, "l1", lambda b: xu[own_chunk(b) * 128:(own_chunk(b) + 1) * 128, :], lambda b: [],
                   NM_OWN, D_MODEL, gain, "gain", hT, "hT", 0, xts, pss, ident, small)

    qpanels = [(c0 + pcol, r0 + pcol) for (c0, r0) in ((C_QSB, QROW_SB), (C_QN, QROW_N)) for pcol in range(0, 2048, 256)]
    pf = Prefetch([(lambda c=c: ws.load(w, 0, KC, c, 256)) for (c, _) in qpanels] +
                  [lambda: ws.load(w, 0, KC, C_G, 128)], ahead=1)
    for qi, (cabs, rabs) in enumerate(qpanels):
        pan, pname = pf.get(qi)
        for j0 in (0, 128):
            og, ogn = ostg[cx.nxt("ostg", 2)]
            for th in range(NT_OWN // 512):
                ps, psn = pss[cx.nxt("mm_ps", 8)]
                for k in range(KC):
                    P.op("pe", lambda e, ps=ps, pan=pan, k=k, j0=j0, th=th: e.matmul(
                        ps[:, 0:512], pan[:, k, j0:j0 + 128], hT[:, k, th * 512:(th + 1) * 512],
                        start=(k == 0), stop=(k == KC - 1)), reads=[pname, "hT"], writes=[psn])
                P.op("act", lambda e, og=og, ps=ps, th=th: e.activation(
                    out=og[:, th * 512:(th + 1) * 512], in_=ps[:, 0:512], func=AF.Copy, scale=QSCALE),
                    reads=[psn], writes=[ogn])
            row = rabs + j0
            P.dma("sp", lambda e, og=og, row=row: e.dma_start(out=qT[row:row + 128, :], in_=og[:]),
                  reads=[ogn], writes=[], key=ogn + "_st")
    pan, pname = pf.get(len(qpanels))
    for b in range(NM_OWN):
        ps, psn = pss[cx.nxt("mm_ps", 8)]
        for k in range(KC):
            P.op("pe", lambda e, ps=ps, k=k, b=b: e.matmul(ps[:, 0:128], hT[:, k, b * 128:(b + 1) * 128],
                                                           pan[:, k, 0:128], start=(k == 0), stop=(k == KC - 1)),
                 reads=[pname, "hT"], writes=[psn])
        gs, gsn = gstg[cx.nxt("gstg", 2)]
        P.op("act", lambda e, gs=gs, ps=ps: e.activation(out=gs[:], in_=ps[:, 0:128], func=AF.Copy), reads=[psn],
             writes=[gsn])
        P.dma("sp", lambda e, gs=gs, b=b: e.dma_start(out=gnd[b * 128:(b + 1) * 128, :], in_=gs[:]), reads=[gsn],
              writes=[], key=gsn + "_st")


K_PANELS = [(C_KSB + p, KROW_SB + p) for p in range(0, 2048, 256)] + \
           [(C_KC, KROW_KC), (C_VC, KROW_VC), (C_KSL, KROW_KSL), (C_KW, KROW_KW)]
V_PANELS = [(C_VSB + p, VCOL_SB + p) for p in range(0, 2048, 256)] + [(C_VSL, VCOL_VSL), (C_VW, VCOL_VW)]


def phase_1b(cx, io):
    P = cx.P
    xu, w, kT, vtok, wbf = io["xu"], io["w_in"], io["kT"], io["vtok"], io["wbf"]
    NTILE = SEQ // 1024
    ident = cx.sb("ident_sb", (128, 128), F32)
    gain = cx.sb("gain_sb", (128, D_MODEL), F32)
    hT = cx.sb("hT", (128, KC, 1024), BF16)
    xts = [(cx.sb("xt%d" % i, (128, D_MODEL), F32), "xt%d" % i) for i in range(2)]
    sqt = cx.sb("sq", (128, D_MODEL), BF16)
    small = small_tiles(cx, sqt, "sq")
    pss = [(cx.ps("ps%d" % i), "ps%d" % i) for i in range(8)]
    ws = WStream(cx, "w", KC, 256, npanel=3)
    ostg = [(cx.sb("ostg%d" % i, (128, 1024), BF16), "ostg%d" % i) for i in range(2)]
    vstg = [(cx.sb("vstg%d" % i, (128, 8, 256), BF16), "vstg%d" % i) for i in range(2)]

    P.dma("sp", lambda e: e.dma_start(out=ident[:], in_=io["ident"]), writes=["ident"], key="c_ident")
    P.dma("sp", lambda e: e.dma_start(out=gain[:], in_=io["g_attn"]), writes=["gain"], key="c_gain")
    P.op("dve", lambda e: e.memset(small["eps"][0][:], EPS), writes=["epsc"])
    panels = [("k",) + p for p in K_PANELS] + [("v",) + p for p in V_PANELS]

    def first_load(pi_, c0):
        pan, pname = ws.load(w, 0, KC, c0, 256)
        P.dma("sp", lambda e: e.dma_start(out=wbf[pi_], in_=pan[:, 0:KC, 0:256]), reads=[pname], writes=["wbf"],
              key="wbf_st")
        return pan, pname

    thunks = []
    for t in range(NTILE):
        for pi_, (kind, c0, r0) in enumerate(panels):
            if t == 0:
                thunks.append(lambda pi_=pi_, c0=c0: first_load(pi_, c0))
            else:
                thunks.append(lambda pi_=pi_: ws.load_bf16(wbf[pi_], KC, 256, "wbf"))
    pf = Prefetch(thunks, ahead=2)
    for t in range(NTILE):
        norm_transpose(cx, "l1", lambda b, t=t: xu[t * 1024 + b * 128:t * 1024 + (b + 1) * 128, :], lambda b: [],
                       8, D_MODEL, gain, "gain", hT, "hT", 0, xts, pss, ident, small)
        for pi_, (kind, c0, r0) in enumerate(panels):
            pan, pname = pf.get(t * len(panels) + pi_)
            if kind == "k":
                for j0 in (0, 128):
                    og, ogn = ostg[cx.nxt("ostg", 2)]
                    for th in range(2):
                        ps, psn = pss[cx.nxt("mm_ps", 8)]
                        for k in range(KC):
                            P.op("pe", lambda e, ps=ps, pan=pan, k=k, j0=j0, th=th: e.matmul(
                                ps[:, 0:512], pan[:, k, j0:j0 + 128], hT[:, k, th * 512:(th + 1) * 512],
                                start=(k == 0), stop=(k == KC - 1)), reads=[pname, "hT"], writes=[psn])
                        P.op("act", lambda e, og=og, ps=ps, th=th: e.activation(
                            out=og[:, th * 512:(th + 1) * 512], in_=ps[:, 0:512], func=AF.Copy),
                            reads=[psn], writes=[ogn])
                    row = r0 + j0
                    P.dma("sp", lambda e, og=og, row=row, t=t: e.dma_start(
                        out=kT[row:row + 128, t * 1024:(t + 1) * 1024], in_=og[:]), reads=[ogn], writes=[],
                        key=ogn + "_st")
            else:
                vs, vsn = vstg[cx.nxt("vstg", 2)]
                for b in range(8):
                    ps, psn = pss[cx.nxt("mm_ps", 8)]
                    for k in range(KC):
                        P.op("pe", lambda e, ps=ps, pan=pan, k=k, b=b: e.matmul(
                            ps[:, 0:256], hT[:, k, b * 128:(b + 1) * 128], pan[:, k, 0:256],
                            start=(k == 0), stop=(k == KC - 1)), reads=[pname, "hT"], writes=[psn])
                    P.op("act", lambda e, vs=vs, ps=ps, b=b: e.activation(out=vs[:, b, :], in_=ps[:, 0:256],
                                                                          func=AF.Copy), reads=[psn], writes=[vsn])
                dst = vtok[t * 1024:(t + 1) * 1024, r0:r0 + 256].rearrange("(b p) c -> p b c", p=128)
                P.dma("sp", lambda e, vs=vs, dst=dst: e.dma_start(out=dst, in_=vs[:]), reads=[vsn], writes=[],
                      key=vsn + "_st")


def sb_consts():
    j = np.arange(128)
    negU = np.where(j[:, None] >= j[None, :], -1.0, 0.0).astype(NPBF)
    tri = (j[:, None] < j[None, :]).astype(np.float32).astype(NPBF)
    negones = np.full((128, 1), -1.0, dtype=np.float32).astype(NPBF)
    return {"negU": negU, "tri_strict": tri, "negones": negones}


def nsa_slopes():
    h = np.arange(1, 17, dtype=np.float32)
    return (2.0 ** (-8.0 * h / 16)).astype(np.float32)


def nsa_tables(c, NM):
    NJ, NCU, NUC = 16 * NM, 64 * NM, 8 * NM
    NCC = max(1, NCU // 128)
    sl = nsa_slopes()
    i = np.arange(128)
    T = {}
    rel = np.arange(NUC)
    T["ak"] = (sl[None, :, None] * (i[:, None, None] - 64.0 - 128.0 * rel[None, None, :])).astype(np.float32)
    ac = np.zeros((128, 16, NM, NCC), np.float32)
    cm = np.zeros((128, NM, NCC, 128), np.float32)
    wm = np.zeros((128, NM, 5, 128), np.float32)
    bonus = np.zeros((128, NM, NJ), np.float32)
    tl = np.arange(128)
    for m in range(NM):
        u0 = 128 * (8 * m + 7)
        ut = u0 + tl
        for ncc in range(NCC):
            nu = 128 * ncc + i
            cend = 16 * nu + 31
            ac[:, :, m, ncc] = sl[None, :] * (cend[:, None] - (u0 + 64.0))
            cm[:, m, ncc, :] = ((cend[:, None] <= ut[None, :]) & (nu[:, None] >= 8 * (7 - c))).astype(np.float32)
        for r in range(5):
            uk = 128 * (8 * m + 3 + r) + i
            d = ut[None, :] - uk[:, None]
            wm[:, m, r, :] = ((d >= 0) & (d < 512) & (uk[:, None] >= 128 * (7 - c))).astype(np.float32)
        cur = ut // 64
        jj = np.arange(NJ)
        valid = (jj[None, :] <= cur[:, None]) & (jj[None, :] >= 2 * (7 - c))
        forced = (jj[None, :] == 2 * (7 - c)) | (jj[None, :] == cur[:, None]) | (jj[None, :] == cur[:, None] - 1)
        bonus[:, m, :] = np.where(valid, np.where(forced, 1e6, 0.0), -1e30)
    T["ac"] = np.minimum(ac, 45.0)
    T["cm"] = cm.astype(NPBF)
    T["wm"] = wm.astype(NPBF)
    T["bonus"] = bonus
    T["ex"] = np.broadcast_to((np.arange(NJ) >= 2 * (7 - c)).astype(np.float32), (128, NJ)).copy()
    T["tri_incl"] = (i[:, None] <= i[None, :]).astype(np.float32).astype(NPBF)
    T["exf"] = np.broadcast_to((np.arange(8) >= (7 - c)).astype(np.float32), (128, 8)).copy()
    E = np.zeros((NJ, NUC, 128), np.float32)
    for uc in range(NUC):
        for half in range(2):
            if 2 * uc + half < NJ:
                E[2 * uc + half, uc, half * 64:(half + 1) * 64] = 1.0
    T["E"] = E.astype(NPBF)
    caug = np.zeros((128, NCC, 1 + NJ), np.float32)
    caug[:, :, 0] = 1.0
    for ncc in range(NCC):
        for il in range(128):
            nu = 128 * ncc + il
            for j in range(NJ):
                for mm in range(4):
                    for nn in range(2):
                        if 4 * j - mm - nn == nu:
                            caug[il, ncc, 1 + j] += 1.0
    T["caug"] = caug.astype(NPBF)
    T["ident"] = np.eye(128, dtype=np.float32)
    return T


def _bcast_mid(ap2d, n):
    a = ap2d.ap
    return bass.AP(ap2d.tensor, ap2d.offset, [list(a[0]), [0, n], list(a[-1])])


def phase_2a(cx, io):
    P = cx.P
    S = SEQ
    NCH = S // 128
    kTd, vtokd, qTd, out = io["kT"], io["vtok"], io["qT"], io["osb"]
    negU = cx.sb("negU_sb", (128, 128), BF16)
    tri = cx.sb("tri_sb", (128, 128), BF16)
    negones = cx.sb("negones_sb", (128, 1), BF16)
    exf = cx.sb("exf_sb", (128, 8), F32)
    P.dma("sp", lambda e: e.dma_start(out=negU[:], in_=io["negU"]), writes=["negU"], key="ld_c0")
    P.dma("sp", lambda e: e.dma_start(out=tri[:], in_=io["tri_strict"]), writes=["tri"], key="ld_c1")
    P.dma("sp", lambda e: e.dma_start(out=negones[:], in_=io["negones"]), writes=["negones"], key="ld_c2")
    P.dma("sp", lambda e: e.dma_start(out=exf[:], in_=io["exf"]), writes=["exf"], key="ld_c3")
    HPC = 2
    sets = []
    for s_ in range(2):
        sets.append({
            "k": (cx.sb("kTb%d" % s_, (128, HPC, S), BF16), "kTb%d" % s_),
            "v": (cx.sb("vb%d" % s_, (128, HPC, NCH, 128), BF16), "vb%d" % s_),
            "q": (cx.sb("qb%d" % s_, (128, HPC, NT_OWN), BF16), "qb%d" % s_)})
    psZ = [[(cx.ps("psZ%d_%d" % (i, j)), "psZ%d_%d" % (i, j)) for j in range(2)] for i in range(HPC)]
    psC = [(cx.ps("psC%d" % i, (128, 8)), "psC%d" % i) for i in range(HPC)]
    psO = [(cx.ps("psO%d" % i), "psO%d" % i) for i in range(HPC)]
    esb = [[(cx.sb("esb%d_%d" % (i, j), (128, 512), F32), "esb%d_%d" % (i, j)) for j in range(2)] for i in range(HPC)]
    spb = [[(cx.sb("spb%d_%d" % (i, j), (128, 512), BF16), "spb%d_%d" % (i, j)) for j in range(2)] for i in range(HPC)]
    Ab = [[(cx.sb("Ab%d_%d" % (i, j), (128, 512), BF16), "Ab%d_%d" % (i, j)) for j in range(2)] for i in range(HPC)]
    acc = [(cx.sb("acc%d" % i, (128, 4, 128), F32), "acc%d" % i) for i in range(HPC * 2)]
    car = [(cx.sb("car%d" % i, (128, 4), F32), "car%d" % i) for i in range(HPC)]
    ecb = [(cx.sb("ec%d" % i, (128, 4), F32), "ec%d" % i) for i in range(HPC)]

    def stage_a(it):
        hh, uc, c0, q0, par = it["hh"], it["uc"], it["c0"], it["q0"], it["par"]
        (kT, kTn), (qT, qn) = it["k"], it["q"]
        pz, pzn = psZ[hh][par]
        es, esn = esb[hh][par]
        sp, spn = spb[hh][par]
        P.op("pe", lambda e: e.matmul(pz[:, c0:512], kT[:, hh, uc * 128:(uc + 1) * 128], qT[:, hh, q0 + c0:q0 + 512],
                                      start=True, stop=False), reads=[kTn, qn], writes=[pzn])
        P.op("act", lambda e: e.activation(out=es[:, c0:512], in_=pz[:, c0:512], func=AF.Exp), reads=[pzn],
             writes=[esn])
        P.op("act", lambda e: e.activation(out=sp[:, c0:512], in_=es[:, c0:512], func=AF.Ln, bias=1.0, scale=1.0),
             reads=[esn], writes=[spn])
        if it["diag"]:
            P.op("dve", lambda e: e.tensor_tensor(out=sp[:, c0:c0 + 128], in0=sp[:, c0:c0 + 128], in1=tri[:],
                                                  op=ALU.mult), reads=[spn, "tri"], writes=[spn])
        if uc < 7:
            P.op("dve", lambda e: e.tensor_scalar(out=sp[:, c0:512], in0=sp[:, c0:512], scalar1=exf[:, uc:uc + 1],
                                                  scalar2=None, op0=ALU.mult), reads=[spn, "exf"], writes=[spn])

    def stage_b(it):
        hh, c0, b0, par = it["hh"], it["c0"], it["b0"], it["par"]
        pz, pzn = psZ[hh][par]
        pc, pcn = psC[hh]
        sp, spn = spb[hh][par]
        A, An = Ab[hh][par]
        P.op("pe", lambda e: e.matmul(pz[:, c0:512], negU[:], sp[:, c0:512], start=False, stop=True),
             reads=[spn, "negU"], writes=[pzn])
        for b in range(b0, 4):
            P.op("pe", lambda e, b=b: e.matmul(pc[:, b:b + 1], sp[:, b * 128:(b + 1) * 128], negones[:], start=True,
                                               stop=True), reads=[spn, "negones"], writes=[pcn])
        P.op("act", lambda e: e.activation(out=A[:, c0:512], in_=pz[:, c0:512], func=AF.Exp), reads=[pzn],
             writes=[An])
        if it["diag"]:
            P.op("dve", lambda e: e.tensor_tensor(out=A[:, c0:c0 + 128], in0=A[:, c0:c0 + 128], in1=tri[:],
                                                  op=ALU.mult), reads=[An, "tri"], writes=[An])

    def stage_c(it):
        hh, uc, b0, par, uq = it["hh"], it["uc"], it["b0"], it["par"], it["uq"]
        (v, vn) = it["v"]
        pc, pcn = psC[hh]
        po, pon = psO[hh]
        A, An = Ab[hh][par]
        ac, acn = it["acc"]
        cr, crn = car[hh]
        ec, ecn = ecb[hh]
        for b in range(b0, 4):
            P.op("pe", lambda e, b=b: e.matmul(po[:, b * 128:(b + 1) * 128], A[:, b * 128:(b + 1) * 128],
                                               v[:, hh, uc, :], start=True, stop=True), reads=[An, vn], writes=[pon])
        for b in range(b0, 4):
            if uc == uq[b]:
                P.op("dve", lambda e, b=b: e.tensor_copy(out=ac[:, b, :], in_=po[:, b * 128:(b + 1) * 128]),
                     reads=[pon], writes=[acn])
            else:
                P.op("dve", lambda e, b=b: e.scalar_tensor_tensor(
                    out=ac[:, b, :], in0=po[:, b * 128:(b + 1) * 128], scalar=ec[:, b:b + 1], in1=ac[:, b, :],
                    op0=ALU.mult, op1=ALU.add), reads=[pon, ecn, acn], writes=[acn])
        if uc > 0:
            if it["diag"]:
                P.op("dve", lambda e: e.tensor_copy(out=cr[:, b0:b0 + 1], in_=pc[:, b0:b0 + 1]), reads=[pcn],
                     writes=[crn])
                if b0 + 1 < 4:
                    P.op("dve", lambda e: e.tensor_tensor(out=cr[:, b0 + 1:4], in0=cr[:, b0 + 1:4],
                                                          in1=pc[:, b0 + 1:4], op=ALU.add), reads=[pcn, crn],
                         writes=[crn])
            else:
                P.op("dve", lambda e: e.tensor_tensor(out=cr[:, b0:4], in0=cr[:, b0:4], in1=pc[:, b0:4], op=ALU.add),
                     reads=[pcn, crn], writes=[crn])
            P.op("act", lambda e: e.activation(out=ec[:, b0:4], in_=cr[:, b0:4], func=AF.Exp), reads=[crn],
                 writes=[ecn])
        if it["last"]:
            h, G = it["h"], it["G"]
            dst = out[512 * G:512 * G + 512, h * 128:(h + 1) * 128].rearrange("(b p) d -> p b d", p=128)
            P.dma("sp", lambda e: e.dma_start(out=dst, in_=ac[:]), reads=[acn], key=acn + "_st")

    for pair in range(16 // HPC):
        st = sets[pair % 2]
        (kT, kTn), (v, vn), (qT, qn) = st["k"], st["v"], st["q"]
        for hh in range(HPC):
            h = pair * HPC + hh
            P.dma("sp", lambda e, hh=hh, h=h, kT=kT: e.dma_start(
                out=kT[:, hh, :], in_=kTd[KROW_SB + 128 * h:KROW_SB + 128 * (h + 1), :]), writes=[kTn],
                key=kTn + "_%d" % hh)
            P.dma("sp", lambda e, hh=hh, h=h, v=v: e.dma_start(
                out=v[:, hh, :, :],
                in_=vtokd[:, VCOL_SB + 128 * h:VCOL_SB + 128 * (h + 1)].rearrange("(c i) d -> i c d", i=128)),
                writes=[vn], key=vn + "_%d" % hh)
            P.dma("sp", lambda e, hh=hh, h=h, qT=qT: e.dma_start(
                out=qT[:, hh, :], in_=qTd[QROW_SB + 128 * h:QROW_SB + 128 * (h + 1), :]), writes=[qn],
                key=qn + "_%d" % hh)
        items = []
        step = 0
        for G in range(NM_OWN // 4):
            uq = [own_chunk(4 * G + b) for b in range(4)]
            accs = {hh: acc[cx.nxt("acc%d" % hh, 2) * HPC + hh] for hh in range(HPC)}
            for uc in range(uq[3], -1, -1):
                b0 = min(b for b in range(4) if uq[b] >= uc)
                for hh in range(HPC):
                    items.append({"hh": hh, "h": pair * HPC + hh, "G": G, "uc": uc, "b0": b0, "c0": b0 * 128,
                                  "diag": uc == uq[b0], "par": step % 2, "uq": uq, "q0": 512 * G, "acc": accs[hh],
                                  "last": uc == 0, "k": (kT, kTn), "q": (qT, qn), "v": (v, vn)})
                step += 1
        n = len(items)
        for i in range(n + 2):
            if i < n:
                stage_a(items[i])
            if 0 <= i - 1 < n:
                stage_b(items[i - 1])
            if 0 <= i - 2 < n:
                stage_c(items[i - 2])


def phase_2b(cx, io):
    P = cx.P
    S, NM = SEQ, NM_OWN
    NJ, NCU, NUC = 16 * NM, 64 * NM, 8 * NM
    NCC = max(1, NCU // 128)
    NT = NM * 128
    NCV = 129 + NJ
    kTd, vtokd, qTd = io["kT"], io["vtok"], io["qT"]
    d_w1 = {"k": io["w_k1"], "v": io["w_v1"]}
    d_w2 = {"k": io["w_k2"], "v": io["w_v2"]}
    d_pos = {"k": io["posTk"], "v": io["posTv"]}
    d_ak, d_ac, d_cm, d_wm, d_tri = io["ak"], io["ac"], io["cm"], io["wm"], io["tri_incl"]
    d_bonus, d_ex, d_E, d_caug, d_ident = io["bonus"], io["ex"], io["E"], io["caug"], io["ident"]
    d_gn = io["gn"][:, 0:48].rearrange("(m p) c -> p m c", p=128)
    out = io["onsa"]

    def const(name, d, shape, dt):
        t = cx.sb(name + "_sb", shape, dt)
        P.dma("sp", lambda e: e.dma_start(out=t[:], in_=d), writes=[name], key="ld_" + name)
        return t

    ak = const("ak", d_ak, (128, 16, NUC), F32)
    ac = const("ac", d_ac, (128, 16, NM, NCC), F32)
    cm = const("cm", d_cm, (128, NM, NCC, 128), BF16)
    wm = const("wm", d_wm, (128, NM, 5, 128), BF16)
    tri = const("tri", d_tri, (128, 128), BF16)
    bonus = const("bonus", d_bonus, (128, NM, NJ), F32)
    ex = const("ex", d_ex, (128, NJ), F32)
    E = const("E", d_E, (NJ, NUC, 128), BF16)
    ident = const("ident", d_ident, (128, 128), F32)
    gn = const("gn", d_gn, (128, NM, 48), F32)

    qg = cx.sb("qg", (128, 8, NT), BF16)
    ksl = cx.sb("ksl", (128, S), BF16)
    vsl = cx.sb("vsl_sb", (128, NUC, 129), BF16)
    kw = cx.sb("kw", (128, NM, 5, 128), BF16)
    vw = cx.sb("vw_sb", (128, NM, 5, 129), BF16)
    cbuf = cx.sb("cbuf", (128, S + 16), BF16)
    kcmpT = cx.sb("kcmpT", (128, NCU), BF16)
    vcmp = cx.sb("vcmp", (128, NCC, NCV), BF16)
    gates = cx.sb("gates", (128, NM, 48), F32)
    gtmp = cx.sb("gtmp", (128, NM, 48), F32)
    posf = cx.sb("posf", (128, 32), F32)
    posb = cx.sb("posb", (128, 32), BF16)
    biasT = cx.sb("biasT", (128, 2), F32)
    HT = cx.sb("HT", (128, 2, NCU), BF16)
    ub = cx.sb("ub", (128, NCU), F32)
    u2 = cx.sb("u2", (128, NCU), F32)
    ws = WStream(cx, "w", 32, 256, kpiece=4, npanel=1, nstage=2)
    combs = [(cx.sb("comb%d" % i, (128, 8, 128), F32), "comb%d" % i) for i in range(2)]
    imp = cx.sb("imp", (128, NJ), F32)
    score = cx.sb("score", (128, NJ), F32)
    score2 = cx.sb("score2", (128, NJ), F32)
    m8 = cx.sb("m8", (128, 16), F32)
    sel = cx.sb("sel", (128, NJ), F32)
    selT = cx.sb("selT", (NJ, 128), BF16)
    Pts = [(cx.sb("Pt%d" % i, (128, 512), BF16), "Pt%d" % i) for i in range(2)]
    mts = [(cx.sb("mt%d" % i, (128, 128), BF16), "mt%d" % i) for i in range(2)]
    rts = [(cx.sb("rt%d" % i, (128, 4), F32), "rt%d" % i) for i in range(4)]
    psS = [(cx.ps("psS%d" % i), "psS%d" % i) for i in range(2)]
    psA = [(cx.ps("psA%d" % i), "psA%d" % i) for i in range(4)]
    psM = (cx.ps("psM"), "psM")
    psX = (cx.ps("psX"), "psX")

    P.op("pool", lambda e: e.memset(vsl[:, :, 128:129], 1.0), writes=["vsl"])
    P.op("pool", lambda e: e.memset(vw[:, :, :, 128:129], 1.0), writes=["vw"])
    P.op("pool", lambda e: e.memset(cbuf[:, S:S + 16], 0.0), writes=["cbuf"])
    P.op("act", lambda e: e.activation(out=gtmp[:], in_=gn[:], func=AF.Exp, scale=-1.0), reads=["gn"], writes=["gtmp"])
    P.op("dve", lambda e: e.tensor_scalar(out=gtmp[:], in0=gtmp[:], scalar1=1.0, scalar2=None, op0=ALU.add),
         reads=["gtmp"], writes=["gtmp"])
    P.op("dve", lambda e: e.reciprocal(out=gates[:], in_=gtmp[:]), reads=["gtmp"], writes=["gates"])

    def mlp(g, which):
        r0 = (KROW_KC if which == "k" else KROW_VC) + 128 * g
        P.dma("sp", lambda e: e.dma_start(out=cbuf[:, 0:S], in_=kTd[r0:r0 + 128, :]), writes=["cbuf"], key="ld_cbuf")
        P.dma("sp", lambda e: e.dma_start(out=posf[:], in_=d_pos[which]), writes=["posf"], key="ld_pos")
        P.op("pool", lambda e: e.tensor_copy(out=posb[:], in_=posf[:]), reads=["posf"], writes=["posb"])
        pan, pname = ws.load(d_w1[which], 0, 32, 0, 256)
        px, pxn = psX
        for hc in range(2):
            for l in range(32):
                P.op("pe", lambda e, hc=hc, l=l: e.matmul(px[:, hc:hc + 1], pan[:, l, hc * 128:(hc + 1) * 128],
                                                          posb[:, l:l + 1], start=(l == 0), stop=(l == 31)),
                     reads=[pname, "posb"], writes=[pxn])
            P.op("dve", lambda e, hc=hc: e.tensor_copy(out=biasT[:, hc:hc + 1], in_=px[:, hc:hc + 1]), reads=[pxn],
                 writes=["biasT"])
        for hc in range(2):
            ps, psn = psS[hc]
            for l in range(32):
                P.op("pe", lambda e, ps=ps, hc=hc, l=l: e.matmul(ps[:, 0:NCU], pan[:, l, hc * 128:(hc + 1) * 128],
                                                                cbuf[:, l:l + 16 * (NCU - 1) + 1:16], start=(l == 0),
                                                                stop=(l == 31)),
                     reads=[pname, "cbuf"], writes=[psn])
            P.op("act", lambda e, ps=ps, hc=hc: e.activation(out=ub[:], in_=ps[:, 0:NCU], func=AF.Identity,
                                                            bias=biasT[:, hc:hc + 1], scale=1.0),
                 reads=[psn, "biasT"], writes=["ub"])
            P.op("dve", lambda e: e.tensor_tensor(out=u2[:], in0=ub[:], in1=ub[:], op=ALU.mult), reads=["ub"],
                 writes=["u2"])
            P.op("dve", lambda e: e.tensor_scalar(out=u2[:], in0=u2[:], scalar1=0.044715, scalar2=1.0, op0=ALU.mult,
                                                  op1=ALU.add), reads=["u2"], writes=["u2"])
            P.op("dve", lambda e: e.tensor_tensor(out=u2[:], in0=u2[:], in1=ub[:], op=ALU.mult), reads=["u2", "ub"],
                 writes=["u2"])
            P.op("act", lambda e: e.activation(out=u2[:], in_=u2[:], func=AF.Exp, scale=-1.5957691216057308),
                 reads=["u2"], writes=["u2"])
            P.op("dve", lambda e: e.tensor_scalar(out=u2[:], in0=u2[:], scalar1=1.0, scalar2=None, op0=ALU.add),
                 reads=["u2"], writes=["u2"])
            P.op("dve", lambda e: e.reciprocal(out=u2[:], in_=u2[:]), reads=["u2"], writes=["u2"])
            P.op("dve", lambda e, hc=hc: e.tensor_tensor(out=HT[:, hc, :], in0=u2[:], in1=ub[:], op=ALU.mult),
                 reads=["u2", "ub"], writes=["HT"])
        pan2, p2name = ws.load(d_w2[which], 0, 2, 0, 128)
        if which == "k":
            ps, psn = psS[0]
            for hc in range(2):
                P.op("pe", lambda e, hc=hc: e.matmul(ps[:, 0:NCU], pan2[:, hc, 0:128], HT[:, hc, :], start=(hc == 0),
                                                     stop=(hc == 1)), reads=[p2name, "HT"], writes=[psn])
            P.op("act", lambda e: e.activation(out=kcmpT[:], in_=ps[:, 0:NCU], func=AF.Copy), reads=[psn],
                 writes=["kcmpT"])
        else:
            for ncc in range(NCC):
                ps, psn = psS[ncc % 2]
                for hc in range(2):
                    P.op("pe", lambda e, ps=ps, hc=hc, ncc=ncc: e.matmul(ps[:, 0:128],
                                                                        HT[:, hc, ncc * 128:(ncc + 1) * 128],
                                                                        pan2[:, hc, 0:128], start=(hc == 0),
                                                                        stop=(hc == 1)),
                         reads=[p2name, "HT"], writes=[psn])
                P.op("act", lambda e, ps=ps, ncc=ncc: e.activation(out=vcmp[:, ncc, 0:128], in_=ps[:, 0:128],
                                                                  func=AF.Copy), reads=[psn], writes=["vcmp"])

    def chunk(g, m, quad, keysT, bias_ap_fn, mask2d, mask_name, vrhs, vname, ncv, first, last, kname):
        si = cx.nxt("psS", 2)
        ps, psn = psS[si]
        pt, ptn = Pts[si]
        rhs = qg[:, 4 * quad:4 * quad + 4, m * 128:(m + 1) * 128]
        P.op("pe", lambda e: e.matmul(ps[:, 0:512].rearrange("p (h t) -> p h t", h=4), keysT, rhs, start=True,
                                      stop=True), reads=[kname, "qg"], writes=[psn])
        for h in range(4):
            b = bias_ap_fn(8 * g + 4 * quad + h)
            P.op("act", lambda e, h=h, b=b: e.activation(out=pt[:, h * 128:(h + 1) * 128],
                                                          in_=ps[:, h * 128:(h + 1) * 128], func=AF.Exp, bias=b,
                                                          scale=1.0), reads=[psn, "ak", "ac"], writes=[ptn])
        if mask2d is not None:
            pt3 = pt[:, 0:512].rearrange("p (h t) -> p h t", h=4)
            P.op("dve", lambda e: e.tensor_tensor(out=pt3, in0=pt3, in1=_bcast_mid(mask2d, 4), op=ALU.mult),
                 reads=[ptn, mask_name], writes=[ptn])
        def s2():
            for h in range(4):
                pa, pan_ = psA[h]
                P.op("pe", lambda e, h=h, pa=pa: e.matmul(pa[:, 0:ncv], pt[:, h * 128:(h + 1) * 128], vrhs,
                                                          start=first, stop=last), reads=[ptn, vname], writes=[pan_])
        return s2

    def evac(g, m, quad, br, comb, cname, first_branch, do_imp):
        for h in range(4):
            pa, pan_ = psA[h]
            hl = 4 * quad + h
            head = 8 * g + hl
            rt, rtn = rts[cx.nxt("rt", 4)]
            P.op("dve", lambda e, pa=pa, rt=rt: e.tensor_scalar(out=rt[:, 0:1], in0=pa[:, 128:129], scalar1=1e-30,
                                                               scalar2=None, op0=ALU.max), reads=[pan_],
                 writes=[rtn])
            P.op("dve", lambda e, rt=rt: e.reciprocal(out=rt[:, 1:2], in_=rt[:, 0:1]), reads=[rtn], writes=[rtn])
            col = head * 3 + br
            P.op("dve", lambda e, rt=rt, col=col: e.tensor_tensor(out=rt[:, 2:3], in0=rt[:, 1:2],
                                                                 in1=gates[:, m, col:col + 1], op=ALU.mult),
                 reads=[rtn, "gates"], writes=[rtn])
            if first_branch:
                P.op("dve", lambda e, pa=pa, rt=rt, hl=hl: e.tensor_scalar(out=comb[:, hl, :], in0=pa[:, 0:128],
                                                                          scalar1=rt[:, 2:3], scalar2=None,
                                                                          op0=ALU.mult), reads=[pan_, rtn],
                     writes=[cname])
            else:
                P.op("dve", lambda e, pa=pa, rt=rt, hl=hl: e.scalar_tensor_tensor(
                    out=comb[:, hl, :], in0=pa[:, 0:128], scalar=rt[:, 2:3], in1=comb[:, hl, :], op0=ALU.mult,
                    op1=ALU.add), reads=[pan_, rtn, cname], writes=[cname])
            if do_imp:
                if hl == 0:
                    P.op("dve", lambda e, pa=pa, rt=rt: e.tensor_scalar(out=imp[:], in0=pa[:, 129:129 + NJ],
                                                                       scalar1=rt[:, 1:2], scalar2=None,
                                                                       op0=ALU.mult), reads=[pan_, rtn],
                         writes=["imp"])
                else:
                    P.op("dve", lambda e, pa=pa, rt=rt: e.scalar_tensor_tensor(
                        out=imp[:], in0=pa[:, 129:129 + NJ], scalar=rt[:, 1:2], in1=imp[:], op0=ALU.mult,
                        op1=ALU.add), reads=[pan_, rtn, "imp"], writes=["imp"])

    class Skew:
        pending, after = None, []

        def chunk(self, s1):
            s2 = s1()
            self.flush()
            self.pending = s2

        def flush(self):
            if self.pending is not None:
                self.pending()
                self.pending = None
            for f in self.after:
                f()
            self.after = []

        def defer(self, f):
            if self.pending is None:
                f()
            else:
                self.after.append(f)

    sk = Skew()
    for g in range(2):
        sk.flush()
        P.dma("sp", lambda e, g=g: e.dma_start(
            out=qg[:], in_=qTd[QROW_N + 1024 * g:QROW_N + 1024 * (g + 1), :].rearrange("(h d) t -> d h t", d=128)),
            writes=["qg"], key="ld_qg")
        P.dma("sp", lambda e, g=g: e.dma_start(out=ksl[:], in_=kTd[KROW_KSL + 128 * g:KROW_KSL + 128 * (g + 1), :]),
              writes=["ksl"], key="ld_ksl")
        P.dma("sp", lambda e, g=g: e.dma_start(
            out=vsl[:, :, 0:128],
            in_=vtokd[:, VCOL_VSL + 128 * g:VCOL_VSL + 128 * (g + 1)].rearrange("(c i) d -> i c d", i=128)),
            writes=["vsl"], key="ld_vsl")
        for m in range(NM):
            u0 = (8 * m + 3) * 128
            P.dma("sp", lambda e, g=g, m=m, u0=u0: e.dma_start(
                out=kw[:, m, :, :],
                in_=kTd[KROW_KW + 128 * g:KROW_KW + 128 * (g + 1), u0:u0 + 640].rearrange("d (r i) -> d r i", i=128)),
                writes=["kw"], key="ld_kw")
            P.dma("sp", lambda e, g=g, m=m, u0=u0: e.dma_start(
                out=vw[:, m, :, 0:128],
                in_=vtokd[u0:u0 + 640, VCOL_VW + 128 * g:VCOL_VW + 128 * (g + 1)].rearrange("(r i) d -> i r d", i=128)),
                writes=["vw"], key="ld_vw")
        P.dma("sp", lambda e: e.dma_start(out=vcmp[:, :, 128:NCV], in_=d_caug), writes=["vcmp"], key="ld_caug")
        mlp(g, "k")
        mlp(g, "v")
        for m in range(NM):
            comb, cname = combs[cx.nxt("comb", 2)]
            nccs = list(range((64 * m + 62) // 128 + 1))
            for quad in range(2):
                for ii, ncc in enumerate(nccs):
                    sk.chunk(lambda quad=quad, ii=ii, ncc=ncc, m=m: chunk(
                        g, m, quad, kcmpT[:, ncc * 128:(ncc + 1) * 128],
                        lambda head, ncc=ncc: ac[:, head, m, ncc:ncc + 1],
                        cm[:, m, ncc, :], "cm", vcmp[:, ncc, :], "vcmp", NCV, ii == 0, ii == len(nccs) - 1, "kcmpT"))
                sk.defer(lambda g=g, quad=quad, m=m, comb=comb, cname=cname: evac(g, m, quad, 0, comb, cname, True, True))
            sk.flush()
            P.op("dve", lambda e, m=m: e.tensor_tensor(out=score[:], in0=imp[:], in1=bonus[:, m, :], op=ALU.add),
                 reads=["imp", "bonus"], writes=["score"])
            P.op("dve", lambda e: e.max(out=m8[:, 0:8], in_=score[:]), reads=["score"], writes=["m8"])
            P.op("dve", lambda e: e.match_replace(out=score2[:], in_to_replace=m8[:, 0:8], in_values=score[:],
                                                  imm_value=-3.0e38), reads=["score", "m8"], writes=["score2"])
            P.op("dve", lambda e: e.max(out=m8[:, 8:16], in_=score2[:]), reads=["score2"], writes=["m8"])
            P.op("dve", lambda e: e.tensor_scalar(out=sel[:], in0=score[:], scalar1=m8[:, 15:16], scalar2=None,
                                                  op0=ALU.is_ge), reads=["score", "m8"], writes=["sel"])
            P.op("dve", lambda e: e.tensor_tensor(out=sel[:], in0=sel[:], in1=ex[:], op=ALU.mult),
                 reads=["sel", "ex"], writes=["sel"])
            px, pxn = psX
            P.op("pe", lambda e: e.transpose(px[0:NJ, 0:128], sel[:], ident[:]), reads=["sel", "ident"], writes=[pxn])
            P.op("act", lambda e: e.activation(out=selT[:], in_=px[0:NJ, 0:128], func=AF.Copy), reads=[pxn],
                 writes=["selT"])
            for quad in range(2):
                ucs = list(range(8 * m + 8))
                for ii, uc in enumerate(ucs):
                    def s1(quad=quad, ii=ii, uc=uc, m=m, nuc=len(ucs)):
                        pm, pmn = psM if cx.nxt("psM", 2) == 0 else psX
                        mt, mtn = mts[cx.nxt("mt", 2)]
                        P.op("pe", lambda e: e.matmul(pm[:, 0:128], E[:, uc, :], selT[:], start=True, stop=True),
                             reads=["E", "selT"], writes=[pmn])
                        if uc == 8 * m + 7:
                            P.op("dve", lambda e: e.tensor_tensor(out=mt[:], in0=pm[:, 0:128], in1=tri[:],
                                                                  op=ALU.mult), reads=[pmn, "tri"], writes=[mtn])
                        else:
                            P.op("act", lambda e: e.activation(out=mt[:], in_=pm[:, 0:128], func=AF.Copy),
                                 reads=[pmn], writes=[mtn])
                        rel = 8 * m + 7 - uc
                        return chunk(g, m, quad, ksl[:, uc * 128:(uc + 1) * 128],
                                     lambda head, rel=rel: ak[:, head, rel:rel + 1],
                                     mt[:], mtn, vsl[:, uc, :], "vsl", 129, ii == 0, ii == nuc - 1, "ksl")
                    sk.chunk(s1)
                sk.defer(lambda g=g, quad=quad, m=m, comb=comb, cname=cname: evac(g, m, quad, 1, comb, cname, False, False))
            for quad in range(2):
                for r in range(5):
                    sk.chunk(lambda quad=quad, r=r, m=m: chunk(
                        g, m, quad, kw[:, m, r, :], lambda head, r=r: ak[:, head, 4 - r:5 - r],
                        wm[:, m, r, :], "wm", vw[:, m, r, :], "vw", 129, r == 0, r == 4, "kw"))
                sk.defer(lambda g=g, quad=quad, m=m, comb=comb, cname=cname: evac(g, m, quad, 2, comb, cname, False, False))
            dst = out[m * 128:(m + 1) * 128, g * 1024:(g + 1) * 1024]
            sk.defer(lambda comb=comb, dst=dst, cname=cname: P.dma(
                "sp", lambda e: e.dma_start(out=dst, in_=comb[:].rearrange("p h d -> p (h d)")),
                reads=[cname], key=cname + "_st"))
    sk.flush()


def phase_3(cx, io, NPASS=5):
    P = cx.P
    D, DFF, NTOK = D_MODEL, D_FF, NT_OWN
    NB = NTOK // 128
    HALF = D // 2
    xu, osb, onsa, out, x1d, yacc = io["xu"], io["osb"], io["onsa"], io["out"], io["x1d"], io["yacc"]
    w_out, w_gate, w_up, w_down = io["w_out"], io["w_gate"], io["w_up"], io["w_down"]
    xrow = lambda b: xu[own_chunk(b) * 128:(own_chunk(b) + 1) * 128, :]
    npan_ff = DFF // 256
    per = -(-npan_ff // NPASS)
    maxk_act = per * 2

    ident = cx.sb("ident_sb", (128, 128), F32)
    gain = cx.sb("gain_sb", (128, D), F32)
    hT = cx.sb("hT", (128, KC, NTOK), BF16)
    actT = cx.sb("actT", (128, maxk_act, NTOK), BF16)
    xts = [(cx.sb("xt0", (128, D), F32), "xt0")]
    ws = WStream(cx, "w", max(KC, maxk_act), 256, kpiece=4, npanel=3, nstage=3)
    sqv = actT[:].rearrange("p k n -> p (k n)")
    small = small_tiles(cx, sqv, "actT")
    thunks = [(lambda col=cp * 256: ws.begin(w_out, 0, KC, col, 256)) for cp in range(D // 256)]
    for p_ in range(NPASS):
        a0, a1 = p_ * per, min(npan_ff, p_ * per + per)
        if a0 >= a1:
            continue
        for pp_ in range(a0, a1):
            thunks.append(lambda col=pp_ * 256: ws.begin(w_gate, 0, KC, col, 256))
            thunks.append(lambda col=pp_ * 256: ws.begin(w_up, 0, KC, col, 256))
        for cp in range(D // 256):
            thunks.append(lambda col=cp * 256, a0=a0, a1=a1: ws.begin(w_down, a0 * 2, (a1 - a0) * 2, col, 256))
    pf = Prefetch(thunks, ahead=2)
    pfi = [0]

    def next_panel():
        r = pf.get(pfi[0])
        pfi[0] += 1
        return r

    pss = [(cx.ps("ps%d" % i), "ps%d" % i) for i in range(8)]
    ept = [(cx.sb("ept%d" % i, (128, 256), F32), "ept%d" % i) for i in range(4)]
    epo = [(cx.sb("epo%d" % i, (128, 256), F32), "epo%d" % i) for i in range(2)]
    sgt = [(cx.sb("sg%d" % i, (128, 512), F32), "sg%d" % i) for i in range(2)]

    P.dma("sp", lambda e: e.dma_start(out=ident[:], in_=io["ident"]), writes=["ident"], key="c_ident")
    P.op("dve", lambda e: e.memset(small["eps"][0][:], EPS), writes=["epsc"])

    P.dma("sp", lambda e: e.dma_start(out=gain[:, 0:HALF], in_=io["g_sb"]), writes=["gain"], key="c_gain")
    norm_transpose(cx, "sb", lambda b: osb[b * 128:(b + 1) * 128, :], lambda b: [], NB, HALF, gain, "gain", hT, "hT",
                   0, xts, pss, ident, small)
    P.dma("sp", lambda e: e.dma_start(out=gain[:, 0:HALF], in_=io["g_nsa"]), writes=["gain"], key="c_gain")
    norm_transpose(cx, "nsa", lambda b: onsa[b * 128:(b + 1) * 128, :], lambda b: [], NB, HALF, gain, "gain", hT, "hT",
                   KC // 2, xts, pss, ident, small)

    def tok_major_gemm(W, kc0, nk, actbuf, actname, prev_row, prev_name_fn, dst, dst_name_fn):
        for cp in range(D // 256):
            col = cp * 256
            pan, pname = next_panel()
            tiles = {}

            def issue_load(b, cp=cp, col=col):
                ti = cx.nxt("ept", 4)
                pt, ptn = ept[ti]
                srcp = prev_row(b)[:, col:col + 256]
                P.dma("sp", lambda e: e.dma_start(out=pt[:], in_=srcp), reads=[prev_name_fn(b, cp)], writes=[ptn],
                      key=ptn)
                tiles[b] = (pt, ptn)

            issue_load(0)
            issue_load(1)
            for b in range(NB):
                ps, psn = pss[cx.nxt("mm_ps", 8)]
                for k in range(nk):
                    P.op("pe", lambda e, ps=ps, pan=pan, k=k, b=b: e.matmul(
                        ps[:, 0:256], actbuf[:, k, b * 128:(b + 1) * 128], pan[:, k, 0:256],
                        start=(k == 0), stop=(k == nk - 1)), reads=[pname, actname], writes=[psn])
                if b + 2 < NB:
                    issue_load(b + 2)
                pt, ptn = tiles[b]
                po, pon = epo[cx.nxt("epo", 2)]
                P.op("dve", lambda e, po=po, ps=ps, pt=pt: e.tensor_tensor(out=po[:], in0=ps[:, 0:256], in1=pt[:],
                                                                          op=ALU.add),
                     reads=[psn, ptn], writes=[pon])
                dstp = dst[b * 128:(b + 1) * 128, col:col + 256]
                P.dma("sp", lambda e, po=po, dstp=dstp: e.dma_start(out=dstp, in_=po[:]),
                      reads=[pon], writes=[dst_name_fn(b, cp)], key=pon + "_st")
                pf.tick()

    tok_major_gemm(w_out, 0, KC, hT, "hT", xrow, lambda b, cp: "x_in", x1d, lambda b, cp: "x1d_%d_%d" % (b, cp))

    P.dma("sp", lambda e: e.dma_start(out=gain[:], in_=io["g_ffn"]), writes=["gain"], key="c_gain")
    norm_transpose(cx, "ffn", lambda b: x1d[b * 128:(b + 1) * 128, :],
                   lambda b: ["x1d_%d_%d" % (b, cp) for cp in range(D // 256)], NB, D, gain, "gain", hT, "hT", 0,
                   xts, pss, ident, small)

    TW = 512
    NTH = NTOK // TW
    lastp = 0
    for p in range(NPASS):
        pan0 = p * per
        pan1 = min(npan_ff, pan0 + per)
        if pan0 >= pan1:
            continue
        lastp = p
        for pp in range(pan0, pan1):
            col = pp * 256
            gpan, gname = next_panel()
            gps = {}
            for j in range(2):
                for th in range(NTH):
                    ps, psn = pss[cx.nxt("mm_ps", 8)]
                    gps[(j, th)] = (ps, psn)
                    for k in range(KC):
                        P.op("pe", lambda e, ps=ps, gpan=gpan, k=k, j=j, th=th: e.matmul(
                            ps[:, 0:TW], gpan[:, k, j * 128:(j + 1) * 128], hT[:, k, th * TW:(th + 1) * TW],
                            start=(k == 0), stop=(k == KC - 1)), reads=[gname, "hT"], writes=[psn])
                    pf.tick()
            upan, uname = next_panel()
            for j in range(2):
                for th in range(NTH):
                    ps, psn = pss[cx.nxt("mm_ps", 8)]
                    for k in range(KC):
                        P.op("pe", lambda e, ps=ps, upan=upan, k=k, j=j, th=th: e.matmul(
                            ps[:, 0:TW], upan[:, k, j * 128:(j + 1) * 128], hT[:, k, th * TW:(th + 1) * TW],
                            start=(k == 0), stop=(k == KC - 1)), reads=[uname, "hT"], writes=[psn])
                    pf.tick()
                    gp, gpn = gps[(j, th)]
                    sg, sgn = sgt[cx.nxt("sg", 2)]
                    P.op("act", lambda e, sg=sg, gp=gp: e.activation(out=sg[:, 0:TW], in_=gp[:, 0:TW], func=AF.Silu),
                         reads=[gpn], writes=[sgn])
                    kk = (pp - pan0) * 2 + j
                    P.op("dve", lambda e, sg=sg, ps=ps, kk=kk, th=th: e.tensor_tensor(
                        out=actT[:, kk, th * TW:(th + 1) * TW], in0=sg[:, 0:TW], in1=ps[:, 0:TW], op=ALU.mult),
                        reads=[sgn, psn], writes=["actT"])
        nk = (pan1 - pan0) * 2
        if p == 0:
            prow, pfn = (lambda b: x1d[b * 128:(b + 1) * 128, :]), (lambda b, cp: "x1d_%d_%d" % (b, cp))
        else:
            prow, pfn = (lambda b: yacc[b * 128:(b + 1) * 128, :]), (lambda b, cp, p=p: "yacc%d_%d_%d" % (p - 1, b, cp))
        tok_major_gemm(w_down, pan0 * 2, nk, actT, "actT", prow, pfn, yacc,
                       lambda b, cp, p=p: "yacc%d_%d_%d" % (p, b, cp))

    P.dma("sp", lambda e: e.dma_start(out=gain[:], in_=io["g_fin"]), writes=["gain"], key="c_gain")
    sq, sqn = small["sq"]
    ss = small["ss"][0]
    for b in range(NB):
        xt, xname = xts[0]
        deps = ["yacc%d_%d_%d" % (lastp, b, cp) for cp in range(D // 256)]
        P.dma("sp", lambda e, b=b: e.dma_start(out=xt[:], in_=yacc[b * 128:(b + 1) * 128, :]), reads=deps,
              writes=[xname], key=xname)
        P.op("act", lambda e: e.activation(out=sq[:, 0:D], in_=xt[:], func=AF.Square, accum_out=ss[:, 0:1]),
             reads=[xname], writes=[sqn, "ss"])
        P.op("act", lambda e: e.activation(out=ss[:, 1:2], in_=ss[:, 0:1], func=AF.Sqrt, scale=1.0 / D,
                                           bias=small["eps"][0][:, 0:1]), reads=["ss", "epsc"], writes=["ssb"])
        P.op("dve", lambda e: e.reciprocal(out=ss[:, 2:3], in_=ss[:, 1:2]), reads=["ssb"], writes=["ssc"])
        P.op("dve", lambda e: e.scalar_tensor_tensor(out=xt[:], in0=xt[:], scalar=ss[:, 2:3], in1=gain[:],
                                                     op0=ALU.mult, op1=ALU.mult),
             reads=[xname, "ssc", "gain"], writes=[xname])
        P.dma("sp", lambda e, b=b: e.dma_start(out=out[b * 128:(b + 1) * 128, :], in_=xt[:]), reads=[xname],
              writes=[], key="out_st")


def _tables_spec():
    NM = NM_OWN
    NJ, NCU, NUC = 16 * NM, 64 * NM, 8 * NM
    NCC = NCU // 128
    return {"ak": ((128, 16, NUC), F32), "ac": ((128, 16, NM, NCC), F32), "cm": ((128, NM, NCC, 128), BF16),
            "wm": ((128, NM, 5, 128), BF16), "tri_incl": ((128, 128), BF16), "bonus": ((128, NM, NJ), F32),
            "ex": ((128, NJ), F32), "E": ((NJ, NUC, 128), BF16), "caug": ((128, NCC, 1 + NJ), BF16),
            "ident": ((128, 128), F32), "exf": ((128, 8), F32), "negU": ((128, 128), BF16),
            "tri_strict": ((128, 128), BF16), "negones": ((128, 1), BF16)}


def build_program(phases=("1a", "1b", "2a", "2b", "3")):
    cx = Ctx()
    io = {}
    io["xu"] = cx.din("xu", (SEQ, D_MODEL), F32)
    io["w_in"] = cx.din("w_in", (D_MODEL, D_IN_PAD), F32)
    for n in ("g_attn", "g_ffn", "g_fin"):
        io[n] = cx.din(n, (128, D_MODEL), F32)
    for n in ("g_sb", "g_nsa"):
        io[n] = cx.din(n, (128, D_MODEL // 2), F32)
    for n, (shape, dt) in _tables_spec().items():
        io[n] = cx.din(n, shape, dt)
    for n in ("w_k1", "w_v1"):
        io[n] = cx.din(n, (4096, 256), F32)
    for n in ("w_k2", "w_v2"):
        io[n] = cx.din(n, (256, 128), F32)
    for n in ("posTk", "posTv"):
        io[n] = cx.din(n, (128, 32), F32)
    io["w_out"] = cx.din("w_out", (D_MODEL, D_MODEL), F32)
    io["w_gate"] = cx.din("w_gate", (D_MODEL, D_FF), F32)
    io["w_up"] = cx.din("w_up", (D_MODEL, D_FF), F32)
    io["w_down"] = cx.din("w_down", (D_FF, D_MODEL), F32)
    io["out"] = cx.dout("out", (NT_OWN, D_MODEL), F32)
    io["qT"] = cx.dint("qT_scr", (QROWS, NT_OWN), BF16)
    io["gn"] = cx.dint("gn_scr", (NT_OWN, 128), F32)
    io["kT"] = cx.dint("kT_scr", (KROWS, SEQ), BF16)
    io["vtok"] = cx.dint("vtok_scr", (SEQ, VCOLS), BF16)
    io["wbf"] = cx.dint("wbf_scr", (len(K_PANELS) + len(V_PANELS), 128, KC, 256), BF16)
    io["osb"] = cx.dint("osb_scr", (NT_OWN, 2048), F32)
    io["onsa"] = cx.dint("onsa_scr", (NT_OWN, 2048), F32)
    io["x1d"] = cx.dint("x1d_scr", (NT_OWN, D_MODEL), F32)
    io["yacc"] = cx.dint("yacc_scr", (NT_OWN, D_MODEL), F32)
    fns = {"1a": phase_1a, "1b": phase_1b, "2a": phase_2a, "2b": phase_2b, "3": phase_3}
    first = True
    for ph in phases:
        if not first:
            cx.begin()
        first = False
        fns[ph](cx, io)
        cx.end()
    return cx.finish()


def _bc(g, n=128):
    g = np.asarray(g, np.float32).reshape(-1)
    return np.ascontiguousarray(np.broadcast_to(g, (n, g.shape[0])))


def _own_rows(c):
    return np.concatenate([np.arange(128) + 128 * (c + 8 * m) for m in range(NM_OWN)])


def kernel(x, attn_norm, w_in, pos_cmp_k, pos_cmp_v, w_cmp_k1, w_cmp_k2, w_cmp_v1, w_cmp_v2,
           norm_sb, norm_nsa, w_out, ffn_norm, w_gate, w_up, w_down, final_norm):
    f32 = lambda a: np.ascontiguousarray(np.asarray(a, np.float32))
    x2 = f32(x)[0]
    cores = list(range(NCORES))
    w_pad = np.zeros((D_MODEL, D_IN_PAD), np.float32)
    w_pad[:, :D_IN] = f32(w_in)[0]
    common = {"w_in": w_pad, "g_attn": _bc(attn_norm), "g_ffn": _bc(ffn_norm), "g_fin": _bc(final_norm),
              "g_sb": _bc(norm_sb), "g_nsa": _bc(norm_nsa),
              "w_k1": f32(w_cmp_k1)[0], "w_k2": f32(w_cmp_k2)[0], "w_v1": f32(w_cmp_v1)[0], "w_v2": f32(w_cmp_v2)[0],
              "posTk": np.ascontiguousarray(f32(pos_cmp_k)[0].T), "posTv": np.ascontiguousarray(f32(pos_cmp_v)[0].T),
              "w_out": f32(w_out)[0], "w_gate": f32(w_gate)[0], "w_up": f32(w_up)[0], "w_down": f32(w_down)[0]}
    common.update(sb_consts())
    in_maps = []
    for c in cores:
        d = dict(common)
        shift = 128 * (7 - c)
        xu = np.zeros((SEQ, D_MODEL), np.float32)
        xu[shift:] = x2[:SEQ - shift]
        d["xu"] = xu
        d.update(nsa_tables(c, NM_OWN))
        in_maps.append(d)
    nc = build_program()
    res = run_bass_kernel_spmd(nc, in_maps, core_ids=cores).results
    out = np.zeros((1, SEQ, D_MODEL), np.float32)
    for c in cores:
        out[0, _own_rows(c)] = np.asarray(res[c]["out"])
    return out
```
